# Optimizing a Trainium2 kernel written in Bass

```python
import jax, jax.numpy as jnp
from jax import lax
import numpy as np

D_MODEL = 1024
BATCH = 4
SEQ = 4096
DEPTH = 1

N_META = 16
GRID_W = 64
HEAD_DIM = 64
N_DIR = 2
RWKV_HEADS = 8
RWKV_WIDTH = RWKV_HEADS * HEAD_DIM
DECAY_LORA = 64
ICL_LORA = 64
GATE_LORA = 160
LNX_EPS = 64e-5
ATTN_Q_HEADS = 8
ATTN_KV_HEADS = 2
ATTN_GROUP = ATTN_Q_HEADS // ATTN_KV_HEADS
ATTN_Q_WIDTH = ATTN_Q_HEADS * HEAD_DIM
ATTN_KV_WIDTH = ATTN_KV_HEADS * HEAD_DIM
Q_BLOCK = 128
ROPE_THETA = 10000.0
AXIS_DIM = HEAD_DIM // 2
AXIS_FREQS = AXIS_DIM // 2
D_FF = 4 * D_MODEL
NORM_EPS = 1e-6
RWKV_IN_WIDTH = 3 * RWKV_WIDTH + N_DIR * DECAY_LORA + N_DIR * ICL_LORA + GATE_LORA
IN_WIDTH = RWKV_IN_WIDTH + ATTN_Q_WIDTH + 2 * ATTN_KV_WIDTH + 2 * D_MODEL
IN_SPLITS = [RWKV_IN_WIDTH,
             RWKV_IN_WIDTH + ATTN_Q_WIDTH,
             RWKV_IN_WIDTH + ATTN_Q_WIDTH + ATTN_KV_WIDTH,
             RWKV_IN_WIDTH + ATTN_Q_WIDTH + 2 * ATTN_KV_WIDTH,
             RWKV_IN_WIDTH + ATTN_Q_WIDTH + 2 * ATTN_KV_WIDTH + D_MODEL]
RWKV_SPLITS = [RWKV_WIDTH, 2 * RWKV_WIDTH, 3 * RWKV_WIDTH,
               3 * RWKV_WIDTH + N_DIR * DECAY_LORA,
               3 * RWKV_WIDTH + N_DIR * DECAY_LORA + N_DIR * ICL_LORA]

kernel_name = "hybrid_rwkv7_axial_gqa_gated_encoder"


def rms_norm(x, g, eps=NORM_EPS):
    xf = x.astype(jnp.float32)
    y = xf * lax.rsqrt(jnp.mean(xf * xf, axis=-1, keepdims=True) + eps)
    return (y * g.astype(jnp.float32)).astype(x.dtype)


def centred_shift(p, mu):
    prev = jnp.pad(p[:, :-1], ((0, 0), (1, 0), (0, 0)))
    nxt = jnp.pad(p[:, 1:], ((0, 0), (0, 1), (0, 0)))
    return p + mu[0] * (prev - p) + mu[1] * (nxt - p)


def to_dirs(fwd, bwd):
    B, L, _ = fwd.shape
    t = jnp.stack([fwd, jnp.flip(bwd, axis=1)], axis=0).astype(jnp.float32)
    return t.reshape(N_DIR, B, L, RWKV_HEADS, HEAD_DIM).transpose(2, 0, 1, 3, 4)


def rwkv7_step(S, inp):
    r, w, k, v, kk, kka = inp
    sa = jnp.einsum('dbhij,dbhj->dbhi', S, -kk)
    S = S * w[..., None, :] + sa[..., :, None] * kka[..., None, :] + v[..., :, None] * k[..., None, :]
    y = jnp.einsum('dbhij,dbhj->dbhi', S, r)
    return S, y


def rwkv7_mixer(p, shift_mu, w0, w2, a0, a2, g2, k_k, k_a, r_k, lnx_g, lnx_b):
    B, L, _ = p.shape
    z = centred_shift(p, shift_mu)
    r, k, v, wd, ad, gd = jnp.split(z, RWKV_SPLITS, axis=-1)
    wd = wd.reshape(B, L, N_DIR, DECAY_LORA)
    ad = ad.reshape(B, L, N_DIR, ICL_LORA)
    w_log = -jax.nn.softplus(-(w0 + jnp.einsum('bldr,drc->bldc', jnp.tanh(wd), w2))) - 0.5
    decay = jnp.exp(-jnp.exp(w_log.astype(jnp.float32)))
    a = jax.nn.sigmoid(a0 + jnp.einsum('bldr,drc->bldc', ad, a2))
    g = jax.nn.sigmoid(gd) @ g2
    kk = (k * k_k).reshape(B, L, RWKV_HEADS, HEAD_DIM).astype(jnp.float32)
    kk = kk / jnp.maximum(jnp.linalg.norm(kk, axis=-1, keepdims=True), 1e-12)
    kk = kk.reshape(B, L, RWKV_WIDTH)
    k_dir = k[:, :, None, :] * (1.0 + (a - 1.0) * k_a)
    kka = kk[:, :, None, :] * a
    xs = (to_dirs(r, r), to_dirs(decay[:, :, 0], decay[:, :, 1]),
          to_dirs(k_dir[:, :, 0], k_dir[:, :, 1]), to_dirs(v, v), to_dirs(kk, kk),
          to_dirs(kka[:, :, 0], kka[:, :, 1]))
    S0 = jnp.zeros((N_DIR, B, RWKV_HEADS, HEAD_DIM, HEAD_DIM), jnp.float32)
    _, ys = lax.scan(rwkv7_step, S0, xs)
    ys = ys.transpose(1, 2, 0, 3, 4)
    y = ys[0] + jnp.flip(ys[1], axis=1)
    mu = jnp.mean(y, axis=-1, keepdims=True)
    var = jnp.mean(jnp.square(y - mu), axis=-1, keepdims=True)
    y = (y - mu) * lax.rsqrt(var + LNX_EPS)
    y = y * lnx_g.reshape(RWKV_HEADS, HEAD_DIM) + lnx_b.reshape(RWKV_HEADS, HEAD_DIM)
    rh = r.reshape(B, L, RWKV_HEADS, HEAD_DIM).astype(jnp.float32)
    kh = jnp.mean(k_dir, axis=2).reshape(B, L, RWKV_HEADS, HEAD_DIM).astype(jnp.float32)
    vh = v.reshape(B, L, RWKV_HEADS, HEAD_DIM).astype(jnp.float32)
    bonus = jnp.sum(rh * kh * r_k, axis=-1, keepdims=True) * vh
    out = (y + bonus).reshape(B, L, RWKV_WIDTH).astype(p.dtype)
    return out * g


def axial_rope_tables(n_tokens):
    rows = n_tokens // GRID_W
    inv_freq = ROPE_THETA ** (-jnp.arange(AXIS_FREQS, dtype=jnp.float32) * 2.0 / AXIS_DIM)
    row_ang = jnp.arange(rows, dtype=jnp.float32)[:, None] * inv_freq
    col_ang = jnp.arange(GRID_W, dtype=jnp.float32)[:, None] * inv_freq
    grid = jnp.stack([jnp.broadcast_to(row_ang[:, None, :], (rows, GRID_W, AXIS_FREQS)),
                      jnp.broadcast_to(col_ang[None, :, :], (rows, GRID_W, AXIS_FREQS))], axis=2)
    grid = grid.reshape(rows * GRID_W, 2, AXIS_FREQS)
    ang = jnp.concatenate([jnp.zeros((N_META, 2, AXIS_FREQS), jnp.float32), grid], axis=0)
    return jnp.cos(ang), jnp.sin(ang)


def apply_axial_rope(x, cos, sin):
    B, L, H, _ = x.shape
    xf = x.astype(jnp.float32).reshape(B, L, H, 2, 2, AXIS_FREQS)
    x1, x2 = xf[..., 0, :], xf[..., 1, :]
    c, s = cos[None, :, None], sin[None, :, None]
    out = jnp.stack([x1 * c - x2 * s, x2 * c + x1 * s], axis=-2)
    return out.reshape(B, L, H, HEAD_DIM).astype(x.dtype)


def gqa_axial_attention(q, k, v, q_norm_g, k_norm_g, cos, sin):
    B, L, _ = q.shape
    q = rms_norm(q.reshape(B, L, ATTN_Q_HEADS, HEAD_DIM), q_norm_g)
    k = rms_norm(k.reshape(B, L, ATTN_KV_HEADS, HEAD_DIM), k_norm_g)
    q = apply_axial_rope(q, cos, sin)
    k = apply_axial_rope(k, cos, sin)
    v = v.reshape(B, L, ATTN_KV_HEADS, HEAD_DIM)
    q = q.reshape(B, L, ATTN_KV_HEADS, ATTN_GROUP, HEAD_DIM) * (HEAD_DIM ** -0.5)

    def block_attn(qb):
        s = jnp.einsum('bqhgn,bkhn->bhgqk', qb, k).astype(jnp.float32)
        pr = jax.nn.softmax(s, axis=-1).astype(v.dtype)
        return jnp.einsum('bhgqk,bkhn->bqhgn', pr, v)

    meta_out = block_attn(q[:, :N_META])
    n_blk = (L - N_META) // Q_BLOCK
    qr = q[:, N_META:].reshape(B, n_blk, Q_BLOCK, ATTN_KV_HEADS, ATTN_GROUP, HEAD_DIM)
    real_out = lax.map(block_attn, qr.transpose(1, 0, 2, 3, 4, 5))
    real_out = real_out.transpose(1, 0, 2, 3, 4, 5).reshape(B, L - N_META, ATTN_Q_WIDTH)
    return jnp.concatenate([meta_out.reshape(B, N_META, ATTN_Q_WIDTH), real_out], axis=1)


def setup_inputs(seed: int = 0) -> dict:
    key = jax.random.key(seed)
    ks = jax.random.split(key, 26)
    f32 = jnp.float32

    def nrm(k, shape, scale):
        return jax.random.normal(k, shape, f32) * scale

    ramp = jnp.linspace(-6.5, -1.5, RWKV_WIDTH, dtype=f32)
    return {
        "x": nrm(ks[0], (BATCH, SEQ, D_MODEL), 1.0),
        "meta_tokens": nrm(ks[1], (N_META, D_MODEL), 1.0),
        "mix_norm_g": 1.0 + nrm(ks[2], (DEPTH, D_MODEL), 0.02),
        "w_in": nrm(ks[3], (DEPTH, D_MODEL, IN_WIDTH), D_MODEL ** -0.5),
        "rwkv_shift": jax.random.uniform(ks[4], (DEPTH, 2, RWKV_IN_WIDTH), f32, 0.0, 0.5),
        "decay_w0": ramp + nrm(ks[5], (DEPTH, N_DIR, RWKV_WIDTH), 0.1),
        "decay_w2": nrm(ks[6], (DEPTH, N_DIR, DECAY_LORA, RWKV_WIDTH), 0.5 * DECAY_LORA ** -0.5),
        "icl_a0": nrm(ks[7], (DEPTH, N_DIR, RWKV_WIDTH), 0.1),
        "icl_a2": nrm(ks[8], (DEPTH, N_DIR, ICL_LORA, RWKV_WIDTH), 0.5 * ICL_LORA ** -0.5),
        "gate_w2": nrm(ks[9], (DEPTH, GATE_LORA, RWKV_WIDTH), GATE_LORA ** -0.5),
        "k_k": 0.85 + nrm(ks[10], (DEPTH, RWKV_WIDTH), 0.02),
        "k_a": 1.0 + nrm(ks[11], (DEPTH, RWKV_WIDTH), 0.02),
        "r_k": nrm(ks[12], (DEPTH, RWKV_HEADS, HEAD_DIM), 0.1),
        "lnx_g": 1.0 + nrm(ks[13], (DEPTH, RWKV_WIDTH), 0.02),
        "lnx_b": nrm(ks[14], (DEPTH, RWKV_WIDTH), 0.02),
        "q_norm_g": 1.0 + nrm(ks[15], (DEPTH, HEAD_DIM), 0.02),
        "k_norm_g": 1.0 + nrm(ks[16], (DEPTH, HEAD_DIM), 0.02),
        "w_branch_rwkv": nrm(ks[17], (DEPTH, RWKV_WIDTH, D_MODEL), RWKV_WIDTH ** -0.5),
        "w_branch_attn": nrm(ks[18], (DEPTH, ATTN_Q_WIDTH, D_MODEL), ATTN_Q_WIDTH ** -0.5),
        "w_out": nrm(ks[19], (DEPTH, D_MODEL, D_MODEL), D_MODEL ** -0.5),
        "ffn_norm_g": 1.0 + nrm(ks[20], (DEPTH, D_MODEL), 0.02),
        "w_ff1": nrm(ks[21], (DEPTH, D_MODEL, D_FF), D_MODEL ** -0.5),
        "w_ff2": nrm(ks[22], (DEPTH, D_FF, D_MODEL), D_FF ** -0.5),
        "final_norm_g": 1.0 + nrm(ks[23], (D_MODEL,), 0.02),
    }


def reference(x, meta_tokens, mix_norm_g, w_in, rwkv_shift, decay_w0, decay_w2, icl_a0, icl_a2,
              gate_w2, k_k, k_a, r_k, lnx_g, lnx_b, q_norm_g, k_norm_g, w_branch_rwkv,
              w_branch_attn, w_out, ffn_norm_g, w_ff1, w_ff2, final_norm_g):
    B, n_tok, D = x.shape
    meta = jnp.broadcast_to(meta_tokens[None].astype(x.dtype), (B, N_META, D))
    h = jnp.concatenate([meta, x], axis=1)
    cos, sin = axial_rope_tables(n_tok)
    for i in range(DEPTH):
        u = rms_norm(h, mix_norm_g[i])
        proj = u @ w_in[i]
        p_rwkv, p_q, p_k, p_v, p_gate_a, p_gate_b = jnp.split(proj, IN_SPLITS, axis=-1)
        y_a = rwkv7_mixer(p_rwkv, rwkv_shift[i], decay_w0[i], decay_w2[i], icl_a0[i], icl_a2[i],
                          gate_w2[i], k_k[i], k_a[i], r_k[i], lnx_g[i], lnx_b[i])
        y_b = gqa_axial_attention(p_q, p_k, p_v, q_norm_g[i], k_norm_g[i], cos, sin)
        merged = (jax.nn.sigmoid(p_gate_a) * (y_a @ w_branch_rwkv[i])
                  + jax.nn.sigmoid(p_gate_b) * (y_b @ w_branch_attn[i]))
        h = h + merged @ w_out[i]
        f = rms_norm(h, ffn_norm_g[i]) @ w_ff1[i]
        h = h + jnp.square(jax.nn.relu(f)) @ w_ff2[i]
    return rms_norm(h, final_norm_g)[:, N_META:]
```

```python
import numpy as np
import concourse.bass as bass
import concourse.mybir as mybir

F32 = mybir.dt.float32
BF16 = mybir.dt.bfloat16
AF = mybir.ActivationFunctionType
ALU = mybir.AluOpType
AX = mybir.AxisListType

ENGS = ["pe", "dve", "act", "pool", "sp"]
SEM_WRAP = 2048
NSLOT = 8


class Res:
    __slots__ = ("name", "w", "rd")

    def __init__(self, name):
        self.name = name
        self.w = None
        self.rd = []


class Rec:
    __slots__ = ("fn", "deps", "sig", "dma", "signo", "pre")

    def __init__(self, fn, dma):
        self.fn = fn
        self.deps = set()
        self.sig = False
        self.dma = dma
        self.signo = None
        self.pre = None


class FW:
    def __init__(self, nc, same_engine_sync=True):
        self.nc = nc
        self.eng = {"pe": nc.tensor, "dve": nc.vector, "act": nc.scalar,
                    "pool": nc.gpsimd, "sp": nc.sync}
        self.ops = {e: [] for e in ENGS}
        self.ndma = {e: 0 for e in ENGS}
        self.same = same_engine_sync

    def op(self, e, fn, reads=(), writes=(), dma=False):
        lst = self.ops[e]
        idx = len(lst)
        rec = Rec(fn, None)
        if dma:
            rec.dma = self.ndma[e]
            self.ndma[e] += 1
        key = (e, idx)
        deps = set()
        for r in reads:
            if r.w is not None:
                deps.add(r.w)
        for w in writes:
            if w.w is not None:
                deps.add(w.w)
            for k in w.rd:
                deps.add(k)
        for d in deps:
            de, di = d
            drec = self.ops[de][di]
            if drec.dma is None:
                if de == e and (not self.same) and e != "pool":
                    continue
                if de == e and e == "pe":
                    continue
                drec.sig = True
            rec.deps.add(d)
        for r in reads:
            r.rd.append(key)
        for w in writes:
            w.w = key
            w.rd = []
        lst.append(rec)
        return key

    def barrier(self, resources=()):
        b = Res("barrier")
        keys = []
        for e in ENGS:
            for i in range(len(self.ops[e]) - 1, -1, -1):
                if self.ops[e][i].fn is not None:
                    keys.append((e, i))
                    break
        dkeys = []
        for e in ENGS:
            n = 0
            for i in range(len(self.ops[e]) - 1, -1, -1):
                if self.ops[e][i].dma is not None:
                    dkeys.append((e, i))
                    n += 1
                    if n >= NSLOT:
                        break
        self._pending_barrier = keys + dkeys
        for e in ENGS:
            rec = Rec(None, None)
            for d in keys + dkeys:
                de, di = d
                drec = self.ops[de][di]
                if drec.dma is None:
                    if de == e:
                        continue
                    drec.sig = True
                rec.deps.add(d)
            self.ops[e].append(rec)

    def emit(self, final_waits=()):
        nc = self.nc
        nsig = {}
        for e in ENGS:
            n = 0
            for rec in self.ops[e]:
                if rec.dma is None and rec.sig:
                    n += 1
                    rec.signo = n
            nsig[e] = n
        import contextlib
        with contextlib.ExitStack() as st:
            sems = {}
            for e in ENGS:
                k = (nsig[e] + SEM_WRAP - 1) // SEM_WRAP
                sems[e] = [st.enter_context(nc.semaphore(f"s_{e}_{i}")) for i in range(max(k, 1))]
            dsem = {}
            for e in ENGS:
                if self.ndma[e]:
                    dsem[e] = [st.enter_context(nc.semaphore(f"d_{e}_{i}")) for i in range(NSLOT)]
            block = st.enter_context(nc.Block())

            def target(dep):
                de, di = dep
                drec = self.ops[de][di]
                if drec.dma is not None:
                    n = drec.dma
                    return (dsem[de][n % NSLOT], 16 * (n // NSLOT + 1))
                s = drec.signo
                return (sems[de][(s - 1) // SEM_WRAP], (s - 1) % SEM_WRAP + 1)

            def run(e, eng):
                waited = {}
                for rec in self.ops[e]:
                    tg = {}
                    for dep in rec.deps:
                        sem, val = target(dep)
                        kk = id(sem)
                        if kk not in tg or tg[kk][1] < val:
                            tg[kk] = (sem, val)
                    if rec.dma is not None and rec.dma >= NSLOT:
                        n = rec.dma
                        sem = dsem[e][n % NSLOT]
                        val = 16 * (n // NSLOT)
                        kk = id(sem)
                        if kk not in tg or tg[kk][1] < val:
                            tg[kk] = (sem, val)
                    for kk, (sem, val) in tg.items():
                        if waited.get(kk, 0) >= val:
                            continue
                        waited[kk] = val
                        eng.wait_ge(sem, val)
                    if rec.fn is None:
                        continue
                    inst = rec.fn(eng)
                    if rec.dma is not None:
                        inst.then_inc(dsem[e][rec.dma % NSLOT], 16)
                    elif rec.sig:
                        s = rec.signo
                        inst.then_inc(sems[e][(s - 1) // SEM_WRAP], 1)

            @block.tensor
            def _(eng):
                run("pe", eng)

            @block.vector
            def _(eng):
                run("dve", eng)

            @block.scalar
            def _(eng):
                run("act", eng)

            @block.gpsimd
            def _(eng):
                run("pool", eng)

            @block.sync
            def _(eng):
                run("sp", eng)

import contextlib

L = 4112
OWN = 2064
D = 1024
NCH = 8
INW = 4768
RW = 1952
EPS = 1e-6


class Tl:
    def __init__(self, t, name):
        self.t = t
        self.r = Res(name)

    def __getitem__(self, k):
        return self.t[k]


class KB:
    def __init__(self, nc, debug=None):
        self.nc = nc
        self.fw = FW(nc)
        self.st = contextlib.ExitStack()
        self.debug = debug or []
        self.dbg_out = {}
        self.out_keys = []
        self.cnt = 0
        import os
        self.use_r32 = os.environ.get('USE_R32', '0') == '1'

    def setup_mem(self):
        self.AW = 49000
        self.arena = self.st.enter_context(self.nc.sbuf_tensor("arena", [128, self.AW], F32))
        self.psum = self.st.enter_context(self.nc.psum_tensor("psum", [128, 4096], F32))
        self.top = 0
        self.hi = self.AW
        self.pbanks = [self._pview(i) for i in range(8)]
        self.pb_i = 0

    def _pview(self, i):
        return Tl(self.psum[:, 512 * i:512 * (i + 1)], f"bank{i}")

    def bank(self):
        b = self.pbanks[self.pb_i % 8]
        self.pb_i += 1
        return b

    def mark(self):
        return self.top

    def release(self, m):
        self.top = m

    def sb(self, shape, dt, name=None):
        self.cnt += 1
        name = f"{name or 't'}_{self.cnt}"
        p = shape[0]
        n = 1
        for x in shape[1:]:
            n *= x
        words = n if dt == F32 else (n + 1) // 2
        if getattr(self, 'from_top', False):
            self.hi -= words
            off = self.hi
        else:
            off = self.top
            self.top += words
        assert self.top <= self.hi, f"arena overflow {self.top} {self.hi}"
        ap = self.arena[0:p, off:off + words]
        if dt != F32:
            ap = ap.bitcast(dt)
            if n % 2:
                ap = ap[:, 0:n]
        if len(shape) == 3:
            ap = ap.rearrange("p (a b) -> p a b", a=shape[1])
        elif len(shape) == 4:
            ap = ap.rearrange("p (a b c) -> p a b c", a=shape[1], b=shape[2])
        return Tl(ap, name)

    def dram_in(self, name, shape, dt=F32):
        t = self.nc.dram_tensor(name, list(shape), dt, kind="ExternalInput")
        return Tl(t.ap(), name)

    def dram_out(self, name, shape, dt=F32):
        t = self.nc.dram_tensor(name, list(shape), dt, kind="ExternalOutput")
        return Tl(t.ap(), name)

    def dram_tmp(self, name, shape, dt=F32):
        t = self.nc.dram_tensor(name, list(shape), dt, kind="Internal")
        return Tl(t.ap(), name)

    def dma(self, out_ap, in_ap, reads, writes, q="sp", **kw):
        return self.fw.op(q, lambda e: e.dma_start(out=out_ap, in_=in_ap, **kw),
                          reads=[x.r for x in reads], writes=[x.r for x in writes], dma=True)

    def mm(self, out_ap, lhsT_ap, rhs_ap, reads, writes, start=True, stop=True, r32=False):
        if r32 and self.use_r32 and lhsT_ap.dtype == F32 and rhs_ap.dtype == F32:
            lhsT_ap = lhsT_ap.bitcast(mybir.dt.float32r)
            rhs_ap = rhs_ap.bitcast(mybir.dt.float32r)
        return self.fw.op("pe", lambda e: e.matmul(out_ap, lhsT_ap, rhs_ap, start=start, stop=stop),
                          reads=[x.r for x in reads], writes=[x.r for x in writes])

    def tr(self, out_ap, in_ap, ident_ap, reads, writes):
        return self.fw.op("pe", lambda e: e.transpose(out_ap, in_ap, ident_ap),
                          reads=[x.r for x in reads], writes=[x.r for x in writes])

    def act(self, out_ap, in_ap, func, reads, writes, **kw):
        return self.fw.op("act", lambda e: e.activation(out=out_ap, in_=in_ap, func=func, **kw),
                          reads=[x.r for x in reads], writes=[x.r for x in writes])

    def v(self, eng, meth, reads, writes, *a, **kw):
        return self.fw.op(eng, lambda e: getattr(e, meth)(*a, **kw),
                          reads=[x.r for x in reads], writes=[x.r for x in writes])

    def dump(self, name, tl, ap, shape, dt=F32):
        if name not in self.debug:
            return
        o = self.dram_out("dbg_" + name, shape, dt)
        k = self.dma(o.t, ap, [tl], [o])
        self.out_keys.append(k)
        self.dbg_out[name] = "dbg_" + name

    def finish(self):
        fw = self.fw
        rec = Rec(None, None)
        for k in self.out_keys:
            rec.deps.add(k)
        fw.ops["sp"].append(rec)
        fw.emit()
        self.st.close()

S0 = float(np.exp(-0.5))
RW_COLS = [(j * 64, 64) for j in range(28)] + [(1792, 128), (1920, 32)]
RW_TILES = len(RW_COLS)


def rwkv_load_w(kb, D_):
    wr = kb.sb([128, NCH, RW], BF16, "wr")
    for a in range(0, RW, 488):
        kb.dma(wr[:, :, a:a + 488], D_["w_in"].t[:, a:a + 488].rearrange("(c p) n -> p c n", p=128), [D_["w_in"]], [wr], q="pool")
    return wr


def rwkv_phase(kb, uT_d, ident, D_, ybwd_d, ya_d, wr):
    fw = kb.fw

    def ld(name, shape):
        t = kb.sb(shape, F32, name)
        src = D_[name]
        kb.dma(t.t, src.t, [src], [t])
        return t
    mu0 = ld("mu0", [128, RW_TILES]); mu1 = ld("mu1", [128, RW_TILES])
    muc = kb.sb([128, RW_TILES], F32, "muc")
    kb.v("dve", "tensor_tensor", [mu0, mu1], [muc], out=muc[:, :], in0=mu0[:, :], in1=mu1[:, :], op=ALU.add)
    kb.v("dve", "tensor_scalar", [muc], [muc], out=muc[:, :], in0=muc[:, :], scalar1=-1.0, scalar2=1.0, op0=ALU.mult, op1=ALU.add)
    w2 = ld("w2", [64, 2, 512]); a2 = ld("a2", [64, 2, 512])
    w0 = ld("w0", [64, 2, 8]); a0 = ld("a0", [64, 2, 8])
    g2a = ld("g2a", [128, 512]); g2b = ld("g2b", [32, 512])
    kkc = ld("kkc", [64, 8]); kac = ld("kac", [64, 8]); rkc = ld("rkc", [64, 8])
    lng = ld("lng", [64, 512]); lnb = ld("lnb", [64, 512])
    masks = ld("masks", [64, 4, 64])
    oka = kb.sb([64, 8], F32, "oka"); kah = kb.sb([64, 8], F32, "kah")
    kb.v("dve", "tensor_scalar", [kac], [oka], out=oka[:, :], in0=kac[:, :], scalar1=-1.0, scalar2=1.0, op0=ALU.mult, op1=ALU.add)
    kb.v("dve", "tensor_scalar", [kac], [kah], out=kah[:, :], in0=kac[:, :], scalar1=0.5, scalar2=None, op0=ALU.mult)
    ones = kb.sb([64, 64], F32, "ones")
    kb.v("dve", "memset", [], [ones], ones[:, :], 1.0)
    Hs = [kb.sb([64, 8, 64], F32, "Hf"), kb.sb([64, 8, 64], F32, "Hb")]

    def T(shape, name, dt=F32):
        return kb.sb(shape, dt, name)

    class Slot:
        pass

    def make_slot(si):
        S = Slot()
        S.B = kb.pbanks[4 * si:4 * si + 4]
        S.uc = T([128, NCH, 66], 'uc', BF16)
        S.sh = [T([128, 64], 'sh') for _ in range(4)]
        S.P = T([128, RW_TILES, 66], "P")
        S.Z = T([128, RW_TILES, 64], "Z")
        S.thw = T([64, 64], "thw")
        S.sg = T([64, 8, 64], "sg")
        S.A = [T([64, 8, 64], "A0"), T([64, 8, 64], "A1")]
        S.CS = T([64, 8, 65], "CS")
        kb.v("dve", "memset", [], [S.CS], S.CS[:, :, :], 0.0)
        S.G = T([64, 8, 64], "G")
        S.Gp = T([64, 8, 64], "Gp")
        S.Gi = T([64, 8, 64], "Gi")
        S.gC = T([64, 8], "gC")
        S.kkr = T([64, 8, 64], "kkr")
        S.ksq = T([64, 8, 64], "ksq")
        S.rn = T([64, 8, 64], "rn")
        S.X1 = S.ksq
        S.X2 = S.rn
        S.t1 = T([64, 8, 64], "t1")
        S.kdir = T([64, 8, 64], "kdir")
        S.kka = T([64, 8, 64], "kka")
        S.QR = T([64, 8, 128], "QR")
        S.KdT = T([64, 8, 64], "KdT")
        S.AdT = T([64, 8, 64], "AdT")
        S.Kq_t = T([64, 512], "Kq_t")
        S.Ad_t = T([64, 512], "Ad_t")
        S.Kd_t = T([64, 512], "Kd_t")
        S.V_t = T([64, 512], "V_t")
        S.NTs = [S.kkr, S.Gi]
        S.Ns = [S.ksq, S.Gp]
        S.MkT = S.G
        S.PaT = S.sg
        S.PkT = S.rn
        S.X = T([64, 8, 128], "X")
        S.AT = S.t1
        S.RqpT = S.kdir
        S.ybt = S.Kd_t
        S.yt = S.Ad_t
        S.ysq = S.Kq_t
        S.st = T([64, 48], "st")
        S.sgd = T([128, 2, 64], "sgd")
        S.ee = S.kka
        S.yout = S.ysq
        return S
    slots = [make_slot(0), make_slot(1)]

    RS, KS, VS = slice(0, 8), slice(8, 16), slice(16, 24)
    NSTAGE = 50

    def bc(ap, shape, axis):
        return ap.unsqueeze(axis).to_broadcast(shape)

    def v3(bank, p, a):
        return bank.t[0:p, 0:512].rearrange("p (a b) -> p a b", a=a)

    def chunk(S, c0, cn, d, need_out, H, epilogue):
        B = S.B
        P, Z, uc = S.P, S.Z, S.uc
        ns = 0
        wdt = 24 + d
        tiles = list(range(24)) + [wdt] + ([26, 27, 28, 29] if epilogue == "final" else [26 + d])
        lo, hi = max(c0 - 1, 0), min(c0 + cn + 1, L)
        n = hi - lo
        off = lo - (c0 - 1)
        kb.dma(uc[:, :, 0:n], uT_d.t[:, :, lo:hi], [uT_d], [uc])
        if off or (hi - (c0 - 1)) < cn + 2:
            kb.v("pool", "memset", [], [P], P[:, :, :], 0.0)
        groups = [tiles[i:i + 7] for i in range(0, len(tiles), 7)]
        for gi, grp in enumerate(groups):
            pb_ = B[gi % 4]
            pv = pb_.t[:, 0:7 * 66].rearrange("p (a b) -> p a b", a=7)
            for ji, j in enumerate(grp):
                cc, w = RW_COLS[j]
                for c in range(NCH):
                    kb.mm(pv[0:w, ji, 0:n], wr[:, c, cc:cc + w], uc[:, c, 0:n], [wr, uc], [pb_],
                          start=c == 0, stop=c == NCH - 1)
            j0, j1 = grp[0], grp[-1]
            if j1 < 28 and j1 - j0 == len(grp) - 1:
                kb.act(P[0:64, j0:j1 + 1, off:off + n], pv[0:64, 0:len(grp), 0:n], AF.Copy, [pb_], [P])
            else:
                for ji, j in enumerate(grp):
                    w = RW_COLS[j][1]
                    kb.act(P[0:w, j, off:off + n], pv[0:w, ji, 0:n], AF.Copy, [pb_], [P])
            ns += 1; yield
        while ns < 5:
            ns += 1; yield
        for j in tiles:
            w = RW_COLS[j][1]
            eng = "dve" if j % 3 else "pool"
            kb.act(Z[0:w, j, 0:cn], P[0:w, j, 1:cn + 1], AF.Copy, [P, muc], [Z], scale=muc[0:w, j:j + 1])
            ta, tb = S.sh[(2 * tiles.index(j)) % 4], S.sh[(2 * tiles.index(j) + 1) % 4]
            kb.act(ta[0:w, 0:cn], P[0:w, j, 0:cn], AF.Copy, [P, mu0], [ta], scale=mu0[0:w, j:j + 1])
            kb.act(tb[0:w, 0:cn], P[0:w, j, 2:cn + 2], AF.Copy, [P, mu1], [tb], scale=mu1[0:w, j:j + 1])
            kb.v("pool", "tensor_tensor", [Z, ta], [Z], out=Z[0:w, j, 0:cn], in0=Z[0:w, j, 0:cn], in1=ta[0:w, 0:cn], op=ALU.add)
            kb.v("pool", "tensor_tensor", [Z, tb], [Z], out=Z[0:w, j, 0:cn], in0=Z[0:w, j, 0:cn], in1=tb[0:w, 0:cn], op=ALU.add)
            if tiles.index(j) % 3 == 2:
                ns += 1; yield
        while ns < 15:
            ns += 1; yield
        thw, sg, A, CS = S.thw, S.sg, S.A, S.CS
        kb.act(thw[:, 0:cn], Z[0:64, wdt, 0:cn], AF.Tanh, [Z], [thw])
        pl = B[0]
        plv = v3(pl, 64, 8)
        for h in range(8):
            kb.mm(plv[:, h, 0:cn], w2[:, d, 64 * h:64 * h + 64], thw[:, 0:cn], [w2, thw], [pl], r32=True)
        for h in range(8):
            kb.act(sg[:, h, 0:cn], plv[:, h, 0:cn], AF.Sigmoid, [pl, w0], [sg], bias=w0[:, d, h:h + 1])
        ns += 1; yield
        for dd in ((0, 1) if epilogue == "final" else (d,)):
            pa = B[1 + dd]
            pav = v3(pa, 64, 8)
            for h in range(8):
                kb.mm(pav[:, h, 0:cn], a2[:, dd, 64 * h:64 * h + 64], Z[0:64, 26 + dd, 0:cn], [a2, Z], [pa], r32=True)
            for h in range(8):
                kb.act(A[dd][:, h, 0:cn], pav[:, h, 0:cn], AF.Sigmoid, [pa, a0], [A[dd]], bias=a0[:, dd, h:h + 1])
        ns += 1; yield
        Ad_ = A[d]
        sh = [64, 8, cn]
        kkr, ksq, rn, t1, kdir, kka = S.kkr, S.ksq, S.rn, S.t1, S.kdir, S.kka
        kb.v("pool", "tensor_tensor", [Z, kkc], [kkr], out=kkr[:, :, 0:cn], in0=Z[0:64, KS, 0:cn], in1=bc(kkc[:, :], sh, 2), op=ALU.mult)
        kb.act(ksq[:, :, 0:cn], kkr[:, :, 0:cn], AF.Square, [kkr], [ksq])
        pn = B[3]
        pnv = v3(pn, 64, 8)
        for h in range(8):
            kb.mm(pnv[:, h, 0:cn], ones[:, :], ksq[:, h, 0:cn], [ones, ksq], [pn], r32=True)
        kb.act(rn[:, :, 0:cn], pnv[:, :, 0:cn], AF.Ln, [pn], [rn], bias=1e-18, scale=1.0)
        kb.act(rn[:, :, 0:cn], rn[:, :, 0:cn], AF.Exp, [rn], [rn], scale=-0.5)
        kb.v("dve", "tensor_tensor", [kkr, rn], [kkr], out=kkr[:, :, 0:cn], in0=kkr[:, :, 0:cn], in1=rn[:, :, 0:cn], op=ALU.mult)
        ns += 1; yield
        kb.v("pool", "tensor_tensor", [Ad_, kac], [t1], out=t1[:, :, 0:cn], in0=Ad_[:, :, 0:cn], in1=bc(kac[:, :], sh, 2), op=ALU.mult)
        kb.v("pool", "tensor_tensor", [t1, oka], [t1], out=t1[:, :, 0:cn], in0=t1[:, :, 0:cn], in1=bc(oka[:, :], sh, 2), op=ALU.add)
        kb.v("pool", "tensor_tensor", [t1, Z], [kdir], out=kdir[:, :, 0:cn], in0=t1[:, :, 0:cn], in1=Z[0:64, KS, 0:cn], op=ALU.mult)
        kb.v("dve", "tensor_tensor", [kkr, Ad_], [kka], out=kka[:, :, 0:cn], in0=kkr[:, :, 0:cn], in1=Ad_[:, :, 0:cn], op=ALU.mult)
        ns += 1; yield
        for h in range(8):
            kb.v("dve", "tensor_tensor_scan", [ones, sg], [CS], out=CS[:, h, 1:cn + 1], data0=ones[:, 0:cn], data1=sg[:, h, 0:cn],
                 initial=0.0, op0=ALU.mult, op1=ALU.add)
        ns += 1; yield
        G, Gp, Gi, gC, X1, X2 = S.G, S.Gp, S.Gi, S.gC, S.X1, S.X2
        if d == 0:
            kb.act(G[:, :, 0:cn], CS[:, :, 1:cn + 1], AF.Exp, [CS], [G], scale=-S0)
            kb.act(Gp[:, :, 0:cn], CS[:, :, 0:cn], AF.Exp, [CS], [Gp], scale=-S0)
            kb.act(Gi[:, :, 0:cn], CS[:, :, 1:cn + 1], AF.Exp, [CS], [Gi], scale=S0)
        else:
            totb = bc(CS[:, :, cn], sh, 2)
            kb.v("dve", "tensor_tensor", [CS], [X1], out=X1[:, :, 0:cn], in0=CS[:, :, 0:cn], in1=totb, op=ALU.subtract)
            kb.v("dve", "tensor_tensor", [CS], [X2], out=X2[:, :, 0:cn], in0=CS[:, :, 1:cn + 1], in1=totb, op=ALU.subtract)
            kb.act(G[:, :, 0:cn], X1[:, :, 0:cn], AF.Exp, [X1], [G], scale=S0)
            kb.act(Gp[:, :, 0:cn], X2[:, :, 0:cn], AF.Exp, [X2], [Gp], scale=S0)
            kb.act(Gi[:, :, 0:cn], X1[:, :, 0:cn], AF.Exp, [X1], [Gi], scale=-S0)
        kb.act(gC[:, :], CS[:, :, cn], AF.Exp, [CS], [gC], scale=-S0)
        ns += 1; yield
        QR, KdT, AdT = S.QR, S.KdT, S.AdT
        kb.v("dve", "tensor_tensor", [kkr, Gp], [QR], out=QR[:, :, 0:cn], in0=kkr[:, :, 0:cn], in1=Gp[:, :, 0:cn], op=ALU.mult)
        kb.v("pool", "tensor_tensor", [Z, G], [QR], out=QR[:, :, cn:2 * cn], in0=Z[0:64, RS, 0:cn], in1=G[:, :, 0:cn], op=ALU.mult)
        kb.v("pool", "tensor_tensor", [kdir, Gi], [KdT], out=KdT[:, :, 0:cn], in0=kdir[:, :, 0:cn], in1=Gi[:, :, 0:cn], op=ALU.mult)
        kb.v("dve", "tensor_tensor", [kka, Gi], [AdT], out=AdT[:, :, 0:cn], in0=kka[:, :, 0:cn], in1=Gi[:, :, 0:cn], op=ALU.mult)
        ns += 1; yield
        Kq_t, Ad_t, Kd_t, V_t = S.Kq_t, S.Ad_t, S.Kd_t, S.V_t
        for bi, (src, si, dst) in enumerate(((QR, 0, Kq_t), (AdT, 0, Ad_t), (KdT, 0, Kd_t), (Z, 16, V_t))):
            pt = B[bi]
            for h in range(8):
                kb.tr(pt[0:cn, h * 64:(h + 1) * 64], src[0:64, si + h, 0:cn], ident[0:64, 0:64], [src, ident], [pt])
            if bi % 2 == 0:
                kb.act(dst[0:cn, :], pt[0:cn, :], AF.Copy, [pt], [dst])
            else:
                kb.v("dve", "tensor_copy", [pt], [dst], out=dst[0:cn, :], in_=pt[0:cn, :])
                ns += 1; yield
        mS, mI, mN = (0, 1, 2) if d == 0 else (2, 3, 0)
        rw = 2 * cn if need_out else cn
        pSA = [B[0], B[1]]; pSK = [B[2], B[3]]
        NTs, Ns, MkT, PaT, PkT, X = S.NTs, S.Ns, S.MkT, S.PaT, S.PkT, S.X
        for h in range(8):
            hv = pSA[h // 4].t[0:cn, 0:4 * rw].rearrange("p (h s) -> p h s", h=4)
            kb.mm(hv[:, h % 4, :], AdT[:, h, 0:cn], QR[:, h, 0:rw], [AdT, QR], [pSA[h // 4]], r32=True)
            hv2 = pSK[h // 4].t[0:cn, 0:4 * rw].rearrange("p (h s) -> p h s", h=4)
            kb.mm(hv2[:, h % 4, :], KdT[:, h, 0:cn], QR[:, h, 0:rw], [KdT, QR], [pSK[h // 4]], r32=True)
        NT, Nm = NTs[0], Ns[0]
        for half in range(2):
            hv = pSA[half].t[0:cn, 0:4 * rw].rearrange("p (h s) -> p h s", h=4)
            hv2 = pSK[half].t[0:cn, 0:4 * rw].rearrange("p (h s) -> p h s", h=4)
            hs = slice(4 * half, 4 * half + 4)
            kb.v("dve", "tensor_tensor", [pSA[half], masks], [NT], out=NT[0:cn, hs, 0:cn], in0=hv[:, :, 0:cn],
                 in1=bc(masks[0:cn, mS, 0:cn], [cn, 4, cn], 1), op=ALU.mult)
            kb.v("dve", "tensor_tensor", [pSK[half], masks], [MkT], out=MkT[0:cn, hs, 0:cn], in0=hv2[:, :, 0:cn],
                 in1=bc(masks[0:cn, mS, 0:cn], [cn, 4, cn], 1), op=ALU.mult)
            if need_out:
                kb.v("dve", "tensor_tensor", [pSA[half], masks], [PaT], out=PaT[0:cn, hs, 0:cn], in0=hv[:, :, cn:2 * cn],
                     in1=bc(masks[0:cn, mI, 0:cn], [cn, 4, cn], 1), op=ALU.mult)
                kb.v("dve", "tensor_tensor", [pSK[half], masks], [PkT], out=PkT[0:cn, hs, 0:cn], in0=hv2[:, :, cn:2 * cn],
                     in1=bc(masks[0:cn, mI, 0:cn], [cn, 4, cn], 1), op=ALU.mult)
        ns += 1; yield
        pN = B[0]
        pNv = pN.t[0:cn, 0:8 * cn].rearrange("p (h s) -> p h s", h=8)
        pW = B[1]
        pWv = v3(pW, cn, 8)
        for h in range(8):
            kb.mm(pNv[:, h, :], QR[:, h, 0:cn], AdT[:, h, 0:cn], [QR, AdT], [pN], r32=True)
        for h in range(8):
            kb.mm(pWv[:, h, :], MkT[0:cn, h, 0:cn], V_t[0:cn, 64 * h:64 * h + 64], [MkT, V_t], [pW], r32=True)
        kb.v("dve", "tensor_tensor", [pN, masks], [Nm], out=Nm[0:cn, :, 0:cn], in0=pNv,
             in1=bc(masks[0:cn, mN, 0:cn], [cn, 8, cn], 1), op=ALU.mult)
        kb.v("pool", "tensor_copy", [Kq_t], [X], out=X[0:cn, :, 0:64], in_=Kq_t[0:cn, :].rearrange("p (h v) -> p h v", h=8))
        kb.act(X[0:cn, :, 64:128], pWv, AF.Copy, [pW], [X], scale=-1.0)
        ns += 1; yield
        nlev = 5 if cn > 32 else (4 if cn > 16 else 3)
        pF = [B[2], B[3]]

        def factor(PT_, sign):
            for h in range(8):
                fv = v3(pF[h // 4], cn, 4)
                kb.mm(fv[:, h % 4, :], PT_[0:cn, h, 0:cn], X[0:cn, h, :], [PT_, X], [pF[h // 4]], r32=True)
            for half in range(2):
                fv = v3(pF[half], cn, 4)
                hs = slice(4 * half, 4 * half + 4)
                kb.v("dve", "tensor_tensor", [X, pF[half]], [X], out=X[0:cn, hs, :], in0=X[0:cn, hs, :], in1=fv,
                     op=ALU.subtract if sign < 0 else ALU.add)
        factor(NT, -1)
        ns += 1; yield
        cur = 0
        for lvl in range(5):
            if lvl < nlev:
                NTc, Nc = NTs[cur], Ns[cur]
                NTn, Nn = NTs[1 - cur], Ns[1 - cur]
                pa_, pb2 = B[0], B[1]
                pav = pa_.t[0:cn, 0:8 * cn].rearrange("p (h s) -> p h s", h=8)
                pbv = pb2.t[0:cn, 0:8 * cn].rearrange("p (h s) -> p h s", h=8)
                last = lvl == nlev - 1
                for h in range(8):
                    kb.mm(pav[:, h, :], Nc[0:cn, h, 0:cn], NTc[0:cn, h, 0:cn], [Nc, NTc], [pa_], r32=True)
                    if not last:
                        kb.mm(pbv[:, h, :], NTc[0:cn, h, 0:cn], Nc[0:cn, h, 0:cn], [Nc, NTc], [pb2], r32=True)
                kb.act(NTn[0:cn, :, 0:cn], pav, AF.Copy, [pa_], [NTn])
                if not last:
                    kb.act(Nn[0:cn, :, 0:cn], pbv, AF.Copy, [pb2], [Nn])
            ns += 1; yield
            if lvl < nlev:
                factor(NTn, +1)
                cur = 1 - cur
            ns += 1; yield
        AT, RqpT = S.AT, S.RqpT
        pA = B[0]
        pAv = v3(pA, 64, 8)
        for h in range(8):
            kb.mm(pAv[:, h, :], X[0:cn, h, 0:64], Ad_t[0:cn, 64 * h:64 * h + 64], [X, Ad_t], [pA], r32=True)
        kb.v("dve", "tensor_tensor", [ident, pA], [AT], out=AT[:, :, :], in0=bc(ident[0:64, 0:64], [64, 8, 64], 1), in1=pAv, op=ALU.subtract)
        if need_out:
            pR = B[1]
            pRv = pR.t[0:64, 0:8 * cn].rearrange("p (a b) -> p a b", a=8)
            for h in range(8):
                kb.mm(pRv[:, h, :], X[0:cn, h, 0:64], PaT[0:cn, h, 0:cn], [X, PaT], [pR], r32=True)
            kb.v("dve", "tensor_tensor", [QR, pR], [RqpT], out=RqpT[:, :, 0:cn], in0=QR[:, :, cn:2 * cn], in1=pRv, op=ALU.subtract)
        ns += 1; yield
        if need_out:
            pY = B[2]
            pYv = v3(pY, cn, 8)
            for h in range(8):
                kb.mm(pYv[:, h, :], RqpT[:, h, 0:cn], H[:, h, :], [RqpT, H], [pY], start=True, stop=False, r32=True)
                kb.mm(pYv[:, h, :], PkT[0:cn, h, 0:cn], V_t[0:cn, 64 * h:64 * h + 64], [PkT, V_t], [pY], start=False, stop=False, r32=True)
                kb.mm(pYv[:, h, :], PaT[0:cn, h, 0:cn], X[0:cn, h, 64:128], [PaT, X], [pY], start=False, stop=True, r32=True)
        pH = B[3]
        pHv = v3(pH, 64, 8)
        for h in range(8):
            kb.mm(pHv[:, h, :], AT[:, h, :], H[:, h, :], [AT, H], [pH], start=True, stop=False, r32=True)
            kb.mm(pHv[:, h, :], Kd_t[0:cn, 64 * h:64 * h + 64], V_t[0:cn, 64 * h:64 * h + 64], [Kd_t, V_t], [pH], start=False, stop=False, r32=True)
            kb.mm(pHv[:, h, :], Ad_t[0:cn, 64 * h:64 * h + 64], X[0:cn, h, 64:128], [Ad_t, X], [pH], start=False, stop=True, r32=True)
        kb.v("dve", "tensor_tensor", [pH, gC], [H], out=H[:, :, :], in0=pHv, in1=bc(gC[:, :], [64, 8, 64], 2), op=ALU.mult)
        ns += 1; yield
        ybt, yt, ysq, st, sgd, ee, yout = S.ybt, S.yt, S.ysq, S.st, S.sgd, S.ee, S.yout
        if epilogue == "store":
            kb.act(yt[0:cn, :], pY[0:cn, :], AF.Copy, [pY], [yt])
            kb.dma(ybwd_d.t[c0:c0 + cn, :], yt[0:cn, :], [yt], [ybwd_d])
        elif epilogue == "final":
            kb.dma(ybt[0:cn, :], ybwd_d.t[c0:c0 + cn, :], [ybwd_d], [ybt])
            kb.v("dve", "tensor_tensor", [pY, ybt], [yt], out=yt[0:cn, :], in0=pY[0:cn, :], in1=ybt[0:cn, :], op=ALU.add)
            y3 = yt[0:cn, :].rearrange("p (h v) -> p h v", h=8)
            kb.v("dve", "tensor_reduce", [yt], [st], out=st[0:cn, 0:8], in_=y3, axis=AX.X, op=ALU.add)
            kb.act(ysq[0:cn, :], yt[0:cn, :], AF.Square, [yt], [ysq])
            kb.v("dve", "tensor_reduce", [ysq], [st], out=st[0:cn, 8:16], in_=ysq[0:cn, :].rearrange("p (h v) -> p h v", h=8), axis=AX.X, op=ALU.add)
            kb.v("dve", "tensor_scalar", [st], [st], out=st[0:cn, 16:24], in0=st[0:cn, 0:8], scalar1=1.0 / 64, scalar2=None, op0=ALU.mult)
            kb.v("dve", "tensor_tensor", [st], [st], out=st[0:cn, 24:32], in0=st[0:cn, 16:24], in1=st[0:cn, 16:24], op=ALU.mult)
            kb.v("dve", "scalar_tensor_tensor", [st], [st], out=st[0:cn, 32:40], in0=st[0:cn, 8:16], scalar=1.0 / 64, in1=st[0:cn, 24:32],
                 op0=ALU.mult, op1=ALU.subtract)
            kb.act(st[0:cn, 40:48], st[0:cn, 32:40], AF.Sqrt, [st], [st], bias=64e-5, scale=1.0)
            kb.v("dve", "reciprocal", [st], [st], out=st[0:cn, 40:48], in_=st[0:cn, 40:48])
            kb.v("dve", "tensor_tensor", [yt, st], [yt], out=y3, in0=y3, in1=bc(st[0:cn, 16:24], [cn, 8, 64], 2), op=ALU.subtract)
            kb.v("dve", "tensor_tensor", [yt, st], [yt], out=y3, in0=y3, in1=bc(st[0:cn, 40:48], [cn, 8, 64], 2), op=ALU.mult)
            kb.v("pool", "tensor_tensor", [yt, lng], [yt], out=yt[0:cn, :], in0=yt[0:cn, :], in1=lng[0:cn, :], op=ALU.mult)
            kb.v("pool", "tensor_tensor", [yt, lnb], [yt], out=yt[0:cn, :], in0=yt[0:cn, :], in1=lnb[0:cn, :], op=ALU.add)
        ns += 1; yield
        if epilogue == "final":
            kb.v("pool", "tensor_tensor", [A[0], A[1]], [ee], out=ee[:, :, 0:cn], in0=A[0][:, :, 0:cn], in1=A[1][:, :, 0:cn], op=ALU.add)
            kb.v("pool", "tensor_tensor", [ee, kah], [ee], out=ee[:, :, 0:cn], in0=ee[:, :, 0:cn], in1=bc(kah[:, :], sh, 2), op=ALU.mult)
            kb.v("pool", "tensor_tensor", [ee, oka], [ee], out=ee[:, :, 0:cn], in0=ee[:, :, 0:cn], in1=bc(oka[:, :], sh, 2), op=ALU.add)
            kb.v("pool", "tensor_tensor", [ee, Z], [ee], out=ee[:, :, 0:cn], in0=ee[:, :, 0:cn], in1=Z[0:64, KS, 0:cn], op=ALU.mult)
            kb.v("pool", "tensor_tensor", [ee, Z], [ee], out=ee[:, :, 0:cn], in0=ee[:, :, 0:cn], in1=Z[0:64, RS, 0:cn], op=ALU.mult)
            kb.v("pool", "tensor_tensor", [ee, rkc], [ee], out=ee[:, :, 0:cn], in0=ee[:, :, 0:cn], in1=bc(rkc[:, :], sh, 2), op=ALU.mult)
            pS = B[0]
            for h in range(8):
                kb.mm(pS[0:cn, h:h + 1], ee[:, h, 0:cn], ones[:, 0:1], [ee, ones], [pS])
            kb.act(st[0:cn, 0:8], pS[0:cn, 0:8], AF.Copy, [pS], [st])
            kb.v("dve", "tensor_tensor", [V_t, st], [ysq], out=ysq[0:cn, :].rearrange("p (h v) -> p h v", h=8),
                 in0=V_t[0:cn, :].rearrange("p (h v) -> p h v", h=8), in1=bc(st[0:cn, 0:8], [cn, 8, 64], 2), op=ALU.mult)
            kb.v("dve", "tensor_tensor", [yt, ysq], [yt], out=yt[0:cn, :], in0=yt[0:cn, :], in1=ysq[0:cn, :], op=ALU.add)
            kb.act(sgd[:, 0, 0:cn], Z[:, 28, 0:cn], AF.Sigmoid, [Z], [sgd])
            kb.act(sgd[0:32, 1, 0:cn], Z[0:32, 29, 0:cn], AF.Sigmoid, [Z], [sgd])
            pG = B[1]
            kb.mm(pG[0:cn, :], sgd[:, 0, 0:cn], g2a[:, :], [sgd, g2a], [pG], start=True, stop=False, r32=True)
            kb.mm(pG[0:cn, :], sgd[0:32, 1, 0:cn], g2b[:, :], [sgd, g2b], [pG], start=False, stop=True, r32=True)
            kb.v("dve", "tensor_tensor", [yt, pG], [yout], out=yout[0:cn, :], in0=yt[0:cn, :], in1=pG[0:cn, :], op=ALU.mult)
            kb.dma(ya_d.t[c0:c0 + cn, :], yout[0:cn, :], [yout], [ya_d])
        ns += 1; yield
        while ns < NSTAGE:
            ns += 1; yield
        assert ns == NSTAGE, ns

    def chain(S, lst, d, H, epilogue):
        for (c0, cn) in lst:
            yield from chunk(S, c0, cn, d, epilogue is not None, H, epilogue)

    chunks = tiles_of(2048, 64) + [(2048, 16)] + [(2064 + a, b) for a, b in tiles_of(2048, 64)]
    own = [c for c in chunks if c[0] < OWN]
    oth = [c for c in chunks if c[0] >= OWN]
    Hf, Hb = Hs
    kb.v("dve", "memset", [], [Hf], Hf[:, :, :], 0.0)
    kb.v("dve", "memset", [], [Hb], Hb[:, :, :], 0.0)
    g0 = chain(slots[0], own, 0, Hf, "store")
    g1 = chain(slots[1], list(reversed(oth)), 1, Hb, None)
    step = 0
    d0 = d1 = False
    while not (d0 and d1):
        if not d0:
            try:
                next(g0)
            except StopIteration:
                d0 = True
        if step >= NSTAGE // 2 and not d1:
            try:
                next(g1)
            except StopIteration:
                d1 = True
        step += 1
    rown = list(reversed(own))
    g0 = chain(slots[0], rown[0::2], 1, Hb, "final")
    g1 = chain(slots[1], rown[1::2], 1, Hb, "final")
    step = 0
    d0 = d1 = False
    while not (d0 and d1):
        if not d0:
            try:
                next(g0)
            except StopIteration:
                d0 = True
        if step >= NSTAGE // 2 and not d1:
            try:
                next(g1)
            except StopIteration:
                d1 = True
        step += 1


def wload(kb, dram, rows, c0, c1, name):
    kch = rows // 128
    t = kb.sb([128, kch, c1 - c0], BF16, name)
    step = 1024
    for a in range(c0, c1, step):
        b = min(a + step, c1)
        kb.dma(t[:, :, a - c0:b - c0], dram.t[:, a:b].rearrange("(c p) n -> p c n", p=128), [dram], [t], q="pool")
    return t


def merge_phase(kb, uT_d, identb, D_, ya_d, yb_d, h2_d):
    B = kb.pbanks
    Wa = wload(kb, D_["w_a"], 512, 0, 1024, "Wa")
    Wb = wload(kb, D_["w_b"], 512, 0, 1024, "Wb")
    Wo = wload(kb, D_["w_o"], 1024, 0, 1024, "Wo")
    Wg = wload(kb, D_["w_in"], 1024, 2720, 4768, "Wg")
    yaf = kb.sb([128, 512], F32, "yaf"); yab = kb.sb([128, 512], BF16, "yab"); ybb = kb.sb([128, 512], BF16, "ybb")
    yaT = kb.sb([128, 4, 128], BF16, "yaT"); ybT = kb.sb([128, 4, 128], BF16, "ybT")
    sgA = kb.sb([128, 1024], F32, "sgA"); sgB = kb.sb([128, 1024], F32, "sgB")
    mg = kb.sb([128, 1024], BF16, "mg"); mT = kb.sb([128, 8, 128], BF16, "mT")
    xt = kb.sb([128, 1024], F32, "xt2"); h2 = kb.sb([128, 1024], F32, "h2")
    uts = [kb.sb([128, NCH, 128], BF16, "utm") for _ in range(2)]
    for i, (t0, n) in enumerate(tiles_of(OWN, 128)):
        ut = uts[i % 2]
        kb.dma(ut[:, :, 0:n], uT_d.t[:, :, t0:t0 + n], [uT_d], [ut])
        kb.dma(yaf[0:n, :], ya_d.t[t0:t0 + n, :], [ya_d], [yaf])
        kb.dma(ybb[0:n, :], yb_d.t[i, 0:n, :], [yb_d], [ybb])
        kb.dma(xt[0:n, :], D_["h"].t[t0:t0 + n, :], [D_["h"]], [xt])
        kb.v("dve", "tensor_copy", [yaf], [yab], out=yab[0:n, :], in_=yaf[0:n, :])
        for src, dst, bk in ((yab, yaT, B[0]), (ybb, ybT, B[1])):
            pv = bk.t.bitcast(BF16).rearrange("p (c t) -> p c t", c=8)
            for k in range(4):
                kb.tr(pv[:, k, 0:n], src[0:n, k * 128:(k + 1) * 128], identb[0:n, 0:n], [src, identb], [bk])
            kb.act(dst[:, :, 0:n], pv[:, 0:4, 0:n], AF.Copy, [bk], [dst])
        for gi, sg_ in enumerate((sgA, sgB)):
            for hb in range(2):
                bk = B[2 + hb]
                col = gi * 1024 + hb * 512
                for c in range(NCH):
                    kb.mm(bk[0:n, :], ut[:, c, 0:n], Wg[:, c, col:col + 512], [ut, Wg], [bk], start=c == 0, stop=c == NCH - 1)
                kb.act(sg_[0:n, hb * 512:(hb + 1) * 512], bk[0:n, :], AF.Sigmoid, [bk], [sg_])
        for (yT_, W_, sg_) in ((yaT, Wa, sgA), (ybT, Wb, sgB)):
            for hb in range(2):
                bk = B[4 + hb]
                for k in range(4):
                    kb.mm(bk[0:n, :], yT_[:, k, 0:n], W_[:, k, hb * 512:(hb + 1) * 512], [yT_, W_], [bk], start=k == 0, stop=k == 3)
                kb.v("dve", "tensor_tensor", [sg_, bk], [sg_], out=sg_[0:n, hb * 512:(hb + 1) * 512],
                     in0=sg_[0:n, hb * 512:(hb + 1) * 512], in1=bk[0:n, :], op=ALU.mult)
        kb.v("dve", "tensor_tensor", [sgA, sgB], [mg], out=mg[0:n, :], in0=sgA[0:n, :], in1=sgB[0:n, :], op=ALU.add)
        bk = B[6]
        pv = bk.t.bitcast(BF16).rearrange("p (c t) -> p c t", c=8)
        for k in range(8):
            kb.tr(pv[:, k, 0:n], mg[0:n, k * 128:(k + 1) * 128], identb[0:n, 0:n], [mg, identb], [bk])
        kb.act(mT[:, :, 0:n], pv[:, :, 0:n], AF.Copy, [bk], [mT])
        for hb in range(2):
            bk = B[hb]
            for k in range(8):
                kb.mm(bk[0:n, :], mT[:, k, 0:n], Wo[:, k, hb * 512:(hb + 1) * 512], [mT, Wo], [bk], start=k == 0, stop=k == 7)
            kb.v("dve", "tensor_tensor", [xt, bk], [h2], out=h2[0:n, hb * 512:(hb + 1) * 512], in0=xt[0:n, hb * 512:(hb + 1) * 512],
                 in1=bk[0:n, :], op=ALU.add)
        kb.dma(h2_d.t[t0:t0 + n, :], h2[0:n, :], [h2], [h2_d])


def ffn_phase(kb, identb, D_, h2_d, out_d):
    B = kb.pbanks
    GT = 256
    W1 = wload(kb, D_["w_ff1"], 1024, 0, 4096, "W1")
    W2 = wload(kb, D_["w_ff2"], 4096, 0, 1024, "W2")
    gf = kb.sb([128, NCH], F32, "gffn"); gfin = kb.sb([128, 1024], F32, "gfin")
    kb.dma(gf.t, D_["gffn"].t, [D_["gffn"]], [gf])
    kb.dma(gfin.t, D_["gfin"].t, [D_["gfin"]], [gfin])
    h2g = [kb.sb([128, 1024], F32, "h2f") for _ in range(2)]
    xs = kb.sb([128, 1024], BF16, "xs2")
    ss = kb.sb([128, 8], F32, "ss2"); nT = kb.sb([128, 8, GT], BF16, "nT")
    frs = [kb.sb([128, GT], F32, "fr") for _ in range(2)]
    f2T = kb.sb([128, 32, GT], BF16, "f2T")
    h3 = kb.sb([128, 1024], F32, "h3"); ot = kb.sb([128, 1024], F32, "ot")
    junk = ot
    for gi, (g0, gn) in enumerate(tiles_of(OWN, GT)):
        subs = tiles_of(gn, 128)
        for si, (o0, n) in enumerate(subs):
            t0 = g0 + o0
            h2 = h2g[si]
            kb.dma(h2[0:n, :], h2_d.t[t0:t0 + n, :], [h2_d], [h2])
            kb.act(junk[0:n, :], h2[0:n, :], AF.Square, [h2], [junk, ss], accum_out=ss[0:n, 0:1])
            kb.act(ss[0:n, 1:2], ss[0:n, 0:1], AF.Sqrt, [ss], [ss], scale=1.0 / D, bias=EPS)
            kb.v("dve", "reciprocal", [ss], [ss], out=ss[0:n, 2:3], in_=ss[0:n, 1:2])
            kb.v("dve", "tensor_scalar", [h2, ss], [xs], out=xs[0:n, :], in0=h2[0:n, :], scalar1=ss[0:n, 2:3], scalar2=None, op0=ALU.mult)
            bk = B[si]
            pv = bk.t.bitcast(BF16).rearrange("p (c t) -> p c t", c=8)
            for c in range(NCH):
                kb.tr(pv[:, c, 0:n], xs[0:n, c * 128:(c + 1) * 128], identb[0:n, 0:n], [xs, identb], [bk])
            kb.v("dve", "tensor_tensor", [bk, gf], [nT], out=nT[:, :, o0:o0 + n], in0=pv[:, :, 0:n],
                 in1=gf[:, :].unsqueeze(2).to_broadcast([128, NCH, n]), op=ALU.mult)
        for ft in range(32):
            bk = B[2 + ft % 3]
            fr = frs[ft % 2]
            for c in range(NCH):
                kb.mm(bk[:, 0:gn], W1[:, c, ft * 128:(ft + 1) * 128], nT[:, c, 0:gn], [W1, nT], [bk], start=c == 0, stop=c == NCH - 1)
            kb.act(fr[:, 0:gn], bk[:, 0:gn], AF.Relu, [bk], [fr])
            kb.v("dve", "tensor_tensor", [fr], [f2T], out=f2T[:, ft, 0:gn], in0=fr[:, 0:gn], in1=fr[:, 0:gn], op=ALU.mult)
        for si, (o0, n) in enumerate(subs):
            t0 = g0 + o0
            h2 = h2g[si]
            for hb in range(2):
                bk = B[5 + (hb + 2 * si) % 3]
                for k in range(32):
                    kb.mm(bk[0:n, :], f2T[:, k, o0:o0 + n], W2[:, k, hb * 512:(hb + 1) * 512], [f2T, W2], [bk], start=k == 0, stop=k == 31)
                kb.v("dve", "tensor_tensor", [h2, bk], [h3], out=h3[0:n, hb * 512:(hb + 1) * 512], in0=h2[0:n, hb * 512:(hb + 1) * 512],
                     in1=bk[0:n, :], op=ALU.add)
            kb.act(junk[0:n, :], h3[0:n, :], AF.Square, [h3], [junk, ss], accum_out=ss[0:n, 4:5])
            kb.act(ss[0:n, 5:6], ss[0:n, 4:5], AF.Sqrt, [ss], [ss], scale=1.0 / D, bias=EPS)
            kb.v("dve", "reciprocal", [ss], [ss], out=ss[0:n, 6:7], in_=ss[0:n, 5:6])
            kb.v("dve", "scalar_tensor_tensor", [h3, ss, gfin], [ot], out=ot[0:n, :], in0=h3[0:n, :], scalar=ss[0:n, 6:7], in1=gfin[0:n, :],
                 op0=ALU.mult, op1=ALU.mult)
            k = kb.dma(out_d.t[t0:t0 + n, :], ot[0:n, :], [ot], [])
            kb.out_keys.append(k)

def tiles_of(n, step):
    out = []
    t = 0
    while t < n:
        out.append((t, min(step, n - t)))
        t += step
    return out


def build(debug=None, upto="all"):
    nc = bass.Bass("TRN2", target_bir_lowering=False)
    kb = KB(nc, debug)
    kb.setup_mem()
    fw = kb.fw
    h_d = kb.dram_in("h", [L, D])
    gmix_d = kb.dram_in("gmix", [128, NCH])
    ident_d = kb.dram_in("ident", [128, 128])
    out_d = kb.dram_out("out", [OWN, D])

    ident = kb.sb([128, 128], F32, "ident")
    identb = kb.sb([128, 128], BF16, "identb")
    gmix = kb.sb([128, NCH], F32, "gmix")
    kb.dma(ident[:, :], ident_d[:, :], [ident_d], [ident])
    kb.dma(gmix[:, :], gmix_d[:, :], [gmix_d], [gmix])
    kb.v("dve", "tensor_copy", [ident], [identb], out=identb[:, :], in_=ident[:, :])

    win_d = kb.dram_in("w_in", [D, INW])
    kb.from_top = True
    wr_pre = rwkv_load_w(kb, {"w_in": win_d})
    kb.from_top = False
    mA = kb.mark()
    uT_d = kb.dram_tmp("uT_scr", [128, NCH, L], BF16)
    xts = [kb.sb([128, D], F32, "xt") for _ in range(2)]
    xss = [kb.sb([128, D], BF16, "xs") for _ in range(2)]
    junk = kb.sb([128, D], BF16, "junk")
    sss = [kb.sb([128, 4], F32, "ss") for _ in range(2)]
    uts = [kb.sb([128, NCH, 128], BF16, "ut") for _ in range(2)]

    def phase_a_tile(i, t0, n):
        xt = xts[i % 2]
        xs = xss[i % 2]
        ss = sss[i % 2]
        ut = uts[i % 2]
        kb.dma(xt[0:n, :], h_d[t0:t0 + n, :], [h_d], [xt])
        kb.act(junk[0:n, :], xt[0:n, :], AF.Square, [xt], [junk, ss], accum_out=ss[0:n, 0:1])
        kb.act(ss[0:n, 1:2], ss[0:n, 0:1], AF.Sqrt, [ss], [ss], scale=1.0 / D, bias=EPS)
        kb.v("dve", "reciprocal", [ss], [ss], out=ss[0:n, 2:3], in_=ss[0:n, 1:2])
        kb.v("dve", "tensor_scalar", [xt, ss], [xs], out=xs[0:n, :], in0=xt[0:n, :],
             scalar1=ss[0:n, 2:3], scalar2=None, op0=ALU.mult)
        pb = kb.bank()
        pv = pb.t.bitcast(BF16).rearrange("p (c t) -> p c t", c=NCH)
        for c in range(NCH):
            kb.tr(pv[:, c, 0:n], xs[0:n, c * 128:(c + 1) * 128], identb[0:n, 0:n], [xs, identb], [pb])
        kb.v("dve", "tensor_tensor", [pb, gmix], [ut], out=ut[:, :, 0:n], in0=pv[:, :, 0:n],
             in1=gmix[:, :].unsqueeze(2).to_broadcast([128, NCH, n]), op=ALU.mult)
        kb.dma(uT_d.t[:, :, t0:t0 + n], ut[:, :, 0:n], [ut], [uT_d])
        return ut

    ropeC_d = kb.dram_in("ropeC", [L, 64])
    ropeS_d = kb.dram_in("ropeS", [L, 64])
    qg_d = kb.dram_in("qg", [128, 64])
    kg_d = kb.dram_in("kg", [128, 64])
    NT = 33
    NQT = 17
    mB0 = kb.mark()
    y_b = kb.sb([128, NQT, 512], BF16, "y_b")
    mB = kb.mark()
    wqkv = kb.sb([128, NCH, 768], BF16, "wqkv")
    kb.dma(wqkv[:, :, :], win_d.t[:, 1952:2720].rearrange("(c p) n -> p c n", p=128), [win_d], [wqkv], q="pool")
    rcs = [kb.sb([128, 64], F32, "rc") for _ in range(2)]
    rss = [kb.sb([128, 64], F32, "rs") for _ in range(2)]
    qgb = kb.sb([128, 64], F32, "qgb")
    kgb = kb.sb([128, 64], F32, "kgb")
    kb.dma(qgb[:, :], qg_d[:, :], [qg_d], [qgb])
    kb.dma(kgb[:, :], kg_d[:, :], [kg_d], [kgb])
    QT = kb.sb([64, 8, OWN], BF16, "QT")
    KT = kb.sb([64, 2, L], BF16, "KT")
    V1 = kb.sb([128, NT, 2, 65], BF16, "V1")
    kb.v("pool", "memset", [], [V1], V1[:, :, :, 64:65], 1.0)
    sqs = [kb.sb([128, 640], F32, "sq") for _ in range(1)]
    qns = [kb.sb([128, 640], F32, "qn") for _ in range(1)]
    t1s = [kb.sb([128, 640], F32, "t1") for _ in range(1)]
    t2s = [kb.sb([128, 640], F32, "t2") for _ in range(1)]
    qrs = [kb.sb([128, 640], BF16, "qr") for _ in range(2)]
    sms = [kb.sb([128, 32], F32, "sm") for _ in range(2)]
    for i, (t0, n) in enumerate(tiles_of(L, 128)):
        nq = max(0, min(n, OWN - t0))
        ut = phase_a_tile(i, t0, n)
        sq, qn, t1, t2, qr, sm = sqs[0], qns[0], t1s[0], t2s[0], qrs[i % 2], sms[i % 2]
        pq = kb.bank()
        pkv = kb.bank()
        rc, rs = rcs[i % 2], rss[i % 2]
        kb.dma(rc[0:n, :], ropeC_d.t[t0:t0 + n, :], [ropeC_d], [rc])
        kb.dma(rs[0:n, :], ropeS_d.t[t0:t0 + n, :], [ropeS_d], [rs])
        for c in range(NCH):
            if nq:
                kb.mm(pq[0:n, 0:512], ut[:, c, 0:n], wqkv[:, c, 0:512], [ut, wqkv], [pq], start=c == 0, stop=c == NCH - 1)
            kb.mm(pkv[0:n, 0:256], ut[:, c, 0:n], wqkv[:, c, 512:768], [ut, wqkv], [pkv], start=c == 0, stop=c == NCH - 1)
        h0 = 0 if nq else 8
        c0 = h0 * 64
        nh = 10 - h0
        if nq:
            kb.act(sq[0:n, 0:512], pq[0:n, 0:512], AF.Square, [pq], [sq])
        kb.act(sq[0:n, 512:640], pkv[0:n, 0:128], AF.Square, [pkv], [sq])
        kb.v("dve", "tensor_reduce", [sq], [sm], out=sm[0:n, h0:10],
             in_=sq[0:n, c0:640].rearrange("p (h c) -> p h c", c=64), axis=AX.X, op=ALU.add)
        kb.act(sm[0:n, 10 + h0:20], sm[0:n, h0:10], AF.Sqrt, [sm], [sm], scale=1.0 / 64, bias=EPS)
        kb.v("dve", "reciprocal", [sm], [sm], out=sm[0:n, 20 + h0:30], in_=sm[0:n, 10 + h0:20])
        if nq:
            kb.v("dve", "tensor_tensor", [pq, sm], [qn], out=qn[0:n, 0:512].rearrange("p (h c) -> p h c", c=64),
                 in0=pq[0:n, 0:512].rearrange("p (h c) -> p h c", c=64),
                 in1=sm[0:n, 20:28].unsqueeze(2).to_broadcast([n, 8, 64]), op=ALU.mult)
            kb.v("dve", "tensor_tensor", [qn, qgb], [qn], out=qn[0:n, 0:512].rearrange("p (h c) -> p h c", c=64),
                 in0=qn[0:n, 0:512].rearrange("p (h c) -> p h c", c=64),
                 in1=qgb[0:n, :].unsqueeze(1).to_broadcast([n, 8, 64]), op=ALU.mult)
        kb.v("dve", "tensor_tensor", [pkv, sm], [qn], out=qn[0:n, 512:640].rearrange("p (h c) -> p h c", c=64),
             in0=pkv[0:n, 0:128].rearrange("p (h c) -> p h c", c=64),
             in1=sm[0:n, 28:30].unsqueeze(2).to_broadcast([n, 2, 64]), op=ALU.mult)
        kb.v("dve", "tensor_tensor", [qn, kgb], [qn], out=qn[0:n, 512:640].rearrange("p (h c) -> p h c", c=64),
             in0=qn[0:n, 512:640].rearrange("p (h c) -> p h c", c=64),
             in1=kgb[0:n, :].unsqueeze(1).to_broadcast([n, 2, 64]), op=ALU.mult)
        kb.v("dve", "tensor_tensor", [qn, rc], [t1], out=t1[0:n, c0:640].rearrange("p (h c) -> p h c", c=64),
             in0=qn[0:n, c0:640].rearrange("p (h c) -> p h c", c=64),
             in1=rc[0:n, :].unsqueeze(1).to_broadcast([n, nh, 64]), op=ALU.mult)
        for hf in range(2):
            o_v = t2[0:n, c0:640].rearrange("p (h a x f) -> p h a x f", a=2, x=2, f=16)[:, :, :, hf, :]
            i_v = qn[0:n, c0:640].rearrange("p (h a x f) -> p h a x f", a=2, x=2, f=16)[:, :, :, 1 - hf, :]
            s_v = rs[0:n, :].rearrange("p (a x f) -> p a x f", a=2, x=2)[:, :, hf, :].unsqueeze(1).to_broadcast([n, nh, 2, 16])
            kb.v("dve", "tensor_tensor", [qn, rs], [t2], out=o_v, in0=i_v, in1=s_v, op=ALU.mult)
        kb.v("dve", "tensor_tensor", [t1, t2], [qr], out=qr[0:n, c0:640], in0=t1[0:n, c0:640], in1=t2[0:n, c0:640], op=ALU.add)
        ptk = kb.bank()
        ptkv = ptk.t.bitcast(BF16).rearrange("p (c t) -> p c t", c=8)
        for g in range(2):
            kb.tr(ptkv[0:64, g, 0:n], qr[0:n, 512 + g * 64:512 + (g + 1) * 64], identb[0:n, 0:n], [qr, identb], [ptk])
        kb.v("dve", "tensor_copy", [ptk], [KT], out=KT[:, :, t0:t0 + n], in_=ptkv[0:64, 0:2, 0:n])
        if nq:
            ptq = kb.bank()
            ptqv = ptq.t.bitcast(BF16).rearrange("p (c t) -> p c t", c=8)
            for hh in range(8):
                kb.tr(ptqv[0:64, hh, 0:n], qr[0:n, hh * 64:(hh + 1) * 64], identb[0:n, 0:n], [qr, identb], [ptq])
            kb.act(QT[:, :, t0:t0 + nq], ptqv[0:64, :, 0:nq], AF.Copy, [ptq], [QT])
        kb.act(V1[0:n, i, :, 0:64], pkv[0:n, 128:256].rearrange("p (g c) -> p g c", c=64), AF.Copy, [pkv], [V1])
    fw.barrier()
    Es = [kb.sb([128, 4, 128], BF16, "E") for _ in range(3)]
    rcp = kb.sb([128, 8], F32, "rcp")
    Ob = kb.pbanks[0:4]
    Sb = kb.pbanks[4:6]
    ktiles = tiles_of(L, 128)
    steps = []
    for qi, (q0, nq) in enumerate(tiles_of(OWN, 128)):
        for g in range(2):
            for kt, (k0, nk) in enumerate(ktiles):
                steps.append((qi, q0, nq, g, kt, k0, nk))
    NS_ = len(steps)
    pend = {}

    def issue_s(i):
        qi, q0, nq, g, kt, k0, nk = steps[i]
        sT = Sb[i % 2]
        E = Es[i % 3]
        sTv = sT.t[0:nk, 0:4 * nq].rearrange("p (h q) -> p h q", h=4)
        kb.mm(sTv, KT[0:64, g, k0:k0 + nk], QT[0:64, 4 * g:4 * g + 4, q0:q0 + nq], [KT, QT], [sT])
        kb.act(E[0:nk, :, 0:nq], sTv, AF.Exp, [sT], [E], scale=0.125)

    def issue_pv(i):
        qi, q0, nq, g, kt, k0, nk = steps[i]
        E = Es[i % 3]
        for hh in range(4):
            kb.mm(Ob[hh][0:nq, 0:65], E[0:nk, hh, 0:nq], V1[0:nk, kt, g, :], [E, V1], [Ob[hh]],
                  start=kt == 0, stop=kt == len(ktiles) - 1)
        if kt == len(ktiles) - 1:
            for hh in range(4):
                kb.v("dve", "reciprocal", [Ob[hh]], [rcp], out=rcp[0:nq, hh:hh + 1], in_=Ob[hh][0:nq, 64:65])
                hd = 4 * g + hh
                kb.v("dve", "tensor_scalar", [Ob[hh], rcp], [y_b], out=y_b[0:nq, qi, hd * 64:(hd + 1) * 64],
                     in0=Ob[hh][0:nq, 0:64], scalar1=rcp[0:nq, hh:hh + 1], scalar2=None, op0=ALU.mult)

    issue_s(0)
    for i in range(NS_):
        if i + 1 < NS_:
            issue_s(i + 1)
        issue_pv(i)
    kb.dump("y_b", y_b, y_b[:, :, :], [128, NQT, 512], BF16)
    yb_d = kb.dram_tmp("yb_scr", [NQT, 128, 512], BF16)
    kb.dma(yb_d.t[0:16].rearrange("a p c -> p a c"), y_b[:, 0:16, :], [y_b], [yb_d])
    kb.dma(yb_d.t[16, 0:16, :], y_b[0:16, 16, :], [y_b], [yb_d])
    fw.barrier()
    kb.release(mA)
    if upto == "B":
        kb.finish()
        return nc, kb
    names = {"mu0": [128, RW_TILES], "mu1": [128, RW_TILES], "w2": [64, 2, 512], "a2": [64, 2, 512], "w0": [64, 2, 8],
             "a0": [64, 2, 8], "g2a": [128, 512], "g2b": [32, 512], "kkc": [64, 8], "kac": [64, 8], "rkc": [64, 8],
             "lng": [64, 512], "lnb": [64, 512],
             "masks": [64, 4, 64], "w_a": [512, 1024], "w_b": [512, 1024], "w_o": [1024, 1024], "w_ff1": [1024, 4096],
             "w_ff2": [4096, 1024], "gffn": [128, NCH], "gfin": [128, 1024]}
    D_ = {k: kb.dram_in(k, v) for k, v in names.items()}
    D_["w_in"] = win_d
    D_["h"] = h_d
    ybwd_d = kb.dram_tmp("ybwd_scr", [OWN, 512])
    ya_d = kb.dram_tmp("ya_scr", [OWN, 512])
    h2_d = kb.dram_tmp("h2_scr", [OWN, D])
    mC = kb.mark()
    rwkv_phase(kb, uT_d, ident, D_, ybwd_d, ya_d, wr_pre)
    if "ya" in kb.debug:
        o = kb.dram_out("dbg_ya", [OWN, 512])
        kb.out_keys.append(kb.dma(o.t, ya_d.t, [ya_d], [o]))
    fw.barrier()
    kb.release(mC)
    kb.hi = kb.AW
    if upto == "C":
        kb.finish()
        return nc, kb
    merge_phase(kb, uT_d, identb, D_, ya_d, yb_d, h2_d)
    if "h2" in kb.debug:
        o = kb.dram_out("dbg_h2", [OWN, D])
        kb.out_keys.append(kb.dma(o.t, h2_d.t, [h2_d], [o]))
    fw.barrier()
    kb.release(mA)
    ffn_phase(kb, identb, D_, h2_d, out_d)

    kb.finish()
    return nc, kb


def _rope_tables():
    inv = (10000.0 ** (-np.arange(16, dtype=np.float32) * 2.0 / 32)).astype(np.float32)
    rows = (np.arange(64, dtype=np.float32)[:, None] * inv).astype(np.float32)
    cols = (np.arange(64, dtype=np.float32)[:, None] * inv).astype(np.float32)
    ang = np.zeros((L, 2, 16), np.float32)
    grid = np.stack([np.broadcast_to(rows[:, None, :], (64, 64, 16)),
                     np.broadcast_to(cols[None, :, :], (64, 64, 16))], axis=2).reshape(4096, 2, 16)
    ang[16:] = grid
    c, s_ = np.cos(ang).astype(np.float32), np.sin(ang).astype(np.float32)
    cosf = np.stack([c, c], axis=2).reshape(L, 64)
    sinf = np.stack([-s_, s_], axis=2).reshape(L, 64)
    return cosf, sinf


def prep_inputs(inputs, core):
    b, s = core // 2, core % 2
    x = np.asarray(inputs["x"], np.float32)
    meta = np.asarray(inputs["meta_tokens"], np.float32)
    hseq = np.concatenate([meta, x[b]], axis=0)
    if s == 1:
        hseq = hseq[::-1]
    m = {}
    m["h"] = np.ascontiguousarray(hseq)
    m["gmix"] = np.ascontiguousarray(np.asarray(inputs["mix_norm_g"], np.float32)[0].reshape(NCH, 128).T)
    m["ident"] = np.eye(128, dtype=np.float32)
    w_in = np.asarray(inputs["w_in"], np.float32)[0]
    if s == 1:
        w_in = w_in.copy()
        for base in (1536, 1664):
            a = w_in[:, base:base + 64].copy()
            w_in[:, base:base + 64] = w_in[:, base + 64:base + 128]
            w_in[:, base + 64:base + 128] = a
    m["w_in"] = np.ascontiguousarray(w_in)
    cosf, sinf = _rope_tables()
    if s == 1:
        cosf, sinf = cosf[::-1], sinf[::-1]
    m["ropeC"] = np.ascontiguousarray(cosf)
    m["ropeS"] = np.ascontiguousarray(sinf)
    G = lambda k: np.asarray(inputs[k], np.float32)[0]
    sh = G("rwkv_shift")
    w2, w0, a2, a0 = G("decay_w2"), G("decay_w0"), G("icl_a2"), G("icl_a0")
    if s == 1:
        sh = sh[::-1].copy()
        for base in (1536, 1664):
            a = sh[:, base:base + 64].copy()
            sh[:, base:base + 64] = sh[:, base + 64:base + 128]
            sh[:, base + 64:base + 128] = a
        w2, w0, a2, a0 = w2[::-1], w0[::-1], a2[::-1], a0[::-1]
    for nm, row in (("mu0", 0), ("mu1", 1)):
        arr = np.zeros((128, RW_TILES), np.float32)
        for j, (c0, w) in enumerate(RW_COLS):
            arr[:w, j] = sh[row, c0:c0 + w]
        m[nm] = arr
    m["w2"] = np.ascontiguousarray(w2.transpose(1, 0, 2))
    m["a2"] = np.ascontiguousarray(a2.transpose(1, 0, 2))
    m["w0"] = np.ascontiguousarray(w0.reshape(2, 8, 64).transpose(2, 0, 1))
    m["a0"] = np.ascontiguousarray(a0.reshape(2, 8, 64).transpose(2, 0, 1))
    g2 = G("gate_w2")
    m["g2a"] = np.ascontiguousarray(g2[0:128]); m["g2b"] = np.ascontiguousarray(g2[128:160])
    m["kkc"] = np.ascontiguousarray(G("k_k").reshape(8, 64).T)
    m["kac"] = np.ascontiguousarray(G("k_a").reshape(8, 64).T)
    m["rkc"] = np.ascontiguousarray(G("r_k").reshape(8, 64).T)
    m["lng"] = np.ascontiguousarray(np.tile(G("lnx_g")[None, :], (64, 1)))
    m["lnb"] = np.ascontiguousarray(np.tile(G("lnx_b")[None, :], (64, 1)))
    si, ti = np.meshgrid(np.arange(64), np.arange(64), indexing="ij")
    m["masks"] = np.ascontiguousarray(np.stack([si < ti, si <= ti, si > ti, si >= ti], 1).astype(np.float32))
    m["w_a"] = np.ascontiguousarray(G("w_branch_rwkv")); m["w_b"] = np.ascontiguousarray(G("w_branch_attn"))
    m["w_o"] = np.ascontiguousarray(G("w_out")); m["w_ff1"] = np.ascontiguousarray(G("w_ff1")); m["w_ff2"] = np.ascontiguousarray(G("w_ff2"))
    m["gffn"] = np.ascontiguousarray(G("ffn_norm_g").reshape(NCH, 128).T)
    m["gfin"] = np.ascontiguousarray(np.tile(np.asarray(inputs["final_norm_g"], np.float32)[None, :], (128, 1)))
    m["qg"] = np.ascontiguousarray(np.tile(np.asarray(inputs["q_norm_g"], np.float32)[0][None, :], (128, 1)))
    m["kg"] = np.ascontiguousarray(np.tile(np.asarray(inputs["k_norm_g"], np.float32)[0][None, :], (128, 1)))
    return m


_CACHE = {}


def kernel(**inputs):
    from concourse.bass_utils import run_bass_kernel_spmd
    if "nc" not in _CACHE:
        _CACHE["nc"] = build()
    nc, kb = _CACHE["nc"]
    in_maps = [prep_inputs(inputs, c) for c in range(8)]
    res = run_bass_kernel_spmd(nc, in_maps, core_ids=list(range(8)))
    out = np.zeros((4, 4096, D), np.float32)
    for c in range(8):
        b, s = c // 2, c % 2
        o = np.asarray(res.results[c]["out"])
        if s == 0:
            out[b, 0:2048] = o[16:2064]
        else:
            out[b, 2048:4096] = o[0:2048][::-1]
    return out
```

```python
import numpy as np
import concourse.bass as bass
import concourse.mybir as mybir

F32 = mybir.dt.float32
BF16 = mybir.dt.bfloat16
AF = mybir.ActivationFunctionType
ALU = mybir.AluOpType
AX = mybir.AxisListType

ENGS = ["pe", "dve", "act", "pool", "sp"]
SEM_WRAP = 2048
NSLOT = 8


class Res:
    __slots__ = ("name", "w", "rd")

    def __init__(self, name):
        self.name = name
        self.w = None
        self.rd = []


class Rec:
    __slots__ = ("fn", "deps", "sig", "dma", "signo", "pre")

    def __init__(self, fn, dma):
        self.fn = fn
        self.deps = set()
        self.sig = False
        self.dma = dma
        self.signo = None
        self.pre = None


class FW:
    def __init__(self, nc, same_engine_sync=True):
        self.nc = nc
        self.eng = {"pe": nc.tensor, "dve": nc.vector, "act": nc.scalar,
                    "pool": nc.gpsimd, "sp": nc.sync}
        self.ops = {e: [] for e in ENGS}
        self.ndma = {e: 0 for e in ENGS}
        self.same = same_engine_sync

    def op(self, e, fn, reads=(), writes=(), dma=False):
        lst = self.ops[e]
        idx = len(lst)
        rec = Rec(fn, None)
        if dma:
            rec.dma = self.ndma[e]
            self.ndma[e] += 1
        key = (e, idx)
        deps = set()
        for r in reads:
            if r.w is not None:
                deps.add(r.w)
        for w in writes:
            if w.w is not None:
                deps.add(w.w)
            for k in w.rd:
                deps.add(k)
        for d in deps:
            de, di = d
            drec = self.ops[de][di]
            if drec.dma is None:
                if de == e and (not self.same) and e != "pool":
                    continue
                if de == e and e == "pe":
                    continue
                drec.sig = True
            rec.deps.add(d)
        for r in reads:
            r.rd.append(key)
        for w in writes:
            w.w = key
            w.rd = []
        lst.append(rec)
        return key

    def barrier(self, resources=()):
        b = Res("barrier")
        keys = []
        for e in ENGS:
            for i in range(len(self.ops[e]) - 1, -1, -1):
                if self.ops[e][i].fn is not None:
                    keys.append((e, i))
                    break
        dkeys = []
        for e in ENGS:
            n = 0
            for i in range(len(self.ops[e]) - 1, -1, -1):
                if self.ops[e][i].dma is not None:
                    dkeys.append((e, i))
                    n += 1
                    if n >= NSLOT:
                        break
        self._pending_barrier = keys + dkeys
        for e in ENGS:
            rec = Rec(None, None)
            for d in keys + dkeys:
                de, di = d
                drec = self.ops[de][di]
                if drec.dma is None:
                    if de == e:
                        continue
                    drec.sig = True
                rec.deps.add(d)
            self.ops[e].append(rec)

    def emit(self, final_waits=()):
        nc = self.nc
        nsig = {}
        for e in ENGS:
            n = 0
            for rec in self.ops[e]:
                if rec.dma is None and rec.sig:
                    n += 1
                    rec.signo = n
            nsig[e] = n
        import contextlib
        with contextlib.ExitStack() as st:
            sems = {}
            for e in ENGS:
                k = (nsig[e] + SEM_WRAP - 1) // SEM_WRAP
                sems[e] = [st.enter_context(nc.semaphore(f"s_{e}_{i}")) for i in range(max(k, 1))]
            dsem = {}
            for e in ENGS:
                if self.ndma[e]:
                    dsem[e] = [st.enter_context(nc.semaphore(f"d_{e}_{i}")) for i in range(NSLOT)]
            block = st.enter_context(nc.Block())

            def target(dep):
                de, di = dep
                drec = self.ops[de][di]
                if drec.dma is not None:
                    n = drec.dma
                    return (dsem[de][n % NSLOT], 16 * (n // NSLOT + 1))
                s = drec.signo
                return (sems[de][(s - 1) // SEM_WRAP], (s - 1) % SEM_WRAP + 1)

            def run(e, eng):
                waited = {}
                for rec in self.ops[e]:
                    tg = {}
                    for dep in rec.deps:
                        sem, val = target(dep)
                        kk = id(sem)
                        if kk not in tg or tg[kk][1] < val:
                            tg[kk] = (sem, val)
                    if rec.dma is not None and rec.dma >= NSLOT:
                        n = rec.dma
                        sem = dsem[e][n % NSLOT]
                        val = 16 * (n // NSLOT)
                        kk = id(sem)
                        if kk not in tg or tg[kk][1] < val:
                            tg[kk] = (sem, val)
                    for kk, (sem, val) in tg.items():
                        if waited.get(kk, 0) >= val:
                            continue
                        waited[kk] = val
                        eng.wait_ge(sem, val)
                    if rec.fn is None:
                        continue
                    inst = rec.fn(eng)
                    if rec.dma is not None:
                        inst.then_inc(dsem[e][rec.dma % NSLOT], 16)
                    elif rec.sig:
                        s = rec.signo
                        inst.then_inc(sems[e][(s - 1) // SEM_WRAP], 1)

            @block.tensor
            def _(eng):
                run("pe", eng)

            @block.vector
            def _(eng):
                run("dve", eng)

            @block.scalar
            def _(eng):
                run("act", eng)

            @block.gpsimd
            def _(eng):
                run("pool", eng)

            @block.sync
            def _(eng):
                run("sp", eng)

import contextlib

L = 4112
OWN = 2064
D = 1024
NCH = 8
INW = 4768
RW = 1952
EPS = 1e-6


class Tl:
    def __init__(self, t, name):
        self.t = t
        self.r = Res(name)

    def __getitem__(self, k):
        return self.t[k]


class KB:
    def __init__(self, nc, debug=None):
        self.nc = nc
        self.fw = FW(nc)
        self.st = contextlib.ExitStack()
        self.debug = debug or []
        self.dbg_out = {}
        self.out_keys = []
        self.cnt = 0
        import os
        self.use_r32 = os.environ.get('USE_R32', '0') == '1'

    def setup_mem(self):
        self.AW = 49000
        self.arena = self.st.enter_context(self.nc.sbuf_tensor("arena", [128, self.AW], F32))
        self.psum = self.st.enter_context(self.nc.psum_tensor("psum", [128, 4096], F32))
        self.top = 0
        self.hi = self.AW
        self.pbanks = [self._pview(i) for i in range(8)]
        self.pb_i = 0

    def _pview(self, i):
        return Tl(self.psum[:, 512 * i:512 * (i + 1)], f"bank{i}")

    def bank(self):
        b = self.pbanks[self.pb_i % 8]
        self.pb_i += 1
        return b

    def mark(self):
        return self.top

    def release(self, m):
        self.top = m

    def sb(self, shape, dt, name=None):
        self.cnt += 1
        name = f"{name or 't'}_{self.cnt}"
        p = shape[0]
        n = 1
        for x in shape[1:]:
            n *= x
        words = n if dt == F32 else (n + 1) // 2
        if getattr(self, 'from_top', False):
            self.hi -= words
            off = self.hi
        else:
            off = self.top
            self.top += words
        assert self.top <= self.hi, f"arena overflow {self.top} {self.hi}"
        ap = self.arena[0:p, off:off + words]
        if dt != F32:
            ap = ap.bitcast(dt)
            if n % 2:
                ap = ap[:, 0:n]
        if len(shape) == 3:
            ap = ap.rearrange("p (a b) -> p a b", a=shape[1])
        elif len(shape) == 4:
            ap = ap.rearrange("p (a b c) -> p a b c", a=shape[1], b=shape[2])
        return Tl(ap, name)

    def dram_in(self, name, shape, dt=F32):
        t = self.nc.dram_tensor(name, list(shape), dt, kind="ExternalInput")
        return Tl(t.ap(), name)

    def dram_out(self, name, shape, dt=F32):
        t = self.nc.dram_tensor(name, list(shape), dt, kind="ExternalOutput")
        return Tl(t.ap(), name)

    def dram_tmp(self, name, shape, dt=F32):
        t = self.nc.dram_tensor(name, list(shape), dt, kind="Internal")
        return Tl(t.ap(), name)

    def dma(self, out_ap, in_ap, reads, writes, q="sp", **kw):
        return self.fw.op(q, lambda e: e.dma_start(out=out_ap, in_=in_ap, **kw),
                          reads=[x.r for x in reads], writes=[x.r for x in writes], dma=True)

    def mm(self, out_ap, lhsT_ap, rhs_ap, reads, writes, start=True, stop=True, r32=False):
        if r32 and self.use_r32 and lhsT_ap.dtype == F32 and rhs_ap.dtype == F32:
            lhsT_ap = lhsT_ap.bitcast(mybir.dt.float32r)
            rhs_ap = rhs_ap.bitcast(mybir.dt.float32r)
        return self.fw.op("pe", lambda e: e.matmul(out_ap, lhsT_ap, rhs_ap, start=start, stop=stop),
                          reads=[x.r for x in reads], writes=[x.r for x in writes])

    def tr(self, out_ap, in_ap, ident_ap, reads, writes):
        return self.fw.op("pe", lambda e: e.transpose(out_ap, in_ap, ident_ap),
                          reads=[x.r for x in reads], writes=[x.r for x in writes])

    def act(self, out_ap, in_ap, func, reads, writes, **kw):
        return self.fw.op("act", lambda e: e.activation(out=out_ap, in_=in_ap, func=func, **kw),
                          reads=[x.r for x in reads], writes=[x.r for x in writes])

    def v(self, eng, meth, reads, writes, *a, **kw):
        return self.fw.op(eng, lambda e: getattr(e, meth)(*a, **kw),
                          reads=[x.r for x in reads], writes=[x.r for x in writes])

    def dump(self, name, tl, ap, shape, dt=F32):
        if name not in self.debug:
            return
        o = self.dram_out("dbg_" + name, shape, dt)
        k = self.dma(o.t, ap, [tl], [o])
        self.out_keys.append(k)
        self.dbg_out[name] = "dbg_" + name

    def finish(self):
        fw = self.fw
        rec = Rec(None, None)
        for k in self.out_keys:
            rec.deps.add(k)
        fw.ops["sp"].append(rec)
        fw.emit()
        self.st.close()

S0 = float(np.exp(-0.5))
RW_COLS = [(j * 64, 64) for j in range(28)] + [(1792, 128), (1920, 32)]
RW_TILES = len(RW_COLS)


def rwkv_load_w(kb, D_):
    wr = kb.sb([128, NCH, RW], BF16, "wr")
    for a in range(0, RW, 488):
        kb.dma(wr[:, :, a:a + 488], D_["w_in"].t[:, a:a + 488].rearrange("(c p) n -> p c n", p=128), [D_["w_in"]], [wr], q="pool")
    return wr


def rwkv_phase(kb, uT_d, ident, D_, ybwd_d, ya_d, wr):
    fw = kb.fw

    def ld(name, shape):
        t = kb.sb(shape, F32, name)
        src = D_[name]
        kb.dma(t.t, src.t, [src], [t])
        return t
    mu0 = ld("mu0", [128, RW_TILES]); mu1 = ld("mu1", [128, RW_TILES])
    muc = kb.sb([128, RW_TILES], F32, "muc")
    kb.v("dve", "tensor_tensor", [mu0, mu1], [muc], out=muc[:, :], in0=mu0[:, :], in1=mu1[:, :], op=ALU.add)
    kb.v("dve", "tensor_scalar", [muc], [muc], out=muc[:, :], in0=muc[:, :], scalar1=-1.0, scalar2=1.0, op0=ALU.mult, op1=ALU.add)
    w2 = ld("w2", [64, 2, 512]); a2 = ld("a2", [64, 2, 512])
    w0 = ld("w0", [64, 2, 8]); a0 = ld("a0", [64, 2, 8])
    g2a = ld("g2a", [128, 512]); g2b = ld("g2b", [32, 512])
    kkc = ld("kkc", [64, 8]); kac = ld("kac", [64, 8]); rkc = ld("rkc", [64, 8])
    lng = ld("lng", [64, 512]); lnb = ld("lnb", [64, 512])
    masks = ld("masks", [64, 4, 64])
    oka = kb.sb([64, 8], F32, "oka"); kah = kb.sb([64, 8], F32, "kah")
    kb.v("dve", "tensor_scalar", [kac], [oka], out=oka[:, :], in0=kac[:, :], scalar1=-1.0, scalar2=1.0, op0=ALU.mult, op1=ALU.add)
    kb.v("dve", "tensor_scalar", [kac], [kah], out=kah[:, :], in0=kac[:, :], scalar1=0.5, scalar2=None, op0=ALU.mult)
    ones = kb.sb([64, 64], F32, "ones")
    kb.v("dve", "memset", [], [ones], ones[:, :], 1.0)
    Hs = [kb.sb([64, 8, 64], F32, "Hf"), kb.sb([64, 8, 64], F32, "Hb")]

    def T(shape, name, dt=F32):
        return kb.sb(shape, dt, name)

    class Slot:
        pass

    def make_slot(si):
        S = Slot()
        S.B = kb.pbanks[4 * si:4 * si + 4]
        S.uc = T([128, NCH, 66], 'uc', BF16)
        S.sh = [T([128, 64], 'sh') for _ in range(4)]
        S.P = T([128, RW_TILES, 66], "P")
        S.Z = T([128, RW_TILES, 64], "Z")
        S.thw = T([64, 64], "thw")
        S.sg = T([64, 8, 64], "sg")
        S.A = [T([64, 8, 64], "A0"), T([64, 8, 64], "A1")]
        S.CS = T([64, 8, 65], "CS")
        kb.v("dve", "memset", [], [S.CS], S.CS[:, :, :], 0.0)
        S.G = T([64, 8, 64], "G")
        S.Gp = T([64, 8, 64], "Gp")
        S.Gi = T([64, 8, 64], "Gi")
        S.gC = T([64, 8], "gC")
        S.kkr = T([64, 8, 64], "kkr")
        S.ksq = T([64, 8, 64], "ksq")
        S.rn = T([64, 8, 64], "rn")
        S.X1 = S.ksq
        S.X2 = S.rn
        S.t1 = T([64, 8, 64], "t1")
        S.kdir = T([64, 8, 64], "kdir")
        S.kka = T([64, 8, 64], "kka")
        S.QR = T([64, 8, 128], "QR")
        S.KdT = T([64, 8, 64], "KdT")
        S.AdT = T([64, 8, 64], "AdT")
        S.Kq_t = T([64, 512], "Kq_t")
        S.Ad_t = T([64, 512], "Ad_t")
        S.Kd_t = T([64, 512], "Kd_t")
        S.V_t = T([64, 512], "V_t")
        S.NTs = [S.kkr, S.Gi]
        S.Ns = [S.ksq, S.Gp]
        S.MkT = S.G
        S.PaT = S.sg
        S.PkT = S.rn
        S.X = T([64, 8, 128], "X")
        S.AT = S.t1
        S.RqpT = S.kdir
        S.ybt = S.Kd_t
        S.yt = S.Ad_t
        S.ysq = S.Kq_t
        S.st = T([64, 48], "st")
        S.sgd = T([128, 2, 64], "sgd")
        S.ee = S.kka
        S.yout = S.ysq
        return S
    slots = [make_slot(0), make_slot(1)]

    RS, KS, VS = slice(0, 8), slice(8, 16), slice(16, 24)
    NSTAGE = 50

    def bc(ap, shape, axis):
        return ap.unsqueeze(axis).to_broadcast(shape)

    def v3(bank, p, a):
        return bank.t[0:p, 0:512].rearrange("p (a b) -> p a b", a=a)

    def chunk(S, c0, cn, d, need_out, H, epilogue):
        B = S.B
        P, Z, uc = S.P, S.Z, S.uc
        ns = 0
        wdt = 24 + d
        tiles = list(range(24)) + [wdt] + ([26, 27, 28, 29] if epilogue == "final" else [26 + d])
        lo, hi = max(c0 - 1, 0), min(c0 + cn + 1, L)
        n = hi - lo
        off = lo - (c0 - 1)
        kb.dma(uc[:, :, 0:n], uT_d.t[:, :, lo:hi], [uT_d], [uc])
        if off or (hi - (c0 - 1)) < cn + 2:
            kb.v("pool", "memset", [], [P], P[:, :, :], 0.0)
        groups = [tiles[i:i + 7] for i in range(0, len(tiles), 7)]
        for gi, grp in enumerate(groups):
            pb_ = B[gi % 4]
            pv = pb_.t[:, 0:7 * 66].rearrange("p (a b) -> p a b", a=7)
            for ji, j in enumerate(grp):
                cc, w = RW_COLS[j]
                for c in range(NCH):
                    kb.mm(pv[0:w, ji, 0:n], wr[:, c, cc:cc + w], uc[:, c, 0:n], [wr, uc], [pb_],
                          start=c == 0, stop=c == NCH - 1)
            j0, j1 = grp[0], grp[-1]
            if j1 < 28 and j1 - j0 == len(grp) - 1:
                kb.act(P[0:64, j0:j1 + 1, off:off + n], pv[0:64, 0:len(grp), 0:n], AF.Copy, [pb_], [P])
            else:
                for ji, j in enumerate(grp):
                    w = RW_COLS[j][1]
                    kb.act(P[0:w, j, off:off + n], pv[0:w, ji, 0:n], AF.Copy, [pb_], [P])
            ns += 1; yield
        while ns < 5:
            ns += 1; yield
        for j in tiles:
            w = RW_COLS[j][1]
            eng = "dve" if j % 3 else "pool"
            kb.act(Z[0:w, j, 0:cn], P[0:w, j, 1:cn + 1], AF.Copy, [P, muc], [Z], scale=muc[0:w, j:j + 1])
            kb.v("dve", "scalar_tensor_tensor", [P, mu0, Z], [Z], out=Z[0:w, j, 0:cn], in0=P[0:w, j, 0:cn],
                 scalar=mu0[0:w, j:j + 1], in1=Z[0:w, j, 0:cn], op0=ALU.mult, op1=ALU.add)
            kb.v("dve", "scalar_tensor_tensor", [P, mu1, Z], [Z], out=Z[0:w, j, 0:cn], in0=P[0:w, j, 2:cn + 2],
                 scalar=mu1[0:w, j:j + 1], in1=Z[0:w, j, 0:cn], op0=ALU.mult, op1=ALU.add)
            if tiles.index(j) % 3 == 2:
                ns += 1; yield
        while ns < 15:
            ns += 1; yield
        thw, sg, A, CS = S.thw, S.sg, S.A, S.CS
        kb.act(thw[:, 0:cn], Z[0:64, wdt, 0:cn], AF.Tanh, [Z], [thw])
        pl = B[0]
        plv = v3(pl, 64, 8)
        for h in range(8):
            kb.mm(plv[:, h, 0:cn], w2[:, d, 64 * h:64 * h + 64], thw[:, 0:cn], [w2, thw], [pl], r32=True)
        for h in range(8):
            kb.act(sg[:, h, 0:cn], plv[:, h, 0:cn], AF.Sigmoid, [pl, w0], [sg], bias=w0[:, d, h:h + 1])
        ns += 1; yield
        for dd in ((0, 1) if epilogue == "final" else (d,)):
            pa = B[1 + dd]
            pav = v3(pa, 64, 8)
            for h in range(8):
                kb.mm(pav[:, h, 0:cn], a2[:, dd, 64 * h:64 * h + 64], Z[0:64, 26 + dd, 0:cn], [a2, Z], [pa], r32=True)
            for h in range(8):
                kb.act(A[dd][:, h, 0:cn], pav[:, h, 0:cn], AF.Sigmoid, [pa, a0], [A[dd]], bias=a0[:, dd, h:h + 1])
        ns += 1; yield
        Ad_ = A[d]
        sh = [64, 8, cn]
        kkr, ksq, rn, t1, kdir, kka = S.kkr, S.ksq, S.rn, S.t1, S.kdir, S.kka
        kb.v("pool", "tensor_tensor", [Z, kkc], [kkr], out=kkr[:, :, 0:cn], in0=Z[0:64, KS, 0:cn], in1=bc(kkc[:, :], sh, 2), op=ALU.mult)
        kb.act(ksq[:, :, 0:cn], kkr[:, :, 0:cn], AF.Square, [kkr], [ksq])
        pn = B[3]
        pnv = v3(pn, 64, 8)
        for h in range(8):
            kb.mm(pnv[:, h, 0:cn], ones[:, :], ksq[:, h, 0:cn], [ones, ksq], [pn], r32=True)
        kb.act(rn[:, :, 0:cn], pnv[:, :, 0:cn], AF.Ln, [pn], [rn], bias=1e-18, scale=1.0)
        kb.act(rn[:, :, 0:cn], rn[:, :, 0:cn], AF.Exp, [rn], [rn], scale=-0.5)
        kb.v("dve", "tensor_tensor", [kkr, rn], [kkr], out=kkr[:, :, 0:cn], in0=kkr[:, :, 0:cn], in1=rn[:, :, 0:cn], op=ALU.mult)
        ns += 1; yield
        kb.v("pool", "tensor_tensor", [Ad_, kac], [t1], out=t1[:, :, 0:cn], in0=Ad_[:, :, 0:cn], in1=bc(kac[:, :], sh, 2), op=ALU.mult)
        kb.v("pool", "tensor_tensor", [t1, oka], [t1], out=t1[:, :, 0:cn], in0=t1[:, :, 0:cn], in1=bc(oka[:, :], sh, 2), op=ALU.add)
        kb.v("pool", "tensor_tensor", [t1, Z], [kdir], out=kdir[:, :, 0:cn], in0=t1[:, :, 0:cn], in1=Z[0:64, KS, 0:cn], op=ALU.mult)
        kb.v("dve", "tensor_tensor", [kkr, Ad_], [kka], out=kka[:, :, 0:cn], in0=kkr[:, :, 0:cn], in1=Ad_[:, :, 0:cn], op=ALU.mult)
        ns += 1; yield
        for h in range(8):
            kb.v("dve", "tensor_tensor_scan", [ones, sg], [CS], out=CS[:, h, 1:cn + 1], data0=ones[:, 0:cn], data1=sg[:, h, 0:cn],
                 initial=0.0, op0=ALU.mult, op1=ALU.add)
        ns += 1; yield
        G, Gp, Gi, gC, X1, X2 = S.G, S.Gp, S.Gi, S.gC, S.X1, S.X2
        if d == 0:
            kb.act(G[:, :, 0:cn], CS[:, :, 1:cn + 1], AF.Exp, [CS], [G], scale=-S0)
            kb.act(Gp[:, :, 0:cn], CS[:, :, 0:cn], AF.Exp, [CS], [Gp], scale=-S0)
            kb.act(Gi[:, :, 0:cn], CS[:, :, 1:cn + 1], AF.Exp, [CS], [Gi], scale=S0)
        else:
            totb = bc(CS[:, :, cn], sh, 2)
            kb.v("dve", "tensor_tensor", [CS], [X1], out=X1[:, :, 0:cn], in0=CS[:, :, 0:cn], in1=totb, op=ALU.subtract)
            kb.v("dve", "tensor_tensor", [CS], [X2], out=X2[:, :, 0:cn], in0=CS[:, :, 1:cn + 1], in1=totb, op=ALU.subtract)
            kb.act(G[:, :, 0:cn], X1[:, :, 0:cn], AF.Exp, [X1], [G], scale=S0)
            kb.act(Gp[:, :, 0:cn], X2[:, :, 0:cn], AF.Exp, [X2], [Gp], scale=S0)
            kb.act(Gi[:, :, 0:cn], X1[:, :, 0:cn], AF.Exp, [X1], [Gi], scale=-S0)
        kb.act(gC[:, :], CS[:, :, cn], AF.Exp, [CS], [gC], scale=-S0)
        ns += 1; yield
        QR, KdT, AdT = S.QR, S.KdT, S.AdT
        kb.v("dve", "tensor_tensor", [kkr, Gp], [QR], out=QR[:, :, 0:cn], in0=kkr[:, :, 0:cn], in1=Gp[:, :, 0:cn], op=ALU.mult)
        kb.v("pool", "tensor_tensor", [Z, G], [QR], out=QR[:, :, cn:2 * cn], in0=Z[0:64, RS, 0:cn], in1=G[:, :, 0:cn], op=ALU.mult)
        kb.v("pool", "tensor_tensor", [kdir, Gi], [KdT], out=KdT[:, :, 0:cn], in0=kdir[:, :, 0:cn], in1=Gi[:, :, 0:cn], op=ALU.mult)
        kb.v("dve", "tensor_tensor", [kka, Gi], [AdT], out=AdT[:, :, 0:cn], in0=kka[:, :, 0:cn], in1=Gi[:, :, 0:cn], op=ALU.mult)
        ns += 1; yield
        Kq_t, Ad_t, Kd_t, V_t = S.Kq_t, S.Ad_t, S.Kd_t, S.V_t
        for bi, (src, si, dst) in enumerate(((QR, 0, Kq_t), (AdT, 0, Ad_t), (KdT, 0, Kd_t), (Z, 16, V_t))):
            pt = B[bi]
            for h in range(8):
                kb.tr(pt[0:cn, h * 64:(h + 1) * 64], src[0:64, si + h, 0:cn], ident[0:64, 0:64], [src, ident], [pt])
            if bi % 2 == 0:
                kb.act(dst[0:cn, :], pt[0:cn, :], AF.Copy, [pt], [dst])
            else:
                kb.v("dve", "tensor_copy", [pt], [dst], out=dst[0:cn, :], in_=pt[0:cn, :])
                ns += 1; yield
        mS, mI, mN = (0, 1, 2) if d == 0 else (2, 3, 0)
        rw = 2 * cn if need_out else cn
        pSA = [B[0], B[1]]; pSK = [B[2], B[3]]
        NTs, Ns, MkT, PaT, PkT, X = S.NTs, S.Ns, S.MkT, S.PaT, S.PkT, S.X
        for h in range(8):
            hv = pSA[h // 4].t[0:cn, 0:4 * rw].rearrange("p (h s) -> p h s", h=4)
            kb.mm(hv[:, h % 4, :], AdT[:, h, 0:cn], QR[:, h, 0:rw], [AdT, QR], [pSA[h // 4]], r32=True)
            hv2 = pSK[h // 4].t[0:cn, 0:4 * rw].rearrange("p (h s) -> p h s", h=4)
            kb.mm(hv2[:, h % 4, :], KdT[:, h, 0:cn], QR[:, h, 0:rw], [KdT, QR], [pSK[h // 4]], r32=True)
        NT, Nm = NTs[0], Ns[0]
        for half in range(2):
            hv = pSA[half].t[0:cn, 0:4 * rw].rearrange("p (h s) -> p h s", h=4)
            hv2 = pSK[half].t[0:cn, 0:4 * rw].rearrange("p (h s) -> p h s", h=4)
            hs = slice(4 * half, 4 * half + 4)
            kb.v("dve", "tensor_tensor", [pSA[half], masks], [NT], out=NT[0:cn, hs, 0:cn], in0=hv[:, :, 0:cn],
                 in1=bc(masks[0:cn, mS, 0:cn], [cn, 4, cn], 1), op=ALU.mult)
            kb.v("dve", "tensor_tensor", [pSK[half], masks], [MkT], out=MkT[0:cn, hs, 0:cn], in0=hv2[:, :, 0:cn],
                 in1=bc(masks[0:cn, mS, 0:cn], [cn, 4, cn], 1), op=ALU.mult)
            if need_out:
                kb.v("dve", "tensor_tensor", [pSA[half], masks], [PaT], out=PaT[0:cn, hs, 0:cn], in0=hv[:, :, cn:2 * cn],
                     in1=bc(masks[0:cn, mI, 0:cn], [cn, 4, cn], 1), op=ALU.mult)
                kb.v("dve", "tensor_tensor", [pSK[half], masks], [PkT], out=PkT[0:cn, hs, 0:cn], in0=hv2[:, :, cn:2 * cn],
                     in1=bc(masks[0:cn, mI, 0:cn], [cn, 4, cn], 1), op=ALU.mult)
        ns += 1; yield
        pN = B[0]
        pNv = pN.t[0:cn, 0:8 * cn].rearrange("p (h s) -> p h s", h=8)
        pW = B[1]
        pWv = v3(pW, cn, 8)
        for h in range(8):
            kb.mm(pNv[:, h, :], QR[:, h, 0:cn], AdT[:, h, 0:cn], [QR, AdT], [pN], r32=True)
        for h in range(8):
            kb.mm(pWv[:, h, :], MkT[0:cn, h, 0:cn], V_t[0:cn, 64 * h:64 * h + 64], [MkT, V_t], [pW], r32=True)
        kb.v("dve", "tensor_tensor", [pN, masks], [Nm], out=Nm[0:cn, :, 0:cn], in0=pNv,
             in1=bc(masks[0:cn, mN, 0:cn], [cn, 8, cn], 1), op=ALU.mult)
        kb.v("pool", "tensor_copy", [Kq_t], [X], out=X[0:cn, :, 0:64], in_=Kq_t[0:cn, :].rearrange("p (h v) -> p h v", h=8))
        kb.act(X[0:cn, :, 64:128], pWv, AF.Copy, [pW], [X], scale=-1.0)
        ns += 1; yield
        nlev = 5 if cn > 32 else (4 if cn > 16 else 3)
        pF = [B[2], B[3]]

        def factor(PT_, sign):
            for h in range(8):
                fv = v3(pF[h // 4], cn, 4)
                kb.mm(fv[:, h % 4, :], PT_[0:cn, h, 0:cn], X[0:cn, h, :], [PT_, X], [pF[h // 4]], r32=True)
            for half in range(2):
                fv = v3(pF[half], cn, 4)
                hs = slice(4 * half, 4 * half + 4)
                kb.v("dve", "tensor_tensor", [X, pF[half]], [X], out=X[0:cn, hs, :], in0=X[0:cn, hs, :], in1=fv,
                     op=ALU.subtract if sign < 0 else ALU.add)
        factor(NT, -1)
        ns += 1; yield
        cur = 0
        for lvl in range(5):
            if lvl < nlev:
                NTc, Nc = NTs[cur], Ns[cur]
                NTn, Nn = NTs[1 - cur], Ns[1 - cur]
                pa_, pb2 = B[0], B[1]
                pav = pa_.t[0:cn, 0:8 * cn].rearrange("p (h s) -> p h s", h=8)
                pbv = pb2.t[0:cn, 0:8 * cn].rearrange("p (h s) -> p h s", h=8)
                last = lvl == nlev - 1
                for h in range(8):
                    kb.mm(pav[:, h, :], Nc[0:cn, h, 0:cn], NTc[0:cn, h, 0:cn], [Nc, NTc], [pa_], r32=True)
                    if not last:
                        kb.mm(pbv[:, h, :], NTc[0:cn, h, 0:cn], Nc[0:cn, h, 0:cn], [Nc, NTc], [pb2], r32=True)
                kb.act(NTn[0:cn, :, 0:cn], pav, AF.Copy, [pa_], [NTn])
                if not last:
                    kb.act(Nn[0:cn, :, 0:cn], pbv, AF.Copy, [pb2], [Nn])
            ns += 1; yield
            if lvl < nlev:
                factor(NTn, +1)
                cur = 1 - cur
            ns += 1; yield
        AT, RqpT = S.AT, S.RqpT
        pA = B[0]
        pAv = v3(pA, 64, 8)
        for h in range(8):
            kb.mm(pAv[:, h, :], X[0:cn, h, 0:64], Ad_t[0:cn, 64 * h:64 * h + 64], [X, Ad_t], [pA], r32=True)
        kb.v("dve", "tensor_tensor", [ident, pA], [AT], out=AT[:, :, :], in0=bc(ident[0:64, 0:64], [64, 8, 64], 1), in1=pAv, op=ALU.subtract)
        if need_out:
            pR = B[1]
            pRv = pR.t[0:64, 0:8 * cn].rearrange("p (a b) -> p a b", a=8)
            for h in range(8):
                kb.mm(pRv[:, h, :], X[0:cn, h, 0:64], PaT[0:cn, h, 0:cn], [X, PaT], [pR], r32=True)
            kb.v("dve", "tensor_tensor", [QR, pR], [RqpT], out=RqpT[:, :, 0:cn], in0=QR[:, :, cn:2 * cn], in1=pRv, op=ALU.subtract)
        ns += 1; yield
        if need_out:
            pY = B[2]
            pYv = v3(pY, cn, 8)
            for h in range(8):
                kb.mm(pYv[:, h, :], RqpT[:, h, 0:cn], H[:, h, :], [RqpT, H], [pY], start=True, stop=False, r32=True)
                kb.mm(pYv[:, h, :], PkT[0:cn, h, 0:cn], V_t[0:cn, 64 * h:64 * h + 64], [PkT, V_t], [pY], start=False, stop=False, r32=True)
                kb.mm(pYv[:, h, :], PaT[0:cn, h, 0:cn], X[0:cn, h, 64:128], [PaT, X], [pY], start=False, stop=True, r32=True)
        pH = B[3]
        pHv = v3(pH, 64, 8)
        for h in range(8):
            kb.mm(pHv[:, h, :], AT[:, h, :], H[:, h, :], [AT, H], [pH], start=True, stop=False, r32=True)
            kb.mm(pHv[:, h, :], Kd_t[0:cn, 64 * h:64 * h + 64], V_t[0:cn, 64 * h:64 * h + 64], [Kd_t, V_t], [pH], start=False, stop=False, r32=True)
            kb.mm(pHv[:, h, :], Ad_t[0:cn, 64 * h:64 * h + 64], X[0:cn, h, 64:128], [Ad_t, X], [pH], start=False, stop=True, r32=True)
        kb.v("dve", "tensor_tensor", [pH, gC], [H], out=H[:, :, :], in0=pHv, in1=bc(gC[:, :], [64, 8, 64], 2), op=ALU.mult)
        ns += 1; yield
        ybt, yt, ysq, st, sgd, ee, yout = S.ybt, S.yt, S.ysq, S.st, S.sgd, S.ee, S.yout
        if epilogue == "store":
            kb.act(yt[0:cn, :], pY[0:cn, :], AF.Copy, [pY], [yt])
            kb.dma(ybwd_d.t[c0:c0 + cn, :], yt[0:cn, :], [yt], [ybwd_d])
        elif epilogue == "final":
            kb.dma(ybt[0:cn, :], ybwd_d.t[c0:c0 + cn, :], [ybwd_d], [ybt])
            kb.v("dve", "tensor_tensor", [pY, ybt], [yt], out=yt[0:cn, :], in0=pY[0:cn, :], in1=ybt[0:cn, :], op=ALU.add)
            y3 = yt[0:cn, :].rearrange("p (h v) -> p h v", h=8)
            kb.v("dve", "tensor_reduce", [yt], [st], out=st[0:cn, 0:8], in_=y3, axis=AX.X, op=ALU.add)
            kb.act(ysq[0:cn, :], yt[0:cn, :], AF.Square, [yt], [ysq])
            kb.v("dve", "tensor_reduce", [ysq], [st], out=st[0:cn, 8:16], in_=ysq[0:cn, :].rearrange("p (h v) -> p h v", h=8), axis=AX.X, op=ALU.add)
            kb.v("dve", "tensor_scalar", [st], [st], out=st[0:cn, 16:24], in0=st[0:cn, 0:8], scalar1=1.0 / 64, scalar2=None, op0=ALU.mult)
            kb.v("dve", "tensor_tensor", [st], [st], out=st[0:cn, 24:32], in0=st[0:cn, 16:24], in1=st[0:cn, 16:24], op=ALU.mult)
            kb.v("dve", "scalar_tensor_tensor", [st], [st], out=st[0:cn, 32:40], in0=st[0:cn, 8:16], scalar=1.0 / 64, in1=st[0:cn, 24:32],
                 op0=ALU.mult, op1=ALU.subtract)
            kb.act(st[0:cn, 40:48], st[0:cn, 32:40], AF.Sqrt, [st], [st], bias=64e-5, scale=1.0)
            kb.v("dve", "reciprocal", [st], [st], out=st[0:cn, 40:48], in_=st[0:cn, 40:48])
            kb.v("dve", "tensor_tensor", [yt, st], [yt], out=y3, in0=y3, in1=bc(st[0:cn, 16:24], [cn, 8, 64], 2), op=ALU.subtract)
            kb.v("dve", "tensor_tensor", [yt, st], [yt], out=y3, in0=y3, in1=bc(st[0:cn, 40:48], [cn, 8, 64], 2), op=ALU.mult)
            kb.v("pool", "tensor_tensor", [yt, lng], [yt], out=yt[0:cn, :], in0=yt[0:cn, :], in1=lng[0:cn, :], op=ALU.mult)
            kb.v("pool", "tensor_tensor", [yt, lnb], [yt], out=yt[0:cn, :], in0=yt[0:cn, :], in1=lnb[0:cn, :], op=ALU.add)
        ns += 1; yield
        if epilogue == "final":
            kb.v("pool", "tensor_tensor", [A[0], A[1]], [ee], out=ee[:, :, 0:cn], in0=A[0][:, :, 0:cn], in1=A[1][:, :, 0:cn], op=ALU.add)
            kb.v("pool", "tensor_tensor", [ee, kah], [ee], out=ee[:, :, 0:cn], in0=ee[:, :, 0:cn], in1=bc(kah[:, :], sh, 2), op=ALU.mult)
            kb.v("pool", "tensor_tensor", [ee, oka], [ee], out=ee[:, :, 0:cn], in0=ee[:, :, 0:cn], in1=bc(oka[:, :], sh, 2), op=ALU.add)
            kb.v("pool", "tensor_tensor", [ee, Z], [ee], out=ee[:, :, 0:cn], in0=ee[:, :, 0:cn], in1=Z[0:64, KS, 0:cn], op=ALU.mult)
            kb.v("pool", "tensor_tensor", [ee, Z], [ee], out=ee[:, :, 0:cn], in0=ee[:, :, 0:cn], in1=Z[0:64, RS, 0:cn], op=ALU.mult)
            kb.v("pool", "tensor_tensor", [ee, rkc], [ee], out=ee[:, :, 0:cn], in0=ee[:, :, 0:cn], in1=bc(rkc[:, :], sh, 2), op=ALU.mult)
            pS = B[0]
            for h in range(8):
                kb.mm(pS[0:cn, h:h + 1], ee[:, h, 0:cn], ones[:, 0:1], [ee, ones], [pS])
            kb.act(st[0:cn, 0:8], pS[0:cn, 0:8], AF.Copy, [pS], [st])
            kb.v("dve", "tensor_tensor", [V_t, st], [ysq], out=ysq[0:cn, :].rearrange("p (h v) -> p h v", h=8),
                 in0=V_t[0:cn, :].rearrange("p (h v) -> p h v", h=8), in1=bc(st[0:cn, 0:8], [cn, 8, 64], 2), op=ALU.mult)
            kb.v("dve", "tensor_tensor", [yt, ysq], [yt], out=yt[0:cn, :], in0=yt[0:cn, :], in1=ysq[0:cn, :], op=ALU.add)
            kb.act(sgd[:, 0, 0:cn], Z[:, 28, 0:cn], AF.Sigmoid, [Z], [sgd])
            kb.act(sgd[0:32, 1, 0:cn], Z[0:32, 29, 0:cn], AF.Sigmoid, [Z], [sgd])
            pG = B[1]
            kb.mm(pG[0:cn, :], sgd[:, 0, 0:cn], g2a[:, :], [sgd, g2a], [pG], start=True, stop=False, r32=True)
            kb.mm(pG[0:cn, :], sgd[0:32, 1, 0:cn], g2b[:, :], [sgd, g2b], [pG], start=False, stop=True, r32=True)
            kb.v("dve", "tensor_tensor", [yt, pG], [yout], out=yout[0:cn, :], in0=yt[0:cn, :], in1=pG[0:cn, :], op=ALU.mult)
            kb.dma(ya_d.t[c0:c0 + cn, :], yout[0:cn, :], [yout], [ya_d])
        ns += 1; yield
        while ns < NSTAGE:
            ns += 1; yield
        assert ns == NSTAGE, ns

    def chain(S, lst, d, H, epilogue):
        for (c0, cn) in lst:
            yield from chunk(S, c0, cn, d, epilogue is not None, H, epilogue)

    chunks = tiles_of(2048, 64) + [(2048, 16)] + [(2064 + a, b) for a, b in tiles_of(2048, 64)]
    own = [c for c in chunks if c[0] < OWN]
    oth = [c for c in chunks if c[0] >= OWN]
    Hf, Hb = Hs
    kb.v("dve", "memset", [], [Hf], Hf[:, :, :], 0.0)
    kb.v("dve", "memset", [], [Hb], Hb[:, :, :], 0.0)
    g0 = chain(slots[0], own, 0, Hf, "store")
    g1 = chain(slots[1], list(reversed(oth)), 1, Hb, None)
    step = 0
    d0 = d1 = False
    while not (d0 and d1):
        if not d0:
            try:
                next(g0)
            except StopIteration:
                d0 = True
        if step >= NSTAGE // 2 and not d1:
            try:
                next(g1)
            except StopIteration:
                d1 = True
        step += 1
    rown = list(reversed(own))
    g0 = chain(slots[0], rown[0::2], 1, Hb, "final")
    g1 = chain(slots[1], rown[1::2], 1, Hb, "final")
    step = 0
    d0 = d1 = False
    while not (d0 and d1):
        if not d0:
            try:
                next(g0)
            except StopIteration:
                d0 = True
        if step >= NSTAGE // 2 and not d1:
            try:
                next(g1)
            except StopIteration:
                d1 = True
        step += 1


def wload(kb, dram, rows, c0, c1, name):
    kch = rows // 128
    t = kb.sb([128, kch, c1 - c0], BF16, name)
    step = 1024
    for a in range(c0, c1, step):
        b = min(a + step, c1)
        kb.dma(t[:, :, a - c0:b - c0], dram.t[:, a:b].rearrange("(c p) n -> p c n", p=128), [dram], [t], q="pool")
    return t


def merge_phase(kb, uT_d, identb, D_, ya_d, yb_d, h2_d):
    B = kb.pbanks
    Wa = wload(kb, D_["w_a"], 512, 0, 1024, "Wa")
    Wb = wload(kb, D_["w_b"], 512, 0, 1024, "Wb")
    Wo = wload(kb, D_["w_o"], 1024, 0, 1024, "Wo")
    Wg = wload(kb, D_["w_in"], 1024, 2720, 4768, "Wg")
    yaf = kb.sb([128, 512], F32, "yaf"); yab = kb.sb([128, 512], BF16, "yab"); ybb = kb.sb([128, 512], BF16, "ybb")
    yaT = kb.sb([128, 4, 128], BF16, "yaT"); ybT = kb.sb([128, 4, 128], BF16, "ybT")
    sgA = kb.sb([128, 1024], F32, "sgA"); sgB = kb.sb([128, 1024], F32, "sgB")
    mg = kb.sb([128, 1024], BF16, "mg"); mT = kb.sb([128, 8, 128], BF16, "mT")
    xt = kb.sb([128, 1024], F32, "xt2"); h2 = kb.sb([128, 1024], F32, "h2")
    uts = [kb.sb([128, NCH, 128], BF16, "utm") for _ in range(2)]
    for i, (t0, n) in enumerate(tiles_of(OWN, 128)):
        ut = uts[i % 2]
        kb.dma(ut[:, :, 0:n], uT_d.t[:, :, t0:t0 + n], [uT_d], [ut])
        kb.dma(yaf[0:n, :], ya_d.t[t0:t0 + n, :], [ya_d], [yaf])
        kb.dma(ybb[0:n, :], yb_d.t[i, 0:n, :], [yb_d], [ybb])
        kb.dma(xt[0:n, :], D_["h"].t[t0:t0 + n, :], [D_["h"]], [xt])
        kb.v("dve", "tensor_copy", [yaf], [yab], out=yab[0:n, :], in_=yaf[0:n, :])
        for src, dst, bk in ((yab, yaT, B[0]), (ybb, ybT, B[1])):
            pv = bk.t.bitcast(BF16).rearrange("p (c t) -> p c t", c=8)
            for k in range(4):
                kb.tr(pv[:, k, 0:n], src[0:n, k * 128:(k + 1) * 128], identb[0:n, 0:n], [src, identb], [bk])
            kb.act(dst[:, :, 0:n], pv[:, 0:4, 0:n], AF.Copy, [bk], [dst])
        for gi, sg_ in enumerate((sgA, sgB)):
            for hb in range(2):
                bk = B[2 + hb]
                col = gi * 1024 + hb * 512
                for c in range(NCH):
                    kb.mm(bk[0:n, :], ut[:, c, 0:n], Wg[:, c, col:col + 512], [ut, Wg], [bk], start=c == 0, stop=c == NCH - 1)
                kb.act(sg_[0:n, hb * 512:(hb + 1) * 512], bk[0:n, :], AF.Sigmoid, [bk], [sg_])
        for (yT_, W_, sg_) in ((yaT, Wa, sgA), (ybT, Wb, sgB)):
            for hb in range(2):
                bk = B[4 + hb]
                for k in range(4):
                    kb.mm(bk[0:n, :], yT_[:, k, 0:n], W_[:, k, hb * 512:(hb + 1) * 512], [yT_, W_], [bk], start=k == 0, stop=k == 3)
                kb.v("dve", "tensor_tensor", [sg_, bk], [sg_], out=sg_[0:n, hb * 512:(hb + 1) * 512],
                     in0=sg_[0:n, hb * 512:(hb + 1) * 512], in1=bk[0:n, :], op=ALU.mult)
        kb.v("dve", "tensor_tensor", [sgA, sgB], [mg], out=mg[0:n, :], in0=sgA[0:n, :], in1=sgB[0:n, :], op=ALU.add)
        bk = B[6]
        pv = bk.t.bitcast(BF16).rearrange("p (c t) -> p c t", c=8)
        for k in range(8):
            kb.tr(pv[:, k, 0:n], mg[0:n, k * 128:(k + 1) * 128], identb[0:n, 0:n], [mg, identb], [bk])
        kb.act(mT[:, :, 0:n], pv[:, :, 0:n], AF.Copy, [bk], [mT])
        for hb in range(2):
            bk = B[hb]
            for k in range(8):
                kb.mm(bk[0:n, :], mT[:, k, 0:n], Wo[:, k, hb * 512:(hb + 1) * 512], [mT, Wo], [bk], start=k == 0, stop=k == 7)
            kb.v("dve", "tensor_tensor", [xt, bk], [h2], out=h2[0:n, hb * 512:(hb + 1) * 512], in0=xt[0:n, hb * 512:(hb + 1) * 512],
                 in1=bk[0:n, :], op=ALU.add)
        kb.dma(h2_d.t[t0:t0 + n, :], h2[0:n, :], [h2], [h2_d])


def ffn_phase(kb, identb, D_, h2_d, out_d):
    B = kb.pbanks
    GT = 256
    W1 = wload(kb, D_["w_ff1"], 1024, 0, 4096, "W1")
    W2 = wload(kb, D_["w_ff2"], 4096, 0, 1024, "W2")
    gf = kb.sb([128, NCH], F32, "gffn"); gfin = kb.sb([128, 1024], F32, "gfin")
    kb.dma(gf.t, D_["gffn"].t, [D_["gffn"]], [gf])
    kb.dma(gfin.t, D_["gfin"].t, [D_["gfin"]], [gfin])
    h2g = [kb.sb([128, 1024], F32, "h2f") for _ in range(2)]
    xs = kb.sb([128, 1024], BF16, "xs2")
    ss = kb.sb([128, 8], F32, "ss2"); nT = kb.sb([128, 8, GT], BF16, "nT")
    frs = [kb.sb([128, GT], F32, "fr") for _ in range(2)]
    f2T = kb.sb([128, 32, GT], BF16, "f2T")
    h3 = kb.sb([128, 1024], F32, "h3"); ot = kb.sb([128, 1024], F32, "ot")
    junk = ot
    for gi, (g0, gn) in enumerate(tiles_of(OWN, GT)):
        subs = tiles_of(gn, 128)
        for si, (o0, n) in enumerate(subs):
            t0 = g0 + o0
            h2 = h2g[si]
            kb.dma(h2[0:n, :], h2_d.t[t0:t0 + n, :], [h2_d], [h2])
            kb.act(junk[0:n, :], h2[0:n, :], AF.Square, [h2], [junk, ss], accum_out=ss[0:n, 0:1])
            kb.act(ss[0:n, 1:2], ss[0:n, 0:1], AF.Sqrt, [ss], [ss], scale=1.0 / D, bias=EPS)
            kb.v("dve", "reciprocal", [ss], [ss], out=ss[0:n, 2:3], in_=ss[0:n, 1:2])
            kb.v("dve", "tensor_scalar", [h2, ss], [xs], out=xs[0:n, :], in0=h2[0:n, :], scalar1=ss[0:n, 2:3], scalar2=None, op0=ALU.mult)
            bk = B[si]
            pv = bk.t.bitcast(BF16).rearrange("p (c t) -> p c t", c=8)
            for c in range(NCH):
                kb.tr(pv[:, c, 0:n], xs[0:n, c * 128:(c + 1) * 128], identb[0:n, 0:n], [xs, identb], [bk])
            kb.v("dve", "tensor_tensor", [bk, gf], [nT], out=nT[:, :, o0:o0 + n], in0=pv[:, :, 0:n],
                 in1=gf[:, :].unsqueeze(2).to_broadcast([128, NCH, n]), op=ALU.mult)
        for ft in range(32):
            bk = B[2 + ft % 3]
            fr = frs[ft % 2]
            for c in range(NCH):
                kb.mm(bk[:, 0:gn], W1[:, c, ft * 128:(ft + 1) * 128], nT[:, c, 0:gn], [W1, nT], [bk], start=c == 0, stop=c == NCH - 1)
            kb.act(fr[:, 0:gn], bk[:, 0:gn], AF.Relu, [bk], [fr])
            kb.v("dve", "tensor_tensor", [fr], [f2T], out=f2T[:, ft, 0:gn], in0=fr[:, 0:gn], in1=fr[:, 0:gn], op=ALU.mult)
        for si, (o0, n) in enumerate(subs):
            t0 = g0 + o0
            h2 = h2g[si]
            for hb in range(2):
                bk = B[5 + (hb + 2 * si) % 3]
                for k in range(32):
                    kb.mm(bk[0:n, :], f2T[:, k, o0:o0 + n], W2[:, k, hb * 512:(hb + 1) * 512], [f2T, W2], [bk], start=k == 0, stop=k == 31)
                kb.v("dve", "tensor_tensor", [h2, bk], [h3], out=h3[0:n, hb * 512:(hb + 1) * 512], in0=h2[0:n, hb * 512:(hb + 1) * 512],
                     in1=bk[0:n, :], op=ALU.add)
            kb.act(junk[0:n, :], h3[0:n, :], AF.Square, [h3], [junk, ss], accum_out=ss[0:n, 4:5])
            kb.act(ss[0:n, 5:6], ss[0:n, 4:5], AF.Sqrt, [ss], [ss], scale=1.0 / D, bias=EPS)
            kb.v("dve", "reciprocal", [ss], [ss], out=ss[0:n, 6:7], in_=ss[0:n, 5:6])
            kb.v("dve", "scalar_tensor_tensor", [h3, ss, gfin], [ot], out=ot[0:n, :], in0=h3[0:n, :], scalar=ss[0:n, 6:7], in1=gfin[0:n, :],
                 op0=ALU.mult, op1=ALU.mult)
            k = kb.dma(out_d.t[t0:t0 + n, :], ot[0:n, :], [ot], [])
            kb.out_keys.append(k)

def tiles_of(n, step):
    out = []
    t = 0
    while t < n:
        out.append((t, min(step, n - t)))
        t += step
    return out


def build(debug=None, upto="all"):
    nc = bass.Bass("TRN2", target_bir_lowering=False)
    kb = KB(nc, debug)
    kb.setup_mem()
    fw = kb.fw
    h_d = kb.dram_in("h", [L, D])
    gmix_d = kb.dram_in("gmix", [128, NCH])
    ident_d = kb.dram_in("ident", [128, 128])
    out_d = kb.dram_out("out", [OWN, D])

    ident = kb.sb([128, 128], F32, "ident")
    identb = kb.sb([128, 128], BF16, "identb")
    gmix = kb.sb([128, NCH], F32, "gmix")
    kb.dma(ident[:, :], ident_d[:, :], [ident_d], [ident])
    kb.dma(gmix[:, :], gmix_d[:, :], [gmix_d], [gmix])
    kb.v("dve", "tensor_copy", [ident], [identb], out=identb[:, :], in_=ident[:, :])

    win_d = kb.dram_in("w_in", [D, INW])
    kb.from_top = True
    wr_pre = rwkv_load_w(kb, {"w_in": win_d})
    kb.from_top = False
    mA = kb.mark()
    uT_d = kb.dram_tmp("uT_scr", [128, NCH, L], BF16)
    xts = [kb.sb([128, D], F32, "xt") for _ in range(2)]
    xss = [kb.sb([128, D], BF16, "xs") for _ in range(2)]
    junk = kb.sb([128, D], BF16, "junk")
    sss = [kb.sb([128, 4], F32, "ss") for _ in range(2)]
    uts = [kb.sb([128, NCH, 128], BF16, "ut") for _ in range(2)]

    def phase_a_tile(i, t0, n):
        xt = xts[i % 2]
        xs = xss[i % 2]
        ss = sss[i % 2]
        ut = uts[i % 2]
        kb.dma(xt[0:n, :], h_d[t0:t0 + n, :], [h_d], [xt])
        kb.act(junk[0:n, :], xt[0:n, :], AF.Square, [xt], [junk, ss], accum_out=ss[0:n, 0:1])
        kb.act(ss[0:n, 1:2], ss[0:n, 0:1], AF.Sqrt, [ss], [ss], scale=1.0 / D, bias=EPS)
        kb.v("dve", "reciprocal", [ss], [ss], out=ss[0:n, 2:3], in_=ss[0:n, 1:2])
        kb.v("dve", "tensor_scalar", [xt, ss], [xs], out=xs[0:n, :], in0=xt[0:n, :],
             scalar1=ss[0:n, 2:3], scalar2=None, op0=ALU.mult)
        pb = kb.bank()
        pv = pb.t.bitcast(BF16).rearrange("p (c t) -> p c t", c=NCH)
        for c in range(NCH):
            kb.tr(pv[:, c, 0:n], xs[0:n, c * 128:(c + 1) * 128], identb[0:n, 0:n], [xs, identb], [pb])
        kb.v("dve", "tensor_tensor", [pb, gmix], [ut], out=ut[:, :, 0:n], in0=pv[:, :, 0:n],
             in1=gmix[:, :].unsqueeze(2).to_broadcast([128, NCH, n]), op=ALU.mult)
        kb.dma(uT_d.t[:, :, t0:t0 + n], ut[:, :, 0:n], [ut], [uT_d])
        return ut

    ropeC_d = kb.dram_in("ropeC", [L, 64])
    ropeS_d = kb.dram_in("ropeS", [L, 64])
    qg_d = kb.dram_in("qg", [128, 64])
    kg_d = kb.dram_in("kg", [128, 64])
    NT = 33
    NQT = 17
    mB0 = kb.mark()
    y_b = kb.sb([128, NQT, 512], BF16, "y_b")
    mB = kb.mark()
    wqkv = kb.sb([128, NCH, 768], BF16, "wqkv")
    kb.dma(wqkv[:, :, :], win_d.t[:, 1952:2720].rearrange("(c p) n -> p c n", p=128), [win_d], [wqkv], q="pool")
    rcs = [kb.sb([128, 64], F32, "rc") for _ in range(2)]
    rss = [kb.sb([128, 64], F32, "rs") for _ in range(2)]
    qgb = kb.sb([128, 64], F32, "qgb")
    kgb = kb.sb([128, 64], F32, "kgb")
    kb.dma(qgb[:, :], qg_d[:, :], [qg_d], [qgb])
    kb.dma(kgb[:, :], kg_d[:, :], [kg_d], [kgb])
    QT = kb.sb([64, 8, OWN], BF16, "QT")
    KT = kb.sb([64, 2, L], BF16, "KT")
    V1 = kb.sb([128, NT, 2, 65], BF16, "V1")
    kb.v("pool", "memset", [], [V1], V1[:, :, :, 64:65], 1.0)
    sqs = [kb.sb([128, 640], F32, "sq") for _ in range(1)]
    qns = [kb.sb([128, 640], F32, "qn") for _ in range(1)]
    t1s = [kb.sb([128, 640], F32, "t1") for _ in range(1)]
    t2s = [kb.sb([128, 640], F32, "t2") for _ in range(1)]
    qrs = [kb.sb([128, 640], BF16, "qr") for _ in range(2)]
    sms = [kb.sb([128, 32], F32, "sm") for _ in range(2)]
    for i, (t0, n) in enumerate(tiles_of(L, 128)):
        nq = max(0, min(n, OWN - t0))
        ut = phase_a_tile(i, t0, n)
        sq, qn, t1, t2, qr, sm = sqs[0], qns[0], t1s[0], t2s[0], qrs[i % 2], sms[i % 2]
        pq = kb.bank()
        pkv = kb.bank()
        rc, rs = rcs[i % 2], rss[i % 2]
        kb.dma(rc[0:n, :], ropeC_d.t[t0:t0 + n, :], [ropeC_d], [rc])
        kb.dma(rs[0:n, :], ropeS_d.t[t0:t0 + n, :], [ropeS_d], [rs])
        for c in range(NCH):
            if nq:
                kb.mm(pq[0:n, 0:512], ut[:, c, 0:n], wqkv[:, c, 0:512], [ut, wqkv], [pq], start=c == 0, stop=c == NCH - 1)
            kb.mm(pkv[0:n, 0:256], ut[:, c, 0:n], wqkv[:, c, 512:768], [ut, wqkv], [pkv], start=c == 0, stop=c == NCH - 1)
        h0 = 0 if nq else 8
        c0 = h0 * 64
        nh = 10 - h0
        if nq:
            kb.act(sq[0:n, 0:512], pq[0:n, 0:512], AF.Square, [pq], [sq])
        kb.act(sq[0:n, 512:640], pkv[0:n, 0:128], AF.Square, [pkv], [sq])
        kb.v("dve", "tensor_reduce", [sq], [sm], out=sm[0:n, h0:10],
             in_=sq[0:n, c0:640].rearrange("p (h c) -> p h c", c=64), axis=AX.X, op=ALU.add)
        kb.act(sm[0:n, 10 + h0:20], sm[0:n, h0:10], AF.Sqrt, [sm], [sm], scale=1.0 / 64, bias=EPS)
        kb.v("dve", "reciprocal", [sm], [sm], out=sm[0:n, 20 + h0:30], in_=sm[0:n, 10 + h0:20])
        if nq:
            kb.v("dve", "tensor_tensor", [pq, sm], [qn], out=qn[0:n, 0:512].rearrange("p (h c) -> p h c", c=64),
                 in0=pq[0:n, 0:512].rearrange("p (h c) -> p h c", c=64),
                 in1=sm[0:n, 20:28].unsqueeze(2).to_broadcast([n, 8, 64]), op=ALU.mult)
            kb.v("dve", "tensor_tensor", [qn, qgb], [qn], out=qn[0:n, 0:512].rearrange("p (h c) -> p h c", c=64),
                 in0=qn[0:n, 0:512].rearrange("p (h c) -> p h c", c=64),
                 in1=qgb[0:n, :].unsqueeze(1).to_broadcast([n, 8, 64]), op=ALU.mult)
        kb.v("dve", "tensor_tensor", [pkv, sm], [qn], out=qn[0:n, 512:640].rearrange("p (h c) -> p h c", c=64),
             in0=pkv[0:n, 0:128].rearrange("p (h c) -> p h c", c=64),
             in1=sm[0:n, 28:30].unsqueeze(2).to_broadcast([n, 2, 64]), op=ALU.mult)
        kb.v("dve", "tensor_tensor", [qn, kgb], [qn], out=qn[0:n, 512:640].rearrange("p (h c) -> p h c", c=64),
             in0=qn[0:n, 512:640].rearrange("p (h c) -> p h c", c=64),
             in1=kgb[0:n, :].unsqueeze(1).to_broadcast([n, 2, 64]), op=ALU.mult)
        kb.v("dve", "tensor_tensor", [qn, rc], [t1], out=t1[0:n, c0:640].rearrange("p (h c) -> p h c", c=64),
             in0=qn[0:n, c0:640].rearrange("p (h c) -> p h c", c=64),
             in1=rc[0:n, :].unsqueeze(1).to_broadcast([n, nh, 64]), op=ALU.mult)
        for hf in range(2):
            o_v = t2[0:n, c0:640].rearrange("p (h a x f) -> p h a x f", a=2, x=2, f=16)[:, :, :, hf, :]
            i_v = qn[0:n, c0:640].rearrange("p (h a x f) -> p h a x f", a=2, x=2, f=16)[:, :, :, 1 - hf, :]
            s_v = rs[0:n, :].rearrange("p (a x f) -> p a x f", a=2, x=2)[:, :, hf, :].unsqueeze(1).to_broadcast([n, nh, 2, 16])
            kb.v("dve", "tensor_tensor", [qn, rs], [t2], out=o_v, in0=i_v, in1=s_v, op=ALU.mult)
        kb.v("dve", "tensor_tensor", [t1, t2], [qr], out=qr[0:n, c0:640], in0=t1[0:n, c0:640], in1=t2[0:n, c0:640], op=ALU.add)
        ptk = kb.bank()
        ptkv = ptk.t.bitcast(BF16).rearrange("p (c t) -> p c t", c=8)
        for g in range(2):
            kb.tr(ptkv[0:64, g, 0:n], qr[0:n, 512 + g * 64:512 + (g + 1) * 64], identb[0:n, 0:n], [qr, identb], [ptk])
        kb.v("dve", "tensor_copy", [ptk], [KT], out=KT[:, :, t0:t0 + n], in_=ptkv[0:64, 0:2, 0:n])
        if nq:
            ptq = kb.bank()
            ptqv = ptq.t.bitcast(BF16).rearrange("p (c t) -> p c t", c=8)
            for hh in range(8):
                kb.tr(ptqv[0:64, hh, 0:n], qr[0:n, hh * 64:(hh + 1) * 64], identb[0:n, 0:n], [qr, identb], [ptq])
            kb.act(QT[:, :, t0:t0 + nq], ptqv[0:64, :, 0:nq], AF.Copy, [ptq], [QT])
        kb.act(V1[0:n, i, :, 0:64], pkv[0:n, 128:256].rearrange("p (g c) -> p g c", c=64), AF.Copy, [pkv], [V1])
    fw.barrier()
    Es = [kb.sb([128, 4, 128], BF16, "E") for _ in range(3)]
    rcp = kb.sb([128, 8], F32, "rcp")
    Ob = kb.pbanks[0:4]
    Sb = kb.pbanks[4:6]
    ktiles = tiles_of(L, 128)
    steps = []
    for qi, (q0, nq) in enumerate(tiles_of(OWN, 128)):
        for g in range(2):
            for kt, (k0, nk) in enumerate(ktiles):
                steps.append((qi, q0, nq, g, kt, k0, nk))
    NS_ = len(steps)
    pend = {}

    def issue_s(i):
        qi, q0, nq, g, kt, k0, nk = steps[i]
        sT = Sb[i % 2]
        E = Es[i % 3]
        sTv = sT.t[0:nk, 0:4 * nq].rearrange("p (h q) -> p h q", h=4)
        kb.mm(sTv, KT[0:64, g, k0:k0 + nk], QT[0:64, 4 * g:4 * g + 4, q0:q0 + nq], [KT, QT], [sT])
        kb.act(E[0:nk, :, 0:nq], sTv, AF.Exp, [sT], [E], scale=0.125)

    def issue_pv(i):
        qi, q0, nq, g, kt, k0, nk = steps[i]
        E = Es[i % 3]
        for hh in range(4):
            kb.mm(Ob[hh][0:nq, 0:65], E[0:nk, hh, 0:nq], V1[0:nk, kt, g, :], [E, V1], [Ob[hh]],
                  start=kt == 0, stop=kt == len(ktiles) - 1)
        if kt == len(ktiles) - 1:
            for hh in range(4):
                kb.v("dve", "reciprocal", [Ob[hh]], [rcp], out=rcp[0:nq, hh:hh + 1], in_=Ob[hh][0:nq, 64:65])
                hd = 4 * g + hh
                kb.v("dve", "tensor_scalar", [Ob[hh], rcp], [y_b], out=y_b[0:nq, qi, hd * 64:(hd + 1) * 64],
                     in0=Ob[hh][0:nq, 0:64], scalar1=rcp[0:nq, hh:hh + 1], scalar2=None, op0=ALU.mult)

    issue_s(0)
    for i in range(NS_):
        if i + 1 < NS_:
            issue_s(i + 1)
        issue_pv(i)
    kb.dump("y_b", y_b, y_b[:, :, :], [128, NQT, 512], BF16)
    yb_d = kb.dram_tmp("yb_scr", [NQT, 128, 512], BF16)
    kb.dma(yb_d.t[0:16].rearrange("a p c -> p a c"), y_b[:, 0:16, :], [y_b], [yb_d])
    kb.dma(yb_d.t[16, 0:16, :], y_b[0:16, 16, :], [y_b], [yb_d])
    fw.barrier()
    kb.release(mA)
    if upto == "B":
        kb.finish()
        return nc, kb
    names = {"mu0": [128, RW_TILES], "mu1": [128, RW_TILES], "w2": [64, 2, 512], "a2": [64, 2, 512], "w0": [64, 2, 8],
             "a0": [64, 2, 8], "g2a": [128, 512], "g2b": [32, 512], "kkc": [64, 8], "kac": [64, 8], "rkc": [64, 8],
             "lng": [64, 512], "lnb": [64, 512],
             "masks": [64, 4, 64], "w_a": [512, 1024], "w_b": [512, 1024], "w_o": [1024, 1024], "w_ff1": [1024, 4096],
             "w_ff2": [4096, 1024], "gffn": [128, NCH], "gfin": [128, 1024]}
    D_ = {k: kb.dram_in(k, v) for k, v in names.items()}
    D_["w_in"] = win_d
    D_["h"] = h_d
    ybwd_d = kb.dram_tmp("ybwd_scr", [OWN, 512])
    ya_d = kb.dram_tmp("ya_scr", [OWN, 512])
    h2_d = kb.dram_tmp("h2_scr", [OWN, D])
    mC = kb.mark()
    rwkv_phase(kb, uT_d, ident, D_, ybwd_d, ya_d, wr_pre)
    if "ya" in kb.debug:
        o = kb.dram_out("dbg_ya", [OWN, 512])
        kb.out_keys.append(kb.dma(o.t, ya_d.t, [ya_d], [o]))
    fw.barrier()
    kb.release(mC)
    kb.hi = kb.AW
    if upto == "C":
        kb.finish()
        return nc, kb
    merge_phase(kb, uT_d, identb, D_, ya_d, yb_d, h2_d)
    if "h2" in kb.debug:
        o = kb.dram_out("dbg_h2", [OWN, D])
        kb.out_keys.append(kb.dma(o.t, h2_d.t, [h2_d], [o]))
    fw.barrier()
    kb.release(mA)
    ffn_phase(kb, identb, D_, h2_d, out_d)

    kb.finish()
    return nc, kb


def _rope_tables():
    inv = (10000.0 ** (-np.arange(16, dtype=np.float32) * 2.0 / 32)).astype(np.float32)
    rows = (np.arange(64, dtype=np.float32)[:, None] * inv).astype(np.float32)
    cols = (np.arange(64, dtype=np.float32)[:, None] * inv).astype(np.float32)
    ang = np.zeros((L, 2, 16), np.float32)
    grid = np.stack([np.broadcast_to(rows[:, None, :], (64, 64, 16)),
                     np.broadcast_to(cols[None, :, :], (64, 64, 16))], axis=2).reshape(4096, 2, 16)
    ang[16:] = grid
    c, s_ = np.cos(ang).astype(np.float32), np.sin(ang).astype(np.float32)
    cosf = np.stack([c, c], axis=2).reshape(L, 64)
    sinf = np.stack([-s_, s_], axis=2).reshape(L, 64)
    return cosf, sinf


def prep_inputs(inputs, core):
    b, s = core // 2, core % 2
    x = np.asarray(inputs["x"], np.float32)
    meta = np.asarray(inputs["meta_tokens"], np.float32)
    hseq = np.concatenate([meta, x[b]], axis=0)
    if s == 1:
        hseq = hseq[::-1]
    m = {}
    m["h"] = np.ascontiguousarray(hseq)
    m["gmix"] = np.ascontiguousarray(np.asarray(inputs["mix_norm_g"], np.float32)[0].reshape(NCH, 128).T)
    m["ident"] = np.eye(128, dtype=np.float32)
    w_in = np.asarray(inputs["w_in"], np.float32)[0]
    if s == 1:
        w_in = w_in.copy()
        for base in (1536, 1664):
            a = w_in[:, base:base + 64].copy()
            w_in[:, base:base + 64] = w_in[:, base + 64:base + 128]
            w_in[:, base + 64:base + 128] = a
    m["w_in"] = np.ascontiguousarray(w_in)
    cosf, sinf = _rope_tables()
    if s == 1:
        cosf, sinf = cosf[::-1], sinf[::-1]
    m["ropeC"] = np.ascontiguousarray(cosf)
    m["ropeS"] = np.ascontiguousarray(sinf)
    G = lambda k: np.asarray(inputs[k], np.float32)[0]
    sh = G("rwkv_shift")
    w2, w0, a2, a0 = G("decay_w2"), G("decay_w0"), G("icl_a2"), G("icl_a0")
    if s == 1:
        sh = sh[::-1].copy()
        for base in (1536, 1664):
            a = sh[:, base:base + 64].copy()
            sh[:, base:base + 64] = sh[:, base + 64:base + 128]
            sh[:, base + 64:base + 128] = a
        w2, w0, a2, a0 = w2[::-1], w0[::-1], a2[::-1], a0[::-1]
    for nm, row in (("mu0", 0), ("mu1", 1)):
        arr = np.zeros((128, RW_TILES), np.float32)
        for j, (c0, w) in enumerate(RW_COLS):
            arr[:w, j] = sh[row, c0:c0 + w]
        m[nm] = arr
    m["w2"] = np.ascontiguousarray(w2.transpose(1, 0, 2))
    m["a2"] = np.ascontiguousarray(a2.transpose(1, 0, 2))
    m["w0"] = np.ascontiguousarray(w0.reshape(2, 8, 64).transpose(2, 0, 1))
    m["a0"] = np.ascontiguousarray(a0.reshape(2, 8, 64).transpose(2, 0, 1))
    g2 = G("gate_w2")
    m["g2a"] = np.ascontiguousarray(g2[0:128]); m["g2b"] = np.ascontiguousarray(g2[128:160])
    m["kkc"] = np.ascontiguousarray(G("k_k").reshape(8, 64).T)
    m["kac"] = np.ascontiguousarray(G("k_a").reshape(8, 64).T)
    m["rkc"] = np.ascontiguousarray(G("r_k").reshape(8, 64).T)
    m["lng"] = np.ascontiguousarray(np.tile(G("lnx_g")[None, :], (64, 1)))
    m["lnb"] = np.ascontiguousarray(np.tile(G("lnx_b")[None, :], (64, 1)))
    si, ti = np.meshgrid(np.arange(64), np.arange(64), indexing="ij")
    m["masks"] = np.ascontiguousarray(np.stack([si < ti, si <= ti, si > ti, si >= ti], 1).astype(np.float32))
    m["w_a"] = np.ascontiguousarray(G("w_branch_rwkv")); m["w_b"] = np.ascontiguousarray(G("w_branch_attn"))
    m["w_o"] = np.ascontiguousarray(G("w_out")); m["w_ff1"] = np.ascontiguousarray(G("w_ff1")); m["w_ff2"] = np.ascontiguousarray(G("w_ff2"))
    m["gffn"] = np.ascontiguousarray(G("ffn_norm_g").reshape(NCH, 128).T)
    m["gfin"] = np.ascontiguousarray(np.tile(np.asarray(inputs["final_norm_g"], np.float32)[None, :], (128, 1)))
    m["qg"] = np.ascontiguousarray(np.tile(np.asarray(inputs["q_norm_g"], np.float32)[0][None, :], (128, 1)))
    m["kg"] = np.ascontiguousarray(np.tile(np.asarray(inputs["k_norm_g"], np.float32)[0][None, :], (128, 1)))
    return m


_CACHE = {}


def kernel(**inputs):
    from concourse.bass_utils import run_bass_kernel_spmd
    if "nc" not in _CACHE:
        _CACHE["nc"] = build()
    nc, kb = _CACHE["nc"]
    in_maps = [prep_inputs(inputs, c) for c in range(8)]
    res = run_bass_kernel_spmd(nc, in_maps, core_ids=list(range(8)))
    out = np.zeros((4, 4096, D), np.float32)
    for c in range(8):
        b, s = c // 2, c % 2
        o = np.asarray(res.results[c]["out"])
        if s == 0:
            out[b, 0:2048] = o[16:2064]
        else:
            out[b, 2048:4096] = o[0:2048][::-1]
    return out
```

```python
import numpy as np
import concourse.bass as bass
import concourse.mybir as mybir

F32 = mybir.dt.float32
BF16 = mybir.dt.bfloat16
AF = mybir.ActivationFunctionType
ALU = mybir.AluOpType
AX = mybir.AxisListType

ENGS = ["pe", "dve", "act", "pool", "sp"]
SEM_WRAP = 2048
NSLOT = 8


class Res:
    __slots__ = ("name", "w", "rd")

    def __init__(self, name):
        self.name = name
        self.w = None
        self.rd = []


class Rec:
    __slots__ = ("fn", "deps", "sig", "dma", "signo", "pre")

    def __init__(self, fn, dma):
        self.fn = fn
        self.deps = set()
        self.sig = False
        self.dma = dma
        self.signo = None
        self.pre = None


class FW:
    def __init__(self, nc, same_engine_sync=True):
        self.nc = nc
        self.eng = {"pe": nc.tensor, "dve": nc.vector, "act": nc.scalar,
                    "pool": nc.gpsimd, "sp": nc.sync}
        self.ops = {e: [] for e in ENGS}
        self.ndma = {e: 0 for e in ENGS}
        self.same = same_engine_sync

    def op(self, e, fn, reads=(), writes=(), dma=False):
        lst = self.ops[e]
        idx = len(lst)
        rec = Rec(fn, None)
        if dma:
            rec.dma = self.ndma[e]
            self.ndma[e] += 1
        key = (e, idx)
        deps = set()
        for r in reads:
            if r.w is not None:
                deps.add(r.w)
        for w in writes:
            if w.w is not None:
                deps.add(w.w)
            for k in w.rd:
                deps.add(k)
        for d in deps:
            de, di = d
            drec = self.ops[de][di]
            if drec.dma is None:
                if de == e and (not self.same) and e != "pool":
                    continue
                if de == e and e == "pe":
                    continue
                drec.sig = True
            rec.deps.add(d)
        for r in reads:
            r.rd.append(key)
        for w in writes:
            w.w = key
            w.rd = []
        lst.append(rec)
        return key

    def barrier(self, resources=()):
        b = Res("barrier")
        keys = []
        for e in ENGS:
            for i in range(len(self.ops[e]) - 1, -1, -1):
                if self.ops[e][i].fn is not None:
                    keys.append((e, i))
                    break
        dkeys = []
        for e in ENGS:
            n = 0
            for i in range(len(self.ops[e]) - 1, -1, -1):
                if self.ops[e][i].dma is not None:
                    dkeys.append((e, i))
                    n += 1
                    if n >= NSLOT:
                        break
        self._pending_barrier = keys + dkeys
        for e in ENGS:
            rec = Rec(None, None)
            for d in keys + dkeys:
                de, di = d
                drec = self.ops[de][di]
                if drec.dma is None:
                    if de == e:
                        continue
                    drec.sig = True
                rec.deps.add(d)
            self.ops[e].append(rec)

    def emit(self, final_waits=()):
        nc = self.nc
        nsig = {}
        for e in ENGS:
            n = 0
            for rec in self.ops[e]:
                if rec.dma is None and rec.sig:
                    n += 1
                    rec.signo = n
            nsig[e] = n
        import contextlib
        with contextlib.ExitStack() as st:
            sems = {}
            for e in ENGS:
                k = (nsig[e] + SEM_WRAP - 1) // SEM_WRAP
                sems[e] = [st.enter_context(nc.semaphore(f"s_{e}_{i}")) for i in range(max(k, 1))]
            dsem = {}
            for e in ENGS:
                if self.ndma[e]:
                    dsem[e] = [st.enter_context(nc.semaphore(f"d_{e}_{i}")) for i in range(NSLOT)]
            block = st.enter_context(nc.Block())

            def target(dep):
                de, di = dep
                drec = self.ops[de][di]
                if drec.dma is not None:
                    n = drec.dma
                    return (dsem[de][n % NSLOT], 16 * (n // NSLOT + 1))
                s = drec.signo
                return (sems[de][(s - 1) // SEM_WRAP], (s - 1) % SEM_WRAP + 1)

            def run(e, eng):
                waited = {}
                for rec in self.ops[e]:
                    tg = {}
                    for dep in rec.deps:
                        sem, val = target(dep)
                        kk = id(sem)
                        if kk not in tg or tg[kk][1] < val:
                            tg[kk] = (sem, val)
                    if rec.dma is not None and rec.dma >= NSLOT:
                        n = rec.dma
                        sem = dsem[e][n % NSLOT]
                        val = 16 * (n // NSLOT)
                        kk = id(sem)
                        if kk not in tg or tg[kk][1] < val:
                            tg[kk] = (sem, val)
                    for kk, (sem, val) in tg.items():
                        if waited.get(kk, 0) >= val:
                            continue
                        waited[kk] = val
                        eng.wait_ge(sem, val)
                    if rec.fn is None:
                        continue
                    inst = rec.fn(eng)
                    if rec.dma is not None:
                        inst.then_inc(dsem[e][rec.dma % NSLOT], 16)
                    elif rec.sig:
                        s = rec.signo
                        inst.then_inc(sems[e][(s - 1) // SEM_WRAP], 1)

            @block.tensor
            def _(eng):
                run("pe", eng)

            @block.vector
            def _(eng):
                run("dve", eng)

            @block.scalar
            def _(eng):
                run("act", eng)

            @block.gpsimd
            def _(eng):
                run("pool", eng)

            @block.sync
            def _(eng):
                run("sp", eng)

import contextlib

L = 4112
OWN = 2064
D = 1024
NCH = 8
INW = 4768
RW = 1952
EPS = 1e-6


class Tl:
    def __init__(self, t, name):
        self.t = t
        self.r = Res(name)

    def __getitem__(self, k):
        return self.t[k]


class KB:
    def __init__(self, nc, debug=None):
        self.nc = nc
        self.fw = FW(nc)
        self.st = contextlib.ExitStack()
        self.debug = debug or []
        self.dbg_out = {}
        self.out_keys = []
        self.cnt = 0
        import os
        self.use_r32 = os.environ.get('USE_R32', '0') == '1'

    def setup_mem(self):
        self.AW = 49000
        self.arena = self.st.enter_context(self.nc.sbuf_tensor("arena", [128, self.AW], F32))
        self.psum = self.st.enter_context(self.nc.psum_tensor("psum", [128, 4096], F32))
        self.top = 0
        self.hi = self.AW
        self.pbanks = [self._pview(i) for i in range(8)]
        self.pb_i = 0

    def _pview(self, i):
        return Tl(self.psum[:, 512 * i:512 * (i + 1)], f"bank{i}")

    def bank(self):
        b = self.pbanks[self.pb_i % 8]
        self.pb_i += 1
        return b

    def mark(self):
        return self.top

    def release(self, m):
        self.top = m

    def sb(self, shape, dt, name=None):
        self.cnt += 1
        name = f"{name or 't'}_{self.cnt}"
        p = shape[0]
        n = 1
        for x in shape[1:]:
            n *= x
        words = n if dt == F32 else (n + 1) // 2
        if getattr(self, 'from_top', False):
            self.hi -= words
            off = self.hi
        else:
            off = self.top
            self.top += words
        assert self.top <= self.hi, f"arena overflow {self.top} {self.hi}"
        ap = self.arena[0:p, off:off + words]
        if dt != F32:
            ap = ap.bitcast(dt)
            if n % 2:
                ap = ap[:, 0:n]
        if len(shape) == 3:
            ap = ap.rearrange("p (a b) -> p a b", a=shape[1])
        elif len(shape) == 4:
            ap = ap.rearrange("p (a b c) -> p a b c", a=shape[1], b=shape[2])
        return Tl(ap, name)

    def dram_in(self, name, shape, dt=F32):
        t = self.nc.dram_tensor(name, list(shape), dt, kind="ExternalInput")
        return Tl(t.ap(), name)

    def dram_out(self, name, shape, dt=F32):
        t = self.nc.dram_tensor(name, list(shape), dt, kind="ExternalOutput")
        return Tl(t.ap(), name)

    def dram_tmp(self, name, shape, dt=F32):
        t = self.nc.dram_tensor(name, list(shape), dt, kind="Internal")
        return Tl(t.ap(), name)

    def dma(self, out_ap, in_ap, reads, writes, q="sp", **kw):
        return self.fw.op(q, lambda e: e.dma_start(out=out_ap, in_=in_ap, **kw),
                          reads=[x.r for x in reads], writes=[x.r for x in writes], dma=True)

    def mm(self, out_ap, lhsT_ap, rhs_ap, reads, writes, start=True, stop=True, r32=False):
        if r32 and self.use_r32 and lhsT_ap.dtype == F32 and rhs_ap.dtype == F32:
            lhsT_ap = lhsT_ap.bitcast(mybir.dt.float32r)
            rhs_ap = rhs_ap.bitcast(mybir.dt.float32r)
        return self.fw.op("pe", lambda e: e.matmul(out_ap, lhsT_ap, rhs_ap, start=start, stop=stop),
                          reads=[x.r for x in reads], writes=[x.r for x in writes])

    def tr(self, out_ap, in_ap, ident_ap, reads, writes):
        return self.fw.op("pe", lambda e: e.transpose(out_ap, in_ap, ident_ap),
                          reads=[x.r for x in reads], writes=[x.r for x in writes])

    def act(self, out_ap, in_ap, func, reads, writes, **kw):
        return self.fw.op("act", lambda e: e.activation(out=out_ap, in_=in_ap, func=func, **kw),
                          reads=[x.r for x in reads], writes=[x.r for x in writes])

    def v(self, eng, meth, reads, writes, *a, **kw):
        return self.fw.op(eng, lambda e: getattr(e, meth)(*a, **kw),
                          reads=[x.r for x in reads], writes=[x.r for x in writes])

    def dump(self, name, tl, ap, shape, dt=F32):
        if name not in self.debug:
            return
        o = self.dram_out("dbg_" + name, shape, dt)
        k = self.dma(o.t, ap, [tl], [o])
        self.out_keys.append(k)
        self.dbg_out[name] = "dbg_" + name

    def finish(self):
        fw = self.fw
        rec = Rec(None, None)
        for k in self.out_keys:
            rec.deps.add(k)
        fw.ops["sp"].append(rec)
        fw.emit()
        self.st.close()

S0 = float(np.exp(-0.5))
RW_COLS = [(j * 64, 64) for j in range(28)] + [(1792, 128), (1920, 32)]
RW_TILES = len(RW_COLS)


def rwkv_load_w(kb, D_):
    wr = kb.sb([128, NCH, RW], BF16, "wr")
    for a in range(0, RW, 488):
        kb.dma(wr[:, :, a:a + 488], D_["w_in"].t[:, a:a + 488].rearrange("(c p) n -> p c n", p=128), [D_["w_in"]], [wr], q="pool")
    return wr


def rwkv_phase(kb, uT_d, ident, D_, ybwd_d, ya_d, wr):
    fw = kb.fw

    def ld(name, shape):
        t = kb.sb(shape, F32, name)
        src = D_[name]
        kb.dma(t.t, src.t, [src], [t])
        return t
    mu0 = ld("mu0", [128, RW_TILES]); mu1 = ld("mu1", [128, RW_TILES])
    muc = kb.sb([128, RW_TILES], F32, "muc")
    kb.v("dve", "tensor_tensor", [mu0, mu1], [muc], out=muc[:, :], in0=mu0[:, :], in1=mu1[:, :], op=ALU.add)
    kb.v("dve", "tensor_scalar", [muc], [muc], out=muc[:, :], in0=muc[:, :], scalar1=-1.0, scalar2=1.0, op0=ALU.mult, op1=ALU.add)
    w2 = ld("w2", [64, 2, 512]); a2 = ld("a2", [64, 2, 512])
    w0 = ld("w0", [64, 2, 8]); a0 = ld("a0", [64, 2, 8])
    g2a = ld("g2a", [128, 512]); g2b = ld("g2b", [32, 512])
    kkc = ld("kkc", [64, 8]); kac = ld("kac", [64, 8]); rkc = ld("rkc", [64, 8])
    lng = ld("lng", [64, 512]); lnb = ld("lnb", [64, 512])
    masks = ld("masks", [64, 4, 64])
    oka = kb.sb([64, 8], F32, "oka"); kah = kb.sb([64, 8], F32, "kah")
    kb.v("dve", "tensor_scalar", [kac], [oka], out=oka[:, :], in0=kac[:, :], scalar1=-1.0, scalar2=1.0, op0=ALU.mult, op1=ALU.add)
    kb.v("dve", "tensor_scalar", [kac], [kah], out=kah[:, :], in0=kac[:, :], scalar1=0.5, scalar2=None, op0=ALU.mult)
    ones = kb.sb([64, 64], F32, "ones")
    kb.v("dve", "memset", [], [ones], ones[:, :], 1.0)
    Hs = [kb.sb([64, 8, 64], F32, "Hf"), kb.sb([64, 8, 64], F32, "Hb")]

    def T(shape, name, dt=F32):
        return kb.sb(shape, dt, name)

    class Slot:
        pass

    def make_slot(si):
        S = Slot()
        S.B = kb.pbanks[4 * si:4 * si + 4]
        S.uc = T([128, NCH, 66], 'uc', BF16)
        S.sh = [T([128, 64], 'sh') for _ in range(4)]
        S.P = T([128, RW_TILES, 66], "P")
        S.Z = T([128, RW_TILES, 64], "Z")
        S.Pt = [Tl(S.P.t[:, j, :], f'P{j}') for j in range(RW_TILES)]
        S.Zt = [Tl(S.Z.t[:, j, :], f'Z{j}') for j in range(RW_TILES)]
        S.thw = T([64, 64], "thw")
        S.sg = T([64, 8, 64], "sg")
        S.A = [T([64, 8, 64], "A0"), T([64, 8, 64], "A1")]
        S.CS = T([64, 8, 65], "CS")
        kb.v("dve", "memset", [], [S.CS], S.CS[:, :, :], 0.0)
        S.G = T([64, 8, 64], "G")
        S.Gp = T([64, 8, 64], "Gp")
        S.Gi = T([64, 8, 64], "Gi")
        S.gC = T([64, 8], "gC")
        S.kkr = T([64, 8, 64], "kkr")
        S.ksq = T([64, 8, 64], "ksq")
        S.rn = T([64, 8, 64], "rn")
        S.X1 = S.ksq
        S.X2 = S.rn
        S.t1 = T([64, 8, 64], "t1")
        S.kdir = T([64, 8, 64], "kdir")
        S.kka = T([64, 8, 64], "kka")
        S.QR = T([64, 8, 128], "QR")
        S.KdT = T([64, 8, 64], "KdT")
        S.AdT = T([64, 8, 64], "AdT")
        S.Kq_t = T([64, 512], "Kq_t")
        S.Ad_t = T([64, 512], "Ad_t")
        S.Kd_t = T([64, 512], "Kd_t")
        S.V_t = T([64, 512], "V_t")
        S.NTs = [S.kkr, S.Gi]
        S.Ns = [S.ksq, S.Gp]
        S.MkT = S.G
        S.PaT = S.sg
        S.PkT = S.rn
        S.X = T([64, 8, 128], "X")
        S.AT = S.t1
        S.RqpT = S.kdir
        S.ybt = S.Kd_t
        S.yt = S.Ad_t
        S.ysq = S.Kq_t
        S.st = T([64, 48], "st")
        S.sgd = T([128, 2, 64], "sgd")
        S.ee = S.kka
        S.yout = S.ysq
        return S
    slots = [make_slot(0), make_slot(1)]

    RS, KS, VS = slice(0, 8), slice(8, 16), slice(16, 24)
    NSTAGE = 52

    def bc(ap, shape, axis):
        return ap.unsqueeze(axis).to_broadcast(shape)

    def v3(bank, p, a):
        return bank.t[0:p, 0:512].rearrange("p (a b) -> p a b", a=a)

    def chunk(S, c0, cn, d, need_out, H, epilogue):
        B = S.B
        P, Z, uc = S.P, S.Z, S.uc
        Pt, Zt = S.Pt, S.Zt
        ns = 0
        wdt = 24 + d
        tiles = list(range(24)) + [wdt] + ([26, 27, 28, 29] if epilogue == "final" else [26 + d])
        lo, hi = max(c0 - 1, 0), min(c0 + cn + 1, L)
        n = hi - lo
        off = lo - (c0 - 1)
        kb.dma(uc[:, :, 0:n], uT_d.t[:, :, lo:hi], [uT_d], [uc])
        if off or (hi - (c0 - 1)) < cn + 2:
            kb.v("pool", "memset", [], Pt, P[:, :, :], 0.0)
        groups = [tiles[i:i + 2] for i in range(0, len(tiles), 2)]

        def shift(grp):
            for j in grp:
                w = RW_COLS[j][1]
                kb.act(Z[0:w, j, 0:cn], P[0:w, j, 1:cn + 1], AF.Copy, [Pt[j], muc], [Zt[j]], scale=muc[0:w, j:j + 1])
                kb.v("dve", "scalar_tensor_tensor", [Pt[j], mu0, Zt[j]], [Zt[j]], out=Z[0:w, j, 0:cn], in0=P[0:w, j, 0:cn],
                     scalar=mu0[0:w, j:j + 1], in1=Z[0:w, j, 0:cn], op0=ALU.mult, op1=ALU.add)
                kb.v("dve", "scalar_tensor_tensor", [Pt[j], mu1, Zt[j]], [Zt[j]], out=Z[0:w, j, 0:cn], in0=P[0:w, j, 2:cn + 2],
                     scalar=mu1[0:w, j:j + 1], in1=Z[0:w, j, 0:cn], op0=ALU.mult, op1=ALU.add)

        prev = None
        for gi, grp in enumerate(groups):
            pb_ = B[gi % 4]
            pv = pb_.t[:, 0:2 * 66].rearrange("p (a b) -> p a b", a=2)
            for ji, j in enumerate(grp):
                cc, w = RW_COLS[j]
                for c in range(NCH):
                    kb.mm(pv[0:w, ji, 0:n], wr[:, c, cc:cc + w], uc[:, c, 0:n], [wr, uc], [pb_],
                          start=c == 0, stop=c == NCH - 1)
            j0, j1 = grp[0], grp[-1]
            if j1 < 28 and j1 - j0 == len(grp) - 1:
                kb.act(P[0:64, j0:j1 + 1, off:off + n], pv[0:64, 0:len(grp), 0:n], AF.Copy, [pb_], [Pt[jj] for jj in grp])
            else:
                for ji, j in enumerate(grp):
                    w = RW_COLS[j][1]
                    kb.act(P[0:w, j, off:off + n], pv[0:w, ji, 0:n], AF.Copy, [pb_], [Pt[j]])
            if prev is not None:
                shift(prev)
            prev = grp
            ns += 1; yield
        shift(prev)
        ns += 1; yield
        while ns < 17:
            ns += 1; yield
        thw, sg, A, CS = S.thw, S.sg, S.A, S.CS
        kb.act(thw[:, 0:cn], Z[0:64, wdt, 0:cn], AF.Tanh, [Zt[wdt]], [thw])
        pl = B[0]
        plv = v3(pl, 64, 8)
        for h in range(8):
            kb.mm(plv[:, h, 0:cn], w2[:, d, 64 * h:64 * h + 64], thw[:, 0:cn], [w2, thw], [pl], r32=True)
        for h in range(8):
            kb.act(sg[:, h, 0:cn], plv[:, h, 0:cn], AF.Sigmoid, [pl, w0], [sg], bias=w0[:, d, h:h + 1])
        ns += 1; yield
        for dd in ((0, 1) if epilogue == "final" else (d,)):
            pa = B[1 + dd]
            pav = v3(pa, 64, 8)
            for h in range(8):
                kb.mm(pav[:, h, 0:cn], a2[:, dd, 64 * h:64 * h + 64], Z[0:64, 26 + dd, 0:cn], [a2, Zt[26 + dd]], [pa], r32=True)
            for h in range(8):
                kb.act(A[dd][:, h, 0:cn], pav[:, h, 0:cn], AF.Sigmoid, [pa, a0], [A[dd]], bias=a0[:, dd, h:h + 1])
        ns += 1; yield
        Ad_ = A[d]
        sh = [64, 8, cn]
        kkr, ksq, rn, t1, kdir, kka = S.kkr, S.ksq, S.rn, S.t1, S.kdir, S.kka
        kb.v("pool", "tensor_tensor", Zt[8:16] + [kkc], [kkr], out=kkr[:, :, 0:cn], in0=Z[0:64, KS, 0:cn], in1=bc(kkc[:, :], sh, 2), op=ALU.mult)
        kb.act(ksq[:, :, 0:cn], kkr[:, :, 0:cn], AF.Square, [kkr], [ksq])
        pn = B[3]
        pnv = v3(pn, 64, 8)
        for h in range(8):
            kb.mm(pnv[:, h, 0:cn], ones[:, :], ksq[:, h, 0:cn], [ones, ksq], [pn], r32=True)
        kb.act(rn[:, :, 0:cn], pnv[:, :, 0:cn], AF.Ln, [pn], [rn], bias=1e-18, scale=1.0)
        kb.act(rn[:, :, 0:cn], rn[:, :, 0:cn], AF.Exp, [rn], [rn], scale=-0.5)
        kb.v("dve", "tensor_tensor", [kkr, rn], [kkr], out=kkr[:, :, 0:cn], in0=kkr[:, :, 0:cn], in1=rn[:, :, 0:cn], op=ALU.mult)
        ns += 1; yield
        kb.v("pool", "tensor_tensor", [Ad_, kac], [t1], out=t1[:, :, 0:cn], in0=Ad_[:, :, 0:cn], in1=bc(kac[:, :], sh, 2), op=ALU.mult)
        kb.v("pool", "tensor_tensor", [t1, oka], [t1], out=t1[:, :, 0:cn], in0=t1[:, :, 0:cn], in1=bc(oka[:, :], sh, 2), op=ALU.add)
        kb.v("pool", "tensor_tensor", [t1] + Zt[8:16], [kdir], out=kdir[:, :, 0:cn], in0=t1[:, :, 0:cn], in1=Z[0:64, KS, 0:cn], op=ALU.mult)
        kb.v("dve", "tensor_tensor", [kkr, Ad_], [kka], out=kka[:, :, 0:cn], in0=kkr[:, :, 0:cn], in1=Ad_[:, :, 0:cn], op=ALU.mult)
        ns += 1; yield
        for h in range(8):
            kb.v("dve", "tensor_tensor_scan", [ones, sg], [CS], out=CS[:, h, 1:cn + 1], data0=ones[:, 0:cn], data1=sg[:, h, 0:cn],
                 initial=0.0, op0=ALU.mult, op1=ALU.add)
        ns += 1; yield
        G, Gp, Gi, gC, X1, X2 = S.G, S.Gp, S.Gi, S.gC, S.X1, S.X2
        if d == 0:
            kb.act(G[:, :, 0:cn], CS[:, :, 1:cn + 1], AF.Exp, [CS], [G], scale=-S0)
            kb.act(Gp[:, :, 0:cn], CS[:, :, 0:cn], AF.Exp, [CS], [Gp], scale=-S0)
            kb.act(Gi[:, :, 0:cn], CS[:, :, 1:cn + 1], AF.Exp, [CS], [Gi], scale=S0)
        else:
            totb = bc(CS[:, :, cn], sh, 2)
            kb.v("dve", "tensor_tensor", [CS], [X1], out=X1[:, :, 0:cn], in0=CS[:, :, 0:cn], in1=totb, op=ALU.subtract)
            kb.v("dve", "tensor_tensor", [CS], [X2], out=X2[:, :, 0:cn], in0=CS[:, :, 1:cn + 1], in1=totb, op=ALU.subtract)
            kb.act(G[:, :, 0:cn], X1[:, :, 0:cn], AF.Exp, [X1], [G], scale=S0)
            kb.act(Gp[:, :, 0:cn], X2[:, :, 0:cn], AF.Exp, [X2], [Gp], scale=S0)
            kb.act(Gi[:, :, 0:cn], X1[:, :, 0:cn], AF.Exp, [X1], [Gi], scale=-S0)
        kb.act(gC[:, :], CS[:, :, cn], AF.Exp, [CS], [gC], scale=-S0)
        ns += 1; yield
        QR, KdT, AdT = S.QR, S.KdT, S.AdT
        kb.v("dve", "tensor_tensor", [kkr, Gp], [QR], out=QR[:, :, 0:cn], in0=kkr[:, :, 0:cn], in1=Gp[:, :, 0:cn], op=ALU.mult)
        kb.v("pool", "tensor_tensor", Zt[0:8] + [G], [QR], out=QR[:, :, cn:2 * cn], in0=Z[0:64, RS, 0:cn], in1=G[:, :, 0:cn], op=ALU.mult)
        kb.v("pool", "tensor_tensor", [kdir, Gi], [KdT], out=KdT[:, :, 0:cn], in0=kdir[:, :, 0:cn], in1=Gi[:, :, 0:cn], op=ALU.mult)
        kb.v("dve", "tensor_tensor", [kka, Gi], [AdT], out=AdT[:, :, 0:cn], in0=kka[:, :, 0:cn], in1=Gi[:, :, 0:cn], op=ALU.mult)
        ns += 1; yield
        Kq_t, Ad_t, Kd_t, V_t = S.Kq_t, S.Ad_t, S.Kd_t, S.V_t
        for bi, (src, si, dst) in enumerate(((QR, 0, Kq_t), (AdT, 0, Ad_t), (KdT, 0, Kd_t), (Z, 16, V_t))):
            pt = B[bi]
            for h in range(8):
                kb.tr(pt[0:cn, h * 64:(h + 1) * 64], src[0:64, si + h, 0:cn], ident[0:64, 0:64], [Zt[16 + h] if src is Z else src, ident], [pt])
            if bi % 2 == 0:
                kb.act(dst[0:cn, :], pt[0:cn, :], AF.Copy, [pt], [dst])
            else:
                kb.v("dve", "tensor_copy", [pt], [dst], out=dst[0:cn, :], in_=pt[0:cn, :])
                ns += 1; yield
        mS, mI, mN = (0, 1, 2) if d == 0 else (2, 3, 0)
        rw = 2 * cn if need_out else cn
        pSA = [B[0], B[1]]; pSK = [B[2], B[3]]
        NTs, Ns, MkT, PaT, PkT, X = S.NTs, S.Ns, S.MkT, S.PaT, S.PkT, S.X
        for h in range(8):
            hv = pSA[h // 4].t[0:cn, 0:4 * rw].rearrange("p (h s) -> p h s", h=4)
            kb.mm(hv[:, h % 4, :], AdT[:, h, 0:cn], QR[:, h, 0:rw], [AdT, QR], [pSA[h // 4]], r32=True)
            hv2 = pSK[h // 4].t[0:cn, 0:4 * rw].rearrange("p (h s) -> p h s", h=4)
            kb.mm(hv2[:, h % 4, :], KdT[:, h, 0:cn], QR[:, h, 0:rw], [KdT, QR], [pSK[h // 4]], r32=True)
        NT, Nm = NTs[0], Ns[0]
        for half in range(2):
            hv = pSA[half].t[0:cn, 0:4 * rw].rearrange("p (h s) -> p h s", h=4)
            hv2 = pSK[half].t[0:cn, 0:4 * rw].rearrange("p (h s) -> p h s", h=4)
            hs = slice(4 * half, 4 * half + 4)
            kb.v("dve", "tensor_tensor", [pSA[half], masks], [NT], out=NT[0:cn, hs, 0:cn], in0=hv[:, :, 0:cn],
                 in1=bc(masks[0:cn, mS, 0:cn], [cn, 4, cn], 1), op=ALU.mult)
            kb.v("dve", "tensor_tensor", [pSK[half], masks], [MkT], out=MkT[0:cn, hs, 0:cn], in0=hv2[:, :, 0:cn],
                 in1=bc(masks[0:cn, mS, 0:cn], [cn, 4, cn], 1), op=ALU.mult)
            if need_out:
                kb.v("dve", "tensor_tensor", [pSA[half], masks], [PaT], out=PaT[0:cn, hs, 0:cn], in0=hv[:, :, cn:2 * cn],
                     in1=bc(masks[0:cn, mI, 0:cn], [cn, 4, cn], 1), op=ALU.mult)
                kb.v("dve", "tensor_tensor", [pSK[half], masks], [PkT], out=PkT[0:cn, hs, 0:cn], in0=hv2[:, :, cn:2 * cn],
                     in1=bc(masks[0:cn, mI, 0:cn], [cn, 4, cn], 1), op=ALU.mult)
        ns += 1; yield
        pN = B[0]
        pNv = pN.t[0:cn, 0:8 * cn].rearrange("p (h s) -> p h s", h=8)
        pW = B[1]
        pWv = v3(pW, cn, 8)
        for h in range(8):
            kb.mm(pNv[:, h, :], QR[:, h, 0:cn], AdT[:, h, 0:cn], [QR, AdT], [pN], r32=True)
        for h in range(8):
            kb.mm(pWv[:, h, :], MkT[0:cn, h, 0:cn], V_t[0:cn, 64 * h:64 * h + 64], [MkT, V_t], [pW], r32=True)
        kb.v("dve", "tensor_tensor", [pN, masks], [Nm], out=Nm[0:cn, :, 0:cn], in0=pNv,
             in1=bc(masks[0:cn, mN, 0:cn], [cn, 8, cn], 1), op=ALU.mult)
        kb.v("pool", "tensor_copy", [Kq_t], [X], out=X[0:cn, :, 0:64], in_=Kq_t[0:cn, :].rearrange("p (h v) -> p h v", h=8))
        kb.act(X[0:cn, :, 64:128], pWv, AF.Copy, [pW], [X], scale=-1.0)
        ns += 1; yield
        nlev = 5 if cn > 32 else (4 if cn > 16 else 3)
        pF = [B[2], B[3]]

        def factor(PT_, sign):
            for h in range(8):
                fv = v3(pF[h // 4], cn, 4)
                kb.mm(fv[:, h % 4, :], PT_[0:cn, h, 0:cn], X[0:cn, h, :], [PT_, X], [pF[h // 4]], r32=True)
            for half in range(2):
                fv = v3(pF[half], cn, 4)
                hs = slice(4 * half, 4 * half + 4)
                kb.v("dve", "tensor_tensor", [X, pF[half]], [X], out=X[0:cn, hs, :], in0=X[0:cn, hs, :], in1=fv,
                     op=ALU.subtract if sign < 0 else ALU.add)
        factor(NT, -1)
        ns += 1; yield
        cur = 0
        for lvl in range(5):
            if lvl < nlev:
                NTc, Nc = NTs[cur], Ns[cur]
                NTn, Nn = NTs[1 - cur], Ns[1 - cur]
                pa_, pb2 = B[0], B[1]
                pav = pa_.t[0:cn, 0:8 * cn].rearrange("p (h s) -> p h s", h=8)
                pbv = pb2.t[0:cn, 0:8 * cn].rearrange("p (h s) -> p h s", h=8)
                last = lvl == nlev - 1
                for h in range(8):
                    kb.mm(pav[:, h, :], Nc[0:cn, h, 0:cn], NTc[0:cn, h, 0:cn], [Nc, NTc], [pa_], r32=True)
                    if not last:
                        kb.mm(pbv[:, h, :], NTc[0:cn, h, 0:cn], Nc[0:cn, h, 0:cn], [Nc, NTc], [pb2], r32=True)
                kb.act(NTn[0:cn, :, 0:cn], pav, AF.Copy, [pa_], [NTn])
                if not last:
                    kb.act(Nn[0:cn, :, 0:cn], pbv, AF.Copy, [pb2], [Nn])
            ns += 1; yield
            if lvl < nlev:
                factor(NTn, +1)
                cur = 1 - cur
            ns += 1; yield
        AT, RqpT = S.AT, S.RqpT
        pA = B[0]
        pAv = v3(pA, 64, 8)
        for h in range(8):
            kb.mm(pAv[:, h, :], X[0:cn, h, 0:64], Ad_t[0:cn, 64 * h:64 * h + 64], [X, Ad_t], [pA], r32=True)
        kb.v("dve", "tensor_tensor", [ident, pA], [AT], out=AT[:, :, :], in0=bc(ident[0:64, 0:64], [64, 8, 64], 1), in1=pAv, op=ALU.subtract)
        if need_out:
            pR = B[1]
            pRv = pR.t[0:64, 0:8 * cn].rearrange("p (a b) -> p a b", a=8)
            for h in range(8):
                kb.mm(pRv[:, h, :], X[0:cn, h, 0:64], PaT[0:cn, h, 0:cn], [X, PaT], [pR], r32=True)
            kb.v("dve", "tensor_tensor", [QR, pR], [RqpT], out=RqpT[:, :, 0:cn], in0=QR[:, :, cn:2 * cn], in1=pRv, op=ALU.subtract)
        ns += 1; yield
        if need_out:
            pY = B[2]
            pYv = v3(pY, cn, 8)
            for h in range(8):
                kb.mm(pYv[:, h, :], RqpT[:, h, 0:cn], H[:, h, :], [RqpT, H], [pY], start=True, stop=False, r32=True)
                kb.mm(pYv[:, h, :], PkT[0:cn, h, 0:cn], V_t[0:cn, 64 * h:64 * h + 64], [PkT, V_t], [pY], start=False, stop=False, r32=True)
                kb.mm(pYv[:, h, :], PaT[0:cn, h, 0:cn], X[0:cn, h, 64:128], [PaT, X], [pY], start=False, stop=True, r32=True)
        pH = B[3]
        pHv = v3(pH, 64, 8)
        for h in range(8):
            kb.mm(pHv[:, h, :], AT[:, h, :], H[:, h, :], [AT, H], [pH], start=True, stop=False, r32=True)
            kb.mm(pHv[:, h, :], Kd_t[0:cn, 64 * h:64 * h + 64], V_t[0:cn, 64 * h:64 * h + 64], [Kd_t, V_t], [pH], start=False, stop=False, r32=True)
            kb.mm(pHv[:, h, :], Ad_t[0:cn, 64 * h:64 * h + 64], X[0:cn, h, 64:128], [Ad_t, X], [pH], start=False, stop=True, r32=True)
        kb.v("dve", "tensor_tensor", [pH, gC], [H], out=H[:, :, :], in0=pHv, in1=bc(gC[:, :], [64, 8, 64], 2), op=ALU.mult)
        ns += 1; yield
        ybt, yt, ysq, st, sgd, ee, yout = S.ybt, S.yt, S.ysq, S.st, S.sgd, S.ee, S.yout
        if epilogue == "store":
            kb.act(yt[0:cn, :], pY[0:cn, :], AF.Copy, [pY], [yt])
            kb.dma(ybwd_d.t[c0:c0 + cn, :], yt[0:cn, :], [yt], [ybwd_d])
        elif epilogue == "final":
            kb.dma(ybt[0:cn, :], ybwd_d.t[c0:c0 + cn, :], [ybwd_d], [ybt])
            kb.v("dve", "tensor_tensor", [pY, ybt], [yt], out=yt[0:cn, :], in0=pY[0:cn, :], in1=ybt[0:cn, :], op=ALU.add)
            y3 = yt[0:cn, :].rearrange("p (h v) -> p h v", h=8)
            kb.v("dve", "tensor_reduce", [yt], [st], out=st[0:cn, 0:8], in_=y3, axis=AX.X, op=ALU.add)
            kb.act(ysq[0:cn, :], yt[0:cn, :], AF.Square, [yt], [ysq])
            kb.v("dve", "tensor_reduce", [ysq], [st], out=st[0:cn, 8:16], in_=ysq[0:cn, :].rearrange("p (h v) -> p h v", h=8), axis=AX.X, op=ALU.add)
            kb.v("dve", "tensor_scalar", [st], [st], out=st[0:cn, 16:24], in0=st[0:cn, 0:8], scalar1=1.0 / 64, scalar2=None, op0=ALU.mult)
            kb.v("dve", "tensor_tensor", [st], [st], out=st[0:cn, 24:32], in0=st[0:cn, 16:24], in1=st[0:cn, 16:24], op=ALU.mult)
            kb.v("dve", "scalar_tensor_tensor", [st], [st], out=st[0:cn, 32:40], in0=st[0:cn, 8:16], scalar=1.0 / 64, in1=st[0:cn, 24:32],
                 op0=ALU.mult, op1=ALU.subtract)
            kb.act(st[0:cn, 40:48], st[0:cn, 32:40], AF.Sqrt, [st], [st], bias=64e-5, scale=1.0)
            kb.v("dve", "reciprocal", [st], [st], out=st[0:cn, 40:48], in_=st[0:cn, 40:48])
            kb.v("dve", "tensor_tensor", [yt, st], [yt], out=y3, in0=y3, in1=bc(st[0:cn, 16:24], [cn, 8, 64], 2), op=ALU.subtract)
            kb.v("dve", "tensor_tensor", [yt, st], [yt], out=y3, in0=y3, in1=bc(st[0:cn, 40:48], [cn, 8, 64], 2), op=ALU.mult)
            kb.v("pool", "tensor_tensor", [yt, lng], [yt], out=yt[0:cn, :], in0=yt[0:cn, :], in1=lng[0:cn, :], op=ALU.mult)
            kb.v("pool", "tensor_tensor", [yt, lnb], [yt], out=yt[0:cn, :], in0=yt[0:cn, :], in1=lnb[0:cn, :], op=ALU.add)
        ns += 1; yield
        if epilogue == "final":
            kb.v("pool", "tensor_tensor", [A[0], A[1]], [ee], out=ee[:, :, 0:cn], in0=A[0][:, :, 0:cn], in1=A[1][:, :, 0:cn], op=ALU.add)
            kb.v("pool", "tensor_tensor", [ee, kah], [ee], out=ee[:, :, 0:cn], in0=ee[:, :, 0:cn], in1=bc(kah[:, :], sh, 2), op=ALU.mult)
            kb.v("pool", "tensor_tensor", [ee, oka], [ee], out=ee[:, :, 0:cn], in0=ee[:, :, 0:cn], in1=bc(oka[:, :], sh, 2), op=ALU.add)
            kb.v("pool", "tensor_tensor", [ee] + Zt[8:16], [ee], out=ee[:, :, 0:cn], in0=ee[:, :, 0:cn], in1=Z[0:64, KS, 0:cn], op=ALU.mult)
            kb.v("pool", "tensor_tensor", [ee] + Zt[0:8], [ee], out=ee[:, :, 0:cn], in0=ee[:, :, 0:cn], in1=Z[0:64, RS, 0:cn], op=ALU.mult)
            kb.v("pool", "tensor_tensor", [ee, rkc], [ee], out=ee[:, :, 0:cn], in0=ee[:, :, 0:cn], in1=bc(rkc[:, :], sh, 2), op=ALU.mult)
            pS = B[0]
            for h in range(8):
                kb.mm(pS[0:cn, h:h + 1], ee[:, h, 0:cn], ones[:, 0:1], [ee, ones], [pS])
            kb.act(st[0:cn, 0:8], pS[0:cn, 0:8], AF.Copy, [pS], [st])
            kb.v("dve", "tensor_tensor", [V_t, st], [ysq], out=ysq[0:cn, :].rearrange("p (h v) -> p h v", h=8),
                 in0=V_t[0:cn, :].rearrange("p (h v) -> p h v", h=8), in1=bc(st[0:cn, 0:8], [cn, 8, 64], 2), op=ALU.mult)
            kb.v("dve", "tensor_tensor", [yt, ysq], [yt], out=yt[0:cn, :], in0=yt[0:cn, :], in1=ysq[0:cn, :], op=ALU.add)
            kb.act(sgd[:, 0, 0:cn], Z[:, 28, 0:cn], AF.Sigmoid, [Zt[28]], [sgd])
            kb.act(sgd[0:32, 1, 0:cn], Z[0:32, 29, 0:cn], AF.Sigmoid, [Zt[29]], [sgd])
            pG = B[1]
            kb.mm(pG[0:cn, :], sgd[:, 0, 0:cn], g2a[:, :], [sgd, g2a], [pG], start=True, stop=False, r32=True)
            kb.mm(pG[0:cn, :], sgd[0:32, 1, 0:cn], g2b[:, :], [sgd, g2b], [pG], start=False, stop=True, r32=True)
            kb.v("dve", "tensor_tensor", [yt, pG], [yout], out=yout[0:cn, :], in0=yt[0:cn, :], in1=pG[0:cn, :], op=ALU.mult)
            kb.dma(ya_d.t[c0:c0 + cn, :], yout[0:cn, :], [yout], [ya_d])
        ns += 1; yield
        while ns < NSTAGE:
            ns += 1; yield
        assert ns == NSTAGE, ns

    def chain(S, lst, d, H, epilogue):
        for (c0, cn) in lst:
            yield from chunk(S, c0, cn, d, epilogue is not None, H, epilogue)

    chunks = tiles_of(2048, 64) + [(2048, 16)] + [(2064 + a, b) for a, b in tiles_of(2048, 64)]
    own = [c for c in chunks if c[0] < OWN]
    oth = [c for c in chunks if c[0] >= OWN]
    Hf, Hb = Hs
    kb.v("dve", "memset", [], [Hf], Hf[:, :, :], 0.0)
    kb.v("dve", "memset", [], [Hb], Hb[:, :, :], 0.0)
    g0 = chain(slots[0], own, 0, Hf, "store")
    g1 = chain(slots[1], list(reversed(oth)), 1, Hb, None)
    step = 0
    d0 = d1 = False
    while not (d0 and d1):
        if not d0:
            try:
                next(g0)
            except StopIteration:
                d0 = True
        if step >= NSTAGE // 2 and not d1:
            try:
                next(g1)
            except StopIteration:
                d1 = True
        step += 1
    rown = list(reversed(own))
    g0 = chain(slots[0], rown[0::2], 1, Hb, "final")
    g1 = chain(slots[1], rown[1::2], 1, Hb, "final")
    step = 0
    d0 = d1 = False
    while not (d0 and d1):
        if not d0:
            try:
                next(g0)
            except StopIteration:
                d0 = True
        if step >= NSTAGE // 2 and not d1:
            try:
                next(g1)
            except StopIteration:
                d1 = True
        step += 1


def wload(kb, dram, rows, c0, c1, name):
    kch = rows // 128
    t = kb.sb([128, kch, c1 - c0], BF16, name)
    step = 1024
    for a in range(c0, c1, step):
        b = min(a + step, c1)
        kb.dma(t[:, :, a - c0:b - c0], dram.t[:, a:b].rearrange("(c p) n -> p c n", p=128), [dram], [t], q="pool")
    return t


def merge_phase(kb, uT_d, identb, D_, ya_d, yb_d, h2_d):
    B = kb.pbanks
    Wa = wload(kb, D_["w_a"], 512, 0, 1024, "Wa")
    Wb = wload(kb, D_["w_b"], 512, 0, 1024, "Wb")
    Wo = wload(kb, D_["w_o"], 1024, 0, 1024, "Wo")
    Wg = wload(kb, D_["w_in"], 1024, 2720, 4768, "Wg")
    yaf = kb.sb([128, 512], F32, "yaf"); yab = kb.sb([128, 512], BF16, "yab"); ybb = kb.sb([128, 512], BF16, "ybb")
    yaT = kb.sb([128, 4, 128], BF16, "yaT"); ybT = kb.sb([128, 4, 128], BF16, "ybT")
    sgA = kb.sb([128, 1024], F32, "sgA"); sgB = kb.sb([128, 1024], F32, "sgB")
    mg = kb.sb([128, 1024], BF16, "mg"); mT = kb.sb([128, 8, 128], BF16, "mT")
    xt = kb.sb([128, 1024], F32, "xt2"); h2 = kb.sb([128, 1024], F32, "h2")
    uts = [kb.sb([128, NCH, 128], BF16, "utm") for _ in range(2)]
    for i, (t0, n) in enumerate(tiles_of(OWN, 128)):
        ut = uts[i % 2]
        kb.dma(ut[:, :, 0:n], uT_d.t[:, :, t0:t0 + n], [uT_d], [ut])
        kb.dma(yaf[0:n, :], ya_d.t[t0:t0 + n, :], [ya_d], [yaf])
        kb.dma(ybb[0:n, :], yb_d.t[i, 0:n, :], [yb_d], [ybb])
        kb.dma(xt[0:n, :], D_["h"].t[t0:t0 + n, :], [D_["h"]], [xt])
        kb.v("dve", "tensor_copy", [yaf], [yab], out=yab[0:n, :], in_=yaf[0:n, :])
        for src, dst, bk in ((yab, yaT, B[0]), (ybb, ybT, B[1])):
            pv = bk.t.bitcast(BF16).rearrange("p (c t) -> p c t", c=8)
            for k in range(4):
                kb.tr(pv[:, k, 0:n], src[0:n, k * 128:(k + 1) * 128], identb[0:n, 0:n], [src, identb], [bk])
            kb.act(dst[:, :, 0:n], pv[:, 0:4, 0:n], AF.Copy, [bk], [dst])
        for gi, sg_ in enumerate((sgA, sgB)):
            for hb in range(2):
                bk = B[2 + hb]
                col = gi * 1024 + hb * 512
                for c in range(NCH):
                    kb.mm(bk[0:n, :], ut[:, c, 0:n], Wg[:, c, col:col + 512], [ut, Wg], [bk], start=c == 0, stop=c == NCH - 1)
                kb.act(sg_[0:n, hb * 512:(hb + 1) * 512], bk[0:n, :], AF.Sigmoid, [bk], [sg_])
        for (yT_, W_, sg_) in ((yaT, Wa, sgA), (ybT, Wb, sgB)):
            for hb in range(2):
                bk = B[4 + hb]
                for k in range(4):
                    kb.mm(bk[0:n, :], yT_[:, k, 0:n], W_[:, k, hb * 512:(hb + 1) * 512], [yT_, W_], [bk], start=k == 0, stop=k == 3)
                kb.v("dve", "tensor_tensor", [sg_, bk], [sg_], out=sg_[0:n, hb * 512:(hb + 1) * 512],
                     in0=sg_[0:n, hb * 512:(hb + 1) * 512], in1=bk[0:n, :], op=ALU.mult)
        kb.v("dve", "tensor_tensor", [sgA, sgB], [mg], out=mg[0:n, :], in0=sgA[0:n, :], in1=sgB[0:n, :], op=ALU.add)
        bk = B[6]
        pv = bk.t.bitcast(BF16).rearrange("p (c t) -> p c t", c=8)
        for k in range(8):
            kb.tr(pv[:, k, 0:n], mg[0:n, k * 128:(k + 1) * 128], identb[0:n, 0:n], [mg, identb], [bk])
        kb.act(mT[:, :, 0:n], pv[:, :, 0:n], AF.Copy, [bk], [mT])
        for hb in range(2):
            bk = B[hb]
            for k in range(8):
                kb.mm(bk[0:n, :], mT[:, k, 0:n], Wo[:, k, hb * 512:(hb + 1) * 512], [mT, Wo], [bk], start=k == 0, stop=k == 7)
            kb.v("dve", "tensor_tensor", [xt, bk], [h2], out=h2[0:n, hb * 512:(hb + 1) * 512], in0=xt[0:n, hb * 512:(hb + 1) * 512],
                 in1=bk[0:n, :], op=ALU.add)
        kb.dma(h2_d.t[t0:t0 + n, :], h2[0:n, :], [h2], [h2_d])


def ffn_phase(kb, identb, D_, h2_d, out_d):
    B = kb.pbanks
    GT = 256
    W1 = wload(kb, D_["w_ff1"], 1024, 0, 4096, "W1")
    W2 = wload(kb, D_["w_ff2"], 4096, 0, 1024, "W2")
    gf = kb.sb([128, NCH], F32, "gffn"); gfin = kb.sb([128, 1024], F32, "gfin")
    kb.dma(gf.t, D_["gffn"].t, [D_["gffn"]], [gf])
    kb.dma(gfin.t, D_["gfin"].t, [D_["gfin"]], [gfin])
    h2g = [kb.sb([128, 1024], F32, "h2f") for _ in range(2)]
    xs = kb.sb([128, 1024], BF16, "xs2")
    ss = kb.sb([128, 8], F32, "ss2"); nT = kb.sb([128, 8, GT], BF16, "nT")
    frs = [kb.sb([128, GT], F32, "fr") for _ in range(2)]
    f2T = kb.sb([128, 32, GT], BF16, "f2T")
    h3 = kb.sb([128, 1024], F32, "h3"); ot = kb.sb([128, 1024], F32, "ot")
    junk = ot
    for gi, (g0, gn) in enumerate(tiles_of(OWN, GT)):
        subs = tiles_of(gn, 128)
        for si, (o0, n) in enumerate(subs):
            t0 = g0 + o0
            h2 = h2g[si]
            kb.dma(h2[0:n, :], h2_d.t[t0:t0 + n, :], [h2_d], [h2])
            kb.act(junk[0:n, :], h2[0:n, :], AF.Square, [h2], [junk, ss], accum_out=ss[0:n, 0:1])
            kb.act(ss[0:n, 1:2], ss[0:n, 0:1], AF.Sqrt, [ss], [ss], scale=1.0 / D, bias=EPS)
            kb.v("dve", "reciprocal", [ss], [ss], out=ss[0:n, 2:3], in_=ss[0:n, 1:2])
            kb.v("dve", "tensor_scalar", [h2, ss], [xs], out=xs[0:n, :], in0=h2[0:n, :], scalar1=ss[0:n, 2:3], scalar2=None, op0=ALU.mult)
            bk = B[si]
            pv = bk.t.bitcast(BF16).rearrange("p (c t) -> p c t", c=8)
            for c in range(NCH):
                kb.tr(pv[:, c, 0:n], xs[0:n, c * 128:(c + 1) * 128], identb[0:n, 0:n], [xs, identb], [bk])
            kb.v("dve", "tensor_tensor", [bk, gf], [nT], out=nT[:, :, o0:o0 + n], in0=pv[:, :, 0:n],
                 in1=gf[:, :].unsqueeze(2).to_broadcast([128, NCH, n]), op=ALU.mult)
        for ft in range(32):
            bk = B[2 + ft % 3]
            fr = frs[ft % 2]
            for c in range(NCH):
                kb.mm(bk[:, 0:gn], W1[:, c, ft * 128:(ft + 1) * 128], nT[:, c, 0:gn], [W1, nT], [bk], start=c == 0, stop=c == NCH - 1)
            kb.act(fr[:, 0:gn], bk[:, 0:gn], AF.Relu, [bk], [fr])
            kb.v("dve", "tensor_tensor", [fr], [f2T], out=f2T[:, ft, 0:gn], in0=fr[:, 0:gn], in1=fr[:, 0:gn], op=ALU.mult)
        for si, (o0, n) in enumerate(subs):
            t0 = g0 + o0
            h2 = h2g[si]
            for hb in range(2):
                bk = B[5 + (hb + 2 * si) % 3]
                for k in range(32):
                    kb.mm(bk[0:n, :], f2T[:, k, o0:o0 + n], W2[:, k, hb * 512:(hb + 1) * 512], [f2T, W2], [bk], start=k == 0, stop=k == 31)
                kb.v("dve", "tensor_tensor", [h2, bk], [h3], out=h3[0:n, hb * 512:(hb + 1) * 512], in0=h2[0:n, hb * 512:(hb + 1) * 512],
                     in1=bk[0:n, :], op=ALU.add)
            kb.act(junk[0:n, :], h3[0:n, :], AF.Square, [h3], [junk, ss], accum_out=ss[0:n, 4:5])
            kb.act(ss[0:n, 5:6], ss[0:n, 4:5], AF.Sqrt, [ss], [ss], scale=1.0 / D, bias=EPS)
            kb.v("dve", "reciprocal", [ss], [ss], out=ss[0:n, 6:7], in_=ss[0:n, 5:6])
            kb.v("dve", "scalar_tensor_tensor", [h3, ss, gfin], [ot], out=ot[0:n, :], in0=h3[0:n, :], scalar=ss[0:n, 6:7], in1=gfin[0:n, :],
                 op0=ALU.mult, op1=ALU.mult)
            k = kb.dma(out_d.t[t0:t0 + n, :], ot[0:n, :], [ot], [])
            kb.out_keys.append(k)

def tiles_of(n, step):
    out = []
    t = 0
    while t < n:
        out.append((t, min(step, n - t)))
        t += step
    return out


def build(debug=None, upto="all"):
    nc = bass.Bass("TRN2", target_bir_lowering=False)
    kb = KB(nc, debug)
    kb.setup_mem()
    fw = kb.fw
    h_d = kb.dram_in("h", [L, D])
    gmix_d = kb.dram_in("gmix", [128, NCH])
    ident_d = kb.dram_in("ident", [128, 128])
    out_d = kb.dram_out("out", [OWN, D])

    ident = kb.sb([128, 128], F32, "ident")
    identb = kb.sb([128, 128], BF16, "identb")
    gmix = kb.sb([128, NCH], F32, "gmix")
    kb.dma(ident[:, :], ident_d[:, :], [ident_d], [ident])
    kb.dma(gmix[:, :], gmix_d[:, :], [gmix_d], [gmix])
    kb.v("dve", "tensor_copy", [ident], [identb], out=identb[:, :], in_=ident[:, :])

    win_d = kb.dram_in("w_in", [D, INW])
    kb.from_top = True
    wr_pre = rwkv_load_w(kb, {"w_in": win_d})
    kb.from_top = False
    mA = kb.mark()
    uT_d = kb.dram_tmp("uT_scr", [128, NCH, L], BF16)
    xts = [kb.sb([128, D], F32, "xt") for _ in range(2)]
    xss = [kb.sb([128, D], BF16, "xs") for _ in range(2)]
    junk = kb.sb([128, D], BF16, "junk")
    sss = [kb.sb([128, 4], F32, "ss") for _ in range(2)]
    uts = [kb.sb([128, NCH, 128], BF16, "ut") for _ in range(2)]

    def phase_a_tile(i, t0, n):
        xt = xts[i % 2]
        xs = xss[i % 2]
        ss = sss[i % 2]
        ut = uts[i % 2]
        kb.dma(xt[0:n, :], h_d[t0:t0 + n, :], [h_d], [xt])
        kb.act(junk[0:n, :], xt[0:n, :], AF.Square, [xt], [junk, ss], accum_out=ss[0:n, 0:1])
        kb.act(ss[0:n, 1:2], ss[0:n, 0:1], AF.Sqrt, [ss], [ss], scale=1.0 / D, bias=EPS)
        kb.v("dve", "reciprocal", [ss], [ss], out=ss[0:n, 2:3], in_=ss[0:n, 1:2])
        kb.v("dve", "tensor_scalar", [xt, ss], [xs], out=xs[0:n, :], in0=xt[0:n, :],
             scalar1=ss[0:n, 2:3], scalar2=None, op0=ALU.mult)
        pb = kb.bank()
        pv = pb.t.bitcast(BF16).rearrange("p (c t) -> p c t", c=NCH)
        for c in range(NCH):
            kb.tr(pv[:, c, 0:n], xs[0:n, c * 128:(c + 1) * 128], identb[0:n, 0:n], [xs, identb], [pb])
        kb.v("dve", "tensor_tensor", [pb, gmix], [ut], out=ut[:, :, 0:n], in0=pv[:, :, 0:n],
             in1=gmix[:, :].unsqueeze(2).to_broadcast([128, NCH, n]), op=ALU.mult)
        kb.dma(uT_d.t[:, :, t0:t0 + n], ut[:, :, 0:n], [ut], [uT_d])
        return ut

    ropeC_d = kb.dram_in("ropeC", [L, 64])
    ropeS_d = kb.dram_in("ropeS", [L, 64])
    qg_d = kb.dram_in("qg", [128, 64])
    kg_d = kb.dram_in("kg", [128, 64])
    NT = 33
    NQT = 17
    mB0 = kb.mark()
    y_b = kb.sb([128, NQT, 512], BF16, "y_b")
    mB = kb.mark()
    wqkv = kb.sb([128, NCH, 768], BF16, "wqkv")
    kb.dma(wqkv[:, :, :], win_d.t[:, 1952:2720].rearrange("(c p) n -> p c n", p=128), [win_d], [wqkv], q="pool")
    rcs = [kb.sb([128, 64], F32, "rc") for _ in range(2)]
    rss = [kb.sb([128, 64], F32, "rs") for _ in range(2)]
    qgb = kb.sb([128, 64], F32, "qgb")
    kgb = kb.sb([128, 64], F32, "kgb")
    kb.dma(qgb[:, :], qg_d[:, :], [qg_d], [qgb])
    kb.dma(kgb[:, :], kg_d[:, :], [kg_d], [kgb])
    QT = kb.sb([64, 8, OWN], BF16, "QT")
    KT = kb.sb([64, 2, L], BF16, "KT")
    V1 = kb.sb([128, NT, 2, 65], BF16, "V1")
    kb.v("pool", "memset", [], [V1], V1[:, :, :, 64:65], 1.0)
    sqs = [kb.sb([128, 640], F32, "sq") for _ in range(1)]
    qns = [kb.sb([128, 640], F32, "qn") for _ in range(1)]
    t1s = [kb.sb([128, 640], F32, "t1") for _ in range(1)]
    t2s = [kb.sb([128, 640], F32, "t2") for _ in range(1)]
    qrs = [kb.sb([128, 640], BF16, "qr") for _ in range(2)]
    sms = [kb.sb([128, 32], F32, "sm") for _ in range(2)]
    for i, (t0, n) in enumerate(tiles_of(L, 128)):
        nq = max(0, min(n, OWN - t0))
        ut = phase_a_tile(i, t0, n)
        sq, qn, t1, t2, qr, sm = sqs[0], qns[0], t1s[0], t2s[0], qrs[i % 2], sms[i % 2]
        pq = kb.bank()
        pkv = kb.bank()
        rc, rs = rcs[i % 2], rss[i % 2]
        kb.dma(rc[0:n, :], ropeC_d.t[t0:t0 + n, :], [ropeC_d], [rc])
        kb.dma(rs[0:n, :], ropeS_d.t[t0:t0 + n, :], [ropeS_d], [rs])
        for c in range(NCH):
            if nq:
                kb.mm(pq[0:n, 0:512], ut[:, c, 0:n], wqkv[:, c, 0:512], [ut, wqkv], [pq], start=c == 0, stop=c == NCH - 1)
            kb.mm(pkv[0:n, 0:256], ut[:, c, 0:n], wqkv[:, c, 512:768], [ut, wqkv], [pkv], start=c == 0, stop=c == NCH - 1)
        h0 = 0 if nq else 8
        c0 = h0 * 64
        nh = 10 - h0
        if nq:
            kb.act(sq[0:n, 0:512], pq[0:n, 0:512], AF.Square, [pq], [sq])
        kb.act(sq[0:n, 512:640], pkv[0:n, 0:128], AF.Square, [pkv], [sq])
        kb.v("dve", "tensor_reduce", [sq], [sm], out=sm[0:n, h0:10],
             in_=sq[0:n, c0:640].rearrange("p (h c) -> p h c", c=64), axis=AX.X, op=ALU.add)
        kb.act(sm[0:n, 10 + h0:20], sm[0:n, h0:10], AF.Sqrt, [sm], [sm], scale=1.0 / 64, bias=EPS)
        kb.v("dve", "reciprocal", [sm], [sm], out=sm[0:n, 20 + h0:30], in_=sm[0:n, 10 + h0:20])
        if nq:
            kb.v("dve", "tensor_tensor", [pq, sm], [qn], out=qn[0:n, 0:512].rearrange("p (h c) -> p h c", c=64),
                 in0=pq[0:n, 0:512].rearrange("p (h c) -> p h c", c=64),
                 in1=sm[0:n, 20:28].unsqueeze(2).to_broadcast([n, 8, 64]), op=ALU.mult)
            kb.v("dve", "tensor_tensor", [qn, qgb], [qn], out=qn[0:n, 0:512].rearrange("p (h c) -> p h c", c=64),
                 in0=qn[0:n, 0:512].rearrange("p (h c) -> p h c", c=64),
                 in1=qgb[0:n, :].unsqueeze(1).to_broadcast([n, 8, 64]), op=ALU.mult)
        kb.v("dve", "tensor_tensor", [pkv, sm], [qn], out=qn[0:n, 512:640].rearrange("p (h c) -> p h c", c=64),
             in0=pkv[0:n, 0:128].rearrange("p (h c) -> p h c", c=64),
             in1=sm[0:n, 28:30].unsqueeze(2).to_broadcast([n, 2, 64]), op=ALU.mult)
        kb.v("dve", "tensor_tensor", [qn, kgb], [qn], out=qn[0:n, 512:640].rearrange("p (h c) -> p h c", c=64),
             in0=qn[0:n, 512:640].rearrange("p (h c) -> p h c", c=64),
             in1=kgb[0:n, :].unsqueeze(1).to_broadcast([n, 2, 64]), op=ALU.mult)
        kb.v("dve", "tensor_tensor", [qn, rc], [t1], out=t1[0:n, c0:640].rearrange("p (h c) -> p h c", c=64),
             in0=qn[0:n, c0:640].rearrange("p (h c) -> p h c", c=64),
             in1=rc[0:n, :].unsqueeze(1).to_broadcast([n, nh, 64]), op=ALU.mult)
        for hf in range(2):
            o_v = t2[0:n, c0:640].rearrange("p (h a x f) -> p h a x f", a=2, x=2, f=16)[:, :, :, hf, :]
            i_v = qn[0:n, c0:640].rearrange("p (h a x f) -> p h a x f", a=2, x=2, f=16)[:, :, :, 1 - hf, :]
            s_v = rs[0:n, :].rearrange("p (a x f) -> p a x f", a=2, x=2)[:, :, hf, :].unsqueeze(1).to_broadcast([n, nh, 2, 16])
            kb.v("dve", "tensor_tensor", [qn, rs], [t2], out=o_v, in0=i_v, in1=s_v, op=ALU.mult)
        kb.v("dve", "tensor_tensor", [t1, t2], [qr], out=qr[0:n, c0:640], in0=t1[0:n, c0:640], in1=t2[0:n, c0:640], op=ALU.add)
        ptk = kb.bank()
        ptkv = ptk.t.bitcast(BF16).rearrange("p (c t) -> p c t", c=8)
        for g in range(2):
            kb.tr(ptkv[0:64, g, 0:n], qr[0:n, 512 + g * 64:512 + (g + 1) * 64], identb[0:n, 0:n], [qr, identb], [ptk])
        kb.v("dve", "tensor_copy", [ptk], [KT], out=KT[:, :, t0:t0 + n], in_=ptkv[0:64, 0:2, 0:n])
        if nq:
            ptq = kb.bank()
            ptqv = ptq.t.bitcast(BF16).rearrange("p (c t) -> p c t", c=8)
            for hh in range(8):
                kb.tr(ptqv[0:64, hh, 0:n], qr[0:n, hh * 64:(hh + 1) * 64], identb[0:n, 0:n], [qr, identb], [ptq])
            kb.act(QT[:, :, t0:t0 + nq], ptqv[0:64, :, 0:nq], AF.Copy, [ptq], [QT])
        kb.act(V1[0:n, i, :, 0:64], pkv[0:n, 128:256].rearrange("p (g c) -> p g c", c=64), AF.Copy, [pkv], [V1])
    fw.barrier()
    Es = [kb.sb([128, 4, 128], BF16, "E") for _ in range(3)]
    rcp = kb.sb([128, 8], F32, "rcp")
    Ob = kb.pbanks[0:4]
    Sb = kb.pbanks[4:6]
    ktiles = tiles_of(L, 128)
    steps = []
    for qi, (q0, nq) in enumerate(tiles_of(OWN, 128)):
        for g in range(2):
            for kt, (k0, nk) in enumerate(ktiles):
                steps.append((qi, q0, nq, g, kt, k0, nk))
    NS_ = len(steps)
    pend = {}

    def issue_s(i):
        qi, q0, nq, g, kt, k0, nk = steps[i]
        sT = Sb[i % 2]
        E = Es[i % 3]
        sTv = sT.t[0:nk, 0:4 * nq].rearrange("p (h q) -> p h q", h=4)
        kb.mm(sTv, KT[0:64, g, k0:k0 + nk], QT[0:64, 4 * g:4 * g + 4, q0:q0 + nq], [KT, QT], [sT])
        kb.act(E[0:nk, :, 0:nq], sTv, AF.Exp, [sT], [E], scale=0.125)

    def issue_pv(i):
        qi, q0, nq, g, kt, k0, nk = steps[i]
        E = Es[i % 3]
        for hh in range(4):
            kb.mm(Ob[hh][0:nq, 0:65], E[0:nk, hh, 0:nq], V1[0:nk, kt, g, :], [E, V1], [Ob[hh]],
                  start=kt == 0, stop=kt == len(ktiles) - 1)
        if kt == len(ktiles) - 1:
            for hh in range(4):
                kb.v("dve", "reciprocal", [Ob[hh]], [rcp], out=rcp[0:nq, hh:hh + 1], in_=Ob[hh][0:nq, 64:65])
                hd = 4 * g + hh
                kb.v("dve", "tensor_scalar", [Ob[hh], rcp], [y_b], out=y_b[0:nq, qi, hd * 64:(hd + 1) * 64],
                     in0=Ob[hh][0:nq, 0:64], scalar1=rcp[0:nq, hh:hh + 1], scalar2=None, op0=ALU.mult)

    issue_s(0)
    for i in range(NS_):
        if i + 1 < NS_:
            issue_s(i + 1)
        issue_pv(i)
    kb.dump("y_b", y_b, y_b[:, :, :], [128, NQT, 512], BF16)
    yb_d = kb.dram_tmp("yb_scr", [NQT, 128, 512], BF16)
    kb.dma(yb_d.t[0:16].rearrange("a p c -> p a c"), y_b[:, 0:16, :], [y_b], [yb_d])
    kb.dma(yb_d.t[16, 0:16, :], y_b[0:16, 16, :], [y_b], [yb_d])
    fw.barrier()
    kb.release(mA)
    if upto == "B":
        kb.finish()
        return nc, kb
    names = {"mu0": [128, RW_TILES], "mu1": [128, RW_TILES], "w2": [64, 2, 512], "a2": [64, 2, 512], "w0": [64, 2, 8],
             "a0": [64, 2, 8], "g2a": [128, 512], "g2b": [32, 512], "kkc": [64, 8], "kac": [64, 8], "rkc": [64, 8],
             "lng": [64, 512], "lnb": [64, 512],
             "masks": [64, 4, 64], "w_a": [512, 1024], "w_b": [512, 1024], "w_o": [1024, 1024], "w_ff1": [1024, 4096],
             "w_ff2": [4096, 1024], "gffn": [128, NCH], "gfin": [128, 1024]}
    D_ = {k: kb.dram_in(k, v) for k, v in names.items()}
    D_["w_in"] = win_d
    D_["h"] = h_d
    ybwd_d = kb.dram_tmp("ybwd_scr", [OWN, 512])
    ya_d = kb.dram_tmp("ya_scr", [OWN, 512])
    h2_d = kb.dram_tmp("h2_scr", [OWN, D])
    mC = kb.mark()
    rwkv_phase(kb, uT_d, ident, D_, ybwd_d, ya_d, wr_pre)
    if "ya" in kb.debug:
        o = kb.dram_out("dbg_ya", [OWN, 512])
        kb.out_keys.append(kb.dma(o.t, ya_d.t, [ya_d], [o]))
    fw.barrier()
    kb.release(mC)
    kb.hi = kb.AW
    if upto == "C":
        kb.finish()
        return nc, kb
    merge_phase(kb, uT_d, identb, D_, ya_d, yb_d, h2_d)
    if "h2" in kb.debug:
        o = kb.dram_out("dbg_h2", [OWN, D])
        kb.out_keys.append(kb.dma(o.t, h2_d.t, [h2_d], [o]))
    fw.barrier()
    kb.release(mA)
    ffn_phase(kb, identb, D_, h2_d, out_d)

    kb.finish()
    return nc, kb


def _rope_tables():
    inv = (10000.0 ** (-np.arange(16, dtype=np.float32) * 2.0 / 32)).astype(np.float32)
    rows = (np.arange(64, dtype=np.float32)[:, None] * inv).astype(np.float32)
    cols = (np.arange(64, dtype=np.float32)[:, None] * inv).astype(np.float32)
    ang = np.zeros((L, 2, 16), np.float32)
    grid = np.stack([np.broadcast_to(rows[:, None, :], (64, 64, 16)),
                     np.broadcast_to(cols[None, :, :], (64, 64, 16))], axis=2).reshape(4096, 2, 16)
    ang[16:] = grid
    c, s_ = np.cos(ang).astype(np.float32), np.sin(ang).astype(np.float32)
    cosf = np.stack([c, c], axis=2).reshape(L, 64)
    sinf = np.stack([-s_, s_], axis=2).reshape(L, 64)
    return cosf, sinf


def prep_inputs(inputs, core):
    b, s = core // 2, core % 2
    x = np.asarray(inputs["x"], np.float32)
    meta = np.asarray(inputs["meta_tokens"], np.float32)
    hseq = np.concatenate([meta, x[b]], axis=0)
    if s == 1:
        hseq = hseq[::-1]
    m = {}
    m["h"] = np.ascontiguousarray(hseq)
    m["gmix"] = np.ascontiguousarray(np.asarray(inputs["mix_norm_g"], np.float32)[0].reshape(NCH, 128).T)
    m["ident"] = np.eye(128, dtype=np.float32)
    w_in = np.asarray(inputs["w_in"], np.float32)[0]
    if s == 1:
        w_in = w_in.copy()
        for base in (1536, 1664):
            a = w_in[:, base:base + 64].copy()
            w_in[:, base:base + 64] = w_in[:, base + 64:base + 128]
            w_in[:, base + 64:base + 128] = a
    m["w_in"] = np.ascontiguousarray(w_in)
    cosf, sinf = _rope_tables()
    if s == 1:
        cosf, sinf = cosf[::-1], sinf[::-1]
    m["ropeC"] = np.ascontiguousarray(cosf)
    m["ropeS"] = np.ascontiguousarray(sinf)
    G = lambda k: np.asarray(inputs[k], np.float32)[0]
    sh = G("rwkv_shift")
    w2, w0, a2, a0 = G("decay_w2"), G("decay_w0"), G("icl_a2"), G("icl_a0")
    if s == 1:
        sh = sh[::-1].copy()
        for base in (1536, 1664):
            a = sh[:, base:base + 64].copy()
            sh[:, base:base + 64] = sh[:, base + 64:base + 128]
            sh[:, base + 64:base + 128] = a
        w2, w0, a2, a0 = w2[::-1], w0[::-1], a2[::-1], a0[::-1]
    for nm, row in (("mu0", 0), ("mu1", 1)):
        arr = np.zeros((128, RW_TILES), np.float32)
        for j, (c0, w) in enumerate(RW_COLS):
            arr[:w, j] = sh[row, c0:c0 + w]
        m[nm] = arr
    m["w2"] = np.ascontiguousarray(w2.transpose(1, 0, 2))
    m["a2"] = np.ascontiguousarray(a2.transpose(1, 0, 2))
    m["w0"] = np.ascontiguousarray(w0.reshape(2, 8, 64).transpose(2, 0, 1))
    m["a0"] = np.ascontiguousarray(a0.reshape(2, 8, 64).transpose(2, 0, 1))
    g2 = G("gate_w2")
    m["g2a"] = np.ascontiguousarray(g2[0:128]); m["g2b"] = np.ascontiguousarray(g2[128:160])
    m["kkc"] = np.ascontiguousarray(G("k_k").reshape(8, 64).T)
    m["kac"] = np.ascontiguousarray(G("k_a").reshape(8, 64).T)
    m["rkc"] = np.ascontiguousarray(G("r_k").reshape(8, 64).T)
    m["lng"] = np.ascontiguousarray(np.tile(G("lnx_g")[None, :], (64, 1)))
    m["lnb"] = np.ascontiguousarray(np.tile(G("lnx_b")[None, :], (64, 1)))
    si, ti = np.meshgrid(np.arange(64), np.arange(64), indexing="ij")
    m["masks"] = np.ascontiguousarray(np.stack([si < ti, si <= ti, si > ti, si >= ti], 1).astype(np.float32))
    m["w_a"] = np.ascontiguousarray(G("w_branch_rwkv")); m["w_b"] = np.ascontiguousarray(G("w_branch_attn"))
    m["w_o"] = np.ascontiguousarray(G("w_out")); m["w_ff1"] = np.ascontiguousarray(G("w_ff1")); m["w_ff2"] = np.ascontiguousarray(G("w_ff2"))
    m["gffn"] = np.ascontiguousarray(G("ffn_norm_g").reshape(NCH, 128).T)
    m["gfin"] = np.ascontiguousarray(np.tile(np.asarray(inputs["final_norm_g"], np.float32)[None, :], (128, 1)))
    m["qg"] = np.ascontiguousarray(np.tile(np.asarray(inputs["q_norm_g"], np.float32)[0][None, :], (128, 1)))
    m["kg"] = np.ascontiguousarray(np.tile(np.asarray(inputs["k_norm_g"], np.float32)[0][None, :], (128, 1)))
    return m


_CACHE = {}


def kernel(**inputs):
    from concourse.bass_utils import run_bass_kernel_spmd
    if "nc" not in _CACHE:
        _CACHE["nc"] = build()
    nc, kb = _CACHE["nc"]
    in_maps = [prep_inputs(inputs, c) for c in range(8)]
    res = run_bass_kernel_spmd(nc, in_maps, core_ids=list(range(8)))
    out = np.zeros((4, 4096, D), np.float32)
    for c in range(8):
        b, s = c // 2, c % 2
        o = np.asarray(res.results[c]["out"])
        if s == 0:
            out[b, 0:2048] = o[16:2064]
        else:
            out[b, 2048:4096] = o[0:2048][::-1]
    return out
```

```python
import numpy as np
import concourse.bass as bass
import concourse.mybir as mybir

F32 = mybir.dt.float32
BF16 = mybir.dt.bfloat16
AF = mybir.ActivationFunctionType
ALU = mybir.AluOpType
AX = mybir.AxisListType

ENGS = ["pe", "dve", "act", "pool", "sp"]
SEM_WRAP = 2048
NSLOT = 8


class Res:
    __slots__ = ("name", "w", "rd")

    def __init__(self, name):
        self.name = name
        self.w = None
        self.rd = []


class Rec:
    __slots__ = ("fn", "deps", "sig", "dma", "signo", "pre")

    def __init__(self, fn, dma):
        self.fn = fn
        self.deps = set()
        self.sig = False
        self.dma = dma
        self.signo = None
        self.pre = None


class FW:
    def __init__(self, nc, same_engine_sync=True):
        self.nc = nc
        self.eng = {"pe": nc.tensor, "dve": nc.vector, "act": nc.scalar,
                    "pool": nc.gpsimd, "sp": nc.sync}
        self.ops = {e: [] for e in ENGS}
        self.ndma = {e: 0 for e in ENGS}
        self.same = same_engine_sync

    def op(self, e, fn, reads=(), writes=(), dma=False):
        lst = self.ops[e]
        idx = len(lst)
        rec = Rec(fn, None)
        if dma:
            rec.dma = self.ndma[e]
            self.ndma[e] += 1
        key = (e, idx)
        deps = set()
        for r in reads:
            if r.w is not None:
                deps.add(r.w)
        for w in writes:
            if w.w is not None:
                deps.add(w.w)
            for k in w.rd:
                deps.add(k)
        for d in deps:
            de, di = d
            drec = self.ops[de][di]
            if drec.dma is None:
                if de == e and (not self.same) and e != "pool":
                    continue
                if de == e and e == "pe":
                    continue
                drec.sig = True
            rec.deps.add(d)
        for r in reads:
            r.rd.append(key)
        for w in writes:
            w.w = key
            w.rd = []
        lst.append(rec)
        return key

    def barrier(self, resources=()):
        b = Res("barrier")
        keys = []
        for e in ENGS:
            for i in range(len(self.ops[e]) - 1, -1, -1):
                if self.ops[e][i].fn is not None:
                    keys.append((e, i))
                    break
        dkeys = []
        for e in ENGS:
            n = 0
            for i in range(len(self.ops[e]) - 1, -1, -1):
                if self.ops[e][i].dma is not None:
                    dkeys.append((e, i))
                    n += 1
                    if n >= NSLOT:
                        break
        self._pending_barrier = keys + dkeys
        for e in ENGS:
            rec = Rec(None, None)
            for d in keys + dkeys:
                de, di = d
                drec = self.ops[de][di]
                if drec.dma is None:
                    if de == e:
                        continue
                    drec.sig = True
                rec.deps.add(d)
            self.ops[e].append(rec)

    def emit(self, final_waits=()):
        nc = self.nc
        nsig = {}
        for e in ENGS:
            n = 0
            for rec in self.ops[e]:
                if rec.dma is None and rec.sig:
                    n += 1
                    rec.signo = n
            nsig[e] = n
        import contextlib
        with contextlib.ExitStack() as st:
            sems = {}
            for e in ENGS:
                k = (nsig[e] + SEM_WRAP - 1) // SEM_WRAP
                sems[e] = [st.enter_context(nc.semaphore(f"s_{e}_{i}")) for i in range(max(k, 1))]
            dsem = {}
            for e in ENGS:
                if self.ndma[e]:
                    dsem[e] = [st.enter_context(nc.semaphore(f"d_{e}_{i}")) for i in range(NSLOT)]
            block = st.enter_context(nc.Block())

            def target(dep):
                de, di = dep
                drec = self.ops[de][di]
                if drec.dma is not None:
                    n = drec.dma
                    return (dsem[de][n % NSLOT], 16 * (n // NSLOT + 1))
                s = drec.signo
                return (sems[de][(s - 1) // SEM_WRAP], (s - 1) % SEM_WRAP + 1)

            def run(e, eng):
                waited = {}
                for rec in self.ops[e]:
                    tg = {}
                    for dep in rec.deps:
                        sem, val = target(dep)
                        kk = id(sem)
                        if kk not in tg or tg[kk][1] < val:
                            tg[kk] = (sem, val)
                    if rec.dma is not None and rec.dma >= NSLOT:
                        n = rec.dma
                        sem = dsem[e][n % NSLOT]
                        val = 16 * (n // NSLOT)
                        kk = id(sem)
                        if kk not in tg or tg[kk][1] < val:
                            tg[kk] = (sem, val)
                    for kk, (sem, val) in tg.items():
                        if waited.get(kk, 0) >= val:
                            continue
                        waited[kk] = val
                        eng.wait_ge(sem, val)
                    if rec.fn is None:
                        continue
                    inst = rec.fn(eng)
                    if rec.dma is not None:
                        inst.then_inc(dsem[e][rec.dma % NSLOT], 16)
                    elif rec.sig:
                        s = rec.signo
                        inst.then_inc(sems[e][(s - 1) // SEM_WRAP], 1)

            @block.tensor
            def _(eng):
                run("pe", eng)

            @block.vector
            def _(eng):
                run("dve", eng)

            @block.scalar
            def _(eng):
                run("act", eng)

            @block.gpsimd
            def _(eng):
                run("pool", eng)

            @block.sync
            def _(eng):
                run("sp", eng)

import contextlib

L = 4112
OWN = 2064
D = 1024
NCH = 8
INW = 4768
RW = 1952
EPS = 1e-6


class Tl:
    def __init__(self, t, name):
        self.t = t
        self.r = Res(name)

    def __getitem__(self, k):
        return self.t[k]


class KB:
    def __init__(self, nc, debug=None):
        self.nc = nc
        self.fw = FW(nc)
        self.st = contextlib.ExitStack()
        self.debug = debug or []
        self.dbg_out = {}
        self.out_keys = []
        self.cnt = 0
        import os
        self.use_r32 = os.environ.get('USE_R32', '0') == '1'

    def setup_mem(self):
        self.AW = 49000
        self.arena = self.st.enter_context(self.nc.sbuf_tensor("arena", [128, self.AW], F32))
        self.psum = self.st.enter_context(self.nc.psum_tensor("psum", [128, 4096], F32))
        self.top = 0
        self.hi = self.AW
        self.pbanks = [self._pview(i) for i in range(8)]
        self.pb_i = 0

    def _pview(self, i):
        return Tl(self.psum[:, 512 * i:512 * (i + 1)], f"bank{i}")

    def bank(self):
        b = self.pbanks[self.pb_i % 8]
        self.pb_i += 1
        return b

    def mark(self):
        return self.top

    def release(self, m):
        self.top = m

    def sb(self, shape, dt, name=None):
        self.cnt += 1
        name = f"{name or 't'}_{self.cnt}"
        p = shape[0]
        n = 1
        for x in shape[1:]:
            n *= x
        words = n if dt == F32 else (n + 1) // 2
        if getattr(self, 'from_top', False):
            self.hi -= words
            off = self.hi
        else:
            off = self.top
            self.top += words
        assert self.top <= self.hi, f"arena overflow {self.top} {self.hi}"
        ap = self.arena[0:p, off:off + words]
        if dt != F32:
            ap = ap.bitcast(dt)
            if n % 2:
                ap = ap[:, 0:n]
        if len(shape) == 3:
            ap = ap.rearrange("p (a b) -> p a b", a=shape[1])
        elif len(shape) == 4:
            ap = ap.rearrange("p (a b c) -> p a b c", a=shape[1], b=shape[2])
        return Tl(ap, name)

    def dram_in(self, name, shape, dt=F32):
        t = self.nc.dram_tensor(name, list(shape), dt, kind="ExternalInput")
        return Tl(t.ap(), name)

    def dram_out(self, name, shape, dt=F32):
        t = self.nc.dram_tensor(name, list(shape), dt, kind="ExternalOutput")
        return Tl(t.ap(), name)

    def dram_tmp(self, name, shape, dt=F32):
        t = self.nc.dram_tensor(name, list(shape), dt, kind="Internal")
        return Tl(t.ap(), name)

    def dma(self, out_ap, in_ap, reads, writes, q="sp", **kw):
        return self.fw.op(q, lambda e: e.dma_start(out=out_ap, in_=in_ap, **kw),
                          reads=[x.r for x in reads], writes=[x.r for x in writes], dma=True)

    def mm(self, out_ap, lhsT_ap, rhs_ap, reads, writes, start=True, stop=True, r32=False):
        if r32 and self.use_r32 and lhsT_ap.dtype == F32 and rhs_ap.dtype == F32:
            lhsT_ap = lhsT_ap.bitcast(mybir.dt.float32r)
            rhs_ap = rhs_ap.bitcast(mybir.dt.float32r)
        return self.fw.op("pe", lambda e: e.matmul(out_ap, lhsT_ap, rhs_ap, start=start, stop=stop),
                          reads=[x.r for x in reads], writes=[x.r for x in writes])

    def tr(self, out_ap, in_ap, ident_ap, reads, writes):
        return self.fw.op("pe", lambda e: e.transpose(out_ap, in_ap, ident_ap),
                          reads=[x.r for x in reads], writes=[x.r for x in writes])

    def act(self, out_ap, in_ap, func, reads, writes, **kw):
        return self.fw.op("act", lambda e: e.activation(out=out_ap, in_=in_ap, func=func, **kw),
                          reads=[x.r for x in reads], writes=[x.r for x in writes])

    def v(self, eng, meth, reads, writes, *a, **kw):
        return self.fw.op(eng, lambda e: getattr(e, meth)(*a, **kw),
                          reads=[x.r for x in reads], writes=[x.r for x in writes])

    def dump(self, name, tl, ap, shape, dt=F32):
        if name not in self.debug:
            return
        o = self.dram_out("dbg_" + name, shape, dt)
        k = self.dma(o.t, ap, [tl], [o])
        self.out_keys.append(k)
        self.dbg_out[name] = "dbg_" + name

    def finish(self):
        fw = self.fw
        rec = Rec(None, None)
        for k in self.out_keys:
            rec.deps.add(k)
        fw.ops["sp"].append(rec)
        fw.emit()
        self.st.close()

S0 = float(np.exp(-0.5))
RW_COLS = [(j * 64, 64) for j in range(28)] + [(1792, 128), (1920, 32)]
RW_TILES = len(RW_COLS)


def rwkv_load_w(kb, D_):
    wr = kb.sb([128, NCH, RW], BF16, "wr")
    for a in range(0, RW, 488):
        kb.dma(wr[:, :, a:a + 488], D_["w_in"].t[:, a:a + 488].rearrange("(c p) n -> p c n", p=128), [D_["w_in"]], [wr], q="pool")
    return wr


def rwkv_phase(kb, uT_d, ident, D_, ybwd_d, ya_d, wr):
    fw = kb.fw

    def ld(name, shape):
        t = kb.sb(shape, F32, name)
        src = D_[name]
        kb.dma(t.t, src.t, [src], [t])
        return t
    mu0 = ld("mu0", [128, RW_TILES]); mu1 = ld("mu1", [128, RW_TILES])
    muc = kb.sb([128, RW_TILES], F32, "muc")
    kb.v("dve", "tensor_tensor", [mu0, mu1], [muc], out=muc[:, :], in0=mu0[:, :], in1=mu1[:, :], op=ALU.add)
    kb.v("dve", "tensor_scalar", [muc], [muc], out=muc[:, :], in0=muc[:, :], scalar1=-1.0, scalar2=1.0, op0=ALU.mult, op1=ALU.add)
    w2 = ld("w2", [64, 2, 512]); a2 = ld("a2", [64, 2, 512])
    w0 = ld("w0", [64, 2, 8]); a0 = ld("a0", [64, 2, 8])
    g2a = ld("g2a", [128, 512]); g2b = ld("g2b", [32, 512])
    kkc = ld("kkc", [64, 8]); kac = ld("kac", [64, 8]); rkc = ld("rkc", [64, 8])
    lng = ld("lng", [64, 512]); lnb = ld("lnb", [64, 512])
    masks = ld("masks", [64, 4, 64])
    oka = kb.sb([64, 8], F32, "oka"); kah = kb.sb([64, 8], F32, "kah")
    kb.v("dve", "tensor_scalar", [kac], [oka], out=oka[:, :], in0=kac[:, :], scalar1=-1.0, scalar2=1.0, op0=ALU.mult, op1=ALU.add)
    kb.v("dve", "tensor_scalar", [kac], [kah], out=kah[:, :], in0=kac[:, :], scalar1=0.5, scalar2=None, op0=ALU.mult)
    ones = kb.sb([64, 64], F32, "ones")
    kb.v("dve", "memset", [], [ones], ones[:, :], 1.0)
    Hs = [kb.sb([64, 8, 64], F32, "Hf"), kb.sb([64, 8, 64], F32, "Hb")]

    def T(shape, name, dt=F32):
        return kb.sb(shape, dt, name)

    class Slot:
        pass

    def make_slot(si):
        S = Slot()
        S.B = kb.pbanks[4 * si:4 * si + 4]
        S.uc = T([128, NCH, 66], 'uc', BF16)
        S.sh = [T([128, 64], 'sh') for _ in range(4)]
        S.P = T([128, RW_TILES, 66], "P")
        S.Z = T([128, RW_TILES, 64], "Z")
        S.Pt = [Tl(S.P.t[:, j, :], f'P{j}') for j in range(RW_TILES)]
        S.Zt = [Tl(S.Z.t[:, j, :], f'Z{j}') for j in range(RW_TILES)]
        S.thw = T([64, 64], "thw")
        S.sg = T([64, 8, 64], "sg")
        S.A = [T([64, 8, 64], "A0"), T([64, 8, 64], "A1")]
        S.CS = T([64, 8, 65], "CS")
        kb.v("dve", "memset", [], [S.CS], S.CS[:, :, :], 0.0)
        S.G = T([64, 8, 64], "G")
        S.Gp = T([64, 8, 64], "Gp")
        S.Gi = T([64, 8, 64], "Gi")
        S.gC = T([64, 8], "gC")
        S.kkr = T([64, 8, 64], "kkr")
        S.ksq = T([64, 8, 64], "ksq")
        S.rn = T([64, 8, 64], "rn")
        S.X1 = S.ksq
        S.X2 = S.rn
        S.t1 = T([64, 8, 64], "t1")
        S.kdir = T([64, 8, 64], "kdir")
        S.kka = T([64, 8, 64], "kka")
        S.QR = T([64, 8, 128], "QR")
        S.KdT = T([64, 8, 64], "KdT")
        S.AdT = T([64, 8, 64], "AdT")
        S.Kq_t = T([64, 512], "Kq_t")
        S.Ad_t = T([64, 512], "Ad_t")
        S.Kd_t = T([64, 512], "Kd_t")
        S.V_t = T([64, 512], "V_t")
        S.NTs = [S.kkr, S.Gi]
        S.Ns = [S.ksq, S.Gp]
        S.MkT = S.G
        S.PaT = S.sg
        S.PkT = S.rn
        S.X = T([64, 8, 128], "X")
        S.AT = S.t1
        S.RqpT = S.kdir
        S.ybt = S.Kd_t
        S.yt = S.Ad_t
        S.ysq = S.Kq_t
        S.st = T([64, 48], "st")
        S.sgd = T([128, 2, 64], "sgd")
        S.ee = S.kka
        S.yout = S.ysq
        return S
    slots = [make_slot(0), make_slot(1)]

    RS, KS, VS = slice(0, 8), slice(8, 16), slice(16, 24)
    NSTAGE = 52

    def bc(ap, shape, axis):
        return ap.unsqueeze(axis).to_broadcast(shape)

    def v3(bank, p, a):
        return bank.t[0:p, 0:512].rearrange("p (a b) -> p a b", a=a)

    def chunk(S, c0, cn, d, need_out, H, epilogue):
        B = S.B
        P, Z, uc = S.P, S.Z, S.uc
        Pt, Zt = S.Pt, S.Zt
        ns = 0
        wdt = 24 + d
        tiles = list(range(24)) + [wdt] + ([26, 27, 28, 29] if epilogue == "final" else [26 + d])
        lo, hi = max(c0 - 1, 0), min(c0 + cn + 1, L)
        n = hi - lo
        off = lo - (c0 - 1)
        kb.dma(uc[:, :, 0:n], uT_d.t[:, :, lo:hi], [uT_d], [uc])
        if off or (hi - (c0 - 1)) < cn + 2:
            kb.v("pool", "memset", [], Pt, P[:, :, :], 0.0)
        groups = [tiles[i:i + 2] for i in range(0, len(tiles), 2)]

        def shift(grp):
            for j in grp:
                w = RW_COLS[j][1]
                kb.act(Z[0:w, j, 0:cn], P[0:w, j, 1:cn + 1], AF.Copy, [Pt[j], muc], [Zt[j]], scale=muc[0:w, j:j + 1])
                kb.v("dve", "scalar_tensor_tensor", [Pt[j], mu0, Zt[j]], [Zt[j]], out=Z[0:w, j, 0:cn], in0=P[0:w, j, 0:cn],
                     scalar=mu0[0:w, j:j + 1], in1=Z[0:w, j, 0:cn], op0=ALU.mult, op1=ALU.add)
                kb.v("dve", "scalar_tensor_tensor", [Pt[j], mu1, Zt[j]], [Zt[j]], out=Z[0:w, j, 0:cn], in0=P[0:w, j, 2:cn + 2],
                     scalar=mu1[0:w, j:j + 1], in1=Z[0:w, j, 0:cn], op0=ALU.mult, op1=ALU.add)

        prev = None
        for gi, grp in enumerate(groups):
            pb_ = B[gi % 4]
            pv = pb_.t[:, 0:2 * 66].rearrange("p (a b) -> p a b", a=2)
            for ji, j in enumerate(grp):
                cc, w = RW_COLS[j]
                for c in range(NCH):
                    kb.mm(pv[0:w, ji, 0:n], wr[:, c, cc:cc + w], uc[:, c, 0:n], [wr, uc], [pb_],
                          start=c == 0, stop=c == NCH - 1)
            j0, j1 = grp[0], grp[-1]
            if j1 < 28 and j1 - j0 == len(grp) - 1:
                kb.act(P[0:64, j0:j1 + 1, off:off + n], pv[0:64, 0:len(grp), 0:n], AF.Copy, [pb_], [Pt[jj] for jj in grp])
            else:
                for ji, j in enumerate(grp):
                    w = RW_COLS[j][1]
                    kb.act(P[0:w, j, off:off + n], pv[0:w, ji, 0:n], AF.Copy, [pb_], [Pt[j]])
            if prev is not None:
                shift(prev)
            prev = grp
            ns += 1; yield
        shift(prev)
        ns += 1; yield
        while ns < 17:
            ns += 1; yield
        thw, sg, A, CS = S.thw, S.sg, S.A, S.CS
        kb.act(thw[:, 0:cn], Z[0:64, wdt, 0:cn], AF.Tanh, [Zt[wdt]], [thw])
        pl = B[0]
        plv = v3(pl, 64, 8)
        for h in range(8):
            kb.mm(plv[:, h, 0:cn], w2[:, d, 64 * h:64 * h + 64], thw[:, 0:cn], [w2, thw], [pl], r32=True)
        for h in range(8):
            kb.act(sg[:, h, 0:cn], plv[:, h, 0:cn], AF.Sigmoid, [pl, w0], [sg], bias=w0[:, d, h:h + 1])
        ns += 1; yield
        for dd in ((0, 1) if epilogue == "final" else (d,)):
            pa = B[1 + dd]
            pav = v3(pa, 64, 8)
            for h in range(8):
                kb.mm(pav[:, h, 0:cn], a2[:, dd, 64 * h:64 * h + 64], Z[0:64, 26 + dd, 0:cn], [a2, Zt[26 + dd]], [pa], r32=True)
            for h in range(8):
                kb.act(A[dd][:, h, 0:cn], pav[:, h, 0:cn], AF.Sigmoid, [pa, a0], [A[dd]], bias=a0[:, dd, h:h + 1])
        ns += 1; yield
        Ad_ = A[d]
        sh = [64, 8, cn]
        kkr, ksq, rn, t1, kdir, kka = S.kkr, S.ksq, S.rn, S.t1, S.kdir, S.kka
        kb.v("pool", "tensor_tensor", Zt[8:16] + [kkc], [kkr], out=kkr[:, :, 0:cn], in0=Z[0:64, KS, 0:cn], in1=bc(kkc[:, :], sh, 2), op=ALU.mult)
        kb.act(ksq[:, :, 0:cn], kkr[:, :, 0:cn], AF.Square, [kkr], [ksq])
        pn = B[3]
        pnv = v3(pn, 64, 8)
        for h in range(8):
            kb.mm(pnv[:, h, 0:cn], ones[:, :], ksq[:, h, 0:cn], [ones, ksq], [pn], r32=True)
        kb.act(rn[:, :, 0:cn], pnv[:, :, 0:cn], AF.Ln, [pn], [rn], bias=1e-18, scale=1.0)
        kb.act(rn[:, :, 0:cn], rn[:, :, 0:cn], AF.Exp, [rn], [rn], scale=-0.5)
        kb.v("dve", "tensor_tensor", [kkr, rn], [kkr], out=kkr[:, :, 0:cn], in0=kkr[:, :, 0:cn], in1=rn[:, :, 0:cn], op=ALU.mult)
        ns += 1; yield
        kb.v("pool", "tensor_tensor", [Ad_, kac], [t1], out=t1[:, :, 0:cn], in0=Ad_[:, :, 0:cn], in1=bc(kac[:, :], sh, 2), op=ALU.mult)
        kb.v("pool", "tensor_tensor", [t1, oka], [t1], out=t1[:, :, 0:cn], in0=t1[:, :, 0:cn], in1=bc(oka[:, :], sh, 2), op=ALU.add)
        kb.v("pool", "tensor_tensor", [t1] + Zt[8:16], [kdir], out=kdir[:, :, 0:cn], in0=t1[:, :, 0:cn], in1=Z[0:64, KS, 0:cn], op=ALU.mult)
        kb.v("dve", "tensor_tensor", [kkr, Ad_], [kka], out=kka[:, :, 0:cn], in0=kkr[:, :, 0:cn], in1=Ad_[:, :, 0:cn], op=ALU.mult)
        ns += 1; yield
        for h in range(8):
            kb.v("dve", "tensor_tensor_scan", [ones, sg], [CS], out=CS[:, h, 1:cn + 1], data0=ones[:, 0:cn], data1=sg[:, h, 0:cn],
                 initial=0.0, op0=ALU.mult, op1=ALU.add)
        ns += 1; yield
        G, Gp, Gi, gC, X1, X2 = S.G, S.Gp, S.Gi, S.gC, S.X1, S.X2
        if d == 0:
            kb.act(G[:, :, 0:cn], CS[:, :, 1:cn + 1], AF.Exp, [CS], [G], scale=-S0)
            kb.act(Gp[:, :, 0:cn], CS[:, :, 0:cn], AF.Exp, [CS], [Gp], scale=-S0)
            kb.act(Gi[:, :, 0:cn], CS[:, :, 1:cn + 1], AF.Exp, [CS], [Gi], scale=S0)
        else:
            totb = bc(CS[:, :, cn], sh, 2)
            kb.v("dve", "tensor_tensor", [CS], [X1], out=X1[:, :, 0:cn], in0=CS[:, :, 0:cn], in1=totb, op=ALU.subtract)
            kb.v("dve", "tensor_tensor", [CS], [X2], out=X2[:, :, 0:cn], in0=CS[:, :, 1:cn + 1], in1=totb, op=ALU.subtract)
            kb.act(G[:, :, 0:cn], X1[:, :, 0:cn], AF.Exp, [X1], [G], scale=S0)
            kb.act(Gp[:, :, 0:cn], X2[:, :, 0:cn], AF.Exp, [X2], [Gp], scale=S0)
            kb.act(Gi[:, :, 0:cn], X1[:, :, 0:cn], AF.Exp, [X1], [Gi], scale=-S0)
        kb.act(gC[:, :], CS[:, :, cn], AF.Exp, [CS], [gC], scale=-S0)
        ns += 1; yield
        QR, KdT, AdT = S.QR, S.KdT, S.AdT
        kb.v("dve", "tensor_tensor", [kkr, Gp], [QR], out=QR[:, :, 0:cn], in0=kkr[:, :, 0:cn], in1=Gp[:, :, 0:cn], op=ALU.mult)
        kb.v("pool", "tensor_tensor", Zt[0:8] + [G], [QR], out=QR[:, :, cn:2 * cn], in0=Z[0:64, RS, 0:cn], in1=G[:, :, 0:cn], op=ALU.mult)
        kb.v("pool", "tensor_tensor", [kdir, Gi], [KdT], out=KdT[:, :, 0:cn], in0=kdir[:, :, 0:cn], in1=Gi[:, :, 0:cn], op=ALU.mult)
        kb.v("dve", "tensor_tensor", [kka, Gi], [AdT], out=AdT[:, :, 0:cn], in0=kka[:, :, 0:cn], in1=Gi[:, :, 0:cn], op=ALU.mult)
        ns += 1; yield
        Kq_t, Ad_t, Kd_t, V_t = S.Kq_t, S.Ad_t, S.Kd_t, S.V_t
        for bi, (src, si, dst) in enumerate(((QR, 0, Kq_t), (AdT, 0, Ad_t), (KdT, 0, Kd_t), (Z, 16, V_t))):
            pt = B[bi]
            for h in range(8):
                kb.tr(pt[0:cn, h * 64:(h + 1) * 64], src[0:64, si + h, 0:cn], ident[0:64, 0:64], [Zt[16 + h] if src is Z else src, ident], [pt])
            if bi % 2 == 0:
                kb.act(dst[0:cn, :], pt[0:cn, :], AF.Copy, [pt], [dst])
            else:
                kb.v("dve", "tensor_copy", [pt], [dst], out=dst[0:cn, :], in_=pt[0:cn, :])
                ns += 1; yield
        mS, mI, mN = (0, 1, 2) if d == 0 else (2, 3, 0)
        rw = 2 * cn if need_out else cn
        pSA = [B[0], B[1]]; pSK = [B[2], B[3]]
        NTs, Ns, MkT, PaT, PkT, X = S.NTs, S.Ns, S.MkT, S.PaT, S.PkT, S.X
        for h in range(8):
            hv = pSA[h // 4].t[0:cn, 0:4 * rw].rearrange("p (h s) -> p h s", h=4)
            kb.mm(hv[:, h % 4, :], AdT[:, h, 0:cn], QR[:, h, 0:rw], [AdT, QR], [pSA[h // 4]], r32=True)
            hv2 = pSK[h // 4].t[0:cn, 0:4 * rw].rearrange("p (h s) -> p h s", h=4)
            kb.mm(hv2[:, h % 4, :], KdT[:, h, 0:cn], QR[:, h, 0:rw], [KdT, QR], [pSK[h // 4]], r32=True)
        NT, Nm = NTs[0], Ns[0]
        for half in range(2):
            hv = pSA[half].t[0:cn, 0:4 * rw].rearrange("p (h s) -> p h s", h=4)
            hv2 = pSK[half].t[0:cn, 0:4 * rw].rearrange("p (h s) -> p h s", h=4)
            hs = slice(4 * half, 4 * half + 4)
            kb.v("dve", "tensor_tensor", [pSA[half], masks], [NT], out=NT[0:cn, hs, 0:cn], in0=hv[:, :, 0:cn],
                 in1=bc(masks[0:cn, mS, 0:cn], [cn, 4, cn], 1), op=ALU.mult)
            kb.v("dve", "tensor_tensor", [pSK[half], masks], [MkT], out=MkT[0:cn, hs, 0:cn], in0=hv2[:, :, 0:cn],
                 in1=bc(masks[0:cn, mS, 0:cn], [cn, 4, cn], 1), op=ALU.mult)
            if need_out:
                kb.v("dve", "tensor_tensor", [pSA[half], masks], [PaT], out=PaT[0:cn, hs, 0:cn], in0=hv[:, :, cn:2 * cn],
                     in1=bc(masks[0:cn, mI, 0:cn], [cn, 4, cn], 1), op=ALU.mult)
                kb.v("dve", "tensor_tensor", [pSK[half], masks], [PkT], out=PkT[0:cn, hs, 0:cn], in0=hv2[:, :, cn:2 * cn],
                     in1=bc(masks[0:cn, mI, 0:cn], [cn, 4, cn], 1), op=ALU.mult)
        ns += 1; yield
        pN = B[0]
        pNv = pN.t[0:cn, 0:8 * cn].rearrange("p (h s) -> p h s", h=8)
        pW = B[1]
        pWv = v3(pW, cn, 8)
        for h in range(8):
            kb.mm(pNv[:, h, :], QR[:, h, 0:cn], AdT[:, h, 0:cn], [QR, AdT], [pN], r32=True)
        for h in range(8):
            kb.mm(pWv[:, h, :], MkT[0:cn, h, 0:cn], V_t[0:cn, 64 * h:64 * h + 64], [MkT, V_t], [pW], r32=True)
        kb.v("dve", "tensor_tensor", [pN, masks], [Nm], out=Nm[0:cn, :, 0:cn], in0=pNv,
             in1=bc(masks[0:cn, mN, 0:cn], [cn, 8, cn], 1), op=ALU.mult)
        kb.v("pool", "tensor_copy", [Kq_t], [X], out=X[0:cn, :, 0:64], in_=Kq_t[0:cn, :].rearrange("p (h v) -> p h v", h=8))
        kb.act(X[0:cn, :, 64:128], pWv, AF.Copy, [pW], [X], scale=-1.0)
        ns += 1; yield
        nlev = 5 if cn > 32 else (4 if cn > 16 else 3)
        pF = [B[2], B[3]]

        def factor(PT_, sign):
            for h in range(8):
                fv = v3(pF[h // 4], cn, 4)
                kb.mm(fv[:, h % 4, :], PT_[0:cn, h, 0:cn], X[0:cn, h, :], [PT_, X], [pF[h // 4]], r32=True)
            for half in range(2):
                fv = v3(pF[half], cn, 4)
                hs = slice(4 * half, 4 * half + 4)
                kb.v("dve", "tensor_tensor", [X, pF[half]], [X], out=X[0:cn, hs, :], in0=X[0:cn, hs, :], in1=fv,
                     op=ALU.subtract if sign < 0 else ALU.add)
        factor(NT, -1)
        ns += 1; yield
        cur = 0
        for lvl in range(5):
            if lvl < nlev:
                NTc, Nc = NTs[cur], Ns[cur]
                NTn, Nn = NTs[1 - cur], Ns[1 - cur]
                pa_, pb2 = B[0], B[1]
                pav = pa_.t[0:cn, 0:8 * cn].rearrange("p (h s) -> p h s", h=8)
                pbv = pb2.t[0:cn, 0:8 * cn].rearrange("p (h s) -> p h s", h=8)
                last = lvl == nlev - 1
                for h in range(8):
                    kb.mm(pav[:, h, :], Nc[0:cn, h, 0:cn], NTc[0:cn, h, 0:cn], [Nc, NTc], [pa_], r32=True)
                    if not last:
                        kb.mm(pbv[:, h, :], NTc[0:cn, h, 0:cn], Nc[0:cn, h, 0:cn], [Nc, NTc], [pb2], r32=True)
                kb.act(NTn[0:cn, :, 0:cn], pav, AF.Copy, [pa_], [NTn])
                if not last:
                    kb.act(Nn[0:cn, :, 0:cn], pbv, AF.Copy, [pb2], [Nn])
            ns += 1; yield
            if lvl < nlev:
                factor(NTn, +1)
                cur = 1 - cur
            ns += 1; yield
        AT, RqpT = S.AT, S.RqpT
        pA = B[0]
        pAv = v3(pA, 64, 8)
        for h in range(8):
            kb.mm(pAv[:, h, :], X[0:cn, h, 0:64], Ad_t[0:cn, 64 * h:64 * h + 64], [X, Ad_t], [pA], r32=True)
        kb.v("dve", "tensor_tensor", [ident, pA], [AT], out=AT[:, :, :], in0=bc(ident[0:64, 0:64], [64, 8, 64], 1), in1=pAv, op=ALU.subtract)
        if need_out:
            pR = B[1]
            pRv = pR.t[0:64, 0:8 * cn].rearrange("p (a b) -> p a b", a=8)
            for h in range(8):
                kb.mm(pRv[:, h, :], X[0:cn, h, 0:64], PaT[0:cn, h, 0:cn], [X, PaT], [pR], r32=True)
            kb.v("dve", "tensor_tensor", [QR, pR], [RqpT], out=RqpT[:, :, 0:cn], in0=QR[:, :, cn:2 * cn], in1=pRv, op=ALU.subtract)
        ns += 1; yield
        if need_out:
            pY = B[2]
            pYv = v3(pY, cn, 8)
            for h in range(8):
                kb.mm(pYv[:, h, :], RqpT[:, h, 0:cn], H[:, h, :], [RqpT, H], [pY], start=True, stop=False, r32=True)
                kb.mm(pYv[:, h, :], PkT[0:cn, h, 0:cn], V_t[0:cn, 64 * h:64 * h + 64], [PkT, V_t], [pY], start=False, stop=False, r32=True)
                kb.mm(pYv[:, h, :], PaT[0:cn, h, 0:cn], X[0:cn, h, 64:128], [PaT, X], [pY], start=False, stop=True, r32=True)
        pH = B[3]
        pHv = v3(pH, 64, 8)
        for h in range(8):
            kb.mm(pHv[:, h, :], AT[:, h, :], H[:, h, :], [AT, H], [pH], start=True, stop=False, r32=True)
            kb.mm(pHv[:, h, :], Kd_t[0:cn, 64 * h:64 * h + 64], V_t[0:cn, 64 * h:64 * h + 64], [Kd_t, V_t], [pH], start=False, stop=False, r32=True)
            kb.mm(pHv[:, h, :], Ad_t[0:cn, 64 * h:64 * h + 64], X[0:cn, h, 64:128], [Ad_t, X], [pH], start=False, stop=True, r32=True)
        kb.v("dve", "tensor_tensor", [pH, gC], [H], out=H[:, :, :], in0=pHv, in1=bc(gC[:, :], [64, 8, 64], 2), op=ALU.mult)
        ns += 1; yield
        ybt, yt, ysq, st, sgd, ee, yout = S.ybt, S.yt, S.ysq, S.st, S.sgd, S.ee, S.yout
        if epilogue == "store":
            kb.act(yt[0:cn, :], pY[0:cn, :], AF.Copy, [pY], [yt])
            kb.dma(ybwd_d.t[c0:c0 + cn, :], yt[0:cn, :], [yt], [ybwd_d])
        elif epilogue == "final":
            kb.dma(ybt[0:cn, :], ybwd_d.t[c0:c0 + cn, :], [ybwd_d], [ybt])
            kb.v("dve", "tensor_tensor", [pY, ybt], [yt], out=yt[0:cn, :], in0=pY[0:cn, :], in1=ybt[0:cn, :], op=ALU.add)
            y3 = yt[0:cn, :].rearrange("p (h v) -> p h v", h=8)
            kb.v("dve", "tensor_reduce", [yt], [st], out=st[0:cn, 0:8], in_=y3, axis=AX.X, op=ALU.add)
            kb.act(ysq[0:cn, :], yt[0:cn, :], AF.Square, [yt], [ysq])
            kb.v("dve", "tensor_reduce", [ysq], [st], out=st[0:cn, 8:16], in_=ysq[0:cn, :].rearrange("p (h v) -> p h v", h=8), axis=AX.X, op=ALU.add)
            kb.v("dve", "tensor_scalar", [st], [st], out=st[0:cn, 16:24], in0=st[0:cn, 0:8], scalar1=1.0 / 64, scalar2=None, op0=ALU.mult)
            kb.v("dve", "tensor_tensor", [st], [st], out=st[0:cn, 24:32], in0=st[0:cn, 16:24], in1=st[0:cn, 16:24], op=ALU.mult)
            kb.v("dve", "scalar_tensor_tensor", [st], [st], out=st[0:cn, 32:40], in0=st[0:cn, 8:16], scalar=1.0 / 64, in1=st[0:cn, 24:32],
                 op0=ALU.mult, op1=ALU.subtract)
            kb.act(st[0:cn, 40:48], st[0:cn, 32:40], AF.Sqrt, [st], [st], bias=64e-5, scale=1.0)
            kb.v("dve", "reciprocal", [st], [st], out=st[0:cn, 40:48], in_=st[0:cn, 40:48])
            kb.v("dve", "tensor_tensor", [yt, st], [yt], out=y3, in0=y3, in1=bc(st[0:cn, 16:24], [cn, 8, 64], 2), op=ALU.subtract)
            kb.v("dve", "tensor_tensor", [yt, st], [yt], out=y3, in0=y3, in1=bc(st[0:cn, 40:48], [cn, 8, 64], 2), op=ALU.mult)
            kb.v("pool", "tensor_tensor", [yt, lng], [yt], out=yt[0:cn, :], in0=yt[0:cn, :], in1=lng[0:cn, :], op=ALU.mult)
            kb.v("pool", "tensor_tensor", [yt, lnb], [yt], out=yt[0:cn, :], in0=yt[0:cn, :], in1=lnb[0:cn, :], op=ALU.add)
        ns += 1; yield
        if epilogue == "final":
            kb.v("pool", "tensor_tensor", [A[0], A[1]], [ee], out=ee[:, :, 0:cn], in0=A[0][:, :, 0:cn], in1=A[1][:, :, 0:cn], op=ALU.add)
            kb.v("pool", "tensor_tensor", [ee, kah], [ee], out=ee[:, :, 0:cn], in0=ee[:, :, 0:cn], in1=bc(kah[:, :], sh, 2), op=ALU.mult)
            kb.v("pool", "tensor_tensor", [ee, oka], [ee], out=ee[:, :, 0:cn], in0=ee[:, :, 0:cn], in1=bc(oka[:, :], sh, 2), op=ALU.add)
            kb.v("pool", "tensor_tensor", [ee] + Zt[8:16], [ee], out=ee[:, :, 0:cn], in0=ee[:, :, 0:cn], in1=Z[0:64, KS, 0:cn], op=ALU.mult)
            kb.v("pool", "tensor_tensor", [ee] + Zt[0:8], [ee], out=ee[:, :, 0:cn], in0=ee[:, :, 0:cn], in1=Z[0:64, RS, 0:cn], op=ALU.mult)
            kb.v("pool", "tensor_tensor", [ee, rkc], [ee], out=ee[:, :, 0:cn], in0=ee[:, :, 0:cn], in1=bc(rkc[:, :], sh, 2), op=ALU.mult)
            pS = B[0]
            for h in range(8):
                kb.mm(pS[0:cn, h:h + 1], ee[:, h, 0:cn], ones[:, 0:1], [ee, ones], [pS])
            kb.act(st[0:cn, 0:8], pS[0:cn, 0:8], AF.Copy, [pS], [st])
            kb.v("dve", "tensor_tensor", [V_t, st], [ysq], out=ysq[0:cn, :].rearrange("p (h v) -> p h v", h=8),
                 in0=V_t[0:cn, :].rearrange("p (h v) -> p h v", h=8), in1=bc(st[0:cn, 0:8], [cn, 8, 64], 2), op=ALU.mult)
            kb.v("dve", "tensor_tensor", [yt, ysq], [yt], out=yt[0:cn, :], in0=yt[0:cn, :], in1=ysq[0:cn, :], op=ALU.add)
            kb.act(sgd[:, 0, 0:cn], Z[:, 28, 0:cn], AF.Sigmoid, [Zt[28]], [sgd])
            kb.act(sgd[0:32, 1, 0:cn], Z[0:32, 29, 0:cn], AF.Sigmoid, [Zt[29]], [sgd])
            pG = B[1]
            kb.mm(pG[0:cn, :], sgd[:, 0, 0:cn], g2a[:, :], [sgd, g2a], [pG], start=True, stop=False, r32=True)
            kb.mm(pG[0:cn, :], sgd[0:32, 1, 0:cn], g2b[:, :], [sgd, g2b], [pG], start=False, stop=True, r32=True)
            kb.v("dve", "tensor_tensor", [yt, pG], [yout], out=yout[0:cn, :], in0=yt[0:cn, :], in1=pG[0:cn, :], op=ALU.mult)
            kb.dma(ya_d.t[c0:c0 + cn, :], yout[0:cn, :], [yout], [ya_d])
        ns += 1; yield
        while ns < NSTAGE:
            ns += 1; yield
        assert ns == NSTAGE, ns

    def chain(S, lst, d, H, epilogue):
        for (c0, cn) in lst:
            yield from chunk(S, c0, cn, d, epilogue is not None, H, epilogue)

    chunks = tiles_of(2048, 64) + [(2048, 16)] + [(2064 + a, b) for a, b in tiles_of(2048, 64)]
    own = [c for c in chunks if c[0] < OWN]
    oth = [c for c in chunks if c[0] >= OWN]
    Hf, Hb = Hs
    kb.v("dve", "memset", [], [Hf], Hf[:, :, :], 0.0)
    kb.v("dve", "memset", [], [Hb], Hb[:, :, :], 0.0)
    g0 = chain(slots[0], own, 0, Hf, "store")
    g1 = chain(slots[1], list(reversed(oth)), 1, Hb, None)
    step = 0
    d0 = d1 = False
    while not (d0 and d1):
        if not d0:
            try:
                next(g0)
            except StopIteration:
                d0 = True
        if step >= NSTAGE // 2 and not d1:
            try:
                next(g1)
            except StopIteration:
                d1 = True
        step += 1
    rown = list(reversed(own))
    g0 = chain(slots[0], rown[0::2], 1, Hb, "final")
    g1 = chain(slots[1], rown[1::2], 1, Hb, "final")
    step = 0
    d0 = d1 = False
    while not (d0 and d1):
        if not d0:
            try:
                next(g0)
            except StopIteration:
                d0 = True
        if step >= NSTAGE // 2 and not d1:
            try:
                next(g1)
            except StopIteration:
                d1 = True
        step += 1


def wload(kb, dram, rows, c0, c1, name):
    kch = rows // 128
    t = kb.sb([128, kch, c1 - c0], BF16, name)
    step = 1024
    for a in range(c0, c1, step):
        b = min(a + step, c1)
        kb.dma(t[:, :, a - c0:b - c0], dram.t[:, a:b].rearrange("(c p) n -> p c n", p=128), [dram], [t], q="pool")
    return t


def merge_phase(kb, uT_d, identb, D_, ya_d, yb_d, h2_d):
    B = kb.pbanks
    Wa = wload(kb, D_["w_a"], 512, 0, 1024, "Wa")
    Wb = wload(kb, D_["w_b"], 512, 0, 1024, "Wb")
    Wo = wload(kb, D_["w_o"], 1024, 0, 1024, "Wo")
    Wg = wload(kb, D_["w_in"], 1024, 2720, 4768, "Wg")

    class Set:
        pass

    def mk(si):
        S = Set()
        S.B = B[4 * si:4 * si + 4]
        S.yaf = kb.sb([128, 512], F32, "yaf"); S.yab = kb.sb([128, 512], BF16, "yab"); S.ybb = kb.sb([128, 512], BF16, "ybb")
        S.yaT = kb.sb([128, 4, 128], BF16, "yaT"); S.ybT = kb.sb([128, 4, 128], BF16, "ybT")
        S.sgA = kb.sb([128, 1024], F32, "sgA"); S.sgB = kb.sb([128, 1024], F32, "sgB")
        S.mg = kb.sb([128, 1024], BF16, "mg"); S.mT = kb.sb([128, 8, 128], BF16, "mT")
        S.xt = kb.sb([128, 1024], F32, "xt2"); S.h2 = kb.sb([128, 1024], F32, "h2")
        S.ut = kb.sb([128, NCH, 128], BF16, "utm")
        return S
    sets = [mk(0), mk(1)]

    def tile(S, i, t0, n):
        B_ = S.B
        yaf, yab, ybb, yaT, ybT, sgA, sgB, mg, mT, xt, h2, ut = (S.yaf, S.yab, S.ybb, S.yaT, S.ybT, S.sgA, S.sgB, S.mg, S.mT,
                                                                  S.xt, S.h2, S.ut)
        kb.dma(ut[:, :, 0:n], uT_d.t[:, :, t0:t0 + n], [uT_d], [ut])
        kb.dma(yaf[0:n, :], ya_d.t[t0:t0 + n, :], [ya_d], [yaf])
        kb.dma(ybb[0:n, :], yb_d.t[i, 0:n, :], [yb_d], [ybb])
        kb.dma(xt[0:n, :], D_["h"].t[t0:t0 + n, :], [D_["h"]], [xt])
        yield
        for gi, sg_ in enumerate((sgA, sgB)):
            for hb in range(2):
                bk = B_[hb]
                col = gi * 1024 + hb * 512
                for c in range(NCH):
                    kb.mm(bk[0:n, :], ut[:, c, 0:n], Wg[:, c, col:col + 512], [ut, Wg], [bk], start=c == 0, stop=c == NCH - 1)
                kb.act(sg_[0:n, hb * 512:(hb + 1) * 512], bk[0:n, :], AF.Sigmoid, [bk], [sg_])
            yield
        kb.v("dve", "tensor_copy", [yaf], [yab], out=yab[0:n, :], in_=yaf[0:n, :])
        for src, dst, bk in ((yab, yaT, B_[2]), (ybb, ybT, B_[3])):
            pv = bk.t.bitcast(BF16).rearrange("p (c t) -> p c t", c=8)
            for k in range(4):
                kb.tr(pv[:, k, 0:n], src[0:n, k * 128:(k + 1) * 128], identb[0:n, 0:n], [src, identb], [bk])
            kb.act(dst[:, :, 0:n], pv[:, 0:4, 0:n], AF.Copy, [bk], [dst])
        yield
        for (yT_, W_, sg_) in ((yaT, Wa, sgA), (ybT, Wb, sgB)):
            for hb in range(2):
                bk = B_[hb]
                for k in range(4):
                    kb.mm(bk[0:n, :], yT_[:, k, 0:n], W_[:, k, hb * 512:(hb + 1) * 512], [yT_, W_], [bk], start=k == 0, stop=k == 3)
                kb.v("dve", "tensor_tensor", [sg_, bk], [sg_], out=sg_[0:n, hb * 512:(hb + 1) * 512],
                     in0=sg_[0:n, hb * 512:(hb + 1) * 512], in1=bk[0:n, :], op=ALU.mult)
            yield
        kb.v("pool", "tensor_tensor", [sgA, sgB], [mg], out=mg[0:n, :], in0=sgA[0:n, :], in1=sgB[0:n, :], op=ALU.add)
        bk = B_[2]
        pv = bk.t.bitcast(BF16).rearrange("p (c t) -> p c t", c=8)
        for k in range(8):
            kb.tr(pv[:, k, 0:n], mg[0:n, k * 128:(k + 1) * 128], identb[0:n, 0:n], [mg, identb], [bk])
        kb.act(mT[:, :, 0:n], pv[:, :, 0:n], AF.Copy, [bk], [mT])
        yield
        for hb in range(2):
            bk = B_[hb]
            for k in range(8):
                kb.mm(bk[0:n, :], mT[:, k, 0:n], Wo[:, k, hb * 512:(hb + 1) * 512], [mT, Wo], [bk], start=k == 0, stop=k == 7)
            kb.v("dve", "tensor_tensor", [xt, bk], [h2], out=h2[0:n, hb * 512:(hb + 1) * 512], in0=xt[0:n, hb * 512:(hb + 1) * 512],
                 in1=bk[0:n, :], op=ALU.add)
        kb.dma(h2_d.t[t0:t0 + n, :], h2[0:n, :], [h2], [h2_d])
        yield

    def chain(S, lst):
        for (i, t0, n) in lst:
            yield from tile(S, i, t0, n)
    tl = [(i, t0, n) for i, (t0, n) in enumerate(tiles_of(OWN, 128))]
    g0, g1 = chain(sets[0], tl[0::2]), chain(sets[1], tl[1::2])
    step, d0, d1 = 0, False, False
    while not (d0 and d1):
        if not d0:
            try:
                next(g0)
            except StopIteration:
                d0 = True
        if step >= 4 and not d1:
            try:
                next(g1)
            except StopIteration:
                d1 = True
        step += 1


def ffn_phase(kb, identb, D_, h2_d, out_d):
    B = kb.pbanks
    GT = 256
    W1 = wload(kb, D_["w_ff1"], 1024, 0, 4096, "W1")
    W2 = wload(kb, D_["w_ff2"], 4096, 0, 1024, "W2")
    gf = kb.sb([128, NCH], F32, "gffn"); gfin = kb.sb([128, 1024], F32, "gfin")
    kb.dma(gf.t, D_["gffn"].t, [D_["gffn"]], [gf])
    kb.dma(gfin.t, D_["gfin"].t, [D_["gfin"]], [gfin])
    h2g = [kb.sb([128, 1024], F32, "h2f") for _ in range(2)]
    xs = kb.sb([128, 1024], BF16, "xs2")
    ss = kb.sb([128, 8], F32, "ss2"); nT = kb.sb([128, 8, GT], BF16, "nT")
    frs = [kb.sb([128, GT], F32, "fr") for _ in range(2)]
    f2T = kb.sb([128, 32, GT], BF16, "f2T")
    h3 = kb.sb([128, 1024], F32, "h3"); ot = kb.sb([128, 1024], F32, "ot")
    junk = ot
    for gi, (g0, gn) in enumerate(tiles_of(OWN, GT)):
        subs = tiles_of(gn, 128)
        for si, (o0, n) in enumerate(subs):
            t0 = g0 + o0
            h2 = h2g[si]
            kb.dma(h2[0:n, :], h2_d.t[t0:t0 + n, :], [h2_d], [h2])
            kb.act(junk[0:n, :], h2[0:n, :], AF.Square, [h2], [junk, ss], accum_out=ss[0:n, 0:1])
            kb.act(ss[0:n, 1:2], ss[0:n, 0:1], AF.Sqrt, [ss], [ss], scale=1.0 / D, bias=EPS)
            kb.v("dve", "reciprocal", [ss], [ss], out=ss[0:n, 2:3], in_=ss[0:n, 1:2])
            kb.v("dve", "tensor_scalar", [h2, ss], [xs], out=xs[0:n, :], in0=h2[0:n, :], scalar1=ss[0:n, 2:3], scalar2=None, op0=ALU.mult)
            bk = B[si]
            pv = bk.t.bitcast(BF16).rearrange("p (c t) -> p c t", c=8)
            for c in range(NCH):
                kb.tr(pv[:, c, 0:n], xs[0:n, c * 128:(c + 1) * 128], identb[0:n, 0:n], [xs, identb], [bk])
            kb.v("dve", "tensor_tensor", [bk, gf], [nT], out=nT[:, :, o0:o0 + n], in0=pv[:, :, 0:n],
                 in1=gf[:, :].unsqueeze(2).to_broadcast([128, NCH, n]), op=ALU.mult)
        for ft in range(32):
            bk = B[2 + ft % 3]
            fr = frs[ft % 2]
            for c in range(NCH):
                kb.mm(bk[:, 0:gn], W1[:, c, ft * 128:(ft + 1) * 128], nT[:, c, 0:gn], [W1, nT], [bk], start=c == 0, stop=c == NCH - 1)
            kb.act(fr[:, 0:gn], bk[:, 0:gn], AF.Relu, [bk], [fr])
            kb.v("dve", "tensor_tensor", [fr], [f2T], out=f2T[:, ft, 0:gn], in0=fr[:, 0:gn], in1=fr[:, 0:gn], op=ALU.mult)
        for si, (o0, n) in enumerate(subs):
            t0 = g0 + o0
            h2 = h2g[si]
            for hb in range(2):
                bk = B[5 + (hb + 2 * si) % 3]
                for k in range(32):
                    kb.mm(bk[0:n, :], f2T[:, k, o0:o0 + n], W2[:, k, hb * 512:(hb + 1) * 512], [f2T, W2], [bk], start=k == 0, stop=k == 31)
                kb.v("dve", "tensor_tensor", [h2, bk], [h3], out=h3[0:n, hb * 512:(hb + 1) * 512], in0=h2[0:n, hb * 512:(hb + 1) * 512],
                     in1=bk[0:n, :], op=ALU.add)
            kb.act(junk[0:n, :], h3[0:n, :], AF.Square, [h3], [junk, ss], accum_out=ss[0:n, 4:5])
            kb.act(ss[0:n, 5:6], ss[0:n, 4:5], AF.Sqrt, [ss], [ss], scale=1.0 / D, bias=EPS)
            kb.v("dve", "reciprocal", [ss], [ss], out=ss[0:n, 6:7], in_=ss[0:n, 5:6])
            kb.v("dve", "scalar_tensor_tensor", [h3, ss, gfin], [ot], out=ot[0:n, :], in0=h3[0:n, :], scalar=ss[0:n, 6:7], in1=gfin[0:n, :],
                 op0=ALU.mult, op1=ALU.mult)
            k = kb.dma(out_d.t[t0:t0 + n, :], ot[0:n, :], [ot], [])
            kb.out_keys.append(k)

def tiles_of(n, step):
    out = []
    t = 0
    while t < n:
        out.append((t, min(step, n - t)))
        t += step
    return out


def build(debug=None, upto="all"):
    nc = bass.Bass("TRN2", target_bir_lowering=False)
    kb = KB(nc, debug)
    kb.setup_mem()
    fw = kb.fw
    h_d = kb.dram_in("h", [L, D])
    gmix_d = kb.dram_in("gmix", [128, NCH])
    ident_d = kb.dram_in("ident", [128, 128])
    out_d = kb.dram_out("out", [OWN, D])

    ident = kb.sb([128, 128], F32, "ident")
    identb = kb.sb([128, 128], BF16, "identb")
    gmix = kb.sb([128, NCH], F32, "gmix")
    kb.dma(ident[:, :], ident_d[:, :], [ident_d], [ident])
    kb.dma(gmix[:, :], gmix_d[:, :], [gmix_d], [gmix])
    kb.v("dve", "tensor_copy", [ident], [identb], out=identb[:, :], in_=ident[:, :])

    win_d = kb.dram_in("w_in", [D, INW])
    kb.from_top = True
    wr_pre = rwkv_load_w(kb, {"w_in": win_d})
    kb.from_top = False
    mA = kb.mark()
    uT_d = kb.dram_tmp("uT_scr", [128, NCH, L], BF16)
    xts = [kb.sb([128, D], F32, "xt") for _ in range(2)]
    xss = [kb.sb([128, D], BF16, "xs") for _ in range(2)]
    junk = kb.sb([128, D], BF16, "junk")
    sss = [kb.sb([128, 4], F32, "ss") for _ in range(2)]
    uts = [kb.sb([128, NCH, 128], BF16, "ut") for _ in range(2)]

    def phase_a_tile(i, t0, n):
        xt = xts[i % 2]
        xs = xss[i % 2]
        ss = sss[i % 2]
        ut = uts[i % 2]
        kb.dma(xt[0:n, :], h_d[t0:t0 + n, :], [h_d], [xt])
        kb.act(junk[0:n, :], xt[0:n, :], AF.Square, [xt], [junk, ss], accum_out=ss[0:n, 0:1])
        kb.act(ss[0:n, 1:2], ss[0:n, 0:1], AF.Sqrt, [ss], [ss], scale=1.0 / D, bias=EPS)
        kb.v("dve", "reciprocal", [ss], [ss], out=ss[0:n, 2:3], in_=ss[0:n, 1:2])
        kb.v("dve", "tensor_scalar", [xt, ss], [xs], out=xs[0:n, :], in0=xt[0:n, :],
             scalar1=ss[0:n, 2:3], scalar2=None, op0=ALU.mult)
        pb = kb.bank()
        pv = pb.t.bitcast(BF16).rearrange("p (c t) -> p c t", c=NCH)
        for c in range(NCH):
            kb.tr(pv[:, c, 0:n], xs[0:n, c * 128:(c + 1) * 128], identb[0:n, 0:n], [xs, identb], [pb])
        kb.v("dve", "tensor_tensor", [pb, gmix], [ut], out=ut[:, :, 0:n], in0=pv[:, :, 0:n],
             in1=gmix[:, :].unsqueeze(2).to_broadcast([128, NCH, n]), op=ALU.mult)
        kb.dma(uT_d.t[:, :, t0:t0 + n], ut[:, :, 0:n], [ut], [uT_d])
        return ut

    ropeC_d = kb.dram_in("ropeC", [L, 64])
    ropeS_d = kb.dram_in("ropeS", [L, 64])
    qg_d = kb.dram_in("qg", [128, 64])
    kg_d = kb.dram_in("kg", [128, 64])
    NT = 33
    NQT = 17
    mB0 = kb.mark()
    y_b = kb.sb([128, NQT, 512], BF16, "y_b")
    mB = kb.mark()
    wqkv = kb.sb([128, NCH, 768], BF16, "wqkv")
    kb.dma(wqkv[:, :, :], win_d.t[:, 1952:2720].rearrange("(c p) n -> p c n", p=128), [win_d], [wqkv], q="pool")
    rcs = [kb.sb([128, 64], F32, "rc") for _ in range(2)]
    rss = [kb.sb([128, 64], F32, "rs") for _ in range(2)]
    qgb = kb.sb([128, 64], F32, "qgb")
    kgb = kb.sb([128, 64], F32, "kgb")
    kb.dma(qgb[:, :], qg_d[:, :], [qg_d], [qgb])
    kb.dma(kgb[:, :], kg_d[:, :], [kg_d], [kgb])
    QT = kb.sb([64, 8, OWN], BF16, "QT")
    KT = kb.sb([64, 2, L], BF16, "KT")
    V1 = kb.sb([128, NT, 2, 65], BF16, "V1")
    kb.v("pool", "memset", [], [V1], V1[:, :, :, 64:65], 1.0)
    sqs = [kb.sb([128, 640], F32, "sq") for _ in range(1)]
    qns = [kb.sb([128, 640], F32, "qn") for _ in range(1)]
    t1s = [kb.sb([128, 640], F32, "t1") for _ in range(1)]
    t2s = [kb.sb([128, 640], F32, "t2") for _ in range(1)]
    qrs = [kb.sb([128, 640], BF16, "qr") for _ in range(2)]
    sms = [kb.sb([128, 32], F32, "sm") for _ in range(2)]
    for i, (t0, n) in enumerate(tiles_of(L, 128)):
        nq = max(0, min(n, OWN - t0))
        ut = phase_a_tile(i, t0, n)
        sq, qn, t1, t2, qr, sm = sqs[0], qns[0], t1s[0], t2s[0], qrs[i % 2], sms[i % 2]
        pq = kb.bank()
        pkv = kb.bank()
        rc, rs = rcs[i % 2], rss[i % 2]
        kb.dma(rc[0:n, :], ropeC_d.t[t0:t0 + n, :], [ropeC_d], [rc])
        kb.dma(rs[0:n, :], ropeS_d.t[t0:t0 + n, :], [ropeS_d], [rs])
        for c in range(NCH):
            if nq:
                kb.mm(pq[0:n, 0:512], ut[:, c, 0:n], wqkv[:, c, 0:512], [ut, wqkv], [pq], start=c == 0, stop=c == NCH - 1)
            kb.mm(pkv[0:n, 0:256], ut[:, c, 0:n], wqkv[:, c, 512:768], [ut, wqkv], [pkv], start=c == 0, stop=c == NCH - 1)
        h0 = 0 if nq else 8
        c0 = h0 * 64
        nh = 10 - h0
        if nq:
            kb.act(sq[0:n, 0:512], pq[0:n, 0:512], AF.Square, [pq], [sq])
        kb.act(sq[0:n, 512:640], pkv[0:n, 0:128], AF.Square, [pkv], [sq])
        kb.v("dve", "tensor_reduce", [sq], [sm], out=sm[0:n, h0:10],
             in_=sq[0:n, c0:640].rearrange("p (h c) -> p h c", c=64), axis=AX.X, op=ALU.add)
        kb.act(sm[0:n, 10 + h0:20], sm[0:n, h0:10], AF.Sqrt, [sm], [sm], scale=1.0 / 64, bias=EPS)
        kb.v("dve", "reciprocal", [sm], [sm], out=sm[0:n, 20 + h0:30], in_=sm[0:n, 10 + h0:20])
        if nq:
            kb.v("dve", "tensor_tensor", [pq, sm], [qn], out=qn[0:n, 0:512].rearrange("p (h c) -> p h c", c=64),
                 in0=pq[0:n, 0:512].rearrange("p (h c) -> p h c", c=64),
                 in1=sm[0:n, 20:28].unsqueeze(2).to_broadcast([n, 8, 64]), op=ALU.mult)
            kb.v("dve", "tensor_tensor", [qn, qgb], [qn], out=qn[0:n, 0:512].rearrange("p (h c) -> p h c", c=64),
                 in0=qn[0:n, 0:512].rearrange("p (h c) -> p h c", c=64),
                 in1=qgb[0:n, :].unsqueeze(1).to_broadcast([n, 8, 64]), op=ALU.mult)
        kb.v("dve", "tensor_tensor", [pkv, sm], [qn], out=qn[0:n, 512:640].rearrange("p (h c) -> p h c", c=64),
             in0=pkv[0:n, 0:128].rearrange("p (h c) -> p h c", c=64),
             in1=sm[0:n, 28:30].unsqueeze(2).to_broadcast([n, 2, 64]), op=ALU.mult)
        kb.v("dve", "tensor_tensor", [qn, kgb], [qn], out=qn[0:n, 512:640].rearrange("p (h c) -> p h c", c=64),
             in0=qn[0:n, 512:640].rearrange("p (h c) -> p h c", c=64),
             in1=kgb[0:n, :].unsqueeze(1).to_broadcast([n, 2, 64]), op=ALU.mult)
        kb.v("dve", "tensor_tensor", [qn, rc], [t1], out=t1[0:n, c0:640].rearrange("p (h c) -> p h c", c=64),
             in0=qn[0:n, c0:640].rearrange("p (h c) -> p h c", c=64),
             in1=rc[0:n, :].unsqueeze(1).to_broadcast([n, nh, 64]), op=ALU.mult)
        for hf in range(2):
            o_v = t2[0:n, c0:640].rearrange("p (h a x f) -> p h a x f", a=2, x=2, f=16)[:, :, :, hf, :]
            i_v = qn[0:n, c0:640].rearrange("p (h a x f) -> p h a x f", a=2, x=2, f=16)[:, :, :, 1 - hf, :]
            s_v = rs[0:n, :].rearrange("p (a x f) -> p a x f", a=2, x=2)[:, :, hf, :].unsqueeze(1).to_broadcast([n, nh, 2, 16])
            kb.v("dve", "tensor_tensor", [qn, rs], [t2], out=o_v, in0=i_v, in1=s_v, op=ALU.mult)
        kb.v("dve", "tensor_tensor", [t1, t2], [qr], out=qr[0:n, c0:640], in0=t1[0:n, c0:640], in1=t2[0:n, c0:640], op=ALU.add)
        ptk = kb.bank()
        ptkv = ptk.t.bitcast(BF16).rearrange("p (c t) -> p c t", c=8)
        for g in range(2):
            kb.tr(ptkv[0:64, g, 0:n], qr[0:n, 512 + g * 64:512 + (g + 1) * 64], identb[0:n, 0:n], [qr, identb], [ptk])
        kb.v("dve", "tensor_copy", [ptk], [KT], out=KT[:, :, t0:t0 + n], in_=ptkv[0:64, 0:2, 0:n])
        if nq:
            ptq = kb.bank()
            ptqv = ptq.t.bitcast(BF16).rearrange("p (c t) -> p c t", c=8)
            for hh in range(8):
                kb.tr(ptqv[0:64, hh, 0:n], qr[0:n, hh * 64:(hh + 1) * 64], identb[0:n, 0:n], [qr, identb], [ptq])
            kb.act(QT[:, :, t0:t0 + nq], ptqv[0:64, :, 0:nq], AF.Copy, [ptq], [QT])
        kb.act(V1[0:n, i, :, 0:64], pkv[0:n, 128:256].rearrange("p (g c) -> p g c", c=64), AF.Copy, [pkv], [V1])
    fw.barrier()
    Es = [kb.sb([128, 4, 128], BF16, "E") for _ in range(3)]
    rcp = kb.sb([128, 8], F32, "rcp")
    Ob = kb.pbanks[0:4]
    Sb = kb.pbanks[4:6]
    ktiles = tiles_of(L, 128)
    steps = []
    for qi, (q0, nq) in enumerate(tiles_of(OWN, 128)):
        for g in range(2):
            for kt, (k0, nk) in enumerate(ktiles):
                steps.append((qi, q0, nq, g, kt, k0, nk))
    NS_ = len(steps)
    pend = {}

    def issue_s(i):
        qi, q0, nq, g, kt, k0, nk = steps[i]
        sT = Sb[i % 2]
        E = Es[i % 3]
        sTv = sT.t[0:nk, 0:4 * nq].rearrange("p (h q) -> p h q", h=4)
        kb.mm(sTv, KT[0:64, g, k0:k0 + nk], QT[0:64, 4 * g:4 * g + 4, q0:q0 + nq], [KT, QT], [sT])
        kb.act(E[0:nk, :, 0:nq], sTv, AF.Exp, [sT], [E], scale=0.125)

    def issue_pv(i):
        qi, q0, nq, g, kt, k0, nk = steps[i]
        E = Es[i % 3]
        for hh in range(4):
            kb.mm(Ob[hh][0:nq, 0:65], E[0:nk, hh, 0:nq], V1[0:nk, kt, g, :], [E, V1], [Ob[hh]],
                  start=kt == 0, stop=kt == len(ktiles) - 1)
        if kt == len(ktiles) - 1:
            for hh in range(4):
                kb.v("dve", "reciprocal", [Ob[hh]], [rcp], out=rcp[0:nq, hh:hh + 1], in_=Ob[hh][0:nq, 64:65])
                hd = 4 * g + hh
                kb.v("dve", "tensor_scalar", [Ob[hh], rcp], [y_b], out=y_b[0:nq, qi, hd * 64:(hd + 1) * 64],
                     in0=Ob[hh][0:nq, 0:64], scalar1=rcp[0:nq, hh:hh + 1], scalar2=None, op0=ALU.mult)

    issue_s(0)
    for i in range(NS_):
        if i + 1 < NS_:
            issue_s(i + 1)
        issue_pv(i)
    kb.dump("y_b", y_b, y_b[:, :, :], [128, NQT, 512], BF16)
    yb_d = kb.dram_tmp("yb_scr", [NQT, 128, 512], BF16)
    kb.dma(yb_d.t[0:16].rearrange("a p c -> p a c"), y_b[:, 0:16, :], [y_b], [yb_d])
    kb.dma(yb_d.t[16, 0:16, :], y_b[0:16, 16, :], [y_b], [yb_d])
    fw.barrier()
    kb.release(mA)
    if upto == "B":
        kb.finish()
        return nc, kb
    names = {"mu0": [128, RW_TILES], "mu1": [128, RW_TILES], "w2": [64, 2, 512], "a2": [64, 2, 512], "w0": [64, 2, 8],
             "a0": [64, 2, 8], "g2a": [128, 512], "g2b": [32, 512], "kkc": [64, 8], "kac": [64, 8], "rkc": [64, 8],
             "lng": [64, 512], "lnb": [64, 512],
             "masks": [64, 4, 64], "w_a": [512, 1024], "w_b": [512, 1024], "w_o": [1024, 1024], "w_ff1": [1024, 4096],
             "w_ff2": [4096, 1024], "gffn": [128, NCH], "gfin": [128, 1024]}
    D_ = {k: kb.dram_in(k, v) for k, v in names.items()}
    D_["w_in"] = win_d
    D_["h"] = h_d
    ybwd_d = kb.dram_tmp("ybwd_scr", [OWN, 512])
    ya_d = kb.dram_tmp("ya_scr", [OWN, 512])
    h2_d = kb.dram_tmp("h2_scr", [OWN, D])
    mC = kb.mark()
    rwkv_phase(kb, uT_d, ident, D_, ybwd_d, ya_d, wr_pre)
    if "ya" in kb.debug:
        o = kb.dram_out("dbg_ya", [OWN, 512])
        kb.out_keys.append(kb.dma(o.t, ya_d.t, [ya_d], [o]))
    fw.barrier()
    kb.release(mC)
    kb.hi = kb.AW
    if upto == "C":
        kb.finish()
        return nc, kb
    merge_phase(kb, uT_d, identb, D_, ya_d, yb_d, h2_d)
    if "h2" in kb.debug:
        o = kb.dram_out("dbg_h2", [OWN, D])
        kb.out_keys.append(kb.dma(o.t, h2_d.t, [h2_d], [o]))
    fw.barrier()
    kb.release(mA)
    ffn_phase(kb, identb, D_, h2_d, out_d)

    kb.finish()
    return nc, kb


def _rope_tables():
    inv = (10000.0 ** (-np.arange(16, dtype=np.float32) * 2.0 / 32)).astype(np.float32)
    rows = (np.arange(64, dtype=np.float32)[:, None] * inv).astype(np.float32)
    cols = (np.arange(64, dtype=np.float32)[:, None] * inv).astype(np.float32)
    ang = np.zeros((L, 2, 16), np.float32)
    grid = np.stack([np.broadcast_to(rows[:, None, :], (64, 64, 16)),
                     np.broadcast_to(cols[None, :, :], (64, 64, 16))], axis=2).reshape(4096, 2, 16)
    ang[16:] = grid
    c, s_ = np.cos(ang).astype(np.float32), np.sin(ang).astype(np.float32)
    cosf = np.stack([c, c], axis=2).reshape(L, 64)
    sinf = np.stack([-s_, s_], axis=2).reshape(L, 64)
    return cosf, sinf


def prep_inputs(inputs, core):
    b, s = core // 2, core % 2
    x = np.asarray(inputs["x"], np.float32)
    meta = np.asarray(inputs["meta_tokens"], np.float32)
    hseq = np.concatenate([meta, x[b]], axis=0)
    if s == 1:
        hseq = hseq[::-1]
    m = {}
    m["h"] = np.ascontiguousarray(hseq)
    m["gmix"] = np.ascontiguousarray(np.asarray(inputs["mix_norm_g"], np.float32)[0].reshape(NCH, 128).T)
    m["ident"] = np.eye(128, dtype=np.float32)
    w_in = np.asarray(inputs["w_in"], np.float32)[0]
    if s == 1:
        w_in = w_in.copy()
        for base in (1536, 1664):
            a = w_in[:, base:base + 64].copy()
            w_in[:, base:base + 64] = w_in[:, base + 64:base + 128]
            w_in[:, base + 64:base + 128] = a
    m["w_in"] = np.ascontiguousarray(w_in)
    cosf, sinf = _rope_tables()
    if s == 1:
        cosf, sinf = cosf[::-1], sinf[::-1]
    m["ropeC"] = np.ascontiguousarray(cosf)
    m["ropeS"] = np.ascontiguousarray(sinf)
    G = lambda k: np.asarray(inputs[k], np.float32)[0]
    sh = G("rwkv_shift")
    w2, w0, a2, a0 = G("decay_w2"), G("decay_w0"), G("icl_a2"), G("icl_a0")
    if s == 1:
        sh = sh[::-1].copy()
        for base in (1536, 1664):
            a = sh[:, base:base + 64].copy()
            sh[:, base:base + 64] = sh[:, base + 64:base + 128]
            sh[:, base + 64:base + 128] = a
        w2, w0, a2, a0 = w2[::-1], w0[::-1], a2[::-1], a0[::-1]
    for nm, row in (("mu0", 0), ("mu1", 1)):
        arr = np.zeros((128, RW_TILES), np.float32)
        for j, (c0, w) in enumerate(RW_COLS):
            arr[:w, j] = sh[row, c0:c0 + w]
        m[nm] = arr
    m["w2"] = np.ascontiguousarray(w2.transpose(1, 0, 2))
    m["a2"] = np.ascontiguousarray(a2.transpose(1, 0, 2))
    m["w0"] = np.ascontiguousarray(w0.reshape(2, 8, 64).transpose(2, 0, 1))
    m["a0"] = np.ascontiguousarray(a0.reshape(2, 8, 64).transpose(2, 0, 1))
    g2 = G("gate_w2")
    m["g2a"] = np.ascontiguousarray(g2[0:128]); m["g2b"] = np.ascontiguousarray(g2[128:160])
    m["kkc"] = np.ascontiguousarray(G("k_k").reshape(8, 64).T)
    m["kac"] = np.ascontiguousarray(G("k_a").reshape(8, 64).T)
    m["rkc"] = np.ascontiguousarray(G("r_k").reshape(8, 64).T)
    m["lng"] = np.ascontiguousarray(np.tile(G("lnx_g")[None, :], (64, 1)))
    m["lnb"] = np.ascontiguousarray(np.tile(G("lnx_b")[None, :], (64, 1)))
    si, ti = np.meshgrid(np.arange(64), np.arange(64), indexing="ij")
    m["masks"] = np.ascontiguousarray(np.stack([si < ti, si <= ti, si > ti, si >= ti], 1).astype(np.float32))
    m["w_a"] = np.ascontiguousarray(G("w_branch_rwkv")); m["w_b"] = np.ascontiguousarray(G("w_branch_attn"))
    m["w_o"] = np.ascontiguousarray(G("w_out")); m["w_ff1"] = np.ascontiguousarray(G("w_ff1")); m["w_ff2"] = np.ascontiguousarray(G("w_ff2"))
    m["gffn"] = np.ascontiguousarray(G("ffn_norm_g").reshape(NCH, 128).T)
    m["gfin"] = np.ascontiguousarray(np.tile(np.asarray(inputs["final_norm_g"], np.float32)[None, :], (128, 1)))
    m["qg"] = np.ascontiguousarray(np.tile(np.asarray(inputs["q_norm_g"], np.float32)[0][None, :], (128, 1)))
    m["kg"] = np.ascontiguousarray(np.tile(np.asarray(inputs["k_norm_g"], np.float32)[0][None, :], (128, 1)))
    return m


_CACHE = {}


def kernel(**inputs):
    from concourse.bass_utils import run_bass_kernel_spmd
    if "nc" not in _CACHE:
        _CACHE["nc"] = build()
    nc, kb = _CACHE["nc"]
    in_maps = [prep_inputs(inputs, c) for c in range(8)]
    res = run_bass_kernel_spmd(nc, in_maps, core_ids=list(range(8)))
    out = np.zeros((4, 4096, D), np.float32)
    for c in range(8):
        b, s = c // 2, c % 2
        o = np.asarray(res.results[c]["out"])
        if s == 0:
            out[b, 0:2048] = o[16:2064]
        else:
            out[b, 2048:4096] = o[0:2048][::-1]
    return out
```

```python
import numpy as np
import concourse.bass as bass
import concourse.mybir as mybir

F32 = mybir.dt.float32
BF16 = mybir.dt.bfloat16
AF = mybir.ActivationFunctionType
ALU = mybir.AluOpType
AX = mybir.AxisListType

ENGS = ["pe", "dve", "act", "pool", "sp"]
SEM_WRAP = 2048
NSLOT = 8


class Res:
    __slots__ = ("name", "w", "rd")

    def __init__(self, name):
        self.name = name
        self.w = None
        self.rd = []


class Rec:
    __slots__ = ("fn", "deps", "sig", "dma", "signo", "pre")

    def __init__(self, fn, dma):
        self.fn = fn
        self.deps = set()
        self.sig = False
        self.dma = dma
        self.signo = None
        self.pre = None


class FW:
    def __init__(self, nc, same_engine_sync=True):
        self.nc = nc
        self.eng = {"pe": nc.tensor, "dve": nc.vector, "act": nc.scalar,
                    "pool": nc.gpsimd, "sp": nc.sync}
        self.ops = {e: [] for e in ENGS}
        self.ndma = {e: 0 for e in ENGS}
        self.same = same_engine_sync

    def op(self, e, fn, reads=(), writes=(), dma=False):
        lst = self.ops[e]
        idx = len(lst)
        rec = Rec(fn, None)
        if dma:
            rec.dma = self.ndma[e]
            self.ndma[e] += 1
        key = (e, idx)
        deps = set()
        for r in reads:
            if r.w is not None:
                deps.add(r.w)
        for w in writes:
            if w.w is not None:
                deps.add(w.w)
            for k in w.rd:
                deps.add(k)
        for d in deps:
            de, di = d
            drec = self.ops[de][di]
            if drec.dma is None:
                if de == e and (not self.same) and e != "pool":
                    continue
                if de == e and e == "pe":
                    continue
                drec.sig = True
            rec.deps.add(d)
        for r in reads:
            r.rd.append(key)
        for w in writes:
            w.w = key
            w.rd = []
        lst.append(rec)
        return key

    def barrier(self, resources=()):
        b = Res("barrier")
        keys = []
        for e in ENGS:
            for i in range(len(self.ops[e]) - 1, -1, -1):
                if self.ops[e][i].fn is not None:
                    keys.append((e, i))
                    break
        dkeys = []
        for e in ENGS:
            n = 0
            for i in range(len(self.ops[e]) - 1, -1, -1):
                if self.ops[e][i].dma is not None:
                    dkeys.append((e, i))
                    n += 1
                    if n >= NSLOT:
                        break
        self._pending_barrier = keys + dkeys
        for e in ENGS:
            rec = Rec(None, None)
            for d in keys + dkeys:
                de, di = d
                drec = self.ops[de][di]
                if drec.dma is None:
                    if de == e:
                        continue
                    drec.sig = True
                rec.deps.add(d)
            self.ops[e].append(rec)

    def emit(self, final_waits=()):
        nc = self.nc
        nsig = {}
        for e in ENGS:
            n = 0
            for rec in self.ops[e]:
                if rec.dma is None and rec.sig:
                    n += 1
                    rec.signo = n
            nsig[e] = n
        import contextlib
        with contextlib.ExitStack() as st:
            sems = {}
            for e in ENGS:
                k = (nsig[e] + SEM_WRAP - 1) // SEM_WRAP
                sems[e] = [st.enter_context(nc.semaphore(f"s_{e}_{i}")) for i in range(max(k, 1))]
            dsem = {}
            for e in ENGS:
                if self.ndma[e]:
                    dsem[e] = [st.enter_context(nc.semaphore(f"d_{e}_{i}")) for i in range(NSLOT)]
            block = st.enter_context(nc.Block())

            def target(dep):
                de, di = dep
                drec = self.ops[de][di]
                if drec.dma is not None:
                    n = drec.dma
                    return (dsem[de][n % NSLOT], 16 * (n // NSLOT + 1))
                s = drec.signo
                return (sems[de][(s - 1) // SEM_WRAP], (s - 1) % SEM_WRAP + 1)

            def run(e, eng):
                waited = {}
                for rec in self.ops[e]:
                    tg = {}
                    for dep in rec.deps:
                        sem, val = target(dep)
                        kk = id(sem)
                        if kk not in tg or tg[kk][1] < val:
                            tg[kk] = (sem, val)
                    if rec.dma is not None and rec.dma >= NSLOT:
                        n = rec.dma
                        sem = dsem[e][n % NSLOT]
                        val = 16 * (n // NSLOT)
                        kk = id(sem)
                        if kk not in tg or tg[kk][1] < val:
                            tg[kk] = (sem, val)
                    for kk, (sem, val) in tg.items():
                        if waited.get(kk, 0) >= val:
                            continue
                        waited[kk] = val
                        eng.wait_ge(sem, val)
                    if rec.fn is None:
                        continue
                    inst = rec.fn(eng)
                    if rec.dma is not None:
                        inst.then_inc(dsem[e][rec.dma % NSLOT], 16)
                    elif rec.sig:
                        s = rec.signo
                        inst.then_inc(sems[e][(s - 1) // SEM_WRAP], 1)

            @block.tensor
            def _(eng):
                run("pe", eng)

            @block.vector
            def _(eng):
                run("dve", eng)

            @block.scalar
            def _(eng):
                run("act", eng)

            @block.gpsimd
            def _(eng):
                run("pool", eng)

            @block.sync
            def _(eng):
                run("sp", eng)

import contextlib

L = 4112
OWN = 2064
D = 1024
NCH = 8
INW = 4768
RW = 1952
EPS = 1e-6


class Tl:
    def __init__(self, t, name):
        self.t = t
        self.r = Res(name)

    def __getitem__(self, k):
        return self.t[k]


class KB:
    def __init__(self, nc, debug=None):
        self.nc = nc
        self.fw = FW(nc)
        self.st = contextlib.ExitStack()
        self.debug = debug or []
        self.dbg_out = {}
        self.out_keys = []
        self.cnt = 0
        import os
        self.use_r32 = os.environ.get('USE_R32', '0') == '1'

    def setup_mem(self):
        self.AW = 49000
        self.arena = self.st.enter_context(self.nc.sbuf_tensor("arena", [128, self.AW], F32))
        self.psum = self.st.enter_context(self.nc.psum_tensor("psum", [128, 4096], F32))
        self.top = 0
        self.hi = self.AW
        self.pbanks = [self._pview(i) for i in range(8)]
        self.pb_i = 0

    def _pview(self, i):
        return Tl(self.psum[:, 512 * i:512 * (i + 1)], f"bank{i}")

    def bank(self):
        b = self.pbanks[self.pb_i % 8]
        self.pb_i += 1
        return b

    def mark(self):
        return self.top

    def release(self, m):
        self.top = m

    def sb(self, shape, dt, name=None):
        self.cnt += 1
        name = f"{name or 't'}_{self.cnt}"
        p = shape[0]
        n = 1
        for x in shape[1:]:
            n *= x
        words = n if dt == F32 else (n + 1) // 2
        if getattr(self, 'from_top', False):
            self.hi -= words
            off = self.hi
        else:
            off = self.top
            self.top += words
        assert self.top <= self.hi, f"arena overflow {self.top} {self.hi}"
        ap = self.arena[0:p, off:off + words]
        if dt != F32:
            ap = ap.bitcast(dt)
            if n % 2:
                ap = ap[:, 0:n]
        if len(shape) == 3:
            ap = ap.rearrange("p (a b) -> p a b", a=shape[1])
        elif len(shape) == 4:
            ap = ap.rearrange("p (a b c) -> p a b c", a=shape[1], b=shape[2])
        return Tl(ap, name)

    def dram_in(self, name, shape, dt=F32):
        t = self.nc.dram_tensor(name, list(shape), dt, kind="ExternalInput")
        return Tl(t.ap(), name)

    def dram_out(self, name, shape, dt=F32):
        t = self.nc.dram_tensor(name, list(shape), dt, kind="ExternalOutput")
        return Tl(t.ap(), name)

    def dram_tmp(self, name, shape, dt=F32):
        t = self.nc.dram_tensor(name, list(shape), dt, kind="Internal")
        return Tl(t.ap(), name)

    def dma(self, out_ap, in_ap, reads, writes, q="sp", **kw):
        return self.fw.op(q, lambda e: e.dma_start(out=out_ap, in_=in_ap, **kw),
                          reads=[x.r for x in reads], writes=[x.r for x in writes], dma=True)

    def mm(self, out_ap, lhsT_ap, rhs_ap, reads, writes, start=True, stop=True, r32=False):
        if r32 and self.use_r32 and lhsT_ap.dtype == F32 and rhs_ap.dtype == F32:
            lhsT_ap = lhsT_ap.bitcast(mybir.dt.float32r)
            rhs_ap = rhs_ap.bitcast(mybir.dt.float32r)
        return self.fw.op("pe", lambda e: e.matmul(out_ap, lhsT_ap, rhs_ap, start=start, stop=stop),
                          reads=[x.r for x in reads], writes=[x.r for x in writes])

    def tr(self, out_ap, in_ap, ident_ap, reads, writes):
        return self.fw.op("pe", lambda e: e.transpose(out_ap, in_ap, ident_ap),
                          reads=[x.r for x in reads], writes=[x.r for x in writes])

    def act(self, out_ap, in_ap, func, reads, writes, **kw):
        return self.fw.op("act", lambda e: e.activation(out=out_ap, in_=in_ap, func=func, **kw),
                          reads=[x.r for x in reads], writes=[x.r for x in writes])

    def v(self, eng, meth, reads, writes, *a, **kw):
        return self.fw.op(eng, lambda e: getattr(e, meth)(*a, **kw),
                          reads=[x.r for x in reads], writes=[x.r for x in writes])

    def dump(self, name, tl, ap, shape, dt=F32):
        if name not in self.debug:
            return
        o = self.dram_out("dbg_" + name, shape, dt)
        k = self.dma(o.t, ap, [tl], [o])
        self.out_keys.append(k)
        self.dbg_out[name] = "dbg_" + name

    def finish(self):
        fw = self.fw
        rec = Rec(None, None)
        for k in self.out_keys:
            rec.deps.add(k)
        fw.ops["sp"].append(rec)
        fw.emit()
        self.st.close()

S0 = float(np.exp(-0.5))
RW_COLS = [(j * 64, 64) for j in range(28)] + [(1792, 128), (1920, 32)]
RW_TILES = len(RW_COLS)


def rwkv_load_w(kb, D_):
    wr = kb.sb([128, NCH, RW], BF16, "wr")
    for a in range(0, RW, 488):
        kb.dma(wr[:, :, a:a + 488], D_["w_in"].t[:, a:a + 488].rearrange("(c p) n -> p c n", p=128), [D_["w_in"]], [wr], q="pool")
    return wr


def rwkv_phase(kb, uT_d, ident, D_, ybwd_d, ya_d, wr):
    fw = kb.fw

    def ld(name, shape):
        t = kb.sb(shape, F32, name)
        src = D_[name]
        kb.dma(t.t, src.t, [src], [t])
        return t
    mu0 = ld("mu0", [128, RW_TILES]); mu1 = ld("mu1", [128, RW_TILES])
    muc = kb.sb([128, RW_TILES], F32, "muc")
    kb.v("dve", "tensor_tensor", [mu0, mu1], [muc], out=muc[:, :], in0=mu0[:, :], in1=mu1[:, :], op=ALU.add)
    kb.v("dve", "tensor_scalar", [muc], [muc], out=muc[:, :], in0=muc[:, :], scalar1=-1.0, scalar2=1.0, op0=ALU.mult, op1=ALU.add)
    w2 = ld("w2", [64, 2, 512]); a2 = ld("a2", [64, 2, 512])
    w0 = ld("w0", [64, 2, 8]); a0 = ld("a0", [64, 2, 8])
    g2a = ld("g2a", [128, 512]); g2b = ld("g2b", [32, 512])
    kkc = ld("kkc", [64, 8]); kac = ld("kac", [64, 8]); rkc = ld("rkc", [64, 8])
    lng = ld("lng", [64, 512]); lnb = ld("lnb", [64, 512])
    masks = ld("masks", [64, 4, 64])
    oka = kb.sb([64, 8], F32, "oka"); kah = kb.sb([64, 8], F32, "kah")
    kb.v("dve", "tensor_scalar", [kac], [oka], out=oka[:, :], in0=kac[:, :], scalar1=-1.0, scalar2=1.0, op0=ALU.mult, op1=ALU.add)
    kb.v("dve", "tensor_scalar", [kac], [kah], out=kah[:, :], in0=kac[:, :], scalar1=0.5, scalar2=None, op0=ALU.mult)
    ones = kb.sb([64, 64], F32, "ones")
    kb.v("dve", "memset", [], [ones], ones[:, :], 1.0)
    Hs = [kb.sb([64, 8, 64], F32, "Hf"), kb.sb([64, 8, 64], F32, "Hb")]

    def T(shape, name, dt=F32):
        return kb.sb(shape, dt, name)

    class Slot:
        pass

    def make_slot(si):
        S = Slot()
        S.B = kb.pbanks[4 * si:4 * si + 4]
        S.uc = T([128, NCH, 66], 'uc', BF16)
        S.sh = [T([128, 64], 'sh') for _ in range(4)]
        S.P = T([128, RW_TILES, 66], "P")
        S.Z = T([128, RW_TILES, 64], "Z")
        S.Pt = [Tl(S.P.t[:, j, :], f'P{j}') for j in range(RW_TILES)]
        S.Zt = [Tl(S.Z.t[:, j, :], f'Z{j}') for j in range(RW_TILES)]
        S.thw = T([64, 64], "thw")
        S.sg = T([64, 8, 64], "sg")
        S.A = [T([64, 8, 64], "A0"), T([64, 8, 64], "A1")]
        S.CS = T([64, 8, 65], "CS")
        kb.v("dve", "memset", [], [S.CS], S.CS[:, :, :], 0.0)
        S.G = T([64, 8, 64], "G")
        S.Gp = T([64, 8, 64], "Gp")
        S.Gi = T([64, 8, 64], "Gi")
        S.gC = T([64, 8], "gC")
        S.kkr = T([64, 8, 64], "kkr")
        S.ksq = T([64, 8, 64], "ksq")
        S.rn = T([64, 8, 64], "rn")
        S.X1 = S.ksq
        S.X2 = S.rn
        S.t1 = T([64, 8, 64], "t1")
        S.kdir = T([64, 8, 64], "kdir")
        S.kka = T([64, 8, 64], "kka")
        S.QR = T([64, 8, 128], "QR")
        S.KdT = T([64, 8, 64], "KdT")
        S.AdT = T([64, 8, 64], "AdT")
        S.Kq_t = T([64, 512], "Kq_t")
        S.Ad_t = T([64, 512], "Ad_t")
        S.Kd_t = T([64, 512], "Kd_t")
        S.V_t = T([64, 512], "V_t")
        S.NTs = [S.kkr, S.Gi]
        S.Ns = [S.ksq, S.Gp]
        S.MkT = S.G
        S.PaT = S.sg
        S.PkT = S.rn
        S.X = T([64, 8, 128], "X")
        S.AT = S.t1
        S.RqpT = S.kdir
        S.ybt = S.Kd_t
        S.yt = S.Ad_t
        S.ysq = S.Kq_t
        S.st = T([64, 48], "st")
        S.sgd = T([128, 2, 64], "sgd")
        S.ee = S.kka
        S.yout = S.ysq
        return S
    slots = [make_slot(0), make_slot(1)]

    RS, KS, VS = slice(0, 8), slice(8, 16), slice(16, 24)
    NSTAGE = 52

    def bc(ap, shape, axis):
        return ap.unsqueeze(axis).to_broadcast(shape)

    def v3(bank, p, a):
        return bank.t[0:p, 0:512].rearrange("p (a b) -> p a b", a=a)

    def chunk(S, c0, cn, d, need_out, H, epilogue):
        B = S.B
        P, Z, uc = S.P, S.Z, S.uc
        Pt, Zt = S.Pt, S.Zt
        ns = 0
        wdt = 24 + d
        tiles = list(range(24)) + [wdt] + ([26, 27, 28, 29] if epilogue == "final" else [26 + d])
        lo, hi = max(c0 - 1, 0), min(c0 + cn + 1, L)
        n = hi - lo
        off = lo - (c0 - 1)
        kb.dma(uc[:, :, 0:n], uT_d.t[:, :, lo:hi], [uT_d], [uc])
        if off or (hi - (c0 - 1)) < cn + 2:
            kb.v("pool", "memset", [], Pt, P[:, :, :], 0.0)
        groups = [tiles[i:i + 2] for i in range(0, len(tiles), 2)]

        def shift(grp):
            for j in grp:
                w = RW_COLS[j][1]
                kb.act(Z[0:w, j, 0:cn], P[0:w, j, 1:cn + 1], AF.Copy, [Pt[j], muc], [Zt[j]], scale=muc[0:w, j:j + 1])
                kb.v("dve", "scalar_tensor_tensor", [Pt[j], mu0, Zt[j]], [Zt[j]], out=Z[0:w, j, 0:cn], in0=P[0:w, j, 0:cn],
                     scalar=mu0[0:w, j:j + 1], in1=Z[0:w, j, 0:cn], op0=ALU.mult, op1=ALU.add)
                kb.v("dve", "scalar_tensor_tensor", [Pt[j], mu1, Zt[j]], [Zt[j]], out=Z[0:w, j, 0:cn], in0=P[0:w, j, 2:cn + 2],
                     scalar=mu1[0:w, j:j + 1], in1=Z[0:w, j, 0:cn], op0=ALU.mult, op1=ALU.add)

        prev = None
        for gi, grp in enumerate(groups):
            pb_ = B[gi % 4]
            pv = pb_.t[:, 0:2 * 66].rearrange("p (a b) -> p a b", a=2)
            for ji, j in enumerate(grp):
                cc, w = RW_COLS[j]
                for c in range(NCH):
                    kb.mm(pv[0:w, ji, 0:n], wr[:, c, cc:cc + w], uc[:, c, 0:n], [wr, uc], [pb_],
                          start=c == 0, stop=c == NCH - 1)
            j0, j1 = grp[0], grp[-1]
            if j1 < 28 and j1 - j0 == len(grp) - 1:
                kb.act(P[0:64, j0:j1 + 1, off:off + n], pv[0:64, 0:len(grp), 0:n], AF.Copy, [pb_], [Pt[jj] for jj in grp])
            else:
                for ji, j in enumerate(grp):
                    w = RW_COLS[j][1]
                    kb.act(P[0:w, j, off:off + n], pv[0:w, ji, 0:n], AF.Copy, [pb_], [Pt[j]])
            if prev is not None:
                shift(prev)
            prev = grp
            ns += 1; yield
        shift(prev)
        ns += 1; yield
        while ns < 17:
            ns += 1; yield
        thw, sg, A, CS = S.thw, S.sg, S.A, S.CS
        kb.act(thw[:, 0:cn], Z[0:64, wdt, 0:cn], AF.Tanh, [Zt[wdt]], [thw])
        pl = B[0]
        plv = v3(pl, 64, 8)
        for h in range(8):
            kb.mm(plv[:, h, 0:cn], w2[:, d, 64 * h:64 * h + 64], thw[:, 0:cn], [w2, thw], [pl], r32=True)
        for h in range(8):
            kb.act(sg[:, h, 0:cn], plv[:, h, 0:cn], AF.Sigmoid, [pl, w0], [sg], bias=w0[:, d, h:h + 1])
        ns += 1; yield
        for dd in ((0, 1) if epilogue == "final" else (d,)):
            pa = B[1 + dd]
            pav = v3(pa, 64, 8)
            for h in range(8):
                kb.mm(pav[:, h, 0:cn], a2[:, dd, 64 * h:64 * h + 64], Z[0:64, 26 + dd, 0:cn], [a2, Zt[26 + dd]], [pa], r32=True)
            for h in range(8):
                kb.act(A[dd][:, h, 0:cn], pav[:, h, 0:cn], AF.Sigmoid, [pa, a0], [A[dd]], bias=a0[:, dd, h:h + 1])
        ns += 1; yield
        Ad_ = A[d]
        sh = [64, 8, cn]
        kkr, ksq, rn, t1, kdir, kka = S.kkr, S.ksq, S.rn, S.t1, S.kdir, S.kka
        kb.v("pool", "tensor_tensor", Zt[8:16] + [kkc], [kkr], out=kkr[:, :, 0:cn], in0=Z[0:64, KS, 0:cn], in1=bc(kkc[:, :], sh, 2), op=ALU.mult)
        kb.act(ksq[:, :, 0:cn], kkr[:, :, 0:cn], AF.Square, [kkr], [ksq])
        pn = B[3]
        pnv = v3(pn, 64, 8)
        for h in range(8):
            kb.mm(pnv[:, h, 0:cn], ones[:, :], ksq[:, h, 0:cn], [ones, ksq], [pn], r32=True)
        kb.act(rn[:, :, 0:cn], pnv[:, :, 0:cn], AF.Ln, [pn], [rn], bias=1e-18, scale=1.0)
        kb.act(rn[:, :, 0:cn], rn[:, :, 0:cn], AF.Exp, [rn], [rn], scale=-0.5)
        kb.v("dve", "tensor_tensor", [kkr, rn], [kkr], out=kkr[:, :, 0:cn], in0=kkr[:, :, 0:cn], in1=rn[:, :, 0:cn], op=ALU.mult)
        ns += 1; yield
        kb.v("pool", "tensor_tensor", [Ad_, kac], [t1], out=t1[:, :, 0:cn], in0=Ad_[:, :, 0:cn], in1=bc(kac[:, :], sh, 2), op=ALU.mult)
        kb.v("pool", "tensor_tensor", [t1, oka], [t1], out=t1[:, :, 0:cn], in0=t1[:, :, 0:cn], in1=bc(oka[:, :], sh, 2), op=ALU.add)
        kb.v("pool", "tensor_tensor", [t1] + Zt[8:16], [kdir], out=kdir[:, :, 0:cn], in0=t1[:, :, 0:cn], in1=Z[0:64, KS, 0:cn], op=ALU.mult)
        kb.v("dve", "tensor_tensor", [kkr, Ad_], [kka], out=kka[:, :, 0:cn], in0=kkr[:, :, 0:cn], in1=Ad_[:, :, 0:cn], op=ALU.mult)
        ns += 1; yield
        for h in range(8):
            kb.v("dve", "tensor_tensor_scan", [ones, sg], [CS], out=CS[:, h, 1:cn + 1], data0=ones[:, 0:cn], data1=sg[:, h, 0:cn],
                 initial=0.0, op0=ALU.mult, op1=ALU.add)
        ns += 1; yield
        G, Gp, Gi, gC, X1, X2 = S.G, S.Gp, S.Gi, S.gC, S.X1, S.X2
        if d == 0:
            kb.act(G[:, :, 0:cn], CS[:, :, 1:cn + 1], AF.Exp, [CS], [G], scale=-S0)
            kb.act(Gp[:, :, 0:cn], CS[:, :, 0:cn], AF.Exp, [CS], [Gp], scale=-S0)
            kb.act(Gi[:, :, 0:cn], CS[:, :, 1:cn + 1], AF.Exp, [CS], [Gi], scale=S0)
        else:
            totb = bc(CS[:, :, cn], sh, 2)
            kb.v("dve", "tensor_tensor", [CS], [X1], out=X1[:, :, 0:cn], in0=CS[:, :, 0:cn], in1=totb, op=ALU.subtract)
            kb.v("dve", "tensor_tensor", [CS], [X2], out=X2[:, :, 0:cn], in0=CS[:, :, 1:cn + 1], in1=totb, op=ALU.subtract)
            kb.act(G[:, :, 0:cn], X1[:, :, 0:cn], AF.Exp, [X1], [G], scale=S0)
            kb.act(Gp[:, :, 0:cn], X2[:, :, 0:cn], AF.Exp, [X2], [Gp], scale=S0)
            kb.act(Gi[:, :, 0:cn], X1[:, :, 0:cn], AF.Exp, [X1], [Gi], scale=-S0)
        kb.act(gC[:, :], CS[:, :, cn], AF.Exp, [CS], [gC], scale=-S0)
        ns += 1; yield
        QR, KdT, AdT = S.QR, S.KdT, S.AdT
        kb.v("dve", "tensor_tensor", [kkr, Gp], [QR], out=QR[:, :, 0:cn], in0=kkr[:, :, 0:cn], in1=Gp[:, :, 0:cn], op=ALU.mult)
        kb.v("pool", "tensor_tensor", Zt[0:8] + [G], [QR], out=QR[:, :, cn:2 * cn], in0=Z[0:64, RS, 0:cn], in1=G[:, :, 0:cn], op=ALU.mult)
        kb.v("pool", "tensor_tensor", [kdir, Gi], [KdT], out=KdT[:, :, 0:cn], in0=kdir[:, :, 0:cn], in1=Gi[:, :, 0:cn], op=ALU.mult)
        kb.v("dve", "tensor_tensor", [kka, Gi], [AdT], out=AdT[:, :, 0:cn], in0=kka[:, :, 0:cn], in1=Gi[:, :, 0:cn], op=ALU.mult)
        ns += 1; yield
        Kq_t, Ad_t, Kd_t, V_t = S.Kq_t, S.Ad_t, S.Kd_t, S.V_t
        for bi, (src, si, dst) in enumerate(((QR, 0, Kq_t), (AdT, 0, Ad_t), (KdT, 0, Kd_t), (Z, 16, V_t))):
            pt = B[bi]
            for h in range(8):
                kb.tr(pt[0:cn, h * 64:(h + 1) * 64], src[0:64, si + h, 0:cn], ident[0:64, 0:64], [Zt[16 + h] if src is Z else src, ident], [pt])
            if bi % 2 == 0:
                kb.act(dst[0:cn, :], pt[0:cn, :], AF.Copy, [pt], [dst])
            else:
                kb.v("dve", "tensor_copy", [pt], [dst], out=dst[0:cn, :], in_=pt[0:cn, :])
                ns += 1; yield
        mS, mI, mN = (0, 1, 2) if d == 0 else (2, 3, 0)
        rw = 2 * cn if need_out else cn
        pSA = [B[0], B[1]]; pSK = [B[2], B[3]]
        NTs, Ns, MkT, PaT, PkT, X = S.NTs, S.Ns, S.MkT, S.PaT, S.PkT, S.X
        for h in range(8):
            hv = pSA[h // 4].t[0:cn, 0:4 * rw].rearrange("p (h s) -> p h s", h=4)
            kb.mm(hv[:, h % 4, :], AdT[:, h, 0:cn], QR[:, h, 0:rw], [AdT, QR], [pSA[h // 4]], r32=True)
            hv2 = pSK[h // 4].t[0:cn, 0:4 * rw].rearrange("p (h s) -> p h s", h=4)
            kb.mm(hv2[:, h % 4, :], KdT[:, h, 0:cn], QR[:, h, 0:rw], [KdT, QR], [pSK[h // 4]], r32=True)
        NT, Nm = NTs[0], Ns[0]
        for half in range(2):
            hv = pSA[half].t[0:cn, 0:4 * rw].rearrange("p (h s) -> p h s", h=4)
            hv2 = pSK[half].t[0:cn, 0:4 * rw].rearrange("p (h s) -> p h s", h=4)
            hs = slice(4 * half, 4 * half + 4)
            kb.v("dve", "tensor_tensor", [pSA[half], masks], [NT], out=NT[0:cn, hs, 0:cn], in0=hv[:, :, 0:cn],
                 in1=bc(masks[0:cn, mS, 0:cn], [cn, 4, cn], 1), op=ALU.mult)
            kb.v("dve", "tensor_tensor", [pSK[half], masks], [MkT], out=MkT[0:cn, hs, 0:cn], in0=hv2[:, :, 0:cn],
                 in1=bc(masks[0:cn, mS, 0:cn], [cn, 4, cn], 1), op=ALU.mult)
            if need_out:
                kb.v("dve", "tensor_tensor", [pSA[half], masks], [PaT], out=PaT[0:cn, hs, 0:cn], in0=hv[:, :, cn:2 * cn],
                     in1=bc(masks[0:cn, mI, 0:cn], [cn, 4, cn], 1), op=ALU.mult)
                kb.v("dve", "tensor_tensor", [pSK[half], masks], [PkT], out=PkT[0:cn, hs, 0:cn], in0=hv2[:, :, cn:2 * cn],
                     in1=bc(masks[0:cn, mI, 0:cn], [cn, 4, cn], 1), op=ALU.mult)
        ns += 1; yield
        pN = B[0]
        pNv = pN.t[0:cn, 0:8 * cn].rearrange("p (h s) -> p h s", h=8)
        pW = B[1]
        pWv = v3(pW, cn, 8)
        for h in range(8):
            kb.mm(pNv[:, h, :], QR[:, h, 0:cn], AdT[:, h, 0:cn], [QR, AdT], [pN], r32=True)
        for h in range(8):
            kb.mm(pWv[:, h, :], MkT[0:cn, h, 0:cn], V_t[0:cn, 64 * h:64 * h + 64], [MkT, V_t], [pW], r32=True)
        kb.v("dve", "tensor_tensor", [pN, masks], [Nm], out=Nm[0:cn, :, 0:cn], in0=pNv,
             in1=bc(masks[0:cn, mN, 0:cn], [cn, 8, cn], 1), op=ALU.mult)
        kb.v("pool", "tensor_copy", [Kq_t], [X], out=X[0:cn, :, 0:64], in_=Kq_t[0:cn, :].rearrange("p (h v) -> p h v", h=8))
        kb.act(X[0:cn, :, 64:128], pWv, AF.Copy, [pW], [X], scale=-1.0)
        ns += 1; yield
        nlev = 5 if cn > 32 else (4 if cn > 16 else 3)
        pF = [B[2], B[3]]

        def factor(PT_, sign):
            for h in range(8):
                fv = v3(pF[h // 4], cn, 4)
                kb.mm(fv[:, h % 4, :], PT_[0:cn, h, 0:cn], X[0:cn, h, :], [PT_, X], [pF[h // 4]], r32=True)
            for half in range(2):
                fv = v3(pF[half], cn, 4)
                hs = slice(4 * half, 4 * half + 4)
                kb.v("dve", "tensor_tensor", [X, pF[half]], [X], out=X[0:cn, hs, :], in0=X[0:cn, hs, :], in1=fv,
                     op=ALU.subtract if sign < 0 else ALU.add)
        factor(NT, -1)
        ns += 1; yield
        cur = 0
        for lvl in range(5):
            if lvl < nlev:
                NTc, Nc = NTs[cur], Ns[cur]
                NTn, Nn = NTs[1 - cur], Ns[1 - cur]
                pa_, pb2 = B[0], B[1]
                pav = pa_.t[0:cn, 0:8 * cn].rearrange("p (h s) -> p h s", h=8)
                pbv = pb2.t[0:cn, 0:8 * cn].rearrange("p (h s) -> p h s", h=8)
                last = lvl == nlev - 1
                for h in range(8):
                    kb.mm(pav[:, h, :], Nc[0:cn, h, 0:cn], NTc[0:cn, h, 0:cn], [Nc, NTc], [pa_], r32=True)
                    if not last:
                        kb.mm(pbv[:, h, :], NTc[0:cn, h, 0:cn], Nc[0:cn, h, 0:cn], [Nc, NTc], [pb2], r32=True)
                kb.act(NTn[0:cn, :, 0:cn], pav, AF.Copy, [pa_], [NTn])
                if not last:
                    kb.act(Nn[0:cn, :, 0:cn], pbv, AF.Copy, [pb2], [Nn])
            ns += 1; yield
            if lvl < nlev:
                factor(NTn, +1)
                cur = 1 - cur
            ns += 1; yield
        AT, RqpT = S.AT, S.RqpT
        pA = B[0]
        pAv = v3(pA, 64, 8)
        for h in range(8):
            kb.mm(pAv[:, h, :], X[0:cn, h, 0:64], Ad_t[0:cn, 64 * h:64 * h + 64], [X, Ad_t], [pA], r32=True)
        kb.v("dve", "tensor_tensor", [ident, pA], [AT], out=AT[:, :, :], in0=bc(ident[0:64, 0:64], [64, 8, 64], 1), in1=pAv, op=ALU.subtract)
        if need_out:
            pR = B[1]
            pRv = pR.t[0:64, 0:8 * cn].rearrange("p (a b) -> p a b", a=8)
            for h in range(8):
                kb.mm(pRv[:, h, :], X[0:cn, h, 0:64], PaT[0:cn, h, 0:cn], [X, PaT], [pR], r32=True)
            kb.v("dve", "tensor_tensor", [QR, pR], [RqpT], out=RqpT[:, :, 0:cn], in0=QR[:, :, cn:2 * cn], in1=pRv, op=ALU.subtract)
        ns += 1; yield
        if need_out:
            pY = B[2]
            pYv = v3(pY, cn, 8)
            for h in range(8):
                kb.mm(pYv[:, h, :], RqpT[:, h, 0:cn], H[:, h, :], [RqpT, H], [pY], start=True, stop=False, r32=True)
                kb.mm(pYv[:, h, :], PkT[0:cn, h, 0:cn], V_t[0:cn, 64 * h:64 * h + 64], [PkT, V_t], [pY], start=False, stop=False, r32=True)
                kb.mm(pYv[:, h, :], PaT[0:cn, h, 0:cn], X[0:cn, h, 64:128], [PaT, X], [pY], start=False, stop=True, r32=True)
        pH = B[3]
        pHv = v3(pH, 64, 8)
        for h in range(8):
            kb.mm(pHv[:, h, :], AT[:, h, :], H[:, h, :], [AT, H], [pH], start=True, stop=False, r32=True)
            kb.mm(pHv[:, h, :], Kd_t[0:cn, 64 * h:64 * h + 64], V_t[0:cn, 64 * h:64 * h + 64], [Kd_t, V_t], [pH], start=False, stop=False, r32=True)
            kb.mm(pHv[:, h, :], Ad_t[0:cn, 64 * h:64 * h + 64], X[0:cn, h, 64:128], [Ad_t, X], [pH], start=False, stop=True, r32=True)
        kb.v("dve", "tensor_tensor", [pH, gC], [H], out=H[:, :, :], in0=pHv, in1=bc(gC[:, :], [64, 8, 64], 2), op=ALU.mult)
        ns += 1; yield
        ybt, yt, ysq, st, sgd, ee, yout = S.ybt, S.yt, S.ysq, S.st, S.sgd, S.ee, S.yout
        if epilogue == "store":
            kb.act(yt[0:cn, :], pY[0:cn, :], AF.Copy, [pY], [yt])
            kb.dma(ybwd_d.t[c0:c0 + cn, :], yt[0:cn, :], [yt], [ybwd_d])
        elif epilogue == "final":
            kb.dma(ybt[0:cn, :], ybwd_d.t[c0:c0 + cn, :], [ybwd_d], [ybt])
            kb.v("dve", "tensor_tensor", [pY, ybt], [yt], out=yt[0:cn, :], in0=pY[0:cn, :], in1=ybt[0:cn, :], op=ALU.add)
            y3 = yt[0:cn, :].rearrange("p (h v) -> p h v", h=8)
            kb.v("dve", "tensor_reduce", [yt], [st], out=st[0:cn, 0:8], in_=y3, axis=AX.X, op=ALU.add)
            kb.act(ysq[0:cn, :], yt[0:cn, :], AF.Square, [yt], [ysq])
            kb.v("dve", "tensor_reduce", [ysq], [st], out=st[0:cn, 8:16], in_=ysq[0:cn, :].rearrange("p (h v) -> p h v", h=8), axis=AX.X, op=ALU.add)
            kb.v("dve", "tensor_scalar", [st], [st], out=st[0:cn, 16:24], in0=st[0:cn, 0:8], scalar1=1.0 / 64, scalar2=None, op0=ALU.mult)
            kb.v("dve", "tensor_tensor", [st], [st], out=st[0:cn, 24:32], in0=st[0:cn, 16:24], in1=st[0:cn, 16:24], op=ALU.mult)
            kb.v("dve", "scalar_tensor_tensor", [st], [st], out=st[0:cn, 32:40], in0=st[0:cn, 8:16], scalar=1.0 / 64, in1=st[0:cn, 24:32],
                 op0=ALU.mult, op1=ALU.subtract)
            kb.act(st[0:cn, 40:48], st[0:cn, 32:40], AF.Sqrt, [st], [st], bias=64e-5, scale=1.0)
            kb.v("dve", "reciprocal", [st], [st], out=st[0:cn, 40:48], in_=st[0:cn, 40:48])
            kb.v("dve", "tensor_tensor", [yt, st], [yt], out=y3, in0=y3, in1=bc(st[0:cn, 16:24], [cn, 8, 64], 2), op=ALU.subtract)
            kb.v("dve", "tensor_tensor", [yt, st], [yt], out=y3, in0=y3, in1=bc(st[0:cn, 40:48], [cn, 8, 64], 2), op=ALU.mult)
            kb.v("pool", "tensor_tensor", [yt, lng], [yt], out=yt[0:cn, :], in0=yt[0:cn, :], in1=lng[0:cn, :], op=ALU.mult)
            kb.v("pool", "tensor_tensor", [yt, lnb], [yt], out=yt[0:cn, :], in0=yt[0:cn, :], in1=lnb[0:cn, :], op=ALU.add)
        ns += 1; yield
        if epilogue == "final":
            kb.v("pool", "tensor_tensor", [A[0], A[1]], [ee], out=ee[:, :, 0:cn], in0=A[0][:, :, 0:cn], in1=A[1][:, :, 0:cn], op=ALU.add)
            kb.v("pool", "tensor_tensor", [ee, kah], [ee], out=ee[:, :, 0:cn], in0=ee[:, :, 0:cn], in1=bc(kah[:, :], sh, 2), op=ALU.mult)
            kb.v("pool", "tensor_tensor", [ee, oka], [ee], out=ee[:, :, 0:cn], in0=ee[:, :, 0:cn], in1=bc(oka[:, :], sh, 2), op=ALU.add)
            kb.v("pool", "tensor_tensor", [ee] + Zt[8:16], [ee], out=ee[:, :, 0:cn], in0=ee[:, :, 0:cn], in1=Z[0:64, KS, 0:cn], op=ALU.mult)
            kb.v("pool", "tensor_tensor", [ee] + Zt[0:8], [ee], out=ee[:, :, 0:cn], in0=ee[:, :, 0:cn], in1=Z[0:64, RS, 0:cn], op=ALU.mult)
            kb.v("pool", "tensor_tensor", [ee, rkc], [ee], out=ee[:, :, 0:cn], in0=ee[:, :, 0:cn], in1=bc(rkc[:, :], sh, 2), op=ALU.mult)
            pS = B[0]
            for h in range(8):
                kb.mm(pS[0:cn, h:h + 1], ee[:, h, 0:cn], ones[:, 0:1], [ee, ones], [pS])
            kb.act(st[0:cn, 0:8], pS[0:cn, 0:8], AF.Copy, [pS], [st])
            kb.v("dve", "tensor_tensor", [V_t, st], [ysq], out=ysq[0:cn, :].rearrange("p (h v) -> p h v", h=8),
                 in0=V_t[0:cn, :].rearrange("p (h v) -> p h v", h=8), in1=bc(st[0:cn, 0:8], [cn, 8, 64], 2), op=ALU.mult)
            kb.v("dve", "tensor_tensor", [yt, ysq], [yt], out=yt[0:cn, :], in0=yt[0:cn, :], in1=ysq[0:cn, :], op=ALU.add)
            kb.act(sgd[:, 0, 0:cn], Z[:, 28, 0:cn], AF.Sigmoid, [Zt[28]], [sgd])
            kb.act(sgd[0:32, 1, 0:cn], Z[0:32, 29, 0:cn], AF.Sigmoid, [Zt[29]], [sgd])
            pG = B[1]
            kb.mm(pG[0:cn, :], sgd[:, 0, 0:cn], g2a[:, :], [sgd, g2a], [pG], start=True, stop=False, r32=True)
            kb.mm(pG[0:cn, :], sgd[0:32, 1, 0:cn], g2b[:, :], [sgd, g2b], [pG], start=False, stop=True, r32=True)
            kb.v("dve", "tensor_tensor", [yt, pG], [yout], out=yout[0:cn, :], in0=yt[0:cn, :], in1=pG[0:cn, :], op=ALU.mult)
            kb.dma(ya_d.t[c0:c0 + cn, :], yout[0:cn, :], [yout], [ya_d])
        ns += 1; yield
        while ns < NSTAGE:
            ns += 1; yield
        assert ns == NSTAGE, ns

    def chain(S, lst, d, H, epilogue):
        for (c0, cn) in lst:
            yield from chunk(S, c0, cn, d, epilogue is not None, H, epilogue)

    chunks = tiles_of(2048, 64) + [(2048, 16)] + [(2064 + a, b) for a, b in tiles_of(2048, 64)]
    own = [c for c in chunks if c[0] < OWN]
    oth = [c for c in chunks if c[0] >= OWN]
    Hf, Hb = Hs
    kb.v("dve", "memset", [], [Hf], Hf[:, :, :], 0.0)
    kb.v("dve", "memset", [], [Hb], Hb[:, :, :], 0.0)
    g0 = chain(slots[0], own, 0, Hf, "store")
    g1 = chain(slots[1], list(reversed(oth)), 1, Hb, None)
    step = 0
    d0 = d1 = False
    while not (d0 and d1):
        if not d0:
            try:
                next(g0)
            except StopIteration:
                d0 = True
        if step >= NSTAGE // 2 and not d1:
            try:
                next(g1)
            except StopIteration:
                d1 = True
        step += 1
    rown = list(reversed(own))
    g0 = chain(slots[0], rown[0::2], 1, Hb, "final")
    g1 = chain(slots[1], rown[1::2], 1, Hb, "final")
    step = 0
    d0 = d1 = False
    while not (d0 and d1):
        if not d0:
            try:
                next(g0)
            except StopIteration:
                d0 = True
        if step >= NSTAGE // 2 and not d1:
            try:
                next(g1)
            except StopIteration:
                d1 = True
        step += 1


def wload(kb, dram, rows, c0, c1, name):
    kch = rows // 128
    t = kb.sb([128, kch, c1 - c0], BF16, name)
    step = 1024
    for a in range(c0, c1, step):
        b = min(a + step, c1)
        kb.dma(t[:, :, a - c0:b - c0], dram.t[:, a:b].rearrange("(c p) n -> p c n", p=128), [dram], [t], q="pool")
    return t


def merge_phase(kb, uT_d, identb, D_, ya_d, yb_d, h2_d):
    B = kb.pbanks
    Wa = wload(kb, D_["w_a"], 512, 0, 1024, "Wa")
    Wb = wload(kb, D_["w_b"], 512, 0, 1024, "Wb")
    Wo = wload(kb, D_["w_o"], 1024, 0, 1024, "Wo")
    Wg = wload(kb, D_["w_in"], 1024, 2720, 4768, "Wg")

    class Set:
        pass

    def mk(si):
        S = Set()
        S.B = B[4 * si:4 * si + 4]
        S.yaf = kb.sb([128, 512], F32, "yaf"); S.yab = kb.sb([128, 512], BF16, "yab"); S.ybb = kb.sb([128, 512], BF16, "ybb")
        S.yaT = kb.sb([128, 4, 128], BF16, "yaT"); S.ybT = kb.sb([128, 4, 128], BF16, "ybT")
        S.sgA = kb.sb([128, 1024], F32, "sgA"); S.sgB = kb.sb([128, 1024], F32, "sgB")
        S.mg = kb.sb([128, 1024], BF16, "mg"); S.mT = kb.sb([128, 8, 128], BF16, "mT")
        S.xt = kb.sb([128, 1024], F32, "xt2"); S.h2 = kb.sb([128, 1024], F32, "h2")
        S.ut = kb.sb([128, NCH, 128], BF16, "utm")
        return S
    sets = [mk(0), mk(1)]

    def tile(S, i, t0, n):
        B_ = S.B
        yaf, yab, ybb, yaT, ybT, sgA, sgB, mg, mT, xt, h2, ut = (S.yaf, S.yab, S.ybb, S.yaT, S.ybT, S.sgA, S.sgB, S.mg, S.mT,
                                                                  S.xt, S.h2, S.ut)
        kb.dma(ut[:, :, 0:n], uT_d.t[:, :, t0:t0 + n], [uT_d], [ut])
        kb.dma(yaf[0:n, :], ya_d.t[t0:t0 + n, :], [ya_d], [yaf])
        kb.dma(ybb[0:n, :], yb_d.t[i, 0:n, :], [yb_d], [ybb])
        kb.dma(xt[0:n, :], D_["h"].t[t0:t0 + n, :], [D_["h"]], [xt])
        yield
        for gi, sg_ in enumerate((sgA, sgB)):
            for hb in range(2):
                bk = B_[hb]
                col = gi * 1024 + hb * 512
                for c in range(NCH):
                    kb.mm(bk[0:n, :], ut[:, c, 0:n], Wg[:, c, col:col + 512], [ut, Wg], [bk], start=c == 0, stop=c == NCH - 1)
                kb.act(sg_[0:n, hb * 512:(hb + 1) * 512], bk[0:n, :], AF.Sigmoid, [bk], [sg_])
            yield
        kb.v("dve", "tensor_copy", [yaf], [yab], out=yab[0:n, :], in_=yaf[0:n, :])
        for src, dst, bk in ((yab, yaT, B_[2]), (ybb, ybT, B_[3])):
            pv = bk.t.bitcast(BF16).rearrange("p (c t) -> p c t", c=8)
            for k in range(4):
                kb.tr(pv[:, k, 0:n], src[0:n, k * 128:(k + 1) * 128], identb[0:n, 0:n], [src, identb], [bk])
            kb.act(dst[:, :, 0:n], pv[:, 0:4, 0:n], AF.Copy, [bk], [dst])
        yield
        for (yT_, W_, sg_) in ((yaT, Wa, sgA), (ybT, Wb, sgB)):
            for hb in range(2):
                bk = B_[hb]
                for k in range(4):
                    kb.mm(bk[0:n, :], yT_[:, k, 0:n], W_[:, k, hb * 512:(hb + 1) * 512], [yT_, W_], [bk], start=k == 0, stop=k == 3)
                kb.v("dve", "tensor_tensor", [sg_, bk], [sg_], out=sg_[0:n, hb * 512:(hb + 1) * 512],
                     in0=sg_[0:n, hb * 512:(hb + 1) * 512], in1=bk[0:n, :], op=ALU.mult)
            yield
        kb.v("pool", "tensor_tensor", [sgA, sgB], [mg], out=mg[0:n, :], in0=sgA[0:n, :], in1=sgB[0:n, :], op=ALU.add)
        bk = B_[2]
        pv = bk.t.bitcast(BF16).rearrange("p (c t) -> p c t", c=8)
        for k in range(8):
            kb.tr(pv[:, k, 0:n], mg[0:n, k * 128:(k + 1) * 128], identb[0:n, 0:n], [mg, identb], [bk])
        kb.act(mT[:, :, 0:n], pv[:, :, 0:n], AF.Copy, [bk], [mT])
        yield
        for hb in range(2):
            bk = B_[hb]
            for k in range(8):
                kb.mm(bk[0:n, :], mT[:, k, 0:n], Wo[:, k, hb * 512:(hb + 1) * 512], [mT, Wo], [bk], start=k == 0, stop=k == 7)
            kb.v("dve", "tensor_tensor", [xt, bk], [h2], out=h2[0:n, hb * 512:(hb + 1) * 512], in0=xt[0:n, hb * 512:(hb + 1) * 512],
                 in1=bk[0:n, :], op=ALU.add)
        kb.dma(h2_d.t[t0:t0 + n, :], h2[0:n, :], [h2], [h2_d])
        yield

    def chain(S, lst):
        for (i, t0, n) in lst:
            yield from tile(S, i, t0, n)
    tl = [(i, t0, n) for i, (t0, n) in enumerate(tiles_of(OWN, 128))]
    g0, g1 = chain(sets[0], tl[0::2]), chain(sets[1], tl[1::2])
    step, d0, d1 = 0, False, False
    while not (d0 and d1):
        if not d0:
            try:
                next(g0)
            except StopIteration:
                d0 = True
        if step >= 4 and not d1:
            try:
                next(g1)
            except StopIteration:
                d1 = True
        step += 1


def ffn_phase(kb, identb, D_, h2_d, out_d, W1):
    B = kb.pbanks
    GT = 256
    W2 = wload(kb, D_["w_ff2"], 4096, 0, 1024, "W2")
    gf = kb.sb([128, NCH], F32, "gffn"); gfin = kb.sb([128, 1024], F32, "gfin")
    kb.dma(gf.t, D_["gffn"].t, [D_["gffn"]], [gf])
    kb.dma(gfin.t, D_["gfin"].t, [D_["gfin"]], [gfin])
    xs = kb.sb([128, 1024], BF16, "xs2")
    ss = kb.sb([128, 8], F32, "ss2"); ss2 = kb.sb([128, 8], F32, "ss3"); nT = kb.sb([128, 8, GT], BF16, "nT")
    frs = [kb.sb([128, GT], F32, "fr") for _ in range(2)]
    f2T = kb.sb([128, 32, GT], BF16, "f2T")
    h3 = kb.sb([128, 1024], F32, "h3"); ot = kb.sb([128, 1024], F32, "ot")
    junk = kb.sb([128, 1024], BF16, "junkf")
    groups = tiles_of(OWN, GT)
    h2g = [[kb.sb([128, 1024], F32, "h2f") for _ in range(2)] for _ in range(2)]
    nTs = [nT, kb.sb([128, 8, GT], BF16, "nT2")]

    def norm_part(gi):
        g0, gn = groups[gi]
        for si, (o0, n) in enumerate(tiles_of(gn, 128)):
            t0 = g0 + o0
            h2 = h2g[gi % 2][si]
            nT_ = nTs[gi % 2]
            kb.dma(h2[0:n, :], h2_d.t[t0:t0 + n, :], [h2_d], [h2])
            kb.act(junk[0:n, :], h2[0:n, :], AF.Square, [h2], [junk, ss], accum_out=ss[0:n, 0:1])
            kb.act(ss[0:n, 1:2], ss[0:n, 0:1], AF.Sqrt, [ss], [ss], scale=1.0 / D, bias=EPS)
            kb.v("dve", "reciprocal", [ss], [ss], out=ss[0:n, 2:3], in_=ss[0:n, 1:2])
            kb.v("dve", "tensor_scalar", [h2, ss], [xs], out=xs[0:n, :], in0=h2[0:n, :], scalar1=ss[0:n, 2:3], scalar2=None, op0=ALU.mult)
            bk = B[si]
            pv = bk.t.bitcast(BF16).rearrange("p (c t) -> p c t", c=8)
            for c in range(NCH):
                kb.tr(pv[:, c, 0:n], xs[0:n, c * 128:(c + 1) * 128], identb[0:n, 0:n], [xs, identb], [bk])
            kb.v("dve", "tensor_tensor", [bk, gf], [nT_], out=nT_[:, :, o0:o0 + n], in0=pv[:, :, 0:n],
                 in1=gf[:, :].unsqueeze(2).to_broadcast([128, NCH, n]), op=ALU.mult)

    norm_part(0)
    for gi, (g0, gn) in enumerate(groups):
        subs = tiles_of(gn, 128)
        nT_ = nTs[gi % 2]
        for ft in range(32):
            bk = B[2 + ft % 3]
            fr = frs[ft % 2]
            for c in range(NCH):
                kb.mm(bk[:, 0:gn], W1[:, c, ft * 128:(ft + 1) * 128], nT_[:, c, 0:gn], [W1, nT_], [bk], start=c == 0, stop=c == NCH - 1)
            kb.act(fr[:, 0:gn], bk[:, 0:gn], AF.Relu, [bk], [fr])
            kb.v("dve", "tensor_tensor", [fr], [f2T], out=f2T[:, ft, 0:gn], in0=fr[:, 0:gn], in1=fr[:, 0:gn], op=ALU.mult)
        if gi + 1 < len(groups):
            norm_part(gi + 1)
        for si, (o0, n) in enumerate(subs):
            t0 = g0 + o0
            h2 = h2g[gi % 2][si]
            for hb in range(2):
                bk = B[5 + (hb + 2 * si) % 3]
                for k in range(32):
                    kb.mm(bk[0:n, :], f2T[:, k, o0:o0 + n], W2[:, k, hb * 512:(hb + 1) * 512], [f2T, W2], [bk], start=k == 0, stop=k == 31)
                kb.v("dve", "tensor_tensor", [h2, bk], [h3], out=h3[0:n, hb * 512:(hb + 1) * 512], in0=h2[0:n, hb * 512:(hb + 1) * 512],
                     in1=bk[0:n, :], op=ALU.add)
            kb.act(junk[0:n, :], h3[0:n, :], AF.Square, [h3], [junk, ss2], accum_out=ss2[0:n, 4:5])
            kb.act(ss2[0:n, 5:6], ss2[0:n, 4:5], AF.Sqrt, [ss2], [ss2], scale=1.0 / D, bias=EPS)
            kb.v("dve", "reciprocal", [ss2], [ss2], out=ss2[0:n, 6:7], in_=ss2[0:n, 5:6])
            kb.v("dve", "scalar_tensor_tensor", [h3, ss2, gfin], [ot], out=ot[0:n, :], in0=h3[0:n, :], scalar=ss2[0:n, 6:7], in1=gfin[0:n, :],
                 op0=ALU.mult, op1=ALU.mult)
            k = kb.dma(out_d.t[t0:t0 + n, :], ot[0:n, :], [ot], [])
            kb.out_keys.append(k)

def tiles_of(n, step):
    out = []
    t = 0
    while t < n:
        out.append((t, min(step, n - t)))
        t += step
    return out


def build(debug=None, upto="all"):
    nc = bass.Bass("TRN2", target_bir_lowering=False)
    kb = KB(nc, debug)
    kb.setup_mem()
    fw = kb.fw
    h_d = kb.dram_in("h", [L, D])
    gmix_d = kb.dram_in("gmix", [128, NCH])
    ident_d = kb.dram_in("ident", [128, 128])
    out_d = kb.dram_out("out", [OWN, D])

    ident = kb.sb([128, 128], F32, "ident")
    identb = kb.sb([128, 128], BF16, "identb")
    gmix = kb.sb([128, NCH], F32, "gmix")
    kb.dma(ident[:, :], ident_d[:, :], [ident_d], [ident])
    kb.dma(gmix[:, :], gmix_d[:, :], [gmix_d], [gmix])
    kb.v("dve", "tensor_copy", [ident], [identb], out=identb[:, :], in_=ident[:, :])

    win_d = kb.dram_in("w_in", [D, INW])
    kb.from_top = True
    wr_pre = rwkv_load_w(kb, {"w_in": win_d})
    kb.from_top = False
    mA = kb.mark()
    uT_d = kb.dram_tmp("uT_scr", [128, NCH, L], BF16)
    xts = [kb.sb([128, D], F32, "xt") for _ in range(2)]
    xss = [kb.sb([128, D], BF16, "xs") for _ in range(2)]
    junk = kb.sb([128, D], BF16, "junk")
    sss = [kb.sb([128, 4], F32, "ss") for _ in range(2)]
    uts = [kb.sb([128, NCH, 128], BF16, "ut") for _ in range(2)]

    def phase_a_tile(i, t0, n):
        xt = xts[i % 2]
        xs = xss[i % 2]
        ss = sss[i % 2]
        ut = uts[i % 2]
        kb.dma(xt[0:n, :], h_d[t0:t0 + n, :], [h_d], [xt])
        kb.act(junk[0:n, :], xt[0:n, :], AF.Square, [xt], [junk, ss], accum_out=ss[0:n, 0:1])
        kb.act(ss[0:n, 1:2], ss[0:n, 0:1], AF.Sqrt, [ss], [ss], scale=1.0 / D, bias=EPS)
        kb.v("dve", "reciprocal", [ss], [ss], out=ss[0:n, 2:3], in_=ss[0:n, 1:2])
        kb.v("dve", "tensor_scalar", [xt, ss], [xs], out=xs[0:n, :], in0=xt[0:n, :],
             scalar1=ss[0:n, 2:3], scalar2=None, op0=ALU.mult)
        pb = kb.bank()
        pv = pb.t.bitcast(BF16).rearrange("p (c t) -> p c t", c=NCH)
        for c in range(NCH):
            kb.tr(pv[:, c, 0:n], xs[0:n, c * 128:(c + 1) * 128], identb[0:n, 0:n], [xs, identb], [pb])
        kb.v("dve", "tensor_tensor", [pb, gmix], [ut], out=ut[:, :, 0:n], in0=pv[:, :, 0:n],
             in1=gmix[:, :].unsqueeze(2).to_broadcast([128, NCH, n]), op=ALU.mult)
        kb.dma(uT_d.t[:, :, t0:t0 + n], ut[:, :, 0:n], [ut], [uT_d])
        return ut

    ropeC_d = kb.dram_in("ropeC", [L, 64])
    ropeS_d = kb.dram_in("ropeS", [L, 64])
    qg_d = kb.dram_in("qg", [128, 64])
    kg_d = kb.dram_in("kg", [128, 64])
    NT = 33
    NQT = 17
    mB0 = kb.mark()
    y_b = kb.sb([128, NQT, 512], BF16, "y_b")
    mB = kb.mark()
    wqkv = kb.sb([128, NCH, 768], BF16, "wqkv")
    kb.dma(wqkv[:, :, :], win_d.t[:, 1952:2720].rearrange("(c p) n -> p c n", p=128), [win_d], [wqkv], q="pool")
    rcs = [kb.sb([128, 64], F32, "rc") for _ in range(2)]
    rss = [kb.sb([128, 64], F32, "rs") for _ in range(2)]
    qgb = kb.sb([128, 64], F32, "qgb")
    kgb = kb.sb([128, 64], F32, "kgb")
    kb.dma(qgb[:, :], qg_d[:, :], [qg_d], [qgb])
    kb.dma(kgb[:, :], kg_d[:, :], [kg_d], [kgb])
    QT = kb.sb([64, 8, OWN], BF16, "QT")
    KT = kb.sb([64, 2, L], BF16, "KT")
    V1 = kb.sb([128, NT, 2, 65], BF16, "V1")
    kb.v("pool", "memset", [], [V1], V1[:, :, :, 64:65], 1.0)
    sqs = [kb.sb([128, 640], F32, "sq") for _ in range(1)]
    qns = [kb.sb([128, 640], F32, "qn") for _ in range(1)]
    t1s = [kb.sb([128, 640], F32, "t1") for _ in range(1)]
    t2s = [kb.sb([128, 640], F32, "t2") for _ in range(1)]
    qrs = [kb.sb([128, 640], BF16, "qr") for _ in range(2)]
    sms = [kb.sb([128, 32], F32, "sm") for _ in range(2)]
    for i, (t0, n) in enumerate(tiles_of(L, 128)):
        nq = max(0, min(n, OWN - t0))
        ut = phase_a_tile(i, t0, n)
        sq, qn, t1, t2, qr, sm = sqs[0], qns[0], t1s[0], t2s[0], qrs[i % 2], sms[i % 2]
        pq = kb.bank()
        pkv = kb.bank()
        rc, rs = rcs[i % 2], rss[i % 2]
        kb.dma(rc[0:n, :], ropeC_d.t[t0:t0 + n, :], [ropeC_d], [rc])
        kb.dma(rs[0:n, :], ropeS_d.t[t0:t0 + n, :], [ropeS_d], [rs])
        for c in range(NCH):
            if nq:
                kb.mm(pq[0:n, 0:512], ut[:, c, 0:n], wqkv[:, c, 0:512], [ut, wqkv], [pq], start=c == 0, stop=c == NCH - 1)
            kb.mm(pkv[0:n, 0:256], ut[:, c, 0:n], wqkv[:, c, 512:768], [ut, wqkv], [pkv], start=c == 0, stop=c == NCH - 1)
        h0 = 0 if nq else 8
        c0 = h0 * 64
        nh = 10 - h0
        if nq:
            kb.act(sq[0:n, 0:512], pq[0:n, 0:512], AF.Square, [pq], [sq])
        kb.act(sq[0:n, 512:640], pkv[0:n, 0:128], AF.Square, [pkv], [sq])
        kb.v("dve", "tensor_reduce", [sq], [sm], out=sm[0:n, h0:10],
             in_=sq[0:n, c0:640].rearrange("p (h c) -> p h c", c=64), axis=AX.X, op=ALU.add)
        kb.act(sm[0:n, 10 + h0:20], sm[0:n, h0:10], AF.Sqrt, [sm], [sm], scale=1.0 / 64, bias=EPS)
        kb.v("dve", "reciprocal", [sm], [sm], out=sm[0:n, 20 + h0:30], in_=sm[0:n, 10 + h0:20])
        if nq:
            kb.v("dve", "tensor_tensor", [pq, sm], [qn], out=qn[0:n, 0:512].rearrange("p (h c) -> p h c", c=64),
                 in0=pq[0:n, 0:512].rearrange("p (h c) -> p h c", c=64),
                 in1=sm[0:n, 20:28].unsqueeze(2).to_broadcast([n, 8, 64]), op=ALU.mult)
            kb.v("dve", "tensor_tensor", [qn, qgb], [qn], out=qn[0:n, 0:512].rearrange("p (h c) -> p h c", c=64),
                 in0=qn[0:n, 0:512].rearrange("p (h c) -> p h c", c=64),
                 in1=qgb[0:n, :].unsqueeze(1).to_broadcast([n, 8, 64]), op=ALU.mult)
        kb.v("dve", "tensor_tensor", [pkv, sm], [qn], out=qn[0:n, 512:640].rearrange("p (h c) -> p h c", c=64),
             in0=pkv[0:n, 0:128].rearrange("p (h c) -> p h c", c=64),
             in1=sm[0:n, 28:30].unsqueeze(2).to_broadcast([n, 2, 64]), op=ALU.mult)
        kb.v("dve", "tensor_tensor", [qn, kgb], [qn], out=qn[0:n, 512:640].rearrange("p (h c) -> p h c", c=64),
             in0=qn[0:n, 512:640].rearrange("p (h c) -> p h c", c=64),
             in1=kgb[0:n, :].unsqueeze(1).to_broadcast([n, 2, 64]), op=ALU.mult)
        kb.v("dve", "tensor_tensor", [qn, rc], [t1], out=t1[0:n, c0:640].rearrange("p (h c) -> p h c", c=64),
             in0=qn[0:n, c0:640].rearrange("p (h c) -> p h c", c=64),
             in1=rc[0:n, :].unsqueeze(1).to_broadcast([n, nh, 64]), op=ALU.mult)
        for hf in range(2):
            o_v = t2[0:n, c0:640].rearrange("p (h a x f) -> p h a x f", a=2, x=2, f=16)[:, :, :, hf, :]
            i_v = qn[0:n, c0:640].rearrange("p (h a x f) -> p h a x f", a=2, x=2, f=16)[:, :, :, 1 - hf, :]
            s_v = rs[0:n, :].rearrange("p (a x f) -> p a x f", a=2, x=2)[:, :, hf, :].unsqueeze(1).to_broadcast([n, nh, 2, 16])
            kb.v("dve", "tensor_tensor", [qn, rs], [t2], out=o_v, in0=i_v, in1=s_v, op=ALU.mult)
        kb.v("dve", "tensor_tensor", [t1, t2], [qr], out=qr[0:n, c0:640], in0=t1[0:n, c0:640], in1=t2[0:n, c0:640], op=ALU.add)
        ptk = kb.bank()
        ptkv = ptk.t.bitcast(BF16).rearrange("p (c t) -> p c t", c=8)
        for g in range(2):
            kb.tr(ptkv[0:64, g, 0:n], qr[0:n, 512 + g * 64:512 + (g + 1) * 64], identb[0:n, 0:n], [qr, identb], [ptk])
        kb.v("dve", "tensor_copy", [ptk], [KT], out=KT[:, :, t0:t0 + n], in_=ptkv[0:64, 0:2, 0:n])
        if nq:
            ptq = kb.bank()
            ptqv = ptq.t.bitcast(BF16).rearrange("p (c t) -> p c t", c=8)
            for hh in range(8):
                kb.tr(ptqv[0:64, hh, 0:n], qr[0:n, hh * 64:(hh + 1) * 64], identb[0:n, 0:n], [qr, identb], [ptq])
            kb.act(QT[:, :, t0:t0 + nq], ptqv[0:64, :, 0:nq], AF.Copy, [ptq], [QT])
        kb.act(V1[0:n, i, :, 0:64], pkv[0:n, 128:256].rearrange("p (g c) -> p g c", c=64), AF.Copy, [pkv], [V1])
    fw.barrier()
    Es = [kb.sb([128, 4, 128], BF16, "E") for _ in range(3)]
    rcp = kb.sb([128, 8], F32, "rcp")
    Ob = kb.pbanks[0:4]
    Sb = kb.pbanks[4:6]
    ktiles = tiles_of(L, 128)
    steps = []
    for qi, (q0, nq) in enumerate(tiles_of(OWN, 128)):
        for g in range(2):
            for kt, (k0, nk) in enumerate(ktiles):
                steps.append((qi, q0, nq, g, kt, k0, nk))
    NS_ = len(steps)
    pend = {}

    def issue_s(i):
        qi, q0, nq, g, kt, k0, nk = steps[i]
        sT = Sb[i % 2]
        E = Es[i % 3]
        sTv = sT.t[0:nk, 0:4 * nq].rearrange("p (h q) -> p h q", h=4)
        kb.mm(sTv, KT[0:64, g, k0:k0 + nk], QT[0:64, 4 * g:4 * g + 4, q0:q0 + nq], [KT, QT], [sT])
        kb.act(E[0:nk, :, 0:nq], sTv, AF.Exp, [sT], [E], scale=0.125)

    def issue_pv(i):
        qi, q0, nq, g, kt, k0, nk = steps[i]
        E = Es[i % 3]
        for hh in range(4):
            kb.mm(Ob[hh][0:nq, 0:65], E[0:nk, hh, 0:nq], V1[0:nk, kt, g, :], [E, V1], [Ob[hh]],
                  start=kt == 0, stop=kt == len(ktiles) - 1)
        if kt == len(ktiles) - 1:
            for hh in range(4):
                kb.v("dve", "reciprocal", [Ob[hh]], [rcp], out=rcp[0:nq, hh:hh + 1], in_=Ob[hh][0:nq, 64:65])
                hd = 4 * g + hh
                kb.v("dve", "tensor_scalar", [Ob[hh], rcp], [y_b], out=y_b[0:nq, qi, hd * 64:(hd + 1) * 64],
                     in0=Ob[hh][0:nq, 0:64], scalar1=rcp[0:nq, hh:hh + 1], scalar2=None, op0=ALU.mult)

    issue_s(0)
    for i in range(NS_):
        if i + 1 < NS_:
            issue_s(i + 1)
        issue_pv(i)
    kb.dump("y_b", y_b, y_b[:, :, :], [128, NQT, 512], BF16)
    yb_d = kb.dram_tmp("yb_scr", [NQT, 128, 512], BF16)
    kb.dma(yb_d.t[0:16].rearrange("a p c -> p a c"), y_b[:, 0:16, :], [y_b], [yb_d])
    kb.dma(yb_d.t[16, 0:16, :], y_b[0:16, 16, :], [y_b], [yb_d])
    fw.barrier()
    kb.release(mA)
    if upto == "B":
        kb.finish()
        return nc, kb
    names = {"mu0": [128, RW_TILES], "mu1": [128, RW_TILES], "w2": [64, 2, 512], "a2": [64, 2, 512], "w0": [64, 2, 8],
             "a0": [64, 2, 8], "g2a": [128, 512], "g2b": [32, 512], "kkc": [64, 8], "kac": [64, 8], "rkc": [64, 8],
             "lng": [64, 512], "lnb": [64, 512],
             "masks": [64, 4, 64], "w_a": [512, 1024], "w_b": [512, 1024], "w_o": [1024, 1024], "w_ff1": [1024, 4096],
             "w_ff2": [4096, 1024], "gffn": [128, NCH], "gfin": [128, 1024]}
    D_ = {k: kb.dram_in(k, v) for k, v in names.items()}
    D_["w_in"] = win_d
    D_["h"] = h_d
    ybwd_d = kb.dram_tmp("ybwd_scr", [OWN, 512])
    ya_d = kb.dram_tmp("ya_scr", [OWN, 512])
    h2_d = kb.dram_tmp("h2_scr", [OWN, D])
    mC = kb.mark()
    rwkv_phase(kb, uT_d, ident, D_, ybwd_d, ya_d, wr_pre)
    if "ya" in kb.debug:
        o = kb.dram_out("dbg_ya", [OWN, 512])
        kb.out_keys.append(kb.dma(o.t, ya_d.t, [ya_d], [o]))
    fw.barrier()
    kb.release(mC)
    kb.hi = kb.AW
    if upto == "C":
        kb.finish()
        return nc, kb
    kb.from_top = True
    W1_pre = wload(kb, D_["w_ff1"], 1024, 0, 4096, "W1")
    kb.from_top = False
    merge_phase(kb, uT_d, identb, D_, ya_d, yb_d, h2_d)
    if "h2" in kb.debug:
        o = kb.dram_out("dbg_h2", [OWN, D])
        kb.out_keys.append(kb.dma(o.t, h2_d.t, [h2_d], [o]))
    fw.barrier()
    kb.release(mA)
    ffn_phase(kb, identb, D_, h2_d, out_d, W1_pre)

    kb.finish()
    return nc, kb


def _rope_tables():
    inv = (10000.0 ** (-np.arange(16, dtype=np.float32) * 2.0 / 32)).astype(np.float32)
    rows = (np.arange(64, dtype=np.float32)[:, None] * inv).astype(np.float32)
    cols = (np.arange(64, dtype=np.float32)[:, None] * inv).astype(np.float32)
    ang = np.zeros((L, 2, 16), np.float32)
    grid = np.stack([np.broadcast_to(rows[:, None, :], (64, 64, 16)),
                     np.broadcast_to(cols[None, :, :], (64, 64, 16))], axis=2).reshape(4096, 2, 16)
    ang[16:] = grid
    c, s_ = np.cos(ang).astype(np.float32), np.sin(ang).astype(np.float32)
    cosf = np.stack([c, c], axis=2).reshape(L, 64)
    sinf = np.stack([-s_, s_], axis=2).reshape(L, 64)
    return cosf, sinf


def prep_inputs(inputs, core):
    b, s = core // 2, core % 2
    x = np.asarray(inputs["x"], np.float32)
    meta = np.asarray(inputs["meta_tokens"], np.float32)
    hseq = np.concatenate([meta, x[b]], axis=0)
    if s == 1:
        hseq = hseq[::-1]
    m = {}
    m["h"] = np.ascontiguousarray(hseq)
    m["gmix"] = np.ascontiguousarray(np.asarray(inputs["mix_norm_g"], np.float32)[0].reshape(NCH, 128).T)
    m["ident"] = np.eye(128, dtype=np.float32)
    w_in = np.asarray(inputs["w_in"], np.float32)[0]
    if s == 1:
        w_in = w_in.copy()
        for base in (1536, 1664):
            a = w_in[:, base:base + 64].copy()
            w_in[:, base:base + 64] = w_in[:, base + 64:base + 128]
            w_in[:, base + 64:base + 128] = a
    m["w_in"] = np.ascontiguousarray(w_in)
    cosf, sinf = _rope_tables()
    if s == 1:
        cosf, sinf = cosf[::-1], sinf[::-1]
    m["ropeC"] = np.ascontiguousarray(cosf)
    m["ropeS"] = np.ascontiguousarray(sinf)
    G = lambda k: np.asarray(inputs[k], np.float32)[0]
    sh = G("rwkv_shift")
    w2, w0, a2, a0 = G("decay_w2"), G("decay_w0"), G("icl_a2"), G("icl_a0")
    if s == 1:
        sh = sh[::-1].copy()
        for base in (1536, 1664):
            a = sh[:, base:base + 64].copy()
            sh[:, base:base + 64] = sh[:, base + 64:base + 128]
            sh[:, base + 64:base + 128] = a
        w2, w0, a2, a0 = w2[::-1], w0[::-1], a2[::-1], a0[::-1]
    for nm, row in (("mu0", 0), ("mu1", 1)):
        arr = np.zeros((128, RW_TILES), np.float32)
        for j, (c0, w) in enumerate(RW_COLS):
            arr[:w, j] = sh[row, c0:c0 + w]
        m[nm] = arr
    m["w2"] = np.ascontiguousarray(w2.transpose(1, 0, 2))
    m["a2"] = np.ascontiguousarray(a2.transpose(1, 0, 2))
    m["w0"] = np.ascontiguousarray(w0.reshape(2, 8, 64).transpose(2, 0, 1))
    m["a0"] = np.ascontiguousarray(a0.reshape(2, 8, 64).transpose(2, 0, 1))
    g2 = G("gate_w2")
    m["g2a"] = np.ascontiguousarray(g2[0:128]); m["g2b"] = np.ascontiguousarray(g2[128:160])
    m["kkc"] = np.ascontiguousarray(G("k_k").reshape(8, 64).T)
    m["kac"] = np.ascontiguousarray(G("k_a").reshape(8, 64).T)
    m["rkc"] = np.ascontiguousarray(G("r_k").reshape(8, 64).T)
    m["lng"] = np.ascontiguousarray(np.tile(G("lnx_g")[None, :], (64, 1)))
    m["lnb"] = np.ascontiguousarray(np.tile(G("lnx_b")[None, :], (64, 1)))
    si, ti = np.meshgrid(np.arange(64), np.arange(64), indexing="ij")
    m["masks"] = np.ascontiguousarray(np.stack([si < ti, si <= ti, si > ti, si >= ti], 1).astype(np.float32))
    m["w_a"] = np.ascontiguousarray(G("w_branch_rwkv")); m["w_b"] = np.ascontiguousarray(G("w_branch_attn"))
    m["w_o"] = np.ascontiguousarray(G("w_out")); m["w_ff1"] = np.ascontiguousarray(G("w_ff1")); m["w_ff2"] = np.ascontiguousarray(G("w_ff2"))
    m["gffn"] = np.ascontiguousarray(G("ffn_norm_g").reshape(NCH, 128).T)
    m["gfin"] = np.ascontiguousarray(np.tile(np.asarray(inputs["final_norm_g"], np.float32)[None, :], (128, 1)))
    m["qg"] = np.ascontiguousarray(np.tile(np.asarray(inputs["q_norm_g"], np.float32)[0][None, :], (128, 1)))
    m["kg"] = np.ascontiguousarray(np.tile(np.asarray(inputs["k_norm_g"], np.float32)[0][None, :], (128, 1)))
    return m


_CACHE = {}


def kernel(**inputs):
    from concourse.bass_utils import run_bass_kernel_spmd
    if "nc" not in _CACHE:
        _CACHE["nc"] = build()
    nc, kb = _CACHE["nc"]
    in_maps = [prep_inputs(inputs, c) for c in range(8)]
    res = run_bass_kernel_spmd(nc, in_maps, core_ids=list(range(8)))
    out = np.zeros((4, 4096, D), np.float32)
    for c in range(8):
        b, s = c // 2, c % 2
        o = np.asarray(res.results[c]["out"])
        if s == 0:
            out[b, 0:2048] = o[16:2064]
        else:
            out[b, 2048:4096] = o[0:2048][::-1]
    return out
```

```python
import numpy as np
import concourse.bass as bass
import concourse.mybir as mybir

F32 = mybir.dt.float32
BF16 = mybir.dt.bfloat16
AF = mybir.ActivationFunctionType
ALU = mybir.AluOpType
AX = mybir.AxisListType

ENGS = ["pe", "dve", "act", "pool", "sp"]
SEM_WRAP = 2048
NSLOT = 8


class Res:
    __slots__ = ("name", "w", "rd")

    def __init__(self, name):
        self.name = name
        self.w = None
        self.rd = []


class Rec:
    __slots__ = ("fn", "deps", "sig", "dma", "signo", "pre")

    def __init__(self, fn, dma):
        self.fn = fn
        self.deps = set()
        self.sig = False
        self.dma = dma
        self.signo = None
        self.pre = None


class FW:
    def __init__(self, nc, same_engine_sync=True):
        self.nc = nc
        self.eng = {"pe": nc.tensor, "dve": nc.vector, "act": nc.scalar,
                    "pool": nc.gpsimd, "sp": nc.sync}
        self.ops = {e: [] for e in ENGS}
        self.ndma = {e: 0 for e in ENGS}
        self.same = same_engine_sync

    def op(self, e, fn, reads=(), writes=(), dma=False):
        lst = self.ops[e]
        idx = len(lst)
        rec = Rec(fn, None)
        if dma:
            rec.dma = self.ndma[e]
            self.ndma[e] += 1
        key = (e, idx)
        deps = set()
        for r in reads:
            if r.w is not None:
                deps.add(r.w)
        for w in writes:
            if w.w is not None:
                deps.add(w.w)
            for k in w.rd:
                deps.add(k)
        for d in deps:
            de, di = d
            drec = self.ops[de][di]
            if drec.dma is None:
                if de == e and (not self.same) and e != "pool":
                    continue
                if de == e and e == "pe":
                    continue
                drec.sig = True
            rec.deps.add(d)
        for r in reads:
            r.rd.append(key)
        for w in writes:
            w.w = key
            w.rd = []
        lst.append(rec)
        return key

    def barrier(self, resources=()):
        b = Res("barrier")
        keys = []
        for e in ENGS:
            for i in range(len(self.ops[e]) - 1, -1, -1):
                if self.ops[e][i].fn is not None:
                    keys.append((e, i))
                    break
        dkeys = []
        for e in ENGS:
            n = 0
            for i in range(len(self.ops[e]) - 1, -1, -1):
                if self.ops[e][i].dma is not None:
                    dkeys.append((e, i))
                    n += 1
                    if n >= NSLOT:
                        break
        self._pending_barrier = keys + dkeys
        for e in ENGS:
            rec = Rec(None, None)
            for d in keys + dkeys:
                de, di = d
                drec = self.ops[de][di]
                if drec.dma is None:
                    if de == e:
                        continue
                    drec.sig = True
                rec.deps.add(d)
            self.ops[e].append(rec)

    def emit(self, final_waits=()):
        nc = self.nc
        nsig = {}
        for e in ENGS:
            n = 0
            for rec in self.ops[e]:
                if rec.dma is None and rec.sig:
                    n += 1
                    rec.signo = n
            nsig[e] = n
        import contextlib
        with contextlib.ExitStack() as st:
            sems = {}
            for e in ENGS:
                k = (nsig[e] + SEM_WRAP - 1) // SEM_WRAP
                sems[e] = [st.enter_context(nc.semaphore(f"s_{e}_{i}")) for i in range(max(k, 1))]
            dsem = {}
            for e in ENGS:
                if self.ndma[e]:
                    dsem[e] = [st.enter_context(nc.semaphore(f"d_{e}_{i}")) for i in range(NSLOT)]
            block = st.enter_context(nc.Block())

            def target(dep):
                de, di = dep
                drec = self.ops[de][di]
                if drec.dma is not None:
                    n = drec.dma
                    return (dsem[de][n % NSLOT], 16 * (n // NSLOT + 1))
                s = drec.signo
                return (sems[de][(s - 1) // SEM_WRAP], (s - 1) % SEM_WRAP + 1)

            def run(e, eng):
                waited = {}
                for rec in self.ops[e]:
                    tg = {}
                    for dep in rec.deps:
                        sem, val = target(dep)
                        kk = id(sem)
                        if kk not in tg or tg[kk][1] < val:
                            tg[kk] = (sem, val)
                    if rec.dma is not None and rec.dma >= NSLOT:
                        n = rec.dma
                        sem = dsem[e][n % NSLOT]
                        val = 16 * (n // NSLOT)
                        kk = id(sem)
                        if kk not in tg or tg[kk][1] < val:
                            tg[kk] = (sem, val)
                    for kk, (sem, val) in tg.items():
                        if waited.get(kk, 0) >= val:
                            continue
                        waited[kk] = val
                        eng.wait_ge(sem, val)
                    if rec.fn is None:
                        continue
                    inst = rec.fn(eng)
                    if rec.dma is not None:
                        inst.then_inc(dsem[e][rec.dma % NSLOT], 16)
                    elif rec.sig:
                        s = rec.signo
                        inst.then_inc(sems[e][(s - 1) // SEM_WRAP], 1)

            @block.tensor
            def _(eng):
                run("pe", eng)

            @block.vector
            def _(eng):
                run("dve", eng)

            @block.scalar
            def _(eng):
                run("act", eng)

            @block.gpsimd
            def _(eng):
                run("pool", eng)

            @block.sync
            def _(eng):
                run("sp", eng)

import contextlib

L = 4112
OWN = 2064
D = 1024
NCH = 8
INW = 4768
RW = 1952
EPS = 1e-6


class Tl:
    def __init__(self, t, name):
        self.t = t
        self.r = Res(name)

    def __getitem__(self, k):
        return self.t[k]


class KB:
    def __init__(self, nc, debug=None):
        self.nc = nc
        self.fw = FW(nc)
        self.st = contextlib.ExitStack()
        self.debug = debug or []
        self.dbg_out = {}
        self.out_keys = []
        self.cnt = 0
        import os
        self.use_r32 = os.environ.get('USE_R32', '0') == '1'

    def setup_mem(self):
        self.AW = 49000
        self.arena = self.st.enter_context(self.nc.sbuf_tensor("arena", [128, self.AW], F32))
        self.psum = self.st.enter_context(self.nc.psum_tensor("psum", [128, 4096], F32))
        self.top = 0
        self.hi = self.AW
        self.pbanks = [self._pview(i) for i in range(8)]
        self.pb_i = 0

    def _pview(self, i):
        return Tl(self.psum[:, 512 * i:512 * (i + 1)], f"bank{i}")

    def bank(self):
        b = self.pbanks[self.pb_i % 8]
        self.pb_i += 1
        return b

    def mark(self):
        return self.top

    def release(self, m):
        self.top = m

    def sb(self, shape, dt, name=None):
        self.cnt += 1
        name = f"{name or 't'}_{self.cnt}"
        p = shape[0]
        n = 1
        for x in shape[1:]:
            n *= x
        words = n if dt == F32 else (n + 1) // 2
        if getattr(self, 'from_top', False):
            self.hi -= words
            off = self.hi
        else:
            off = self.top
            self.top += words
        assert self.top <= self.hi, f"arena overflow {self.top} {self.hi}"
        ap = self.arena[0:p, off:off + words]
        if dt != F32:
            ap = ap.bitcast(dt)
            if n % 2:
                ap = ap[:, 0:n]
        if len(shape) == 3:
            ap = ap.rearrange("p (a b) -> p a b", a=shape[1])
        elif len(shape) == 4:
            ap = ap.rearrange("p (a b c) -> p a b c", a=shape[1], b=shape[2])
        return Tl(ap, name)

    def dram_in(self, name, shape, dt=F32):
        t = self.nc.dram_tensor(name, list(shape), dt, kind="ExternalInput")
        return Tl(t.ap(), name)

    def dram_out(self, name, shape, dt=F32):
        t = self.nc.dram_tensor(name, list(shape), dt, kind="ExternalOutput")
        return Tl(t.ap(), name)

    def dram_tmp(self, name, shape, dt=F32):
        t = self.nc.dram_tensor(name, list(shape), dt, kind="Internal")
        return Tl(t.ap(), name)

    def dma(self, out_ap, in_ap, reads, writes, q="sp", **kw):
        return self.fw.op(q, lambda e: e.dma_start(out=out_ap, in_=in_ap, **kw),
                          reads=[x.r for x in reads], writes=[x.r for x in writes], dma=True)

    def mm(self, out_ap, lhsT_ap, rhs_ap, reads, writes, start=True, stop=True, r32=False):
        if r32 and self.use_r32 and lhsT_ap.dtype == F32 and rhs_ap.dtype == F32:
            lhsT_ap = lhsT_ap.bitcast(mybir.dt.float32r)
            rhs_ap = rhs_ap.bitcast(mybir.dt.float32r)
        return self.fw.op("pe", lambda e: e.matmul(out_ap, lhsT_ap, rhs_ap, start=start, stop=stop),
                          reads=[x.r for x in reads], writes=[x.r for x in writes])

    def tr(self, out_ap, in_ap, ident_ap, reads, writes):
        return self.fw.op("pe", lambda e: e.transpose(out_ap, in_ap, ident_ap),
                          reads=[x.r for x in reads], writes=[x.r for x in writes])

    def act(self, out_ap, in_ap, func, reads, writes, **kw):
        return self.fw.op("act", lambda e: e.activation(out=out_ap, in_=in_ap, func=func, **kw),
                          reads=[x.r for x in reads], writes=[x.r for x in writes])

    def v(self, eng, meth, reads, writes, *a, **kw):
        return self.fw.op(eng, lambda e: getattr(e, meth)(*a, **kw),
                          reads=[x.r for x in reads], writes=[x.r for x in writes])

    def dump(self, name, tl, ap, shape, dt=F32):
        if name not in self.debug:
            return
        o = self.dram_out("dbg_" + name, shape, dt)
        k = self.dma(o.t, ap, [tl], [o])
        self.out_keys.append(k)
        self.dbg_out[name] = "dbg_" + name

    def finish(self):
        fw = self.fw
        rec = Rec(None, None)
        for k in self.out_keys:
            rec.deps.add(k)
        fw.ops["sp"].append(rec)
        fw.emit()
        self.st.close()

S0 = float(np.exp(-0.5))
RW_COLS = [(j * 64, 64) for j in range(28)] + [(1792, 128), (1920, 32)]
RW_TILES = len(RW_COLS)


def rwkv_load_w(kb, D_):
    wr = kb.sb([128, NCH, RW], BF16, "wr")
    for a in range(0, RW, 488):
        kb.dma(wr[:, :, a:a + 488], D_["w_in"].t[:, a:a + 488].rearrange("(c p) n -> p c n", p=128), [D_["w_in"]], [wr], q="pool")
    return wr


def rwkv_phase(kb, uT_d, ident, D_, ybwd_d, ya_d, wr):
    fw = kb.fw

    def ld(name, shape):
        t = kb.sb(shape, F32, name)
        src = D_[name]
        kb.dma(t.t, src.t, [src], [t])
        return t
    mu0 = ld("mu0", [128, RW_TILES]); mu1 = ld("mu1", [128, RW_TILES])
    muc = kb.sb([128, RW_TILES], F32, "muc")
    kb.v("dve", "tensor_tensor", [mu0, mu1], [muc], out=muc[:, :], in0=mu0[:, :], in1=mu1[:, :], op=ALU.add)
    kb.v("dve", "tensor_scalar", [muc], [muc], out=muc[:, :], in0=muc[:, :], scalar1=-1.0, scalar2=1.0, op0=ALU.mult, op1=ALU.add)
    w2 = ld("w2", [64, 2, 512]); a2 = ld("a2", [64, 2, 512])
    w0 = ld("w0", [64, 2, 8]); a0 = ld("a0", [64, 2, 8])
    g2a = ld("g2a", [128, 512]); g2b = ld("g2b", [32, 512])
    kkc = ld("kkc", [64, 8]); kac = ld("kac", [64, 8]); rkc = ld("rkc", [64, 8])
    lng = ld("lng", [64, 512]); lnb = ld("lnb", [64, 512])
    masks = ld("masks", [64, 4, 64])
    oka = kb.sb([64, 8], F32, "oka"); kah = kb.sb([64, 8], F32, "kah")
    kb.v("dve", "tensor_scalar", [kac], [oka], out=oka[:, :], in0=kac[:, :], scalar1=-1.0, scalar2=1.0, op0=ALU.mult, op1=ALU.add)
    kb.v("dve", "tensor_scalar", [kac], [kah], out=kah[:, :], in0=kac[:, :], scalar1=0.5, scalar2=None, op0=ALU.mult)
    ones = kb.sb([64, 64], F32, "ones")
    kb.v("dve", "memset", [], [ones], ones[:, :], 1.0)
    Hs = [kb.sb([64, 8, 64], F32, "Hf"), kb.sb([64, 8, 64], F32, "Hb")]

    def T(shape, name, dt=F32):
        return kb.sb(shape, dt, name)

    class Slot:
        pass

    def make_slot(si):
        S = Slot()
        S.B = kb.pbanks[4 * si:4 * si + 4]
        S.uc = T([128, NCH, 66], 'uc', BF16)
        S.sh = [T([128, 64], 'sh') for _ in range(4)]
        S.P = T([128, RW_TILES, 66], "P")
        S.Z = T([128, RW_TILES, 64], "Z")
        S.Pt = [Tl(S.P.t[:, j, :], f'P{j}') for j in range(RW_TILES)]
        S.Zt = [Tl(S.Z.t[:, j, :], f'Z{j}') for j in range(RW_TILES)]
        S.thw = T([64, 64], "thw")
        S.sg = T([64, 8, 64], "sg")
        S.A = [T([64, 8, 64], "A0"), T([64, 8, 64], "A1")]
        S.CS = T([64, 8, 65], "CS")
        kb.v("dve", "memset", [], [S.CS], S.CS[:, :, :], 0.0)
        S.G = T([64, 8, 64], "G")
        S.Gp = T([64, 8, 64], "Gp")
        S.Gi = T([64, 8, 64], "Gi")
        S.gC = T([64, 8], "gC")
        S.kkr = T([64, 8, 64], "kkr")
        S.ksq = T([64, 8, 64], "ksq")
        S.rn = T([64, 8, 64], "rn")
        S.X1 = S.ksq
        S.X2 = S.rn
        S.t1 = T([64, 8, 64], "t1")
        S.kdir = T([64, 8, 64], "kdir")
        S.kka = T([64, 8, 64], "kka")
        S.QR = T([64, 8, 128], "QR")
        S.KdT = T([64, 8, 64], "KdT")
        S.AdT = T([64, 8, 64], "AdT")
        S.Kq_t = T([64, 512], "Kq_t")
        S.Ad_t = T([64, 512], "Ad_t")
        S.Kd_t = T([64, 512], "Kd_t")
        S.V_t = T([64, 512], "V_t")
        S.NTs = [S.kkr, S.Gi]
        S.Ns = [S.ksq, S.Gp]
        S.MkT = S.G
        S.PaT = S.sg
        S.PkT = S.rn
        S.X = T([64, 8, 128], "X")
        S.AT = S.t1
        S.RqpT = S.kdir
        S.ybt = S.Kd_t
        S.yt = S.Ad_t
        S.ysq = S.Kq_t
        S.st = T([64, 48], "st")
        S.sgd = T([128, 2, 64], "sgd")
        S.ee = S.kka
        S.yout = S.ysq
        return S
    slots = [make_slot(0), make_slot(1)]

    RS, KS, VS = slice(0, 8), slice(8, 16), slice(16, 24)
    NSTAGE = 52

    def bc(ap, shape, axis):
        return ap.unsqueeze(axis).to_broadcast(shape)

    def v3(bank, p, a):
        return bank.t[0:p, 0:512].rearrange("p (a b) -> p a b", a=a)

    def chunk(S, c0, cn, d, need_out, H, epilogue):
        B = S.B
        P, Z, uc = S.P, S.Z, S.uc
        Pt, Zt = S.Pt, S.Zt
        ns = 0
        wdt = 24 + d
        tiles = list(range(24)) + [wdt] + ([26, 27, 28, 29] if epilogue == "final" else [26 + d])
        lo, hi = max(c0 - 1, 0), min(c0 + cn + 1, L)
        n = hi - lo
        off = lo - (c0 - 1)
        kb.dma(uc[:, :, 0:n], uT_d.t[:, :, lo:hi], [uT_d], [uc])
        if off or (hi - (c0 - 1)) < cn + 2:
            kb.v("pool", "memset", [], Pt, P[:, :, :], 0.0)
        groups = [tiles[i:i + 2] for i in range(0, len(tiles), 2)]

        def shift(grp):
            for j in grp:
                w = RW_COLS[j][1]
                kb.act(Z[0:w, j, 0:cn], P[0:w, j, 1:cn + 1], AF.Copy, [Pt[j], muc], [Zt[j]], scale=muc[0:w, j:j + 1])
                kb.v("dve", "scalar_tensor_tensor", [Pt[j], mu0, Zt[j]], [Zt[j]], out=Z[0:w, j, 0:cn], in0=P[0:w, j, 0:cn],
                     scalar=mu0[0:w, j:j + 1], in1=Z[0:w, j, 0:cn], op0=ALU.mult, op1=ALU.add)
                kb.v("dve", "scalar_tensor_tensor", [Pt[j], mu1, Zt[j]], [Zt[j]], out=Z[0:w, j, 0:cn], in0=P[0:w, j, 2:cn + 2],
                     scalar=mu1[0:w, j:j + 1], in1=Z[0:w, j, 0:cn], op0=ALU.mult, op1=ALU.add)

        prev = None
        for gi, grp in enumerate(groups):
            pb_ = B[gi % 4]
            pv = pb_.t[:, 0:2 * 66].rearrange("p (a b) -> p a b", a=2)
            for ji, j in enumerate(grp):
                cc, w = RW_COLS[j]
                for c in range(NCH):
                    kb.mm(pv[0:w, ji, 0:n], wr[:, c, cc:cc + w], uc[:, c, 0:n], [wr, uc], [pb_],
                          start=c == 0, stop=c == NCH - 1)
            j0, j1 = grp[0], grp[-1]
            if j1 < 28 and j1 - j0 == len(grp) - 1:
                kb.act(P[0:64, j0:j1 + 1, off:off + n], pv[0:64, 0:len(grp), 0:n], AF.Copy, [pb_], [Pt[jj] for jj in grp])
            else:
                for ji, j in enumerate(grp):
                    w = RW_COLS[j][1]
                    kb.act(P[0:w, j, off:off + n], pv[0:w, ji, 0:n], AF.Copy, [pb_], [Pt[j]])
            if prev is not None:
                shift(prev)
            prev = grp
            ns += 1; yield
        shift(prev)
        ns += 1; yield
        while ns < 17:
            ns += 1; yield
        thw, sg, A, CS = S.thw, S.sg, S.A, S.CS
        kb.act(thw[:, 0:cn], Z[0:64, wdt, 0:cn], AF.Tanh, [Zt[wdt]], [thw])
        pl = B[0]
        plv = v3(pl, 64, 8)
        for h in range(8):
            kb.mm(plv[:, h, 0:cn], w2[:, d, 64 * h:64 * h + 64], thw[:, 0:cn], [w2, thw], [pl], r32=True)
        for h in range(8):
            kb.act(sg[:, h, 0:cn], plv[:, h, 0:cn], AF.Sigmoid, [pl, w0], [sg], bias=w0[:, d, h:h + 1])
        ns += 1; yield
        for dd in ((0, 1) if epilogue == "final" else (d,)):
            pa = B[1 + dd]
            pav = v3(pa, 64, 8)
            for h in range(8):
                kb.mm(pav[:, h, 0:cn], a2[:, dd, 64 * h:64 * h + 64], Z[0:64, 26 + dd, 0:cn], [a2, Zt[26 + dd]], [pa], r32=True)
            for h in range(8):
                kb.act(A[dd][:, h, 0:cn], pav[:, h, 0:cn], AF.Sigmoid, [pa, a0], [A[dd]], bias=a0[:, dd, h:h + 1])
        ns += 1; yield
        Ad_ = A[d]
        sh = [64, 8, cn]
        kkr, ksq, rn, t1, kdir, kka = S.kkr, S.ksq, S.rn, S.t1, S.kdir, S.kka
        kb.v("pool", "tensor_tensor", Zt[8:16] + [kkc], [kkr], out=kkr[:, :, 0:cn], in0=Z[0:64, KS, 0:cn], in1=bc(kkc[:, :], sh, 2), op=ALU.mult)
        kb.act(ksq[:, :, 0:cn], kkr[:, :, 0:cn], AF.Square, [kkr], [ksq])
        pn = B[3]
        pnv = v3(pn, 64, 8)
        for h in range(8):
            kb.mm(pnv[:, h, 0:cn], ones[:, :], ksq[:, h, 0:cn], [ones, ksq], [pn], r32=True)
        kb.act(rn[:, :, 0:cn], pnv[:, :, 0:cn], AF.Ln, [pn], [rn], bias=1e-18, scale=1.0)
        kb.act(rn[:, :, 0:cn], rn[:, :, 0:cn], AF.Exp, [rn], [rn], scale=-0.5)
        kb.v("dve", "tensor_tensor", [kkr, rn], [kkr], out=kkr[:, :, 0:cn], in0=kkr[:, :, 0:cn], in1=rn[:, :, 0:cn], op=ALU.mult)
        ns += 1; yield
        kb.v("pool", "tensor_tensor", [Ad_, kac], [t1], out=t1[:, :, 0:cn], in0=Ad_[:, :, 0:cn], in1=bc(kac[:, :], sh, 2), op=ALU.mult)
        kb.v("pool", "tensor_tensor", [t1, oka], [t1], out=t1[:, :, 0:cn], in0=t1[:, :, 0:cn], in1=bc(oka[:, :], sh, 2), op=ALU.add)
        kb.v("pool", "tensor_tensor", [t1] + Zt[8:16], [kdir], out=kdir[:, :, 0:cn], in0=t1[:, :, 0:cn], in1=Z[0:64, KS, 0:cn], op=ALU.mult)
        kb.v("dve", "tensor_tensor", [kkr, Ad_], [kka], out=kka[:, :, 0:cn], in0=kkr[:, :, 0:cn], in1=Ad_[:, :, 0:cn], op=ALU.mult)
        ns += 1; yield
        for h in range(8):
            kb.v("dve", "tensor_tensor_scan", [ones, sg], [CS], out=CS[:, h, 1:cn + 1], data0=ones[:, 0:cn], data1=sg[:, h, 0:cn],
                 initial=0.0, op0=ALU.mult, op1=ALU.add)
        ns += 1; yield
        G, Gp, Gi, gC, X1, X2 = S.G, S.Gp, S.Gi, S.gC, S.X1, S.X2
        if d == 0:
            kb.act(G[:, :, 0:cn], CS[:, :, 1:cn + 1], AF.Exp, [CS], [G], scale=-S0)
            kb.act(Gp[:, :, 0:cn], CS[:, :, 0:cn], AF.Exp, [CS], [Gp], scale=-S0)
            kb.act(Gi[:, :, 0:cn], CS[:, :, 1:cn + 1], AF.Exp, [CS], [Gi], scale=S0)
        else:
            totb = bc(CS[:, :, cn], sh, 2)
            kb.v("dve", "tensor_tensor", [CS], [X1], out=X1[:, :, 0:cn], in0=CS[:, :, 0:cn], in1=totb, op=ALU.subtract)
            kb.v("dve", "tensor_tensor", [CS], [X2], out=X2[:, :, 0:cn], in0=CS[:, :, 1:cn + 1], in1=totb, op=ALU.subtract)
            kb.act(G[:, :, 0:cn], X1[:, :, 0:cn], AF.Exp, [X1], [G], scale=S0)
            kb.act(Gp[:, :, 0:cn], X2[:, :, 0:cn], AF.Exp, [X2], [Gp], scale=S0)
            kb.act(Gi[:, :, 0:cn], X1[:, :, 0:cn], AF.Exp, [X1], [Gi], scale=-S0)
        kb.act(gC[:, :], CS[:, :, cn], AF.Exp, [CS], [gC], scale=-S0)
        ns += 1; yield
        QR, KdT, AdT = S.QR, S.KdT, S.AdT
        kb.v("dve", "tensor_tensor", [kkr, Gp], [QR], out=QR[:, :, 0:cn], in0=kkr[:, :, 0:cn], in1=Gp[:, :, 0:cn], op=ALU.mult)
        kb.v("pool", "tensor_tensor", Zt[0:8] + [G], [QR], out=QR[:, :, cn:2 * cn], in0=Z[0:64, RS, 0:cn], in1=G[:, :, 0:cn], op=ALU.mult)
        kb.v("pool", "tensor_tensor", [kdir, Gi], [KdT], out=KdT[:, :, 0:cn], in0=kdir[:, :, 0:cn], in1=Gi[:, :, 0:cn], op=ALU.mult)
        kb.v("dve", "tensor_tensor", [kka, Gi], [AdT], out=AdT[:, :, 0:cn], in0=kka[:, :, 0:cn], in1=Gi[:, :, 0:cn], op=ALU.mult)
        ns += 1; yield
        Kq_t, Ad_t, Kd_t, V_t = S.Kq_t, S.Ad_t, S.Kd_t, S.V_t
        for bi, (src, si, dst) in enumerate(((QR, 0, Kq_t), (AdT, 0, Ad_t), (KdT, 0, Kd_t), (Z, 16, V_t))):
            pt = B[bi]
            for h in range(8):
                kb.tr(pt[0:cn, h * 64:(h + 1) * 64], src[0:64, si + h, 0:cn], ident[0:64, 0:64], [Zt[16 + h] if src is Z else src, ident], [pt])
            if bi % 2 == 0:
                kb.act(dst[0:cn, :], pt[0:cn, :], AF.Copy, [pt], [dst])
            else:
                kb.v("dve", "tensor_copy", [pt], [dst], out=dst[0:cn, :], in_=pt[0:cn, :])
                ns += 1; yield
        mS, mI, mN = (0, 1, 2) if d == 0 else (2, 3, 0)
        rw = 2 * cn if need_out else cn
        pSA = [B[0], B[1]]; pSK = [B[2], B[3]]
        NTs, Ns, MkT, PaT, PkT, X = S.NTs, S.Ns, S.MkT, S.PaT, S.PkT, S.X
        for h in range(8):
            hv = pSA[h // 4].t[0:cn, 0:4 * rw].rearrange("p (h s) -> p h s", h=4)
            kb.mm(hv[:, h % 4, :], AdT[:, h, 0:cn], QR[:, h, 0:rw], [AdT, QR], [pSA[h // 4]], r32=True)
            hv2 = pSK[h // 4].t[0:cn, 0:4 * rw].rearrange("p (h s) -> p h s", h=4)
            kb.mm(hv2[:, h % 4, :], KdT[:, h, 0:cn], QR[:, h, 0:rw], [KdT, QR], [pSK[h // 4]], r32=True)
        NT, Nm = NTs[0], Ns[0]
        for half in range(2):
            hv = pSA[half].t[0:cn, 0:4 * rw].rearrange("p (h s) -> p h s", h=4)
            hv2 = pSK[half].t[0:cn, 0:4 * rw].rearrange("p (h s) -> p h s", h=4)
            hs = slice(4 * half, 4 * half + 4)
            kb.v("dve", "tensor_tensor", [pSA[half], masks], [NT], out=NT[0:cn, hs, 0:cn], in0=hv[:, :, 0:cn],
                 in1=bc(masks[0:cn, mS, 0:cn], [cn, 4, cn], 1), op=ALU.mult)
            kb.v("dve", "tensor_tensor", [pSK[half], masks], [MkT], out=MkT[0:cn, hs, 0:cn], in0=hv2[:, :, 0:cn],
                 in1=bc(masks[0:cn, mS, 0:cn], [cn, 4, cn], 1), op=ALU.mult)
            if need_out:
                kb.v("dve", "tensor_tensor", [pSA[half], masks], [PaT], out=PaT[0:cn, hs, 0:cn], in0=hv[:, :, cn:2 * cn],
                     in1=bc(masks[0:cn, mI, 0:cn], [cn, 4, cn], 1), op=ALU.mult)
                kb.v("dve", "tensor_tensor", [pSK[half], masks], [PkT], out=PkT[0:cn, hs, 0:cn], in0=hv2[:, :, cn:2 * cn],
                     in1=bc(masks[0:cn, mI, 0:cn], [cn, 4, cn], 1), op=ALU.mult)
        ns += 1; yield
        pN = B[0]
        pNv = pN.t[0:cn, 0:8 * cn].rearrange("p (h s) -> p h s", h=8)
        pW = B[1]
        pWv = v3(pW, cn, 8)
        for h in range(8):
            kb.mm(pNv[:, h, :], QR[:, h, 0:cn], AdT[:, h, 0:cn], [QR, AdT], [pN], r32=True)
        for h in range(8):
            kb.mm(pWv[:, h, :], MkT[0:cn, h, 0:cn], V_t[0:cn, 64 * h:64 * h + 64], [MkT, V_t], [pW], r32=True)
        kb.v("dve", "tensor_tensor", [pN, masks], [Nm], out=Nm[0:cn, :, 0:cn], in0=pNv,
             in1=bc(masks[0:cn, mN, 0:cn], [cn, 8, cn], 1), op=ALU.mult)
        kb.v("pool", "tensor_copy", [Kq_t], [X], out=X[0:cn, :, 0:64], in_=Kq_t[0:cn, :].rearrange("p (h v) -> p h v", h=8))
        kb.act(X[0:cn, :, 64:128], pWv, AF.Copy, [pW], [X], scale=-1.0)
        ns += 1; yield
        nlev = 5 if cn > 32 else (4 if cn > 16 else 3)
        pF = [B[2], B[3]]

        def factor(PT_, sign):
            for h in range(8):
                fv = v3(pF[h // 4], cn, 4)
                kb.mm(fv[:, h % 4, :], PT_[0:cn, h, 0:cn], X[0:cn, h, :], [PT_, X], [pF[h // 4]], r32=True)
            for half in range(2):
                fv = v3(pF[half], cn, 4)
                hs = slice(4 * half, 4 * half + 4)
                kb.v("dve", "tensor_tensor", [X, pF[half]], [X], out=X[0:cn, hs, :], in0=X[0:cn, hs, :], in1=fv,
                     op=ALU.subtract if sign < 0 else ALU.add)
        factor(NT, -1)
        ns += 1; yield
        cur = 0
        for lvl in range(5):
            if lvl < nlev:
                NTc, Nc = NTs[cur], Ns[cur]
                NTn, Nn = NTs[1 - cur], Ns[1 - cur]
                pa_, pb2 = B[0], B[1]
                pav = pa_.t[0:cn, 0:8 * cn].rearrange("p (h s) -> p h s", h=8)
                pbv = pb2.t[0:cn, 0:8 * cn].rearrange("p (h s) -> p h s", h=8)
                last = lvl == nlev - 1
                for h in range(8):
                    kb.mm(pav[:, h, :], Nc[0:cn, h, 0:cn], NTc[0:cn, h, 0:cn], [Nc, NTc], [pa_], r32=True)
                    if not last:
                        kb.mm(pbv[:, h, :], NTc[0:cn, h, 0:cn], Nc[0:cn, h, 0:cn], [Nc, NTc], [pb2], r32=True)
                kb.act(NTn[0:cn, :, 0:cn], pav, AF.Copy, [pa_], [NTn])
                if not last:
                    kb.act(Nn[0:cn, :, 0:cn], pbv, AF.Copy, [pb2], [Nn])
            ns += 1; yield
            if lvl < nlev:
                factor(NTn, +1)
                cur = 1 - cur
            ns += 1; yield
        AT, RqpT = S.AT, S.RqpT
        pA = B[0]
        pAv = v3(pA, 64, 8)
        for h in range(8):
            kb.mm(pAv[:, h, :], X[0:cn, h, 0:64], Ad_t[0:cn, 64 * h:64 * h + 64], [X, Ad_t], [pA], r32=True)
        kb.v("dve", "tensor_tensor", [ident, pA], [AT], out=AT[:, :, :], in0=bc(ident[0:64, 0:64], [64, 8, 64], 1), in1=pAv, op=ALU.subtract)
        if need_out:
            pR = B[1]
            pRv = pR.t[0:64, 0:8 * cn].rearrange("p (a b) -> p a b", a=8)
            for h in range(8):
                kb.mm(pRv[:, h, :], X[0:cn, h, 0:64], PaT[0:cn, h, 0:cn], [X, PaT], [pR], r32=True)
            kb.v("dve", "tensor_tensor", [QR, pR], [RqpT], out=RqpT[:, :, 0:cn], in0=QR[:, :, cn:2 * cn], in1=pRv, op=ALU.subtract)
        ns += 1; yield
        if need_out:
            pY = B[2]
            pYv = v3(pY, cn, 8)
            for h in range(8):
                kb.mm(pYv[:, h, :], RqpT[:, h, 0:cn], H[:, h, :], [RqpT, H], [pY], start=True, stop=False, r32=True)
                kb.mm(pYv[:, h, :], PkT[0:cn, h, 0:cn], V_t[0:cn, 64 * h:64 * h + 64], [PkT, V_t], [pY], start=False, stop=False, r32=True)
                kb.mm(pYv[:, h, :], PaT[0:cn, h, 0:cn], X[0:cn, h, 64:128], [PaT, X], [pY], start=False, stop=True, r32=True)
        pH = B[3]
        pHv = v3(pH, 64, 8)
        for h in range(8):
            kb.mm(pHv[:, h, :], AT[:, h, :], H[:, h, :], [AT, H], [pH], start=True, stop=False, r32=True)
            kb.mm(pHv[:, h, :], Kd_t[0:cn, 64 * h:64 * h + 64], V_t[0:cn, 64 * h:64 * h + 64], [Kd_t, V_t], [pH], start=False, stop=False, r32=True)
            kb.mm(pHv[:, h, :], Ad_t[0:cn, 64 * h:64 * h + 64], X[0:cn, h, 64:128], [Ad_t, X], [pH], start=False, stop=True, r32=True)
        kb.v("dve", "tensor_tensor", [pH, gC], [H], out=H[:, :, :], in0=pHv, in1=bc(gC[:, :], [64, 8, 64], 2), op=ALU.mult)
        ns += 1; yield
        ybt, yt, ysq, st, sgd, ee, yout = S.ybt, S.yt, S.ysq, S.st, S.sgd, S.ee, S.yout
        if epilogue == "store":
            kb.act(yt[0:cn, :], pY[0:cn, :], AF.Copy, [pY], [yt])
            kb.dma(ybwd_d.t[c0:c0 + cn, :], yt[0:cn, :], [yt], [ybwd_d])
        elif epilogue == "final":
            kb.dma(ybt[0:cn, :], ybwd_d.t[c0:c0 + cn, :], [ybwd_d], [ybt])
            kb.v("dve", "tensor_tensor", [pY, ybt], [yt], out=yt[0:cn, :], in0=pY[0:cn, :], in1=ybt[0:cn, :], op=ALU.add)
            y3 = yt[0:cn, :].rearrange("p (h v) -> p h v", h=8)
            kb.v("dve", "tensor_reduce", [yt], [st], out=st[0:cn, 0:8], in_=y3, axis=AX.X, op=ALU.add)
            kb.act(ysq[0:cn, :], yt[0:cn, :], AF.Square, [yt], [ysq])
            kb.v("dve", "tensor_reduce", [ysq], [st], out=st[0:cn, 8:16], in_=ysq[0:cn, :].rearrange("p (h v) -> p h v", h=8), axis=AX.X, op=ALU.add)
            kb.v("dve", "tensor_scalar", [st], [st], out=st[0:cn, 16:24], in0=st[0:cn, 0:8], scalar1=1.0 / 64, scalar2=None, op0=ALU.mult)
            kb.v("dve", "tensor_tensor", [st], [st], out=st[0:cn, 24:32], in0=st[0:cn, 16:24], in1=st[0:cn, 16:24], op=ALU.mult)
            kb.v("dve", "scalar_tensor_tensor", [st], [st], out=st[0:cn, 32:40], in0=st[0:cn, 8:16], scalar=1.0 / 64, in1=st[0:cn, 24:32],
                 op0=ALU.mult, op1=ALU.subtract)
            kb.act(st[0:cn, 40:48], st[0:cn, 32:40], AF.Sqrt, [st], [st], bias=64e-5, scale=1.0)
            kb.v("dve", "reciprocal", [st], [st], out=st[0:cn, 40:48], in_=st[0:cn, 40:48])
            kb.v("dve", "tensor_tensor", [yt, st], [yt], out=y3, in0=y3, in1=bc(st[0:cn, 16:24], [cn, 8, 64], 2), op=ALU.subtract)
            kb.v("dve", "tensor_tensor", [yt, st], [yt], out=y3, in0=y3, in1=bc(st[0:cn, 40:48], [cn, 8, 64], 2), op=ALU.mult)
            kb.v("pool", "tensor_tensor", [yt, lng], [yt], out=yt[0:cn, :], in0=yt[0:cn, :], in1=lng[0:cn, :], op=ALU.mult)
            kb.v("pool", "tensor_tensor", [yt, lnb], [yt], out=yt[0:cn, :], in0=yt[0:cn, :], in1=lnb[0:cn, :], op=ALU.add)
        ns += 1; yield
        if epilogue == "final":
            kb.v("pool", "tensor_tensor", [A[0], A[1]], [ee], out=ee[:, :, 0:cn], in0=A[0][:, :, 0:cn], in1=A[1][:, :, 0:cn], op=ALU.add)
            kb.v("pool", "tensor_tensor", [ee, kah], [ee], out=ee[:, :, 0:cn], in0=ee[:, :, 0:cn], in1=bc(kah[:, :], sh, 2), op=ALU.mult)
            kb.v("pool", "tensor_tensor", [ee, oka], [ee], out=ee[:, :, 0:cn], in0=ee[:, :, 0:cn], in1=bc(oka[:, :], sh, 2), op=ALU.add)
            kb.v("pool", "tensor_tensor", [ee] + Zt[8:16], [ee], out=ee[:, :, 0:cn], in0=ee[:, :, 0:cn], in1=Z[0:64, KS, 0:cn], op=ALU.mult)
            kb.v("pool", "tensor_tensor", [ee] + Zt[0:8], [ee], out=ee[:, :, 0:cn], in0=ee[:, :, 0:cn], in1=Z[0:64, RS, 0:cn], op=ALU.mult)
            kb.v("pool", "tensor_tensor", [ee, rkc], [ee], out=ee[:, :, 0:cn], in0=ee[:, :, 0:cn], in1=bc(rkc[:, :], sh, 2), op=ALU.mult)
            pS = B[0]
            for h in range(8):
                kb.mm(pS[0:cn, h:h + 1], ee[:, h, 0:cn], ones[:, 0:1], [ee, ones], [pS])
            kb.act(st[0:cn, 0:8], pS[0:cn, 0:8], AF.Copy, [pS], [st])
            kb.v("dve", "tensor_tensor", [V_t, st], [ysq], out=ysq[0:cn, :].rearrange("p (h v) -> p h v", h=8),
                 in0=V_t[0:cn, :].rearrange("p (h v) -> p h v", h=8), in1=bc(st[0:cn, 0:8], [cn, 8, 64], 2), op=ALU.mult)
            kb.v("dve", "tensor_tensor", [yt, ysq], [yt], out=yt[0:cn, :], in0=yt[0:cn, :], in1=ysq[0:cn, :], op=ALU.add)
            kb.act(sgd[:, 0, 0:cn], Z[:, 28, 0:cn], AF.Sigmoid, [Zt[28]], [sgd])
            kb.act(sgd[0:32, 1, 0:cn], Z[0:32, 29, 0:cn], AF.Sigmoid, [Zt[29]], [sgd])
            pG = B[1]
            kb.mm(pG[0:cn, :], sgd[:, 0, 0:cn], g2a[:, :], [sgd, g2a], [pG], start=True, stop=False, r32=True)
            kb.mm(pG[0:cn, :], sgd[0:32, 1, 0:cn], g2b[:, :], [sgd, g2b], [pG], start=False, stop=True, r32=True)
            kb.v("dve", "tensor_tensor", [yt, pG], [yout], out=yout[0:cn, :], in0=yt[0:cn, :], in1=pG[0:cn, :], op=ALU.mult)
            kb.dma(ya_d.t[c0:c0 + cn, :], yout[0:cn, :], [yout], [ya_d])
        ns += 1; yield
        while ns < NSTAGE:
            ns += 1; yield
        assert ns == NSTAGE, ns

    def chain(S, lst, d, H, epilogue):
        for (c0, cn) in lst:
            yield from chunk(S, c0, cn, d, epilogue is not None, H, epilogue)

    chunks = tiles_of(2048, 64) + [(2048, 16)] + [(2064 + a, b) for a, b in tiles_of(2048, 64)]
    own = [c for c in chunks if c[0] < OWN]
    oth = [c for c in chunks if c[0] >= OWN]
    Hf, Hb = Hs
    kb.v("dve", "memset", [], [Hf], Hf[:, :, :], 0.0)
    kb.v("dve", "memset", [], [Hb], Hb[:, :, :], 0.0)
    g0 = chain(slots[0], own, 0, Hf, "store")
    g1 = chain(slots[1], list(reversed(oth)), 1, Hb, None)
    step = 0
    d0 = d1 = False
    while not (d0 and d1):
        if not d0:
            try:
                next(g0)
            except StopIteration:
                d0 = True
        if step >= 18 and not d1:
            try:
                next(g1)
            except StopIteration:
                d1 = True
        step += 1
    rown = list(reversed(own))
    g0 = chain(slots[0], rown[0::2], 1, Hb, "final")
    g1 = chain(slots[1], rown[1::2], 1, Hb, "final")
    step = 0
    d0 = d1 = False
    while not (d0 and d1):
        if not d0:
            try:
                next(g0)
            except StopIteration:
                d0 = True
        if step >= 18 and not d1:
            try:
                next(g1)
            except StopIteration:
                d1 = True
        step += 1


def wload(kb, dram, rows, c0, c1, name):
    kch = rows // 128
    t = kb.sb([128, kch, c1 - c0], BF16, name)
    step = 1024
    for a in range(c0, c1, step):
        b = min(a + step, c1)
        kb.dma(t[:, :, a - c0:b - c0], dram.t[:, a:b].rearrange("(c p) n -> p c n", p=128), [dram], [t], q="pool")
    return t


def merge_phase(kb, uT_d, identb, D_, ya_d, yb_d, h2_d):
    B = kb.pbanks
    Wa = wload(kb, D_["w_a"], 512, 0, 1024, "Wa")
    Wb = wload(kb, D_["w_b"], 512, 0, 1024, "Wb")
    Wo = wload(kb, D_["w_o"], 1024, 0, 1024, "Wo")
    Wg = wload(kb, D_["w_in"], 1024, 2720, 4768, "Wg")

    class Set:
        pass

    def mk(si):
        S = Set()
        S.B = B[4 * si:4 * si + 4]
        S.yaf = kb.sb([128, 512], F32, "yaf"); S.yab = kb.sb([128, 512], BF16, "yab"); S.ybb = kb.sb([128, 512], BF16, "ybb")
        S.yaT = kb.sb([128, 4, 128], BF16, "yaT"); S.ybT = kb.sb([128, 4, 128], BF16, "ybT")
        S.sgA = kb.sb([128, 1024], F32, "sgA"); S.sgB = kb.sb([128, 1024], F32, "sgB")
        S.mg = kb.sb([128, 1024], BF16, "mg"); S.mT = kb.sb([128, 8, 128], BF16, "mT")
        S.xt = kb.sb([128, 1024], F32, "xt2"); S.h2 = kb.sb([128, 1024], F32, "h2")
        S.ut = kb.sb([128, NCH, 128], BF16, "utm")
        return S
    sets = [mk(0), mk(1)]

    def tile(S, i, t0, n):
        B_ = S.B
        yaf, yab, ybb, yaT, ybT, sgA, sgB, mg, mT, xt, h2, ut = (S.yaf, S.yab, S.ybb, S.yaT, S.ybT, S.sgA, S.sgB, S.mg, S.mT,
                                                                  S.xt, S.h2, S.ut)
        kb.dma(ut[:, :, 0:n], uT_d.t[:, :, t0:t0 + n], [uT_d], [ut])
        kb.dma(yaf[0:n, :], ya_d.t[t0:t0 + n, :], [ya_d], [yaf])
        kb.dma(ybb[0:n, :], yb_d.t[i, 0:n, :], [yb_d], [ybb])
        kb.dma(xt[0:n, :], D_["h"].t[t0:t0 + n, :], [D_["h"]], [xt])
        yield
        for gi, sg_ in enumerate((sgA, sgB)):
            for hb in range(2):
                bk = B_[hb]
                col = gi * 1024 + hb * 512
                for c in range(NCH):
                    kb.mm(bk[0:n, :], ut[:, c, 0:n], Wg[:, c, col:col + 512], [ut, Wg], [bk], start=c == 0, stop=c == NCH - 1)
                kb.act(sg_[0:n, hb * 512:(hb + 1) * 512], bk[0:n, :], AF.Sigmoid, [bk], [sg_])
            yield
        kb.v("dve", "tensor_copy", [yaf], [yab], out=yab[0:n, :], in_=yaf[0:n, :])
        for src, dst, bk in ((yab, yaT, B_[2]), (ybb, ybT, B_[3])):
            pv = bk.t.bitcast(BF16).rearrange("p (c t) -> p c t", c=8)
            for k in range(4):
                kb.tr(pv[:, k, 0:n], src[0:n, k * 128:(k + 1) * 128], identb[0:n, 0:n], [src, identb], [bk])
            kb.act(dst[:, :, 0:n], pv[:, 0:4, 0:n], AF.Copy, [bk], [dst])
        yield
        for (yT_, W_, sg_) in ((yaT, Wa, sgA), (ybT, Wb, sgB)):
            for hb in range(2):
                bk = B_[hb]
                for k in range(4):
                    kb.mm(bk[0:n, :], yT_[:, k, 0:n], W_[:, k, hb * 512:(hb + 1) * 512], [yT_, W_], [bk], start=k == 0, stop=k == 3)
                kb.v("dve", "tensor_tensor", [sg_, bk], [sg_], out=sg_[0:n, hb * 512:(hb + 1) * 512],
                     in0=sg_[0:n, hb * 512:(hb + 1) * 512], in1=bk[0:n, :], op=ALU.mult)
            yield
        kb.v("pool", "tensor_tensor", [sgA, sgB], [mg], out=mg[0:n, :], in0=sgA[0:n, :], in1=sgB[0:n, :], op=ALU.add)
        bk = B_[2]
        pv = bk.t.bitcast(BF16).rearrange("p (c t) -> p c t", c=8)
        for k in range(8):
            kb.tr(pv[:, k, 0:n], mg[0:n, k * 128:(k + 1) * 128], identb[0:n, 0:n], [mg, identb], [bk])
        kb.act(mT[:, :, 0:n], pv[:, :, 0:n], AF.Copy, [bk], [mT])
        yield
        for hb in range(2):
            bk = B_[hb]
            for k in range(8):
                kb.mm(bk[0:n, :], mT[:, k, 0:n], Wo[:, k, hb * 512:(hb + 1) * 512], [mT, Wo], [bk], start=k == 0, stop=k == 7)
            kb.v("dve", "tensor_tensor", [xt, bk], [h2], out=h2[0:n, hb * 512:(hb + 1) * 512], in0=xt[0:n, hb * 512:(hb + 1) * 512],
                 in1=bk[0:n, :], op=ALU.add)
        kb.dma(h2_d.t[t0:t0 + n, :], h2[0:n, :], [h2], [h2_d])
        yield

    def chain(S, lst):
        for (i, t0, n) in lst:
            yield from tile(S, i, t0, n)
    tl = [(i, t0, n) for i, (t0, n) in enumerate(tiles_of(OWN, 128))]
    g0, g1 = chain(sets[0], tl[0::2]), chain(sets[1], tl[1::2])
    step, d0, d1 = 0, False, False
    while not (d0 and d1):
        if not d0:
            try:
                next(g0)
            except StopIteration:
                d0 = True
        if step >= 4 and not d1:
            try:
                next(g1)
            except StopIteration:
                d1 = True
        step += 1


def ffn_phase(kb, identb, D_, h2_d, out_d, W1):
    B = kb.pbanks
    GT = 256
    W2 = wload(kb, D_["w_ff2"], 4096, 0, 1024, "W2")
    gf = kb.sb([128, NCH], F32, "gffn"); gfin = kb.sb([128, 1024], F32, "gfin")
    kb.dma(gf.t, D_["gffn"].t, [D_["gffn"]], [gf])
    kb.dma(gfin.t, D_["gfin"].t, [D_["gfin"]], [gfin])
    xs = kb.sb([128, 1024], BF16, "xs2")
    ss = kb.sb([128, 8], F32, "ss2"); ss2 = kb.sb([128, 8], F32, "ss3"); nT = kb.sb([128, 8, GT], BF16, "nT")
    frs = [kb.sb([128, GT], F32, "fr") for _ in range(2)]
    f2T = kb.sb([128, 32, GT], BF16, "f2T")
    h3 = kb.sb([128, 1024], F32, "h3"); ot = kb.sb([128, 1024], F32, "ot")
    junk = kb.sb([128, 1024], BF16, "junkf")
    groups = tiles_of(OWN, GT)
    h2g = [[kb.sb([128, 1024], F32, "h2f") for _ in range(2)] for _ in range(2)]
    nTs = [nT, kb.sb([128, 8, GT], BF16, "nT2")]

    def norm_part(gi):
        g0, gn = groups[gi]
        for si, (o0, n) in enumerate(tiles_of(gn, 128)):
            t0 = g0 + o0
            h2 = h2g[gi % 2][si]
            nT_ = nTs[gi % 2]
            kb.dma(h2[0:n, :], h2_d.t[t0:t0 + n, :], [h2_d], [h2])
            kb.act(junk[0:n, :], h2[0:n, :], AF.Square, [h2], [junk, ss], accum_out=ss[0:n, 0:1])
            kb.act(ss[0:n, 1:2], ss[0:n, 0:1], AF.Sqrt, [ss], [ss], scale=1.0 / D, bias=EPS)
            kb.v("dve", "reciprocal", [ss], [ss], out=ss[0:n, 2:3], in_=ss[0:n, 1:2])
            kb.v("dve", "tensor_scalar", [h2, ss], [xs], out=xs[0:n, :], in0=h2[0:n, :], scalar1=ss[0:n, 2:3], scalar2=None, op0=ALU.mult)
            bk = B[si]
            pv = bk.t.bitcast(BF16).rearrange("p (c t) -> p c t", c=8)
            for c in range(NCH):
                kb.tr(pv[:, c, 0:n], xs[0:n, c * 128:(c + 1) * 128], identb[0:n, 0:n], [xs, identb], [bk])
            kb.v("dve", "tensor_tensor", [bk, gf], [nT_], out=nT_[:, :, o0:o0 + n], in0=pv[:, :, 0:n],
                 in1=gf[:, :].unsqueeze(2).to_broadcast([128, NCH, n]), op=ALU.mult)

    norm_part(0)
    for gi, (g0, gn) in enumerate(groups):
        subs = tiles_of(gn, 128)
        nT_ = nTs[gi % 2]
        for ft in range(32):
            bk = B[2 + ft % 3]
            fr = frs[ft % 2]
            for c in range(NCH):
                kb.mm(bk[:, 0:gn], W1[:, c, ft * 128:(ft + 1) * 128], nT_[:, c, 0:gn], [W1, nT_], [bk], start=c == 0, stop=c == NCH - 1)
            kb.act(fr[:, 0:gn], bk[:, 0:gn], AF.Relu, [bk], [fr])
            kb.v("dve", "tensor_tensor", [fr], [f2T], out=f2T[:, ft, 0:gn], in0=fr[:, 0:gn], in1=fr[:, 0:gn], op=ALU.mult)
        if gi + 1 < len(groups):
            norm_part(gi + 1)
        for si, (o0, n) in enumerate(subs):
            t0 = g0 + o0
            h2 = h2g[gi % 2][si]
            for hb in range(2):
                bk = B[5 + (hb + 2 * si) % 3]
                for k in range(32):
                    kb.mm(bk[0:n, :], f2T[:, k, o0:o0 + n], W2[:, k, hb * 512:(hb + 1) * 512], [f2T, W2], [bk], start=k == 0, stop=k == 31)
                kb.v("dve", "tensor_tensor", [h2, bk], [h3], out=h3[0:n, hb * 512:(hb + 1) * 512], in0=h2[0:n, hb * 512:(hb + 1) * 512],
                     in1=bk[0:n, :], op=ALU.add)
            kb.act(junk[0:n, :], h3[0:n, :], AF.Square, [h3], [junk, ss2], accum_out=ss2[0:n, 4:5])
            kb.act(ss2[0:n, 5:6], ss2[0:n, 4:5], AF.Sqrt, [ss2], [ss2], scale=1.0 / D, bias=EPS)
            kb.v("dve", "reciprocal", [ss2], [ss2], out=ss2[0:n, 6:7], in_=ss2[0:n, 5:6])
            kb.v("dve", "scalar_tensor_tensor", [h3, ss2, gfin], [ot], out=ot[0:n, :], in0=h3[0:n, :], scalar=ss2[0:n, 6:7], in1=gfin[0:n, :],
                 op0=ALU.mult, op1=ALU.mult)
            k = kb.dma(out_d.t[t0:t0 + n, :], ot[0:n, :], [ot], [])
            kb.out_keys.append(k)

def tiles_of(n, step):
    out = []
    t = 0
    while t < n:
        out.append((t, min(step, n - t)))
        t += step
    return out


def build(debug=None, upto="all"):
    nc = bass.Bass("TRN2", target_bir_lowering=False)
    kb = KB(nc, debug)
    kb.setup_mem()
    fw = kb.fw
    h_d = kb.dram_in("h", [L, D])
    gmix_d = kb.dram_in("gmix", [128, NCH])
    ident_d = kb.dram_in("ident", [128, 128])
    out_d = kb.dram_out("out", [OWN, D])

    ident = kb.sb([128, 128], F32, "ident")
    identb = kb.sb([128, 128], BF16, "identb")
    gmix = kb.sb([128, NCH], F32, "gmix")
    kb.dma(ident[:, :], ident_d[:, :], [ident_d], [ident])
    kb.dma(gmix[:, :], gmix_d[:, :], [gmix_d], [gmix])
    kb.v("dve", "tensor_copy", [ident], [identb], out=identb[:, :], in_=ident[:, :])

    win_d = kb.dram_in("w_in", [D, INW])
    kb.from_top = True
    wr_pre = rwkv_load_w(kb, {"w_in": win_d})
    kb.from_top = False
    mA = kb.mark()
    uT_d = kb.dram_tmp("uT_scr", [128, NCH, L], BF16)
    xts = [kb.sb([128, D], F32, "xt") for _ in range(2)]
    xss = [kb.sb([128, D], BF16, "xs") for _ in range(2)]
    junk = kb.sb([128, D], BF16, "junk")
    sss = [kb.sb([128, 4], F32, "ss") for _ in range(2)]
    uts = [kb.sb([128, NCH, 128], BF16, "ut") for _ in range(2)]

    def phase_a_tile(i, t0, n):
        xt = xts[i % 2]
        xs = xss[i % 2]
        ss = sss[i % 2]
        ut = uts[i % 2]
        kb.dma(xt[0:n, :], h_d[t0:t0 + n, :], [h_d], [xt])
        kb.act(junk[0:n, :], xt[0:n, :], AF.Square, [xt], [junk, ss], accum_out=ss[0:n, 0:1])
        kb.act(ss[0:n, 1:2], ss[0:n, 0:1], AF.Sqrt, [ss], [ss], scale=1.0 / D, bias=EPS)
        kb.v("dve", "reciprocal", [ss], [ss], out=ss[0:n, 2:3], in_=ss[0:n, 1:2])
        kb.v("dve", "tensor_scalar", [xt, ss], [xs], out=xs[0:n, :], in0=xt[0:n, :],
             scalar1=ss[0:n, 2:3], scalar2=None, op0=ALU.mult)
        pb = kb.bank()
        pv = pb.t.bitcast(BF16).rearrange("p (c t) -> p c t", c=NCH)
        for c in range(NCH):
            kb.tr(pv[:, c, 0:n], xs[0:n, c * 128:(c + 1) * 128], identb[0:n, 0:n], [xs, identb], [pb])
        kb.v("dve", "tensor_tensor", [pb, gmix], [ut], out=ut[:, :, 0:n], in0=pv[:, :, 0:n],
             in1=gmix[:, :].unsqueeze(2).to_broadcast([128, NCH, n]), op=ALU.mult)
        kb.dma(uT_d.t[:, :, t0:t0 + n], ut[:, :, 0:n], [ut], [uT_d])
        return ut

    ropeC_d = kb.dram_in("ropeC", [L, 64])
    ropeS_d = kb.dram_in("ropeS", [L, 64])
    qg_d = kb.dram_in("qg", [128, 64])
    kg_d = kb.dram_in("kg", [128, 64])
    NT = 33
    NQT = 17
    mB0 = kb.mark()
    y_b = kb.sb([128, NQT, 512], BF16, "y_b")
    mB = kb.mark()
    wqkv = kb.sb([128, NCH, 768], BF16, "wqkv")
    kb.dma(wqkv[:, :, :], win_d.t[:, 1952:2720].rearrange("(c p) n -> p c n", p=128), [win_d], [wqkv], q="pool")
    rcs = [kb.sb([128, 64], F32, "rc") for _ in range(2)]
    rss = [kb.sb([128, 64], F32, "rs") for _ in range(2)]
    qgb = kb.sb([128, 64], F32, "qgb")
    kgb = kb.sb([128, 64], F32, "kgb")
    kb.dma(qgb[:, :], qg_d[:, :], [qg_d], [qgb])
    kb.dma(kgb[:, :], kg_d[:, :], [kg_d], [kgb])
    QT = kb.sb([64, 8, OWN], BF16, "QT")
    KT = kb.sb([64, 2, L], BF16, "KT")
    V1 = kb.sb([128, NT, 2, 65], BF16, "V1")
    kb.v("pool", "memset", [], [V1], V1[:, :, :, 64:65], 1.0)
    sqs = [kb.sb([128, 640], F32, "sq") for _ in range(1)]
    qns = [kb.sb([128, 640], F32, "qn") for _ in range(1)]
    t1s = [kb.sb([128, 640], F32, "t1") for _ in range(1)]
    t2s = [kb.sb([128, 640], F32, "t2") for _ in range(1)]
    qrs = [kb.sb([128, 640], BF16, "qr") for _ in range(2)]
    sms = [kb.sb([128, 32], F32, "sm") for _ in range(2)]
    for i, (t0, n) in enumerate(tiles_of(L, 128)):
        nq = max(0, min(n, OWN - t0))
        ut = phase_a_tile(i, t0, n)
        sq, qn, t1, t2, qr, sm = sqs[0], qns[0], t1s[0], t2s[0], qrs[i % 2], sms[i % 2]
        pq = kb.bank()
        pkv = kb.bank()
        rc, rs = rcs[i % 2], rss[i % 2]
        kb.dma(rc[0:n, :], ropeC_d.t[t0:t0 + n, :], [ropeC_d], [rc])
        kb.dma(rs[0:n, :], ropeS_d.t[t0:t0 + n, :], [ropeS_d], [rs])
        for c in range(NCH):
            if nq:
                kb.mm(pq[0:n, 0:512], ut[:, c, 0:n], wqkv[:, c, 0:512], [ut, wqkv], [pq], start=c == 0, stop=c == NCH - 1)
            kb.mm(pkv[0:n, 0:256], ut[:, c, 0:n], wqkv[:, c, 512:768], [ut, wqkv], [pkv], start=c == 0, stop=c == NCH - 1)
        h0 = 0 if nq else 8
        c0 = h0 * 64
        nh = 10 - h0
        if nq:
            kb.act(sq[0:n, 0:512], pq[0:n, 0:512], AF.Square, [pq], [sq])
        kb.act(sq[0:n, 512:640], pkv[0:n, 0:128], AF.Square, [pkv], [sq])
        kb.v("dve", "tensor_reduce", [sq], [sm], out=sm[0:n, h0:10],
             in_=sq[0:n, c0:640].rearrange("p (h c) -> p h c", c=64), axis=AX.X, op=ALU.add)
        kb.act(sm[0:n, 10 + h0:20], sm[0:n, h0:10], AF.Sqrt, [sm], [sm], scale=1.0 / 64, bias=EPS)
        kb.v("dve", "reciprocal", [sm], [sm], out=sm[0:n, 20 + h0:30], in_=sm[0:n, 10 + h0:20])
        if nq:
            kb.v("dve", "tensor_tensor", [pq, sm], [qn], out=qn[0:n, 0:512].rearrange("p (h c) -> p h c", c=64),
                 in0=pq[0:n, 0:512].rearrange("p (h c) -> p h c", c=64),
                 in1=sm[0:n, 20:28].unsqueeze(2).to_broadcast([n, 8, 64]), op=ALU.mult)
            kb.v("dve", "tensor_tensor", [qn, qgb], [qn], out=qn[0:n, 0:512].rearrange("p (h c) -> p h c", c=64),
                 in0=qn[0:n, 0:512].rearrange("p (h c) -> p h c", c=64),
                 in1=qgb[0:n, :].unsqueeze(1).to_broadcast([n, 8, 64]), op=ALU.mult)
        kb.v("dve", "tensor_tensor", [pkv, sm], [qn], out=qn[0:n, 512:640].rearrange("p (h c) -> p h c", c=64),
             in0=pkv[0:n, 0:128].rearrange("p (h c) -> p h c", c=64),
             in1=sm[0:n, 28:30].unsqueeze(2).to_broadcast([n, 2, 64]), op=ALU.mult)
        kb.v("dve", "tensor_tensor", [qn, kgb], [qn], out=qn[0:n, 512:640].rearrange("p (h c) -> p h c", c=64),
             in0=qn[0:n, 512:640].rearrange("p (h c) -> p h c", c=64),
             in1=kgb[0:n, :].unsqueeze(1).to_broadcast([n, 2, 64]), op=ALU.mult)
        kb.v("dve", "tensor_tensor", [qn, rc], [t1], out=t1[0:n, c0:640].rearrange("p (h c) -> p h c", c=64),
             in0=qn[0:n, c0:640].rearrange("p (h c) -> p h c", c=64),
             in1=rc[0:n, :].unsqueeze(1).to_broadcast([n, nh, 64]), op=ALU.mult)
        for hf in range(2):
            o_v = t2[0:n, c0:640].rearrange("p (h a x f) -> p h a x f", a=2, x=2, f=16)[:, :, :, hf, :]
            i_v = qn[0:n, c0:640].rearrange("p (h a x f) -> p h a x f", a=2, x=2, f=16)[:, :, :, 1 - hf, :]
            s_v = rs[0:n, :].rearrange("p (a x f) -> p a x f", a=2, x=2)[:, :, hf, :].unsqueeze(1).to_broadcast([n, nh, 2, 16])
            kb.v("dve", "tensor_tensor", [qn, rs], [t2], out=o_v, in0=i_v, in1=s_v, op=ALU.mult)
        kb.v("dve", "tensor_tensor", [t1, t2], [qr], out=qr[0:n, c0:640], in0=t1[0:n, c0:640], in1=t2[0:n, c0:640], op=ALU.add)
        ptk = kb.bank()
        ptkv = ptk.t.bitcast(BF16).rearrange("p (c t) -> p c t", c=8)
        for g in range(2):
            kb.tr(ptkv[0:64, g, 0:n], qr[0:n, 512 + g * 64:512 + (g + 1) * 64], identb[0:n, 0:n], [qr, identb], [ptk])
        kb.v("dve", "tensor_copy", [ptk], [KT], out=KT[:, :, t0:t0 + n], in_=ptkv[0:64, 0:2, 0:n])
        if nq:
            ptq = kb.bank()
            ptqv = ptq.t.bitcast(BF16).rearrange("p (c t) -> p c t", c=8)
            for hh in range(8):
                kb.tr(ptqv[0:64, hh, 0:n], qr[0:n, hh * 64:(hh + 1) * 64], identb[0:n, 0:n], [qr, identb], [ptq])
            kb.act(QT[:, :, t0:t0 + nq], ptqv[0:64, :, 0:nq], AF.Copy, [ptq], [QT])
        kb.act(V1[0:n, i, :, 0:64], pkv[0:n, 128:256].rearrange("p (g c) -> p g c", c=64), AF.Copy, [pkv], [V1])
    fw.barrier()
    Es = [kb.sb([128, 4, 128], BF16, "E") for _ in range(3)]
    rcp = kb.sb([128, 8], F32, "rcp")
    Ob = kb.pbanks[0:4]
    Sb = kb.pbanks[4:6]
    ktiles = tiles_of(L, 128)
    steps = []
    for qi, (q0, nq) in enumerate(tiles_of(OWN, 128)):
        for g in range(2):
            for kt, (k0, nk) in enumerate(ktiles):
                steps.append((qi, q0, nq, g, kt, k0, nk))
    NS_ = len(steps)
    pend = {}

    def issue_s(i):
        qi, q0, nq, g, kt, k0, nk = steps[i]
        sT = Sb[i % 2]
        E = Es[i % 3]
        sTv = sT.t[0:nk, 0:4 * nq].rearrange("p (h q) -> p h q", h=4)
        kb.mm(sTv, KT[0:64, g, k0:k0 + nk], QT[0:64, 4 * g:4 * g + 4, q0:q0 + nq], [KT, QT], [sT])
        kb.act(E[0:nk, :, 0:nq], sTv, AF.Exp, [sT], [E], scale=0.125)

    def issue_pv(i):
        qi, q0, nq, g, kt, k0, nk = steps[i]
        E = Es[i % 3]
        for hh in range(4):
            kb.mm(Ob[hh][0:nq, 0:65], E[0:nk, hh, 0:nq], V1[0:nk, kt, g, :], [E, V1], [Ob[hh]],
                  start=kt == 0, stop=kt == len(ktiles) - 1)
        if kt == len(ktiles) - 1:
            for hh in range(4):
                kb.v("dve", "reciprocal", [Ob[hh]], [rcp], out=rcp[0:nq, hh:hh + 1], in_=Ob[hh][0:nq, 64:65])
                hd = 4 * g + hh
                kb.v("dve", "tensor_scalar", [Ob[hh], rcp], [y_b], out=y_b[0:nq, qi, hd * 64:(hd + 1) * 64],
                     in0=Ob[hh][0:nq, 0:64], scalar1=rcp[0:nq, hh:hh + 1], scalar2=None, op0=ALU.mult)

    issue_s(0)
    for i in range(NS_):
        if i + 1 < NS_:
            issue_s(i + 1)
        issue_pv(i)
    kb.dump("y_b", y_b, y_b[:, :, :], [128, NQT, 512], BF16)
    yb_d = kb.dram_tmp("yb_scr", [NQT, 128, 512], BF16)
    kb.dma(yb_d.t[0:16].rearrange("a p c -> p a c"), y_b[:, 0:16, :], [y_b], [yb_d])
    kb.dma(yb_d.t[16, 0:16, :], y_b[0:16, 16, :], [y_b], [yb_d])
    fw.barrier()
    kb.release(mA)
    if upto == "B":
        kb.finish()
        return nc, kb
    names = {"mu0": [128, RW_TILES], "mu1": [128, RW_TILES], "w2": [64, 2, 512], "a2": [64, 2, 512], "w0": [64, 2, 8],
             "a0": [64, 2, 8], "g2a": [128, 512], "g2b": [32, 512], "kkc": [64, 8], "kac": [64, 8], "rkc": [64, 8],
             "lng": [64, 512], "lnb": [64, 512],
             "masks": [64, 4, 64], "w_a": [512, 1024], "w_b": [512, 1024], "w_o": [1024, 1024], "w_ff1": [1024, 4096],
             "w_ff2": [4096, 1024], "gffn": [128, NCH], "gfin": [128, 1024]}
    D_ = {k: kb.dram_in(k, v) for k, v in names.items()}
    D_["w_in"] = win_d
    D_["h"] = h_d
    ybwd_d = kb.dram_tmp("ybwd_scr", [OWN, 512])
    ya_d = kb.dram_tmp("ya_scr", [OWN, 512])
    h2_d = kb.dram_tmp("h2_scr", [OWN, D])
    mC = kb.mark()
    rwkv_phase(kb, uT_d, ident, D_, ybwd_d, ya_d, wr_pre)
    if "ya" in kb.debug:
        o = kb.dram_out("dbg_ya", [OWN, 512])
        kb.out_keys.append(kb.dma(o.t, ya_d.t, [ya_d], [o]))
    fw.barrier()
    kb.release(mC)
    kb.hi = kb.AW
    if upto == "C":
        kb.finish()
        return nc, kb
    kb.from_top = True
    W1_pre = wload(kb, D_["w_ff1"], 1024, 0, 4096, "W1")
    kb.from_top = False
    merge_phase(kb, uT_d, identb, D_, ya_d, yb_d, h2_d)
    if "h2" in kb.debug:
        o = kb.dram_out("dbg_h2", [OWN, D])
        kb.out_keys.append(kb.dma(o.t, h2_d.t, [h2_d], [o]))
    fw.barrier()
    kb.release(mA)
    ffn_phase(kb, identb, D_, h2_d, out_d, W1_pre)

    kb.finish()
    return nc, kb


def _rope_tables():
    inv = (10000.0 ** (-np.arange(16, dtype=np.float32) * 2.0 / 32)).astype(np.float32)
    rows = (np.arange(64, dtype=np.float32)[:, None] * inv).astype(np.float32)
    cols = (np.arange(64, dtype=np.float32)[:, None] * inv).astype(np.float32)
    ang = np.zeros((L, 2, 16), np.float32)
    grid = np.stack([np.broadcast_to(rows[:, None, :], (64, 64, 16)),
                     np.broadcast_to(cols[None, :, :], (64, 64, 16))], axis=2).reshape(4096, 2, 16)
    ang[16:] = grid
    c, s_ = np.cos(ang).astype(np.float32), np.sin(ang).astype(np.float32)
    cosf = np.stack([c, c], axis=2).reshape(L, 64)
    sinf = np.stack([-s_, s_], axis=2).reshape(L, 64)
    return cosf, sinf


def prep_inputs(inputs, core):
    b, s = core // 2, core % 2
    x = np.asarray(inputs["x"], np.float32)
    meta = np.asarray(inputs["meta_tokens"], np.float32)
    hseq = np.concatenate([meta, x[b]], axis=0)
    if s == 1:
        hseq = hseq[::-1]
    m = {}
    m["h"] = np.ascontiguousarray(hseq)
    m["gmix"] = np.ascontiguousarray(np.asarray(inputs["mix_norm_g"], np.float32)[0].reshape(NCH, 128).T)
    m["ident"] = np.eye(128, dtype=np.float32)
    w_in = np.asarray(inputs["w_in"], np.float32)[0]
    if s == 1:
        w_in = w_in.copy()
        for base in (1536, 1664):
            a = w_in[:, base:base + 64].copy()
            w_in[:, base:base + 64] = w_in[:, base + 64:base + 128]
            w_in[:, base + 64:base + 128] = a
    m["w_in"] = np.ascontiguousarray(w_in)
    cosf, sinf = _rope_tables()
    if s == 1:
        cosf, sinf = cosf[::-1], sinf[::-1]
    m["ropeC"] = np.ascontiguousarray(cosf)
    m["ropeS"] = np.ascontiguousarray(sinf)
    G = lambda k: np.asarray(inputs[k], np.float32)[0]
    sh = G("rwkv_shift")
    w2, w0, a2, a0 = G("decay_w2"), G("decay_w0"), G("icl_a2"), G("icl_a0")
    if s == 1:
        sh = sh[::-1].copy()
        for base in (1536, 1664):
            a = sh[:, base:base + 64].copy()
            sh[:, base:base + 64] = sh[:, base + 64:base + 128]
            sh[:, base + 64:base + 128] = a
        w2, w0, a2, a0 = w2[::-1], w0[::-1], a2[::-1], a0[::-1]
    for nm, row in (("mu0", 0), ("mu1", 1)):
        arr = np.zeros((128, RW_TILES), np.float32)
        for j, (c0, w) in enumerate(RW_COLS):
            arr[:w, j] = sh[row, c0:c0 + w]
        m[nm] = arr
    m["w2"] = np.ascontiguousarray(w2.transpose(1, 0, 2))
    m["a2"] = np.ascontiguousarray(a2.transpose(1, 0, 2))
    m["w0"] = np.ascontiguousarray(w0.reshape(2, 8, 64).transpose(2, 0, 1))
    m["a0"] = np.ascontiguousarray(a0.reshape(2, 8, 64).transpose(2, 0, 1))
    g2 = G("gate_w2")
    m["g2a"] = np.ascontiguousarray(g2[0:128]); m["g2b"] = np.ascontiguousarray(g2[128:160])
    m["kkc"] = np.ascontiguousarray(G("k_k").reshape(8, 64).T)
    m["kac"] = np.ascontiguousarray(G("k_a").reshape(8, 64).T)
    m["rkc"] = np.ascontiguousarray(G("r_k").reshape(8, 64).T)
    m["lng"] = np.ascontiguousarray(np.tile(G("lnx_g")[None, :], (64, 1)))
    m["lnb"] = np.ascontiguousarray(np.tile(G("lnx_b")[None, :], (64, 1)))
    si, ti = np.meshgrid(np.arange(64), np.arange(64), indexing="ij")
    m["masks"] = np.ascontiguousarray(np.stack([si < ti, si <= ti, si > ti, si >= ti], 1).astype(np.float32))
    m["w_a"] = np.ascontiguousarray(G("w_branch_rwkv")); m["w_b"] = np.ascontiguousarray(G("w_branch_attn"))
    m["w_o"] = np.ascontiguousarray(G("w_out")); m["w_ff1"] = np.ascontiguousarray(G("w_ff1")); m["w_ff2"] = np.ascontiguousarray(G("w_ff2"))
    m["gffn"] = np.ascontiguousarray(G("ffn_norm_g").reshape(NCH, 128).T)
    m["gfin"] = np.ascontiguousarray(np.tile(np.asarray(inputs["final_norm_g"], np.float32)[None, :], (128, 1)))
    m["qg"] = np.ascontiguousarray(np.tile(np.asarray(inputs["q_norm_g"], np.float32)[0][None, :], (128, 1)))
    m["kg"] = np.ascontiguousarray(np.tile(np.asarray(inputs["k_norm_g"], np.float32)[0][None, :], (128, 1)))
    return m


_CACHE = {}


def kernel(**inputs):
    from concourse.bass_utils import run_bass_kernel_spmd
    if "nc" not in _CACHE:
        _CACHE["nc"] = build()
    nc, kb = _CACHE["nc"]
    in_maps = [prep_inputs(inputs, c) for c in range(8)]
    res = run_bass_kernel_spmd(nc, in_maps, core_ids=list(range(8)))
    out = np.zeros((4, 4096, D), np.float32)
    for c in range(8):
        b, s = c // 2, c % 2
        o = np.asarray(res.results[c]["out"])
        if s == 0:
            out[b, 0:2048] = o[16:2064]
        else:
            out[b, 2048:4096] = o[0:2048][::-1]
    return out
```

```python
import numpy as np
import concourse.bass as bass
import concourse.mybir as mybir

F32 = mybir.dt.float32
BF16 = mybir.dt.bfloat16
AF = mybir.ActivationFunctionType
ALU = mybir.AluOpType
AX = mybir.AxisListType

ENGS = ["pe", "dve", "act", "pool", "sp"]
SEM_WRAP = 2048
NSLOT = 8


class Res:
    __slots__ = ("name", "w", "rd")

    def __init__(self, name):
        self.name = name
        self.w = None
        self.rd = []


class Rec:
    __slots__ = ("fn", "deps", "sig", "dma", "signo", "pre")

    def __init__(self, fn, dma):
        self.fn = fn
        self.deps = set()
        self.sig = False
        self.dma = dma
        self.signo = None
        self.pre = None


class FW:
    def __init__(self, nc, same_engine_sync=True):
        self.nc = nc
        self.eng = {"pe": nc.tensor, "dve": nc.vector, "act": nc.scalar,
                    "pool": nc.gpsimd, "sp": nc.sync}
        self.ops = {e: [] for e in ENGS}
        self.ndma = {e: 0 for e in ENGS}
        self.same = same_engine_sync

    def op(self, e, fn, reads=(), writes=(), dma=False):
        lst = self.ops[e]
        idx = len(lst)
        rec = Rec(fn, None)
        if dma:
            rec.dma = self.ndma[e]
            self.ndma[e] += 1
        key = (e, idx)
        deps = set()
        for r in reads:
            if r.w is not None:
                deps.add(r.w)
        for w in writes:
            if w.w is not None:
                deps.add(w.w)
            for k in w.rd:
                deps.add(k)
        for d in deps:
            de, di = d
            drec = self.ops[de][di]
            if drec.dma is None:
                if de == e and (not self.same) and e != "pool":
                    continue
                if de == e and e == "pe":
                    continue
                drec.sig = True
            rec.deps.add(d)
        for r in reads:
            r.rd.append(key)
        for w in writes:
            w.w = key
            w.rd = []
        lst.append(rec)
        return key

    def barrier(self, resources=()):
        b = Res("barrier")
        keys = []
        for e in ENGS:
            for i in range(len(self.ops[e]) - 1, -1, -1):
                if self.ops[e][i].fn is not None:
                    keys.append((e, i))
                    break
        dkeys = []
        for e in ENGS:
            n = 0
            for i in range(len(self.ops[e]) - 1, -1, -1):
                if self.ops[e][i].dma is not None:
                    dkeys.append((e, i))
                    n += 1
                    if n >= NSLOT:
                        break
        self._pending_barrier = keys + dkeys
        for e in ENGS:
            rec = Rec(None, None)
            for d in keys + dkeys:
                de, di = d
                drec = self.ops[de][di]
                if drec.dma is None:
                    if de == e:
                        continue
                    drec.sig = True
                rec.deps.add(d)
            self.ops[e].append(rec)

    def emit(self, final_waits=()):
        nc = self.nc
        nsig = {}
        for e in ENGS:
            n = 0
            for rec in self.ops[e]:
                if rec.dma is None and rec.sig:
                    n += 1
                    rec.signo = n
            nsig[e] = n
        import contextlib
        with contextlib.ExitStack() as st:
            sems = {}
            for e in ENGS:
                k = (nsig[e] + SEM_WRAP - 1) // SEM_WRAP
                sems[e] = [st.enter_context(nc.semaphore(f"s_{e}_{i}")) for i in range(max(k, 1))]
            dsem = {}
            for e in ENGS:
                if self.ndma[e]:
                    dsem[e] = [st.enter_context(nc.semaphore(f"d_{e}_{i}")) for i in range(NSLOT)]
            block = st.enter_context(nc.Block())

            def target(dep):
                de, di = dep
                drec = self.ops[de][di]
                if drec.dma is not None:
                    n = drec.dma
                    return (dsem[de][n % NSLOT], 16 * (n // NSLOT + 1))
                s = drec.signo
                return (sems[de][(s - 1) // SEM_WRAP], (s - 1) % SEM_WRAP + 1)

            def run(e, eng):
                waited = {}
                for rec in self.ops[e]:
                    tg = {}
                    for dep in rec.deps:
                        sem, val = target(dep)
                        kk = id(sem)
                        if kk not in tg or tg[kk][1] < val:
                            tg[kk] = (sem, val)
                    if rec.dma is not None and rec.dma >= NSLOT:
                        n = rec.dma
                        sem = dsem[e][n % NSLOT]
                        val = 16 * (n // NSLOT)
                        kk = id(sem)
                        if kk not in tg or tg[kk][1] < val:
                            tg[kk] = (sem, val)
                    for kk, (sem, val) in tg.items():
                        if waited.get(kk, 0) >= val:
                            continue
                        waited[kk] = val
                        eng.wait_ge(sem, val)
                    if rec.fn is None:
                        continue
                    inst = rec.fn(eng)
                    if rec.dma is not None:
                        inst.then_inc(dsem[e][rec.dma % NSLOT], 16)
                    elif rec.sig:
                        s = rec.signo
                        inst.then_inc(sems[e][(s - 1) // SEM_WRAP], 1)

            @block.tensor
            def _(eng):
                run("pe", eng)

            @block.vector
            def _(eng):
                run("dve", eng)

            @block.scalar
            def _(eng):
                run("act", eng)

            @block.gpsimd
            def _(eng):
                run("pool", eng)

            @block.sync
            def _(eng):
                run("sp", eng)

import contextlib

L = 4112
OWN = 2064
D = 1024
NCH = 8
INW = 4768
RW = 1952
EPS = 1e-6


class Tl:
    def __init__(self, t, name):
        self.t = t
        self.r = Res(name)

    def __getitem__(self, k):
        return self.t[k]


class KB:
    def __init__(self, nc, debug=None):
        self.nc = nc
        self.fw = FW(nc)
        self.st = contextlib.ExitStack()
        self.debug = debug or []
        self.dbg_out = {}
        self.out_keys = []
        self.cnt = 0
        import os
        self.use_r32 = os.environ.get('USE_R32', '0') == '1'

    def setup_mem(self):
        self.AW = 49000
        self.arena = self.st.enter_context(self.nc.sbuf_tensor("arena", [128, self.AW], F32))
        self.psum = self.st.enter_context(self.nc.psum_tensor("psum", [128, 4096], F32))
        self.top = 0
        self.hi = self.AW
        self.pbanks = [self._pview(i) for i in range(8)]
        self.pb_i = 0

    def _pview(self, i):
        return Tl(self.psum[:, 512 * i:512 * (i + 1)], f"bank{i}")

    def bank(self):
        b = self.pbanks[self.pb_i % 8]
        self.pb_i += 1
        return b

    def mark(self):
        return self.top

    def release(self, m):
        self.top = m

    def sb(self, shape, dt, name=None):
        self.cnt += 1
        name = f"{name or 't'}_{self.cnt}"
        p = shape[0]
        n = 1
        for x in shape[1:]:
            n *= x
        words = n if dt == F32 else (n + 1) // 2
        if getattr(self, 'from_top', False):
            self.hi -= words
            off = self.hi
        else:
            off = self.top
            self.top += words
        assert self.top <= self.hi, f"arena overflow {self.top} {self.hi}"
        ap = self.arena[0:p, off:off + words]
        if dt != F32:
            ap = ap.bitcast(dt)
            if n % 2:
                ap = ap[:, 0:n]
        if len(shape) == 3:
            ap = ap.rearrange("p (a b) -> p a b", a=shape[1])
        elif len(shape) == 4:
            ap = ap.rearrange("p (a b c) -> p a b c", a=shape[1], b=shape[2])
        return Tl(ap, name)

    def dram_in(self, name, shape, dt=F32):
        t = self.nc.dram_tensor(name, list(shape), dt, kind="ExternalInput")
        return Tl(t.ap(), name)

    def dram_out(self, name, shape, dt=F32):
        t = self.nc.dram_tensor(name, list(shape), dt, kind="ExternalOutput")
        return Tl(t.ap(), name)

    def dram_tmp(self, name, shape, dt=F32):
        t = self.nc.dram_tensor(name, list(shape), dt, kind="Internal")
        return Tl(t.ap(), name)

    def dma(self, out_ap, in_ap, reads, writes, q="sp", **kw):
        return self.fw.op(q, lambda e: e.dma_start(out=out_ap, in_=in_ap, **kw),
                          reads=[x.r for x in reads], writes=[x.r for x in writes], dma=True)

    def mm(self, out_ap, lhsT_ap, rhs_ap, reads, writes, start=True, stop=True, r32=False):
        if r32 and self.use_r32 and lhsT_ap.dtype == F32 and rhs_ap.dtype == F32:
            lhsT_ap = lhsT_ap.bitcast(mybir.dt.float32r)
            rhs_ap = rhs_ap.bitcast(mybir.dt.float32r)
        return self.fw.op("pe", lambda e: e.matmul(out_ap, lhsT_ap, rhs_ap, start=start, stop=stop),
                          reads=[x.r for x in reads], writes=[x.r for x in writes])

    def tr(self, out_ap, in_ap, ident_ap, reads, writes):
        return self.fw.op("pe", lambda e: e.transpose(out_ap, in_ap, ident_ap),
                          reads=[x.r for x in reads], writes=[x.r for x in writes])

    def act(self, out_ap, in_ap, func, reads, writes, **kw):
        return self.fw.op("act", lambda e: e.activation(out=out_ap, in_=in_ap, func=func, **kw),
                          reads=[x.r for x in reads], writes=[x.r for x in writes])

    def v(self, eng, meth, reads, writes, *a, **kw):
        return self.fw.op(eng, lambda e: getattr(e, meth)(*a, **kw),
                          reads=[x.r for x in reads], writes=[x.r for x in writes])

    def dump(self, name, tl, ap, shape, dt=F32):
        if name not in self.debug:
            return
        o = self.dram_out("dbg_" + name, shape, dt)
        k = self.dma(o.t, ap, [tl], [o])
        self.out_keys.append(k)
        self.dbg_out[name] = "dbg_" + name

    def finish(self):
        fw = self.fw
        rec = Rec(None, None)
        for k in self.out_keys:
            rec.deps.add(k)
        fw.ops["sp"].append(rec)
        fw.emit()
        self.st.close()

S0 = float(np.exp(-0.5))
RW_COLS = [(j * 64, 64) for j in range(28)] + [(1792, 128), (1920, 32)]
RW_TILES = len(RW_COLS)


def rwkv_load_w(kb, D_):
    wr = kb.sb([128, NCH, RW], BF16, "wr")
    for a in range(0, RW, 488):
        kb.dma(wr[:, :, a:a + 488], D_["w_in"].t[:, a:a + 488].rearrange("(c p) n -> p c n", p=128), [D_["w_in"]], [wr], q="pool")
    return wr


def rwkv_phase(kb, uT_d, ident, D_, ybwd_d, ya_d, wr):
    fw = kb.fw

    def ld(name, shape):
        t = kb.sb(shape, F32, name)
        src = D_[name]
        kb.dma(t.t, src.t, [src], [t])
        return t
    mu0 = ld("mu0", [128, RW_TILES]); mu1 = ld("mu1", [128, RW_TILES])
    muc = kb.sb([128, RW_TILES], F32, "muc")
    kb.v("dve", "tensor_tensor", [mu0, mu1], [muc], out=muc[:, :], in0=mu0[:, :], in1=mu1[:, :], op=ALU.add)
    kb.v("dve", "tensor_scalar", [muc], [muc], out=muc[:, :], in0=muc[:, :], scalar1=-1.0, scalar2=1.0, op0=ALU.mult, op1=ALU.add)
    w2 = ld("w2", [64, 2, 512]); a2 = ld("a2", [64, 2, 512])
    w0 = ld("w0", [64, 2, 8]); a0 = ld("a0", [64, 2, 8])
    g2a = ld("g2a", [128, 512]); g2b = ld("g2b", [32, 512])
    kkc = ld("kkc", [64, 8]); kac = ld("kac", [64, 8]); rkc = ld("rkc", [64, 8])
    lng = ld("lng", [64, 512]); lnb = ld("lnb", [64, 512])
    masks = ld("masks", [64, 4, 64])
    oka = kb.sb([64, 8], F32, "oka"); kah = kb.sb([64, 8], F32, "kah")
    kb.v("dve", "tensor_scalar", [kac], [oka], out=oka[:, :], in0=kac[:, :], scalar1=-1.0, scalar2=1.0, op0=ALU.mult, op1=ALU.add)
    kb.v("dve", "tensor_scalar", [kac], [kah], out=kah[:, :], in0=kac[:, :], scalar1=0.5, scalar2=None, op0=ALU.mult)
    ones = kb.sb([64, 64], F32, "ones")
    kb.v("dve", "memset", [], [ones], ones[:, :], 1.0)
    Hs = [kb.sb([64, 8, 64], F32, "Hf"), kb.sb([64, 8, 64], F32, "Hb")]

    def T(shape, name, dt=F32):
        return kb.sb(shape, dt, name)

    class Slot:
        pass

    def make_slot(si):
        S = Slot()
        S.B = kb.pbanks[4 * si:4 * si + 4]
        S.uc = T([128, NCH, 66], 'uc', BF16)
        S.sh = [T([128, 64], 'sh') for _ in range(4)]
        S.P = T([128, RW_TILES, 66], "P")
        S.Z = T([128, RW_TILES, 64], "Z")
        S.Pt = [Tl(S.P.t[:, j, :], f'P{j}') for j in range(RW_TILES)]
        S.Zt = [Tl(S.Z.t[:, j, :], f'Z{j}') for j in range(RW_TILES)]
        S.thw = T([64, 64], "thw")
        S.sg = T([64, 8, 64], "sg")
        S.A = [T([64, 8, 64], "A0"), T([64, 8, 64], "A1")]
        S.CS = T([64, 8, 65], "CS")
        kb.v("dve", "memset", [], [S.CS], S.CS[:, :, :], 0.0)
        S.G = T([64, 8, 64], "G")
        S.Gp = T([64, 8, 64], "Gp")
        S.Gi = T([64, 8, 64], "Gi")
        S.gC = T([64, 8], "gC")
        S.kkr = T([64, 8, 64], "kkr")
        S.ksq = T([64, 8, 64], "ksq")
        S.rn = T([64, 8, 64], "rn")
        S.X1 = S.ksq
        S.X2 = S.rn
        S.t1 = T([64, 8, 64], "t1")
        S.kdir = T([64, 8, 64], "kdir")
        S.kka = T([64, 8, 64], "kka")
        S.QR = T([64, 8, 128], "QR")
        S.KdT = T([64, 8, 64], "KdT")
        S.AdT = T([64, 8, 64], "AdT")
        S.Kq_t = T([64, 512], "Kq_t")
        S.Ad_t = T([64, 512], "Ad_t")
        S.Kd_t = T([64, 512], "Kd_t")
        S.V_t = T([64, 512], "V_t")
        S.NTs = [S.kkr, S.Gi]
        S.Ns = [S.ksq, S.Gp]
        S.MkT = S.G
        S.PaT = S.sg
        S.PkT = S.rn
        S.X = T([64, 8, 128], "X")
        S.AT = S.t1
        S.RqpT = S.kdir
        S.ybt = S.Kd_t
        S.yt = S.Ad_t
        S.ysq = S.Kq_t
        S.st = T([64, 48], "st")
        S.sgd = T([128, 2, 64], "sgd")
        S.ee = S.kka
        S.yout = S.ysq
        return S
    slots = [make_slot(0), make_slot(1)]

    RS, KS, VS = slice(0, 8), slice(8, 16), slice(16, 24)
    NSTAGE = 52

    def bc(ap, shape, axis):
        return ap.unsqueeze(axis).to_broadcast(shape)

    def v3(bank, p, a):
        return bank.t[0:p, 0:512].rearrange("p (a b) -> p a b", a=a)

    def chunk(S, c0, cn, d, need_out, H, epilogue):
        B = S.B
        P, Z, uc = S.P, S.Z, S.uc
        Pt, Zt = S.Pt, S.Zt
        ns = 0
        wdt = 24 + d
        tiles = list(range(24)) + [wdt] + ([26, 27, 28, 29] if epilogue == "final" else [26 + d])
        lo, hi = max(c0 - 1, 0), min(c0 + cn + 1, L)
        n = hi - lo
        off = lo - (c0 - 1)
        kb.dma(uc[:, :, 0:n], uT_d.t[:, :, lo:hi], [uT_d], [uc])
        if off or (hi - (c0 - 1)) < cn + 2:
            kb.v("pool", "memset", [], Pt, P[:, :, :], 0.0)
        groups = [tiles[i:i + 2] for i in range(0, len(tiles), 2)]

        def shift(grp):
            for j in grp:
                w = RW_COLS[j][1]
                kb.act(Z[0:w, j, 0:cn], P[0:w, j, 1:cn + 1], AF.Copy, [Pt[j], muc], [Zt[j]], scale=muc[0:w, j:j + 1])
                kb.v("dve", "scalar_tensor_tensor", [Pt[j], mu0, Zt[j]], [Zt[j]], out=Z[0:w, j, 0:cn], in0=P[0:w, j, 0:cn],
                     scalar=mu0[0:w, j:j + 1], in1=Z[0:w, j, 0:cn], op0=ALU.mult, op1=ALU.add)
                kb.v("dve", "scalar_tensor_tensor", [Pt[j], mu1, Zt[j]], [Zt[j]], out=Z[0:w, j, 0:cn], in0=P[0:w, j, 2:cn + 2],
                     scalar=mu1[0:w, j:j + 1], in1=Z[0:w, j, 0:cn], op0=ALU.mult, op1=ALU.add)

        prev = None
        for gi, grp in enumerate(groups):
            pb_ = B[gi % 4]
            pv = pb_.t[:, 0:2 * 66].rearrange("p (a b) -> p a b", a=2)
            for ji, j in enumerate(grp):
                cc, w = RW_COLS[j]
                for c in range(NCH):
                    kb.mm(pv[0:w, ji, 0:n], wr[:, c, cc:cc + w], uc[:, c, 0:n], [wr, uc], [pb_],
                          start=c == 0, stop=c == NCH - 1)
            j0, j1 = grp[0], grp[-1]
            if j1 < 28 and j1 - j0 == len(grp) - 1:
                kb.act(P[0:64, j0:j1 + 1, off:off + n], pv[0:64, 0:len(grp), 0:n], AF.Copy, [pb_], [Pt[jj] for jj in grp])
            else:
                for ji, j in enumerate(grp):
                    w = RW_COLS[j][1]
                    kb.act(P[0:w, j, off:off + n], pv[0:w, ji, 0:n], AF.Copy, [pb_], [Pt[j]])
            if prev is not None:
                shift(prev)
            prev = grp
            ns += 1; yield
        shift(prev)
        ns += 1; yield
        while ns < 17:
            ns += 1; yield
        thw, sg, A, CS = S.thw, S.sg, S.A, S.CS
        kb.act(thw[:, 0:cn], Z[0:64, wdt, 0:cn], AF.Tanh, [Zt[wdt]], [thw])
        pl = B[0]
        plv = v3(pl, 64, 8)
        for h in range(8):
            kb.mm(plv[:, h, 0:cn], w2[:, d, 64 * h:64 * h + 64], thw[:, 0:cn], [w2, thw], [pl], r32=True)
        for h in range(8):
            kb.act(sg[:, h, 0:cn], plv[:, h, 0:cn], AF.Sigmoid, [pl, w0], [sg], bias=w0[:, d, h:h + 1])
        ns += 1; yield
        for dd in ((0, 1) if epilogue == "final" else (d,)):
            pa = B[1 + dd]
            pav = v3(pa, 64, 8)
            for h in range(8):
                kb.mm(pav[:, h, 0:cn], a2[:, dd, 64 * h:64 * h + 64], Z[0:64, 26 + dd, 0:cn], [a2, Zt[26 + dd]], [pa], r32=True)
            for h in range(8):
                kb.act(A[dd][:, h, 0:cn], pav[:, h, 0:cn], AF.Sigmoid, [pa, a0], [A[dd]], bias=a0[:, dd, h:h + 1])
        ns += 1; yield
        Ad_ = A[d]
        sh = [64, 8, cn]
        kkr, ksq, rn, t1, kdir, kka = S.kkr, S.ksq, S.rn, S.t1, S.kdir, S.kka
        kb.v("pool", "tensor_tensor", Zt[8:16] + [kkc], [kkr], out=kkr[:, :, 0:cn], in0=Z[0:64, KS, 0:cn], in1=bc(kkc[:, :], sh, 2), op=ALU.mult)
        kb.act(ksq[:, :, 0:cn], kkr[:, :, 0:cn], AF.Square, [kkr], [ksq])
        pn = B[3]
        pnv = v3(pn, 64, 8)
        for h in range(8):
            kb.mm(pnv[:, h, 0:cn], ones[:, :], ksq[:, h, 0:cn], [ones, ksq], [pn], r32=True)
        kb.act(rn[:, :, 0:cn], pnv[:, :, 0:cn], AF.Ln, [pn], [rn], bias=1e-18, scale=1.0)
        kb.act(rn[:, :, 0:cn], rn[:, :, 0:cn], AF.Exp, [rn], [rn], scale=-0.5)
        kb.v("dve", "tensor_tensor", [kkr, rn], [kkr], out=kkr[:, :, 0:cn], in0=kkr[:, :, 0:cn], in1=rn[:, :, 0:cn], op=ALU.mult)
        ns += 1; yield
        kb.v("pool", "tensor_tensor", [Ad_, kac], [t1], out=t1[:, :, 0:cn], in0=Ad_[:, :, 0:cn], in1=bc(kac[:, :], sh, 2), op=ALU.mult)
        kb.v("pool", "tensor_tensor", [t1, oka], [t1], out=t1[:, :, 0:cn], in0=t1[:, :, 0:cn], in1=bc(oka[:, :], sh, 2), op=ALU.add)
        kb.v("pool", "tensor_tensor", [t1] + Zt[8:16], [kdir], out=kdir[:, :, 0:cn], in0=t1[:, :, 0:cn], in1=Z[0:64, KS, 0:cn], op=ALU.mult)
        kb.v("dve", "tensor_tensor", [kkr, Ad_], [kka], out=kka[:, :, 0:cn], in0=kkr[:, :, 0:cn], in1=Ad_[:, :, 0:cn], op=ALU.mult)
        ns += 1; yield
        for h in range(8):
            kb.v("dve", "tensor_tensor_scan", [ones, sg], [CS], out=CS[:, h, 1:cn + 1], data0=ones[:, 0:cn], data1=sg[:, h, 0:cn],
                 initial=0.0, op0=ALU.mult, op1=ALU.add)
        ns += 1; yield
        G, Gp, Gi, gC, X1, X2 = S.G, S.Gp, S.Gi, S.gC, S.X1, S.X2
        if d == 0:
            kb.act(G[:, :, 0:cn], CS[:, :, 1:cn + 1], AF.Exp, [CS], [G], scale=-S0)
            kb.act(Gp[:, :, 0:cn], CS[:, :, 0:cn], AF.Exp, [CS], [Gp], scale=-S0)
            kb.act(Gi[:, :, 0:cn], CS[:, :, 1:cn + 1], AF.Exp, [CS], [Gi], scale=S0)
        else:
            totb = bc(CS[:, :, cn], sh, 2)
            kb.v("dve", "tensor_tensor", [CS], [X1], out=X1[:, :, 0:cn], in0=CS[:, :, 0:cn], in1=totb, op=ALU.subtract)
            kb.v("dve", "tensor_tensor", [CS], [X2], out=X2[:, :, 0:cn], in0=CS[:, :, 1:cn + 1], in1=totb, op=ALU.subtract)
            kb.act(G[:, :, 0:cn], X1[:, :, 0:cn], AF.Exp, [X1], [G], scale=S0)
            kb.act(Gp[:, :, 0:cn], X2[:, :, 0:cn], AF.Exp, [X2], [Gp], scale=S0)
            kb.act(Gi[:, :, 0:cn], X1[:, :, 0:cn], AF.Exp, [X1], [Gi], scale=-S0)
        kb.act(gC[:, :], CS[:, :, cn], AF.Exp, [CS], [gC], scale=-S0)
        ns += 1; yield
        QR, KdT, AdT = S.QR, S.KdT, S.AdT
        kb.v("dve", "tensor_tensor", [kkr, Gp], [QR], out=QR[:, :, 0:cn], in0=kkr[:, :, 0:cn], in1=Gp[:, :, 0:cn], op=ALU.mult)
        kb.v("pool", "tensor_tensor", Zt[0:8] + [G], [QR], out=QR[:, :, cn:2 * cn], in0=Z[0:64, RS, 0:cn], in1=G[:, :, 0:cn], op=ALU.mult)
        kb.v("pool", "tensor_tensor", [kdir, Gi], [KdT], out=KdT[:, :, 0:cn], in0=kdir[:, :, 0:cn], in1=Gi[:, :, 0:cn], op=ALU.mult)
        kb.v("dve", "tensor_tensor", [kka, Gi], [AdT], out=AdT[:, :, 0:cn], in0=kka[:, :, 0:cn], in1=Gi[:, :, 0:cn], op=ALU.mult)
        ns += 1; yield
        Kq_t, Ad_t, Kd_t, V_t = S.Kq_t, S.Ad_t, S.Kd_t, S.V_t
        for bi, (src, si, dst) in enumerate(((QR, 0, Kq_t), (AdT, 0, Ad_t), (KdT, 0, Kd_t), (Z, 16, V_t))):
            pt = B[bi]
            for h in range(8):
                kb.tr(pt[0:cn, h * 64:(h + 1) * 64], src[0:64, si + h, 0:cn], ident[0:64, 0:64], [Zt[16 + h] if src is Z else src, ident], [pt])
            if bi % 2 == 0:
                kb.act(dst[0:cn, :], pt[0:cn, :], AF.Copy, [pt], [dst])
            else:
                kb.v("dve", "tensor_copy", [pt], [dst], out=dst[0:cn, :], in_=pt[0:cn, :])
                ns += 1; yield
        mS, mI, mN = (0, 1, 2) if d == 0 else (2, 3, 0)
        rw = 2 * cn if need_out else cn
        pSA = [B[0], B[1]]; pSK = [B[2], B[3]]
        NTs, Ns, MkT, PaT, PkT, X = S.NTs, S.Ns, S.MkT, S.PaT, S.PkT, S.X
        for h in range(8):
            hv = pSA[h // 4].t[0:cn, 0:4 * rw].rearrange("p (h s) -> p h s", h=4)
            kb.mm(hv[:, h % 4, :], AdT[:, h, 0:cn], QR[:, h, 0:rw], [AdT, QR], [pSA[h // 4]], r32=True)
            hv2 = pSK[h // 4].t[0:cn, 0:4 * rw].rearrange("p (h s) -> p h s", h=4)
            kb.mm(hv2[:, h % 4, :], KdT[:, h, 0:cn], QR[:, h, 0:rw], [KdT, QR], [pSK[h // 4]], r32=True)
        NT, Nm = NTs[0], Ns[0]
        for half in range(2):
            hv = pSA[half].t[0:cn, 0:4 * rw].rearrange("p (h s) -> p h s", h=4)
            hv2 = pSK[half].t[0:cn, 0:4 * rw].rearrange("p (h s) -> p h s", h=4)
            hs = slice(4 * half, 4 * half + 4)
            kb.v("dve", "tensor_tensor", [pSA[half], masks], [NT], out=NT[0:cn, hs, 0:cn], in0=hv[:, :, 0:cn],
                 in1=bc(masks[0:cn, mS, 0:cn], [cn, 4, cn], 1), op=ALU.mult)
            kb.v("dve", "tensor_tensor", [pSK[half], masks], [MkT], out=MkT[0:cn, hs, 0:cn], in0=hv2[:, :, 0:cn],
                 in1=bc(masks[0:cn, mS, 0:cn], [cn, 4, cn], 1), op=ALU.mult)
            if need_out:
                kb.v("dve", "tensor_tensor", [pSA[half], masks], [PaT], out=PaT[0:cn, hs, 0:cn], in0=hv[:, :, cn:2 * cn],
                     in1=bc(masks[0:cn, mI, 0:cn], [cn, 4, cn], 1), op=ALU.mult)
                kb.v("dve", "tensor_tensor", [pSK[half], masks], [PkT], out=PkT[0:cn, hs, 0:cn], in0=hv2[:, :, cn:2 * cn],
                     in1=bc(masks[0:cn, mI, 0:cn], [cn, 4, cn], 1), op=ALU.mult)
        ns += 1; yield
        pN = B[0]
        pNv = pN.t[0:cn, 0:8 * cn].rearrange("p (h s) -> p h s", h=8)
        pW = B[1]
        pWv = v3(pW, cn, 8)
        for h in range(8):
            kb.mm(pNv[:, h, :], QR[:, h, 0:cn], AdT[:, h, 0:cn], [QR, AdT], [pN], r32=True)
        for h in range(8):
            kb.mm(pWv[:, h, :], MkT[0:cn, h, 0:cn], V_t[0:cn, 64 * h:64 * h + 64], [MkT, V_t], [pW], r32=True)
        kb.v("dve", "tensor_tensor", [pN, masks], [Nm], out=Nm[0:cn, :, 0:cn], in0=pNv,
             in1=bc(masks[0:cn, mN, 0:cn], [cn, 8, cn], 1), op=ALU.mult)
        kb.v("pool", "tensor_copy", [Kq_t], [X], out=X[0:cn, :, 0:64], in_=Kq_t[0:cn, :].rearrange("p (h v) -> p h v", h=8))
        kb.act(X[0:cn, :, 64:128], pWv, AF.Copy, [pW], [X], scale=-1.0)
        ns += 1; yield
        nlev = 5 if cn > 32 else (4 if cn > 16 else 3)
        pF = [B[2], B[3]]

        def factor(PT_, sign):
            for h in range(8):
                fv = v3(pF[h // 4], cn, 4)
                kb.mm(fv[:, h % 4, :], PT_[0:cn, h, 0:cn], X[0:cn, h, :], [PT_, X], [pF[h // 4]], r32=True)
            for half in range(2):
                fv = v3(pF[half], cn, 4)
                hs = slice(4 * half, 4 * half + 4)
                kb.v("dve", "tensor_tensor", [X, pF[half]], [X], out=X[0:cn, hs, :], in0=X[0:cn, hs, :], in1=fv,
                     op=ALU.subtract if sign < 0 else ALU.add)
        factor(NT, -1)
        ns += 1; yield
        cur = 0
        for lvl in range(5):
            if lvl < nlev:
                NTc, Nc = NTs[cur], Ns[cur]
                NTn, Nn = NTs[1 - cur], Ns[1 - cur]
                pa_, pb2 = B[0], B[1]
                pav = pa_.t[0:cn, 0:8 * cn].rearrange("p (h s) -> p h s", h=8)
                pbv = pb2.t[0:cn, 0:8 * cn].rearrange("p (h s) -> p h s", h=8)
                last = lvl == nlev - 1
                for h in range(8):
                    kb.mm(pav[:, h, :], Nc[0:cn, h, 0:cn], NTc[0:cn, h, 0:cn], [Nc, NTc], [pa_], r32=True)
                    if not last:
                        kb.mm(pbv[:, h, :], NTc[0:cn, h, 0:cn], Nc[0:cn, h, 0:cn], [Nc, NTc], [pb2], r32=True)
                kb.act(NTn[0:cn, :, 0:cn], pav, AF.Copy, [pa_], [NTn])
                if not last:
                    kb.act(Nn[0:cn, :, 0:cn], pbv, AF.Copy, [pb2], [Nn])
            ns += 1; yield
            if lvl < nlev:
                factor(NTn, +1)
                cur = 1 - cur
            ns += 1; yield
        AT, RqpT = S.AT, S.RqpT
        pA = B[0]
        pAv = v3(pA, 64, 8)
        for h in range(8):
            kb.mm(pAv[:, h, :], X[0:cn, h, 0:64], Ad_t[0:cn, 64 * h:64 * h + 64], [X, Ad_t], [pA], r32=True)
        kb.v("dve", "tensor_tensor", [ident, pA], [AT], out=AT[:, :, :], in0=bc(ident[0:64, 0:64], [64, 8, 64], 1), in1=pAv, op=ALU.subtract)
        if need_out:
            pR = B[1]
            pRv = pR.t[0:64, 0:8 * cn].rearrange("p (a b) -> p a b", a=8)
            for h in range(8):
                kb.mm(pRv[:, h, :], X[0:cn, h, 0:64], PaT[0:cn, h, 0:cn], [X, PaT], [pR], r32=True)
            kb.v("dve", "tensor_tensor", [QR, pR], [RqpT], out=RqpT[:, :, 0:cn], in0=QR[:, :, cn:2 * cn], in1=pRv, op=ALU.subtract)
        ns += 1; yield
        if need_out:
            pY = B[2]
            pYv = v3(pY, cn, 8)
            for h in range(8):
                kb.mm(pYv[:, h, :], RqpT[:, h, 0:cn], H[:, h, :], [RqpT, H], [pY], start=True, stop=False, r32=True)
                kb.mm(pYv[:, h, :], PkT[0:cn, h, 0:cn], V_t[0:cn, 64 * h:64 * h + 64], [PkT, V_t], [pY], start=False, stop=False, r32=True)
                kb.mm(pYv[:, h, :], PaT[0:cn, h, 0:cn], X[0:cn, h, 64:128], [PaT, X], [pY], start=False, stop=True, r32=True)
        pH = B[3]
        pHv = v3(pH, 64, 8)
        for h in range(8):
            kb.mm(pHv[:, h, :], AT[:, h, :], H[:, h, :], [AT, H], [pH], start=True, stop=False, r32=True)
            kb.mm(pHv[:, h, :], Kd_t[0:cn, 64 * h:64 * h + 64], V_t[0:cn, 64 * h:64 * h + 64], [Kd_t, V_t], [pH], start=False, stop=False, r32=True)
            kb.mm(pHv[:, h, :], Ad_t[0:cn, 64 * h:64 * h + 64], X[0:cn, h, 64:128], [Ad_t, X], [pH], start=False, stop=True, r32=True)
        kb.v("dve", "tensor_tensor", [pH, gC], [H], out=H[:, :, :], in0=pHv, in1=bc(gC[:, :], [64, 8, 64], 2), op=ALU.mult)
        ns += 1; yield
        ybt, yt, ysq, st, sgd, ee, yout = S.ybt, S.yt, S.ysq, S.st, S.sgd, S.ee, S.yout
        if epilogue == "store":
            kb.act(yt[0:cn, :], pY[0:cn, :], AF.Copy, [pY], [yt])
            kb.dma(ybwd_d.t[c0:c0 + cn, :], yt[0:cn, :], [yt], [ybwd_d])
        elif epilogue == "final":
            kb.dma(ybt[0:cn, :], ybwd_d.t[c0:c0 + cn, :], [ybwd_d], [ybt])
            kb.v("dve", "tensor_tensor", [pY, ybt], [yt], out=yt[0:cn, :], in0=pY[0:cn, :], in1=ybt[0:cn, :], op=ALU.add)
            y3 = yt[0:cn, :].rearrange("p (h v) -> p h v", h=8)
            kb.v("dve", "tensor_reduce", [yt], [st], out=st[0:cn, 0:8], in_=y3, axis=AX.X, op=ALU.add)
            kb.act(ysq[0:cn, :], yt[0:cn, :], AF.Square, [yt], [ysq])
            kb.v("dve", "tensor_reduce", [ysq], [st], out=st[0:cn, 8:16], in_=ysq[0:cn, :].rearrange("p (h v) -> p h v", h=8), axis=AX.X, op=ALU.add)
            kb.v("dve", "tensor_scalar", [st], [st], out=st[0:cn, 16:24], in0=st[0:cn, 0:8], scalar1=1.0 / 64, scalar2=None, op0=ALU.mult)
            kb.v("dve", "tensor_tensor", [st], [st], out=st[0:cn, 24:32], in0=st[0:cn, 16:24], in1=st[0:cn, 16:24], op=ALU.mult)
            kb.v("dve", "scalar_tensor_tensor", [st], [st], out=st[0:cn, 32:40], in0=st[0:cn, 8:16], scalar=1.0 / 64, in1=st[0:cn, 24:32],
                 op0=ALU.mult, op1=ALU.subtract)
            kb.act(st[0:cn, 40:48], st[0:cn, 32:40], AF.Sqrt, [st], [st], bias=64e-5, scale=1.0)
            kb.v("dve", "reciprocal", [st], [st], out=st[0:cn, 40:48], in_=st[0:cn, 40:48])
            kb.v("dve", "tensor_tensor", [yt, st], [yt], out=y3, in0=y3, in1=bc(st[0:cn, 16:24], [cn, 8, 64], 2), op=ALU.subtract)
            kb.v("dve", "tensor_tensor", [yt, st], [yt], out=y3, in0=y3, in1=bc(st[0:cn, 40:48], [cn, 8, 64], 2), op=ALU.mult)
            kb.v("pool", "tensor_tensor", [yt, lng], [yt], out=yt[0:cn, :], in0=yt[0:cn, :], in1=lng[0:cn, :], op=ALU.mult)
            kb.v("pool", "tensor_tensor", [yt, lnb], [yt], out=yt[0:cn, :], in0=yt[0:cn, :], in1=lnb[0:cn, :], op=ALU.add)
        ns += 1; yield
        if epilogue == "final":
            kb.v("pool", "tensor_tensor", [A[0], A[1]], [ee], out=ee[:, :, 0:cn], in0=A[0][:, :, 0:cn], in1=A[1][:, :, 0:cn], op=ALU.add)
            kb.v("pool", "tensor_tensor", [ee, kah], [ee], out=ee[:, :, 0:cn], in0=ee[:, :, 0:cn], in1=bc(kah[:, :], sh, 2), op=ALU.mult)
            kb.v("pool", "tensor_tensor", [ee, oka], [ee], out=ee[:, :, 0:cn], in0=ee[:, :, 0:cn], in1=bc(oka[:, :], sh, 2), op=ALU.add)
            kb.v("pool", "tensor_tensor", [ee] + Zt[8:16], [ee], out=ee[:, :, 0:cn], in0=ee[:, :, 0:cn], in1=Z[0:64, KS, 0:cn], op=ALU.mult)
            kb.v("pool", "tensor_tensor", [ee] + Zt[0:8], [ee], out=ee[:, :, 0:cn], in0=ee[:, :, 0:cn], in1=Z[0:64, RS, 0:cn], op=ALU.mult)
            kb.v("pool", "tensor_tensor", [ee, rkc], [ee], out=ee[:, :, 0:cn], in0=ee[:, :, 0:cn], in1=bc(rkc[:, :], sh, 2), op=ALU.mult)
            pS = B[0]
            for h in range(8):
                kb.mm(pS[0:cn, h:h + 1], ee[:, h, 0:cn], ones[:, 0:1], [ee, ones], [pS])
            kb.act(st[0:cn, 0:8], pS[0:cn, 0:8], AF.Copy, [pS], [st])
            kb.v("dve", "tensor_tensor", [V_t, st], [ysq], out=ysq[0:cn, :].rearrange("p (h v) -> p h v", h=8),
                 in0=V_t[0:cn, :].rearrange("p (h v) -> p h v", h=8), in1=bc(st[0:cn, 0:8], [cn, 8, 64], 2), op=ALU.mult)
            kb.v("dve", "tensor_tensor", [yt, ysq], [yt], out=yt[0:cn, :], in0=yt[0:cn, :], in1=ysq[0:cn, :], op=ALU.add)
            kb.act(sgd[:, 0, 0:cn], Z[:, 28, 0:cn], AF.Sigmoid, [Zt[28]], [sgd])
            kb.act(sgd[0:32, 1, 0:cn], Z[0:32, 29, 0:cn], AF.Sigmoid, [Zt[29]], [sgd])
            pG = B[1]
            kb.mm(pG[0:cn, :], sgd[:, 0, 0:cn], g2a[:, :], [sgd, g2a], [pG], start=True, stop=False, r32=True)
            kb.mm(pG[0:cn, :], sgd[0:32, 1, 0:cn], g2b[:, :], [sgd, g2b], [pG], start=False, stop=True, r32=True)
            kb.v("dve", "tensor_tensor", [yt, pG], [yout], out=yout[0:cn, :], in0=yt[0:cn, :], in1=pG[0:cn, :], op=ALU.mult)
            kb.dma(ya_d.t[c0:c0 + cn, :], yout[0:cn, :], [yout], [ya_d])
        ns += 1; yield
        while ns < NSTAGE:
            ns += 1; yield
        assert ns == NSTAGE, ns

    def chain(S, lst, d, H, epilogue):
        for (c0, cn) in lst:
            yield from chunk(S, c0, cn, d, epilogue is not None, H, epilogue)

    chunks = tiles_of(2048, 64) + [(2048, 16)] + [(2064 + a, b) for a, b in tiles_of(2048, 64)]
    own = [c for c in chunks if c[0] < OWN]
    oth = [c for c in chunks if c[0] >= OWN]
    Hf, Hb = Hs
    kb.v("dve", "memset", [], [Hf], Hf[:, :, :], 0.0)
    kb.v("dve", "memset", [], [Hb], Hb[:, :, :], 0.0)
    g0 = chain(slots[0], own, 0, Hf, "store")
    g1 = chain(slots[1], list(reversed(oth)), 1, Hb, None)
    step = 0
    d0 = d1 = False
    while not (d0 and d1):
        if not d0:
            try:
                next(g0)
            except StopIteration:
                d0 = True
        if step >= 10 and not d1:
            try:
                next(g1)
            except StopIteration:
                d1 = True
        step += 1
    rown = list(reversed(own))
    g0 = chain(slots[0], rown[0::2], 1, Hb, "final")
    g1 = chain(slots[1], rown[1::2], 1, Hb, "final")
    step = 0
    d0 = d1 = False
    while not (d0 and d1):
        if not d0:
            try:
                next(g0)
            except StopIteration:
                d0 = True
        if step >= 10 and not d1:
            try:
                next(g1)
            except StopIteration:
                d1 = True
        step += 1


def wload(kb, dram, rows, c0, c1, name):
    kch = rows // 128
    t = kb.sb([128, kch, c1 - c0], BF16, name)
    step = 1024
    for a in range(c0, c1, step):
        b = min(a + step, c1)
        kb.dma(t[:, :, a - c0:b - c0], dram.t[:, a:b].rearrange("(c p) n -> p c n", p=128), [dram], [t], q="pool")
    return t


def merge_phase(kb, uT_d, identb, D_, ya_d, yb_d, h2_d):
    B = kb.pbanks
    Wa = wload(kb, D_["w_a"], 512, 0, 1024, "Wa")
    Wb = wload(kb, D_["w_b"], 512, 0, 1024, "Wb")
    Wo = wload(kb, D_["w_o"], 1024, 0, 1024, "Wo")
    Wg = wload(kb, D_["w_in"], 1024, 2720, 4768, "Wg")

    class Set:
        pass

    def mk(si):
        S = Set()
        S.B = B[4 * si:4 * si + 4]
        S.yaf = kb.sb([128, 512], F32, "yaf"); S.yab = kb.sb([128, 512], BF16, "yab"); S.ybb = kb.sb([128, 512], BF16, "ybb")
        S.yaT = kb.sb([128, 4, 128], BF16, "yaT"); S.ybT = kb.sb([128, 4, 128], BF16, "ybT")
        S.sgA = kb.sb([128, 1024], F32, "sgA"); S.sgB = kb.sb([128, 1024], F32, "sgB")
        S.mg = kb.sb([128, 1024], BF16, "mg"); S.mT = kb.sb([128, 8, 128], BF16, "mT")
        S.xt = kb.sb([128, 1024], F32, "xt2"); S.h2 = kb.sb([128, 1024], F32, "h2")
        S.ut = kb.sb([128, NCH, 128], BF16, "utm")
        return S
    sets = [mk(0), mk(1)]

    def tile(S, i, t0, n):
        B_ = S.B
        yaf, yab, ybb, yaT, ybT, sgA, sgB, mg, mT, xt, h2, ut = (S.yaf, S.yab, S.ybb, S.yaT, S.ybT, S.sgA, S.sgB, S.mg, S.mT,
                                                                  S.xt, S.h2, S.ut)
        kb.dma(ut[:, :, 0:n], uT_d.t[:, :, t0:t0 + n], [uT_d], [ut])
        kb.dma(yaf[0:n, :], ya_d.t[t0:t0 + n, :], [ya_d], [yaf])
        kb.dma(ybb[0:n, :], yb_d.t[i, 0:n, :], [yb_d], [ybb])
        kb.dma(xt[0:n, :], D_["h"].t[t0:t0 + n, :], [D_["h"]], [xt])
        yield
        for gi, sg_ in enumerate((sgA, sgB)):
            for hb in range(2):
                bk = B_[hb]
                col = gi * 1024 + hb * 512
                for c in range(NCH):
                    kb.mm(bk[0:n, :], ut[:, c, 0:n], Wg[:, c, col:col + 512], [ut, Wg], [bk], start=c == 0, stop=c == NCH - 1)
                kb.act(sg_[0:n, hb * 512:(hb + 1) * 512], bk[0:n, :], AF.Sigmoid, [bk], [sg_])
            yield
        kb.v("dve", "tensor_copy", [yaf], [yab], out=yab[0:n, :], in_=yaf[0:n, :])
        for src, dst, bk in ((yab, yaT, B_[2]), (ybb, ybT, B_[3])):
            pv = bk.t.bitcast(BF16).rearrange("p (c t) -> p c t", c=8)
            for k in range(4):
                kb.tr(pv[:, k, 0:n], src[0:n, k * 128:(k + 1) * 128], identb[0:n, 0:n], [src, identb], [bk])
            kb.act(dst[:, :, 0:n], pv[:, 0:4, 0:n], AF.Copy, [bk], [dst])
        yield
        for (yT_, W_, sg_) in ((yaT, Wa, sgA), (ybT, Wb, sgB)):
            for hb in range(2):
                bk = B_[hb]
                for k in range(4):
                    kb.mm(bk[0:n, :], yT_[:, k, 0:n], W_[:, k, hb * 512:(hb + 1) * 512], [yT_, W_], [bk], start=k == 0, stop=k == 3)
                kb.v("dve", "tensor_tensor", [sg_, bk], [sg_], out=sg_[0:n, hb * 512:(hb + 1) * 512],
                     in0=sg_[0:n, hb * 512:(hb + 1) * 512], in1=bk[0:n, :], op=ALU.mult)
            yield
        kb.v("pool", "tensor_tensor", [sgA, sgB], [mg], out=mg[0:n, :], in0=sgA[0:n, :], in1=sgB[0:n, :], op=ALU.add)
        bk = B_[2]
        pv = bk.t.bitcast(BF16).rearrange("p (c t) -> p c t", c=8)
        for k in range(8):
            kb.tr(pv[:, k, 0:n], mg[0:n, k * 128:(k + 1) * 128], identb[0:n, 0:n], [mg, identb], [bk])
        kb.act(mT[:, :, 0:n], pv[:, :, 0:n], AF.Copy, [bk], [mT])
        yield
        for hb in range(2):
            bk = B_[hb]
            for k in range(8):
                kb.mm(bk[0:n, :], mT[:, k, 0:n], Wo[:, k, hb * 512:(hb + 1) * 512], [mT, Wo], [bk], start=k == 0, stop=k == 7)
            kb.v("dve", "tensor_tensor", [xt, bk], [h2], out=h2[0:n, hb * 512:(hb + 1) * 512], in0=xt[0:n, hb * 512:(hb + 1) * 512],
                 in1=bk[0:n, :], op=ALU.add)
        kb.dma(h2_d.t[t0:t0 + n, :], h2[0:n, :], [h2], [h2_d])
        yield

    def chain(S, lst):
        for (i, t0, n) in lst:
            yield from tile(S, i, t0, n)
    tl = [(i, t0, n) for i, (t0, n) in enumerate(tiles_of(OWN, 128))]
    g0, g1 = chain(sets[0], tl[0::2]), chain(sets[1], tl[1::2])
    step, d0, d1 = 0, False, False
    while not (d0 and d1):
        if not d0:
            try:
                next(g0)
            except StopIteration:
                d0 = True
        if step >= 4 and not d1:
            try:
                next(g1)
            except StopIteration:
                d1 = True
        step += 1


def ffn_phase(kb, identb, D_, h2_d, out_d, W1):
    B = kb.pbanks
    GT = 256
    W2 = wload(kb, D_["w_ff2"], 4096, 0, 1024, "W2")
    gf = kb.sb([128, NCH], F32, "gffn"); gfin = kb.sb([128, 1024], F32, "gfin")
    kb.dma(gf.t, D_["gffn"].t, [D_["gffn"]], [gf])
    kb.dma(gfin.t, D_["gfin"].t, [D_["gfin"]], [gfin])
    xs = kb.sb([128, 1024], BF16, "xs2")
    ss = kb.sb([128, 8], F32, "ss2"); ss2 = kb.sb([128, 8], F32, "ss3"); nT = kb.sb([128, 8, GT], BF16, "nT")
    frs = [kb.sb([128, GT], F32, "fr") for _ in range(2)]
    f2T = kb.sb([128, 32, GT], BF16, "f2T")
    h3 = kb.sb([128, 1024], F32, "h3"); ot = kb.sb([128, 1024], F32, "ot")
    junk = kb.sb([128, 1024], BF16, "junkf")
    groups = tiles_of(OWN, GT)
    h2g = [[kb.sb([128, 1024], F32, "h2f") for _ in range(2)] for _ in range(2)]
    nTs = [nT, kb.sb([128, 8, GT], BF16, "nT2")]

    def norm_part(gi):
        g0, gn = groups[gi]
        for si, (o0, n) in enumerate(tiles_of(gn, 128)):
            t0 = g0 + o0
            h2 = h2g[gi % 2][si]
            nT_ = nTs[gi % 2]
            kb.dma(h2[0:n, :], h2_d.t[t0:t0 + n, :], [h2_d], [h2])
            kb.act(junk[0:n, :], h2[0:n, :], AF.Square, [h2], [junk, ss], accum_out=ss[0:n, 0:1])
            kb.act(ss[0:n, 1:2], ss[0:n, 0:1], AF.Sqrt, [ss], [ss], scale=1.0 / D, bias=EPS)
            kb.v("dve", "reciprocal", [ss], [ss], out=ss[0:n, 2:3], in_=ss[0:n, 1:2])
            kb.v("dve", "tensor_scalar", [h2, ss], [xs], out=xs[0:n, :], in0=h2[0:n, :], scalar1=ss[0:n, 2:3], scalar2=None, op0=ALU.mult)
            bk = B[si]
            pv = bk.t.bitcast(BF16).rearrange("p (c t) -> p c t", c=8)
            for c in range(NCH):
                kb.tr(pv[:, c, 0:n], xs[0:n, c * 128:(c + 1) * 128], identb[0:n, 0:n], [xs, identb], [bk])
            kb.v("dve", "tensor_tensor", [bk, gf], [nT_], out=nT_[:, :, o0:o0 + n], in0=pv[:, :, 0:n],
                 in1=gf[:, :].unsqueeze(2).to_broadcast([128, NCH, n]), op=ALU.mult)

    norm_part(0)
    for gi, (g0, gn) in enumerate(groups):
        subs = tiles_of(gn, 128)
        nT_ = nTs[gi % 2]
        for ft in range(32):
            bk = B[2 + ft % 3]
            fr = frs[ft % 2]
            for c in range(NCH):
                kb.mm(bk[:, 0:gn], W1[:, c, ft * 128:(ft + 1) * 128], nT_[:, c, 0:gn], [W1, nT_], [bk], start=c == 0, stop=c == NCH - 1)
            kb.act(fr[:, 0:gn], bk[:, 0:gn], AF.Relu, [bk], [fr])
            kb.v("dve", "tensor_tensor", [fr], [f2T], out=f2T[:, ft, 0:gn], in0=fr[:, 0:gn], in1=fr[:, 0:gn], op=ALU.mult)
        if gi + 1 < len(groups):
            norm_part(gi + 1)
        for si, (o0, n) in enumerate(subs):
            t0 = g0 + o0
            h2 = h2g[gi % 2][si]
            for hb in range(2):
                bk = B[5 + (hb + 2 * si) % 3]
                for k in range(32):
                    kb.mm(bk[0:n, :], f2T[:, k, o0:o0 + n], W2[:, k, hb * 512:(hb + 1) * 512], [f2T, W2], [bk], start=k == 0, stop=k == 31)
                kb.v("dve", "tensor_tensor", [h2, bk], [h3], out=h3[0:n, hb * 512:(hb + 1) * 512], in0=h2[0:n, hb * 512:(hb + 1) * 512],
                     in1=bk[0:n, :], op=ALU.add)
            kb.act(junk[0:n, :], h3[0:n, :], AF.Square, [h3], [junk, ss2], accum_out=ss2[0:n, 4:5])
            kb.act(ss2[0:n, 5:6], ss2[0:n, 4:5], AF.Sqrt, [ss2], [ss2], scale=1.0 / D, bias=EPS)
            kb.v("dve", "reciprocal", [ss2], [ss2], out=ss2[0:n, 6:7], in_=ss2[0:n, 5:6])
            kb.v("dve", "scalar_tensor_tensor", [h3, ss2, gfin], [ot], out=ot[0:n, :], in0=h3[0:n, :], scalar=ss2[0:n, 6:7], in1=gfin[0:n, :],
                 op0=ALU.mult, op1=ALU.mult)
            k = kb.dma(out_d.t[t0:t0 + n, :], ot[0:n, :], [ot], [])
            kb.out_keys.append(k)

def tiles_of(n, step):
    out = []
    t = 0
    while t < n:
        out.append((t, min(step, n - t)))
        t += step
    return out


def build(debug=None, upto="all"):
    nc = bass.Bass("TRN2", target_bir_lowering=False)
    kb = KB(nc, debug)
    kb.setup_mem()
    fw = kb.fw
    h_d = kb.dram_in("h", [L, D])
    gmix_d = kb.dram_in("gmix", [128, NCH])
    ident_d = kb.dram_in("ident", [128, 128])
    out_d = kb.dram_out("out", [OWN, D])

    ident = kb.sb([128, 128], F32, "ident")
    identb = kb.sb([128, 128], BF16, "identb")
    gmix = kb.sb([128, NCH], F32, "gmix")
    kb.dma(ident[:, :], ident_d[:, :], [ident_d], [ident])
    kb.dma(gmix[:, :], gmix_d[:, :], [gmix_d], [gmix])
    kb.v("dve", "tensor_copy", [ident], [identb], out=identb[:, :], in_=ident[:, :])

    win_d = kb.dram_in("w_in", [D, INW])
    kb.from_top = True
    wr_pre = rwkv_load_w(kb, {"w_in": win_d})
    kb.from_top = False
    mA = kb.mark()
    uT_d = kb.dram_tmp("uT_scr", [128, NCH, L], BF16)
    xts = [kb.sb([128, D], F32, "xt") for _ in range(2)]
    xss = [kb.sb([128, D], BF16, "xs") for _ in range(2)]
    junk = kb.sb([128, D], BF16, "junk")
    sss = [kb.sb([128, 4], F32, "ss") for _ in range(2)]
    uts = [kb.sb([128, NCH, 128], BF16, "ut") for _ in range(2)]

    def phase_a_tile(i, t0, n):
        xt = xts[i % 2]
        xs = xss[i % 2]
        ss = sss[i % 2]
        ut = uts[i % 2]
        kb.dma(xt[0:n, :], h_d[t0:t0 + n, :], [h_d], [xt])
        kb.act(junk[0:n, :], xt[0:n, :], AF.Square, [xt], [junk, ss], accum_out=ss[0:n, 0:1])
        kb.act(ss[0:n, 1:2], ss[0:n, 0:1], AF.Sqrt, [ss], [ss], scale=1.0 / D, bias=EPS)
        kb.v("dve", "reciprocal", [ss], [ss], out=ss[0:n, 2:3], in_=ss[0:n, 1:2])
        kb.v("dve", "tensor_scalar", [xt, ss], [xs], out=xs[0:n, :], in0=xt[0:n, :],
             scalar1=ss[0:n, 2:3], scalar2=None, op0=ALU.mult)
        pb = kb.bank()
        pv = pb.t.bitcast(BF16).rearrange("p (c t) -> p c t", c=NCH)
        for c in range(NCH):
            kb.tr(pv[:, c, 0:n], xs[0:n, c * 128:(c + 1) * 128], identb[0:n, 0:n], [xs, identb], [pb])
        kb.v("dve", "tensor_tensor", [pb, gmix], [ut], out=ut[:, :, 0:n], in0=pv[:, :, 0:n],
             in1=gmix[:, :].unsqueeze(2).to_broadcast([128, NCH, n]), op=ALU.mult)
        kb.dma(uT_d.t[:, :, t0:t0 + n], ut[:, :, 0:n], [ut], [uT_d])
        return ut

    ropeC_d = kb.dram_in("ropeC", [L, 64])
    ropeS_d = kb.dram_in("ropeS", [L, 64])
    qg_d = kb.dram_in("qg", [128, 64])
    kg_d = kb.dram_in("kg", [128, 64])
    NT = 33
    NQT = 17
    mB0 = kb.mark()
    y_b = kb.sb([128, NQT, 512], BF16, "y_b")
    mB = kb.mark()
    wqkv = kb.sb([128, NCH, 768], BF16, "wqkv")
    kb.dma(wqkv[:, :, :], win_d.t[:, 1952:2720].rearrange("(c p) n -> p c n", p=128), [win_d], [wqkv], q="pool")
    rcs = [kb.sb([128, 64], F32, "rc") for _ in range(2)]
    rss = [kb.sb([128, 64], F32, "rs") for _ in range(2)]
    qgb = kb.sb([128, 64], F32, "qgb")
    kgb = kb.sb([128, 64], F32, "kgb")
    kb.dma(qgb[:, :], qg_d[:, :], [qg_d], [qgb])
    kb.dma(kgb[:, :], kg_d[:, :], [kg_d], [kgb])
    QT = kb.sb([64, 8, OWN], BF16, "QT")
    KT = kb.sb([64, 2, L], BF16, "KT")
    V1 = kb.sb([128, NT, 2, 65], BF16, "V1")
    kb.v("pool", "memset", [], [V1], V1[:, :, :, 64:65], 1.0)
    sqs = [kb.sb([128, 640], F32, "sq") for _ in range(1)]
    qns = [kb.sb([128, 640], F32, "qn") for _ in range(1)]
    t1s = [kb.sb([128, 640], F32, "t1") for _ in range(1)]
    t2s = [kb.sb([128, 640], F32, "t2") for _ in range(1)]
    qrs = [kb.sb([128, 640], BF16, "qr") for _ in range(2)]
    sms = [kb.sb([128, 32], F32, "sm") for _ in range(2)]
    for i, (t0, n) in enumerate(tiles_of(L, 128)):
        nq = max(0, min(n, OWN - t0))
        ut = phase_a_tile(i, t0, n)
        sq, qn, t1, t2, qr, sm = sqs[0], qns[0], t1s[0], t2s[0], qrs[i % 2], sms[i % 2]
        pq = kb.bank()
        pkv = kb.bank()
        rc, rs = rcs[i % 2], rss[i % 2]
        kb.dma(rc[0:n, :], ropeC_d.t[t0:t0 + n, :], [ropeC_d], [rc])
        kb.dma(rs[0:n, :], ropeS_d.t[t0:t0 + n, :], [ropeS_d], [rs])
        for c in range(NCH):
            if nq:
                kb.mm(pq[0:n, 0:512], ut[:, c, 0:n], wqkv[:, c, 0:512], [ut, wqkv], [pq], start=c == 0, stop=c == NCH - 1)
            kb.mm(pkv[0:n, 0:256], ut[:, c, 0:n], wqkv[:, c, 512:768], [ut, wqkv], [pkv], start=c == 0, stop=c == NCH - 1)
        h0 = 0 if nq else 8
        c0 = h0 * 64
        nh = 10 - h0
        if nq:
            kb.act(sq[0:n, 0:512], pq[0:n, 0:512], AF.Square, [pq], [sq])
        kb.act(sq[0:n, 512:640], pkv[0:n, 0:128], AF.Square, [pkv], [sq])
        kb.v("dve", "tensor_reduce", [sq], [sm], out=sm[0:n, h0:10],
             in_=sq[0:n, c0:640].rearrange("p (h c) -> p h c", c=64), axis=AX.X, op=ALU.add)
        kb.act(sm[0:n, 10 + h0:20], sm[0:n, h0:10], AF.Sqrt, [sm], [sm], scale=1.0 / 64, bias=EPS)
        kb.v("dve", "reciprocal", [sm], [sm], out=sm[0:n, 20 + h0:30], in_=sm[0:n, 10 + h0:20])
        if nq:
            kb.v("dve", "tensor_tensor", [pq, sm], [qn], out=qn[0:n, 0:512].rearrange("p (h c) -> p h c", c=64),
                 in0=pq[0:n, 0:512].rearrange("p (h c) -> p h c", c=64),
                 in1=sm[0:n, 20:28].unsqueeze(2).to_broadcast([n, 8, 64]), op=ALU.mult)
            kb.v("dve", "tensor_tensor", [qn, qgb], [qn], out=qn[0:n, 0:512].rearrange("p (h c) -> p h c", c=64),
                 in0=qn[0:n, 0:512].rearrange("p (h c) -> p h c", c=64),
                 in1=qgb[0:n, :].unsqueeze(1).to_broadcast([n, 8, 64]), op=ALU.mult)
        kb.v("dve", "tensor_tensor", [pkv, sm], [qn], out=qn[0:n, 512:640].rearrange("p (h c) -> p h c", c=64),
             in0=pkv[0:n, 0:128].rearrange("p (h c) -> p h c", c=64),
             in1=sm[0:n, 28:30].unsqueeze(2).to_broadcast([n, 2, 64]), op=ALU.mult)
        kb.v("dve", "tensor_tensor", [qn, kgb], [qn], out=qn[0:n, 512:640].rearrange("p (h c) -> p h c", c=64),
             in0=qn[0:n, 512:640].rearrange("p (h c) -> p h c", c=64),
             in1=kgb[0:n, :].unsqueeze(1).to_broadcast([n, 2, 64]), op=ALU.mult)
        kb.v("dve", "tensor_tensor", [qn, rc], [t1], out=t1[0:n, c0:640].rearrange("p (h c) -> p h c", c=64),
             in0=qn[0:n, c0:640].rearrange("p (h c) -> p h c", c=64),
             in1=rc[0:n, :].unsqueeze(1).to_broadcast([n, nh, 64]), op=ALU.mult)
        for hf in range(2):
            o_v = t2[0:n, c0:640].rearrange("p (h a x f) -> p h a x f", a=2, x=2, f=16)[:, :, :, hf, :]
            i_v = qn[0:n, c0:640].rearrange("p (h a x f) -> p h a x f", a=2, x=2, f=16)[:, :, :, 1 - hf, :]
            s_v = rs[0:n, :].rearrange("p (a x f) -> p a x f", a=2, x=2)[:, :, hf, :].unsqueeze(1).to_broadcast([n, nh, 2, 16])
            kb.v("dve", "tensor_tensor", [qn, rs], [t2], out=o_v, in0=i_v, in1=s_v, op=ALU.mult)
        kb.v("dve", "tensor_tensor", [t1, t2], [qr], out=qr[0:n, c0:640], in0=t1[0:n, c0:640], in1=t2[0:n, c0:640], op=ALU.add)
        ptk = kb.bank()
        ptkv = ptk.t.bitcast(BF16).rearrange("p (c t) -> p c t", c=8)
        for g in range(2):
            kb.tr(ptkv[0:64, g, 0:n], qr[0:n, 512 + g * 64:512 + (g + 1) * 64], identb[0:n, 0:n], [qr, identb], [ptk])
        kb.v("dve", "tensor_copy", [ptk], [KT], out=KT[:, :, t0:t0 + n], in_=ptkv[0:64, 0:2, 0:n])
        if nq:
            ptq = kb.bank()
            ptqv = ptq.t.bitcast(BF16).rearrange("p (c t) -> p c t", c=8)
            for hh in range(8):
                kb.tr(ptqv[0:64, hh, 0:n], qr[0:n, hh * 64:(hh + 1) * 64], identb[0:n, 0:n], [qr, identb], [ptq])
            kb.act(QT[:, :, t0:t0 + nq], ptqv[0:64, :, 0:nq], AF.Copy, [ptq], [QT])
        kb.act(V1[0:n, i, :, 0:64], pkv[0:n, 128:256].rearrange("p (g c) -> p g c", c=64), AF.Copy, [pkv], [V1])
    fw.barrier()
    Es = [kb.sb([128, 4, 128], BF16, "E") for _ in range(3)]
    rcp = kb.sb([128, 8], F32, "rcp")
    Ob = kb.pbanks[0:4]
    Sb = kb.pbanks[4:6]
    ktiles = tiles_of(L, 128)
    steps = []
    for qi, (q0, nq) in enumerate(tiles_of(OWN, 128)):
        for g in range(2):
            for kt, (k0, nk) in enumerate(ktiles):
                steps.append((qi, q0, nq, g, kt, k0, nk))
    NS_ = len(steps)
    pend = {}

    def issue_s(i):
        qi, q0, nq, g, kt, k0, nk = steps[i]
        sT = Sb[i % 2]
        E = Es[i % 3]
        sTv = sT.t[0:nk, 0:4 * nq].rearrange("p (h q) -> p h q", h=4)
        kb.mm(sTv, KT[0:64, g, k0:k0 + nk], QT[0:64, 4 * g:4 * g + 4, q0:q0 + nq], [KT, QT], [sT])
        kb.act(E[0:nk, :, 0:nq], sTv, AF.Exp, [sT], [E], scale=0.125)

    def issue_pv(i):
        qi, q0, nq, g, kt, k0, nk = steps[i]
        E = Es[i % 3]
        for hh in range(4):
            kb.mm(Ob[hh][0:nq, 0:65], E[0:nk, hh, 0:nq], V1[0:nk, kt, g, :], [E, V1], [Ob[hh]],
                  start=kt == 0, stop=kt == len(ktiles) - 1)
        if kt == len(ktiles) - 1:
            for hh in range(4):
                kb.v("dve", "reciprocal", [Ob[hh]], [rcp], out=rcp[0:nq, hh:hh + 1], in_=Ob[hh][0:nq, 64:65])
                hd = 4 * g + hh
                kb.v("dve", "tensor_scalar", [Ob[hh], rcp], [y_b], out=y_b[0:nq, qi, hd * 64:(hd + 1) * 64],
                     in0=Ob[hh][0:nq, 0:64], scalar1=rcp[0:nq, hh:hh + 1], scalar2=None, op0=ALU.mult)

    issue_s(0)
    for i in range(NS_):
        if i + 1 < NS_:
            issue_s(i + 1)
        issue_pv(i)
    kb.dump("y_b", y_b, y_b[:, :, :], [128, NQT, 512], BF16)
    yb_d = kb.dram_tmp("yb_scr", [NQT, 128, 512], BF16)
    kb.dma(yb_d.t[0:16].rearrange("a p c -> p a c"), y_b[:, 0:16, :], [y_b], [yb_d])
    kb.dma(yb_d.t[16, 0:16, :], y_b[0:16, 16, :], [y_b], [yb_d])
    fw.barrier()
    kb.release(mA)
    if upto == "B":
        kb.finish()
        return nc, kb
    names = {"mu0": [128, RW_TILES], "mu1": [128, RW_TILES], "w2": [64, 2, 512], "a2": [64, 2, 512], "w0": [64, 2, 8],
             "a0": [64, 2, 8], "g2a": [128, 512], "g2b": [32, 512], "kkc": [64, 8], "kac": [64, 8], "rkc": [64, 8],
             "lng": [64, 512], "lnb": [64, 512],
             "masks": [64, 4, 64], "w_a": [512, 1024], "w_b": [512, 1024], "w_o": [1024, 1024], "w_ff1": [1024, 4096],
             "w_ff2": [4096, 1024], "gffn": [128, NCH], "gfin": [128, 1024]}
    D_ = {k: kb.dram_in(k, v) for k, v in names.items()}
    D_["w_in"] = win_d
    D_["h"] = h_d
    ybwd_d = kb.dram_tmp("ybwd_scr", [OWN, 512])
    ya_d = kb.dram_tmp("ya_scr", [OWN, 512])
    h2_d = kb.dram_tmp("h2_scr", [OWN, D])
    mC = kb.mark()
    rwkv_phase(kb, uT_d, ident, D_, ybwd_d, ya_d, wr_pre)
    if "ya" in kb.debug:
        o = kb.dram_out("dbg_ya", [OWN, 512])
        kb.out_keys.append(kb.dma(o.t, ya_d.t, [ya_d], [o]))
    fw.barrier()
    kb.release(mC)
    kb.hi = kb.AW
    if upto == "C":
        kb.finish()
        return nc, kb
    kb.from_top = True
    W1_pre = wload(kb, D_["w_ff1"], 1024, 0, 4096, "W1")
    kb.from_top = False
    merge_phase(kb, uT_d, identb, D_, ya_d, yb_d, h2_d)
    if "h2" in kb.debug:
        o = kb.dram_out("dbg_h2", [OWN, D])
        kb.out_keys.append(kb.dma(o.t, h2_d.t, [h2_d], [o]))
    fw.barrier()
    kb.release(mA)
    ffn_phase(kb, identb, D_, h2_d, out_d, W1_pre)

    kb.finish()
    return nc, kb


def _rope_tables():
    inv = (10000.0 ** (-np.arange(16, dtype=np.float32) * 2.0 / 32)).astype(np.float32)
    rows = (np.arange(64, dtype=np.float32)[:, None] * inv).astype(np.float32)
    cols = (np.arange(64, dtype=np.float32)[:, None] * inv).astype(np.float32)
    ang = np.zeros((L, 2, 16), np.float32)
    grid = np.stack([np.broadcast_to(rows[:, None, :], (64, 64, 16)),
                     np.broadcast_to(cols[None, :, :], (64, 64, 16))], axis=2).reshape(4096, 2, 16)
    ang[16:] = grid
    c, s_ = np.cos(ang).astype(np.float32), np.sin(ang).astype(np.float32)
    cosf = np.stack([c, c], axis=2).reshape(L, 64)
    sinf = np.stack([-s_, s_], axis=2).reshape(L, 64)
    return cosf, sinf


def prep_inputs(inputs, core):
    b, s = core // 2, core % 2
    x = np.asarray(inputs["x"], np.float32)
    meta = np.asarray(inputs["meta_tokens"], np.float32)
    hseq = np.concatenate([meta, x[b]], axis=0)
    if s == 1:
        hseq = hseq[::-1]
    m = {}
    m["h"] = np.ascontiguousarray(hseq)
    m["gmix"] = np.ascontiguousarray(np.asarray(inputs["mix_norm_g"], np.float32)[0].reshape(NCH, 128).T)
    m["ident"] = np.eye(128, dtype=np.float32)
    w_in = np.asarray(inputs["w_in"], np.float32)[0]
    if s == 1:
        w_in = w_in.copy()
        for base in (1536, 1664):
            a = w_in[:, base:base + 64].copy()
            w_in[:, base:base + 64] = w_in[:, base + 64:base + 128]
            w_in[:, base + 64:base + 128] = a
    m["w_in"] = np.ascontiguousarray(w_in)
    cosf, sinf = _rope_tables()
    if s == 1:
        cosf, sinf = cosf[::-1], sinf[::-1]
    m["ropeC"] = np.ascontiguousarray(cosf)
    m["ropeS"] = np.ascontiguousarray(sinf)
    G = lambda k: np.asarray(inputs[k], np.float32)[0]
    sh = G("rwkv_shift")
    w2, w0, a2, a0 = G("decay_w2"), G("decay_w0"), G("icl_a2"), G("icl_a0")
    if s == 1:
        sh = sh[::-1].copy()
        for base in (1536, 1664):
            a = sh[:, base:base + 64].copy()
            sh[:, base:base + 64] = sh[:, base + 64:base + 128]
            sh[:, base + 64:base + 128] = a
        w2, w0, a2, a0 = w2[::-1], w0[::-1], a2[::-1], a0[::-1]
    for nm, row in (("mu0", 0), ("mu1", 1)):
        arr = np.zeros((128, RW_TILES), np.float32)
        for j, (c0, w) in enumerate(RW_COLS):
            arr[:w, j] = sh[row, c0:c0 + w]
        m[nm] = arr
    m["w2"] = np.ascontiguousarray(w2.transpose(1, 0, 2))
    m["a2"] = np.ascontiguousarray(a2.transpose(1, 0, 2))
    m["w0"] = np.ascontiguousarray(w0.reshape(2, 8, 64).transpose(2, 0, 1))
    m["a0"] = np.ascontiguousarray(a0.reshape(2, 8, 64).transpose(2, 0, 1))
    g2 = G("gate_w2")
    m["g2a"] = np.ascontiguousarray(g2[0:128]); m["g2b"] = np.ascontiguousarray(g2[128:160])
    m["kkc"] = np.ascontiguousarray(G("k_k").reshape(8, 64).T)
    m["kac"] = np.ascontiguousarray(G("k_a").reshape(8, 64).T)
    m["rkc"] = np.ascontiguousarray(G("r_k").reshape(8, 64).T)
    m["lng"] = np.ascontiguousarray(np.tile(G("lnx_g")[None, :], (64, 1)))
    m["lnb"] = np.ascontiguousarray(np.tile(G("lnx_b")[None, :], (64, 1)))
    si, ti = np.meshgrid(np.arange(64), np.arange(64), indexing="ij")
    m["masks"] = np.ascontiguousarray(np.stack([si < ti, si <= ti, si > ti, si >= ti], 1).astype(np.float32))
    m["w_a"] = np.ascontiguousarray(G("w_branch_rwkv")); m["w_b"] = np.ascontiguousarray(G("w_branch_attn"))
    m["w_o"] = np.ascontiguousarray(G("w_out")); m["w_ff1"] = np.ascontiguousarray(G("w_ff1")); m["w_ff2"] = np.ascontiguousarray(G("w_ff2"))
    m["gffn"] = np.ascontiguousarray(G("ffn_norm_g").reshape(NCH, 128).T)
    m["gfin"] = np.ascontiguousarray(np.tile(np.asarray(inputs["final_norm_g"], np.float32)[None, :], (128, 1)))
    m["qg"] = np.ascontiguousarray(np.tile(np.asarray(inputs["q_norm_g"], np.float32)[0][None, :], (128, 1)))
    m["kg"] = np.ascontiguousarray(np.tile(np.asarray(inputs["k_norm_g"], np.float32)[0][None, :], (128, 1)))
    return m


_CACHE = {}


def kernel(**inputs):
    from concourse.bass_utils import run_bass_kernel_spmd
    if "nc" not in _CACHE:
        _CACHE["nc"] = build()
    nc, kb = _CACHE["nc"]
    in_maps = [prep_inputs(inputs, c) for c in range(8)]
    res = run_bass_kernel_spmd(nc, in_maps, core_ids=list(range(8)))
    out = np.zeros((4, 4096, D), np.float32)
    for c in range(8):
        b, s = c // 2, c % 2
        o = np.asarray(res.results[c]["out"])
        if s == 0:
            out[b, 0:2048] = o[16:2064]
        else:
            out[b, 2048:4096] = o[0:2048][::-1]
    return out
```
